# Optimizing a Trainium2 kernel written in Bass

```python
import functools
import jax, jax.numpy as jnp
from jax import lax
import numpy as np

D_MODEL = 1024
BATCH = 16
SEQ = 2048
DEPTH = 1
DEC_BATCH = 128
DEC_SEQ = 1
PAST_LEN = 16384
PAGE_SIZE = 128

N_HEADS = 16
N_KV_HEADS = 4
HEAD_DIM = 64
GROUP = N_HEADS // N_KV_HEADS
D_ATTN = N_HEADS * HEAD_DIM
D_KV = N_KV_HEADS * HEAD_DIM
WINDOW = 128
BLOCK = WINDOW
D_RNN = D_MODEL
N_RNN_BLOCKS = 16
RNN_BLOCK = D_RNN // N_RNN_BLOCKS
CONV_WIDTH = 4
LRU_C = 8.0
D_FF = 4 * D_MODEL
SPLITS = [D_ATTN, D_ATTN + D_KV, D_ATTN + 2 * D_KV, D_ATTN + 2 * D_KV + D_RNN,
          D_ATTN + 2 * D_KV + 2 * D_RNN, D_ATTN + 2 * D_KV + 2 * D_RNN + D_MODEL]
D_IN = D_ATTN + 2 * D_KV + 2 * D_RNN + 2 * D_MODEL
EPS = 1e-6

kernel_name = 'hybrid_swa_sink_rglru_decode_step'


def rmsnorm(x, g):
    xf = x.astype(jnp.float32)
    y = xf * lax.rsqrt(jnp.mean(xf * xf, axis=-1, keepdims=True) + EPS) * g.astype(jnp.float32)
    return y.astype(x.dtype)


def sink_softmax(s, mask, sink):
    s = jnp.where(mask, s, -jnp.inf)
    m = jnp.maximum(jnp.max(s, axis=-1, keepdims=True), sink)
    p = jnp.exp(s - m)
    return p / (jnp.sum(p, axis=-1, keepdims=True) + jnp.exp(sink - m))


def window_attention_prompt(q, k, v, sinks):
    B, T = q.shape[:2]
    nb = T // BLOCK
    qb = q.reshape(B, nb, BLOCK, N_KV_HEADS, GROUP, HEAD_DIM)
    kb = k.reshape(B, nb, BLOCK, N_KV_HEADS, HEAD_DIM)
    vb = v.reshape(B, nb, BLOCK, N_KV_HEADS, HEAD_DIM)
    pad = ((0, 0), (1, 0), (0, 0), (0, 0), (0, 0))
    kk = jnp.concatenate([jnp.pad(kb, pad)[:, :-1], kb], axis=2)
    vv = jnp.concatenate([jnp.pad(vb, pad)[:, :-1], vb], axis=2)
    s = jnp.einsum('bnqhgd,bnkhd->bnhgqk', qb, kk,
                   preferred_element_type=jnp.float32) * (HEAD_DIM ** -0.5)
    qi = jnp.arange(BLOCK)[:, None]
    kj = jnp.arange(2 * BLOCK)[None, :]
    diff = qi + BLOCK - kj
    band = (diff >= 0) & (diff <= WINDOW)
    has_prev = (jnp.arange(nb)[:, None] > 0) | (kj >= BLOCK)
    mask = (band[None] & has_prev[:, None, :])[None, :, None, None]
    sink = sinks.astype(jnp.float32).reshape(N_KV_HEADS, GROUP)[None, None, :, :, None, None]
    p = sink_softmax(s, mask, sink)
    o = jnp.einsum('bnhgqk,bnkhd->bnqhgd', p.astype(v.dtype), vv)
    return o.reshape(B, T, D_ATTN), k[:, -WINDOW:], v[:, -WINDOW:]


def window_attention_sample(k_buf, v_buf, q, k, v, sinks):
    B, T = q.shape[:2]
    qh = q.reshape(B, T, N_KV_HEADS, GROUP, HEAD_DIM)
    kk = jnp.concatenate([k_buf.astype(k.dtype), k], axis=1)
    vv = jnp.concatenate([v_buf.astype(v.dtype), v], axis=1)
    s = jnp.einsum('bqhgd,bkhd->bhgqk', qh, kk,
                   preferred_element_type=jnp.float32) * (HEAD_DIM ** -0.5)
    diff = jnp.arange(T)[:, None] + WINDOW - jnp.arange(WINDOW + T)[None, :]
    mask = ((diff >= 0) & (diff <= WINDOW))[None, None, None]
    sink = sinks.astype(jnp.float32).reshape(N_KV_HEADS, GROUP)[None, :, :, None, None]
    p = sink_softmax(s, mask, sink)
    o = jnp.einsum('bhgqk,bkhd->bqhgd', p.astype(v.dtype), vv)
    return o.reshape(B, T, D_ATTN), kk[:, -WINDOW:], vv[:, -WINDOW:]


def causal_conv(u, prev, conv_w, conv_b):
    T = u.shape[1]
    up = jnp.concatenate([prev.astype(u.dtype), u], axis=1)
    out = conv_b + sum(up[:, i:i + T] * conv_w[i] for i in range(CONV_WIDTH))
    return out, up[:, -(CONV_WIDTH - 1):]


def rg_lru(x, h0, positions, w_a, b_a, w_x, b_x, lam):
    B, T, _ = x.shape
    xf = x.astype(jnp.float32)
    xb = xf.reshape(B, T, N_RNN_BLOCKS, RNN_BLOCK)
    r = jax.nn.sigmoid(jnp.einsum('btnc,ncd->btnd', xb, w_a.astype(jnp.float32)).reshape(B, T, D_RNN)
                       + b_a.astype(jnp.float32))
    i = jax.nn.sigmoid(jnp.einsum('btnc,ncd->btnd', xb, w_x.astype(jnp.float32)).reshape(B, T, D_RNN)
                       + b_x.astype(jnp.float32))
    log_a = -LRU_C * r * jax.nn.softplus(-lam.astype(jnp.float32))
    a = jnp.exp(log_a)
    mult = jnp.where((positions == 0)[None, :, None], 1.0, jnp.sqrt(-jnp.expm1(2.0 * log_a)))
    b = mult * i * xf
    b = b.at[:, 0].add(a[:, 0] * h0.astype(jnp.float32))

    def combine(left, right):
        a1, b1 = left
        a2, b2 = right
        return a1 * a2, a2 * b1 + b2

    _, h = lax.associative_scan(combine, (a, b), axis=1)
    return h, h[:, -1]


def hybrid_layer(x, positions, attn_fn, conv_prev, h0, w_in, w_out, sinks, conv_w, conv_b,
                 lru_w_a, lru_b_a, lru_w_x, lru_b_x, lru_lambda, w_up, w_down,
                 g_pre_mix, g_post_mix, g_pre_ffn, g_post_ffn):
    B, T = x.shape[:2]
    xn = rmsnorm(x, g_pre_mix)
    z = jnp.einsum('btd,de->bte', xn, w_in)
    q, k, v, u, g_branch, gate_a, gate_r = jnp.split(z, SPLITS, axis=-1)
    attn_out, k_state, v_state = attn_fn(q.reshape(B, T, N_HEADS, HEAD_DIM),
                                         k.reshape(B, T, N_KV_HEADS, HEAD_DIM),
                                         v.reshape(B, T, N_KV_HEADS, HEAD_DIM), sinks)
    uc, conv_state = causal_conv(u, conv_prev, conv_w, conv_b)
    h, h_last = rg_lru(uc, h0, positions, lru_w_a, lru_b_a, lru_w_x, lru_b_x, lru_lambda)
    rnn_out = jax.nn.gelu(g_branch.astype(jnp.float32)) * h
    merged = (jax.nn.sigmoid(gate_a.astype(jnp.float32)) * attn_out.astype(jnp.float32)
              + jax.nn.sigmoid(gate_r.astype(jnp.float32)) * rnn_out).astype(x.dtype)
    mix = jnp.einsum('btd,de->bte', merged, w_out)
    x = x + rmsnorm(mix, g_post_mix)
    hn = rmsnorm(x, g_pre_ffn)
    f = jnp.einsum('btf,fd->btd', jnp.square(jax.nn.relu(jnp.einsum('btd,df->btf', hn, w_up))), w_down)
    x = x + rmsnorm(f, g_post_ffn)
    return x, k_state, v_state, conv_state, h_last


def setup_inputs(seed: int = 0) -> dict:
    key = jax.random.key(seed)
    ks = jax.random.split(key, 24)
    f32 = jnp.float32
    nrm = lambda k, shape, scale: jax.random.normal(k, shape, f32) * scale
    a0 = jax.random.uniform(ks[14], (DEPTH, D_RNN), f32, 0.9, 0.999)
    return {
        'x_prompt': nrm(ks[0], (BATCH, SEQ, D_MODEL), 1.0),
        'x_sample': nrm(ks[1], (DEC_BATCH, DEC_SEQ, D_MODEL), 1.0),
        'cache_k_win': nrm(ks[2], (DEPTH, DEC_BATCH, WINDOW, N_KV_HEADS, HEAD_DIM), 1.0),
        'cache_v_win': nrm(ks[3], (DEPTH, DEC_BATCH, WINDOW, N_KV_HEADS, HEAD_DIM), 1.0),
        'state_conv': nrm(ks[4], (DEPTH, DEC_BATCH, CONV_WIDTH - 1, D_RNN), 1.0),
        'state_lru': nrm(ks[5], (DEPTH, DEC_BATCH, D_RNN), 0.5),
        'w_in': nrm(ks[6], (DEPTH, D_MODEL, D_IN), D_MODEL ** -0.5),
        'w_out': nrm(ks[7], (DEPTH, D_MODEL, D_MODEL), D_MODEL ** -0.5),
        'sinks': nrm(ks[8], (DEPTH, N_HEADS), 0.5),
        'conv_w': nrm(ks[9], (DEPTH, CONV_WIDTH, D_RNN), CONV_WIDTH ** -0.5),
        'conv_b': nrm(ks[10], (DEPTH, D_RNN), 0.02),
        'lru_w_a': nrm(ks[11], (DEPTH, N_RNN_BLOCKS, RNN_BLOCK, RNN_BLOCK), RNN_BLOCK ** -0.5),
        'lru_b_a': nrm(ks[12], (DEPTH, D_RNN), 0.02),
        'lru_w_x': nrm(ks[13], (DEPTH, N_RNN_BLOCKS, RNN_BLOCK, RNN_BLOCK), RNN_BLOCK ** -0.5),
        'lru_b_x': nrm(ks[15], (DEPTH, D_RNN), 0.02),
        'lru_lambda': jnp.log(a0) - jnp.log1p(-a0),
        'w_up': nrm(ks[16], (DEPTH, D_MODEL, D_FF), D_MODEL ** -0.5),
        'w_down': nrm(ks[17], (DEPTH, D_FF, D_MODEL), D_FF ** -0.5),
        'g_pre_mix': 1.0 + nrm(ks[18], (DEPTH, D_MODEL), 0.02),
        'g_post_mix': 1.0 + nrm(ks[19], (DEPTH, D_MODEL), 0.02),
        'g_pre_ffn': 1.0 + nrm(ks[20], (DEPTH, D_MODEL), 0.02),
        'g_post_ffn': 1.0 + nrm(ks[21], (DEPTH, D_MODEL), 0.02),
    }


def reference(x_prompt, x_sample, cache_k_win, cache_v_win, state_conv, state_lru,
              w_in, w_out, sinks, conv_w, conv_b, lru_w_a, lru_b_a, lru_w_x, lru_b_x, lru_lambda,
              w_up, w_down, g_pre_mix, g_post_mix, g_pre_ffn, g_post_ffn):
    B, T = x_prompt.shape[:2]
    DB, DT = x_sample.shape[:2]
    pos_prompt = jnp.arange(T)
    pos_sample = PAST_LEN + jnp.arange(DT)
    yp, ys = x_prompt, x_sample
    kp, vp, cp, hp, kq, vq, cq, hq = [], [], [], [], [], [], [], []
    for l in range(DEPTH):
        lp = (w_in[l], w_out[l], sinks[l], conv_w[l], conv_b[l], lru_w_a[l], lru_b_a[l],
              lru_w_x[l], lru_b_x[l], lru_lambda[l], w_up[l], w_down[l],
              g_pre_mix[l], g_post_mix[l], g_pre_ffn[l], g_post_ffn[l])
        yp, k_s, v_s, c_s, h_s = hybrid_layer(
            yp, pos_prompt, window_attention_prompt,
            jnp.zeros((B, CONV_WIDTH - 1, D_RNN), yp.dtype), jnp.zeros((B, D_RNN), jnp.float32), *lp)
        kp.append(k_s); vp.append(v_s); cp.append(c_s); hp.append(h_s)
        ys, k_s, v_s, c_s, h_s = hybrid_layer(
            ys, pos_sample, functools.partial(window_attention_sample, cache_k_win[l], cache_v_win[l]),
            state_conv[l], state_lru[l], *lp)
        kq.append(k_s); vq.append(v_s); cq.append(c_s); hq.append(h_s)
    k_win_prompt = jnp.stack(kp)
    v_win_prompt = jnp.stack(vp)
    conv_prompt = jnp.stack(cp)
    lru_prompt = jnp.stack(hp)
    k_win_sample = jnp.stack(kq)
    v_win_sample = jnp.stack(vq)
    conv_sample = jnp.stack(cq)
    lru_sample = jnp.stack(hq)
    return (yp, ys, k_win_prompt, v_win_prompt, conv_prompt, lru_prompt,
            k_win_sample, v_win_sample, conv_sample, lru_sample)
```

```python
from contextlib import ExitStack
import numpy as np
import concourse.bass as bass
import concourse.mybir as mybir
from concourse.bass_utils import run_bass_kernel_spmd

F32 = mybir.dt.float32
BF16 = mybir.dt.bfloat16
AF = mybir.ActivationFunctionType
ALU = mybir.AluOpType
AX = mybir.AxisListType

NCORES = 8
D = 1024
SEQ = 2048
NSEQ = 2
NSAMP = 16
TT = 512
NPIECE = 29
NR = 3
EPS = 1e-6
NEG = -30000.0
GC1 = 0.7978845608028654
GC2 = 0.044715
STOP = 99


class Buf:
    __slots__ = ("name", "w", "r", "dsem", "dcnt", "excl")

    def __init__(self, name, excl=False):
        self.name = name
        self.excl = excl
        self.w = None
        self.r = []
        self.dsem = None
        self.dcnt = 0


class SemC:
    __slots__ = ("s", "cnt", "is_dma")

    def __init__(self, s, is_dma):
        self.s = s
        self.cnt = 0
        self.is_dma = is_dma


class Ctx:
    recording = True
    ops = []
    cap = []
    pend = None


class Op:
    __slots__ = ("eng", "calls", "reads", "writes", "dur", "ts", "kind", "dma", "idx", "lat")


def _free_size(ap):
    try:
        return int(ap.free_size())
    except Exception:
        return 512


def _estimate(engname, calls):
    d = 0.0
    ts = None
    for (m, a, k) in calls:
        name = getattr(m, "__name__", "")
        if engname == "PE":
            ap = k.get("rhs", k.get("identity"))
            n = _free_size(ap) if ap is not None else 128
            d += (max(n, 48) + 16) / 2200.0
        elif engname == "ACT":
            ap = k.get("in_")
            n = _free_size(ap) if ap is not None else 512
            d += 0.12 + n / 1100.0
            f = k.get("func")
            if f == AF.Exp:
                ts = "E"
            elif f == AF.Tanh:
                ts = "T"
            elif f == AF.Sqrt:
                ts = "S"
            elif f == AF.Ln:
                ts = "L"
            elif f == AF.Gelu_apprx_tanh:
                ts = "G"
        elif engname == "DVE":
            ap = k.get("in0", k.get("in_", k.get("data0", k.get("out", k.get("ap")))))
            if ap is None and a:
                ap = a[0]
            n = _free_size(ap) if ap is not None else 512
            mult = 2.0 if "scan" in name else 1.0
            d += 0.1 + mult * n / 760.0
        else:
            ap = k.get("in0", k.get("in_", k.get("out", k.get("ap"))))
            if ap is None and a:
                ap = a[0]
            n = _free_size(ap) if ap is not None else 512
            d += 0.25 + n / 450.0
    return d, ts


class _Dummy:
    def then_inc(self, *a, **k):
        return self


class Rec:
    def __init__(self, real):
        object.__setattr__(self, "_real", real)

    def __getattr__(self, name):
        real_m = getattr(object.__getattribute__(self, "_real"), name)

        def f(*a, **k):
            Ctx.cap.append((real_m, a, k))
            return _Dummy()
        f.__name__ = name
        return f


class NcProxy:
    def __init__(self, real):
        self._real = real
        self.tensor = Rec(real.tensor)
        self.vector = Rec(real.vector)
        self.scalar = Rec(real.scalar)
        self.gpsimd = Rec(real.gpsimd)

    def __getattr__(self, name):
        return getattr(self._real, name)


class Eng:
    def __init__(self, h, semc, name):
        self.h = h
        self.sc = semc
        self.name = name
        self.waited = {}

    def wait(self, tok):
        sc, val = tok
        if sc.is_dma:
            val = sc.cnt
        if self.waited.get(id(sc), 0) >= val:
            return
        self.h.wait_ge(sc.s, val)
        self.waited[id(sc)] = val

    def _deps(self, reads, writes):
        for b in reads:
            if b.w is not None:
                self.wait(b.w)
        for b in writes:
            if b.w is not None:
                self.wait(b.w)
            for t in b.r:
                self.wait(t)

    def op(self, fn, reads=(), writes=(), sig=True):
        del Ctx.cap[:]
        fn()
        calls = list(Ctx.cap)
        reads = list(reads)
        writes = list(writes)
        if Ctx.pend is not None:
            pc, pr, pw = Ctx.pend
            calls = pc + calls
            reads = pr + [b for b in reads if b not in pr]
            writes = pw + [b for b in writes if b not in pw]
            Ctx.pend = None
        if not sig:
            Ctx.pend = (calls, reads, writes)
            return
        ex = [b for b in reads if b.excl]
        if ex:
            writes = writes + [b for b in ex if b not in writes]
            reads = [b for b in reads if not b.excl]
        o = Op()
        o.eng, o.calls, o.reads, o.writes, o.kind, o.dma = self, calls, reads, writes, "c", None
        o.dur, o.ts = _estimate(self.name, calls)
        o.lat = 0.0
        o.idx = len(Ctx.ops)
        Ctx.ops.append(o)

    def emit(self, o):
        self._deps(o.reads, o.writes)
        inst = None
        for (m, a, k) in o.calls:
            inst = m(*a, **k)
        inst.then_inc(self.sc.s, 1)
        self.sc.cnt += 1
        tok = (self.sc, self.sc.cnt)
        for b in o.reads:
            b.r.append(tok)
        for b in o.writes:
            b.w = tok
            b.r = []


def schedule(ops):
    n = len(ops)
    if not hasattr(schedule, 'n_setup'):
        schedule.n_setup = 0
    lastw, readers = {}, {}
    preds = [set() for _ in range(n)]
    for i, o in enumerate(ops):
        for b in o.reads:
            w = lastw.get(id(b))
            if w is not None:
                preds[i].add(w)
        for b in o.writes:
            w = lastw.get(id(b))
            if w is not None:
                preds[i].add(w)
            for r in readers.get(id(b), ()):
                preds[i].add(r)
        for b in o.reads:
            readers.setdefault(id(b), []).append(i)
        for b in o.writes:
            lastw[id(b)] = i
            readers[id(b)] = []
        preds[i].discard(i)
    succs = [[] for _ in range(n)]
    npred = [0] * n
    for i in range(n):
        npred[i] = len(preds[i])
        for p in preds[i]:
            succs[p].append(i)
    ready_time = [0.0] * n
    finish = [0.0] * n
    start = [0.0] * n
    engs = {}
    for o in ops:
        engs.setdefault(id(o.eng), o.eng)
    eng_free = {k: 0.0 for k in engs}
    ready = {k: [] for k in engs}
    cur_ts = [None]
    for i in range(n):
        if npred[i] == 0:
            ready[id(ops[i].eng)].append(i)
    done = 0
    order = []
    while done < n:
        best = None
        for k, lst in ready.items():
            if not lst:
                continue
            ef = eng_free[k]
            isact = engs[k].name == "ACT"
            for i in lst[:32]:
                st = max(ready_time[i], ef)
                pen = 0.0
                if isact and ops[i].ts is not None:
                    t_ = ops[i].ts
                    if t_ == "T":
                        if cur_ts[0] not in ("E", "G"):
                            pen = 1.3
                    elif t_ != cur_ts[0]:
                        pen = 1.3
                if i < schedule.n_setup:
                    st, pen = 0.0, 0.0
                key = (st + pen, i)
                if best is None or key < best[0]:
                    best = (key, i, k, st + pen)
        _, i, k, st = best
        o = ops[i]
        ready[k].remove(i)
        start[i] = st
        if o.kind == "dma":
            eng_free[k] = st + o.dur
            finish[i] = st + o.dur + o.lat
        else:
            finish[i] = st + o.dur
            eng_free[k] = finish[i]
            if o.eng.name == "ACT" and o.ts is not None:
                if o.ts == "T":
                    if cur_ts[0] not in ("E", "G"):
                        cur_ts[0] = "E"
                else:
                    cur_ts[0] = o.ts
        order.append(i)
        done += 1
        for sidx in succs[i]:
            hop = 0.0 if ops[sidx].eng is o.eng and o.kind != "dma" else 0.25
            rt = finish[i] + hop
            if rt > ready_time[sidx]:
                ready_time[sidx] = rt
            npred[sidx] -= 1
            if npred[sidx] == 0:
                lst = ready[id(ops[sidx].eng)]
                lo, hi = 0, len(lst)
                while lo < hi:
                    mid = (lo + hi) // 2
                    if lst[mid] < sidx:
                        lo = mid + 1
                    else:
                        hi = mid
                lst.insert(lo, sidx)
    schedule.start = start
    schedule.finish = finish
    return order, max(finish) if n else 0.0


def _flat(lst):
    out = []
    for x in lst:
        if isinstance(x, (list, tuple)):
            out.extend(_flat(x))
        elif x is not None:
            out.append(x)
    return out


def build_program(debug=False):
    nc = bass.Bass("TRN2", target_bir_lowering=False)

    def din(name, shape, dt=F32):
        return nc.dram_tensor(name, list(shape), dt, kind="ExternalInput").ap()

    def dout(name, shape, dt=F32):
        return nc.dram_tensor(name, list(shape), dt, kind="ExternalOutput").ap()

    x_p = din("x_prompt", [NSEQ, SEQ, D])
    x_s = din("x_sample", [NSAMP, D])
    ck = din("cache_k", [NSAMP, 128, 256])
    cv = din("cache_v", [NSAMP, 128, 256])
    s_conv = din("state_conv", [NSAMP, 3, D])
    s_lru = din("state_lru", [NSAMP, D])
    w_in = din("w_in", [D, 5632])
    w_out = din("w_out", [D, D])
    sinks = din("sinks", [16])
    conv_w = din("conv_w", [4, D])
    conv_b = din("conv_b", [D])
    lw_a = din("lru_w_a", [16, 64, 64])
    lb_a = din("lru_b_a", [D])
    lw_x = din("lru_w_x", [16, 64, 64])
    lb_x = din("lru_b_x", [D])
    lam = din("lru_lambda", [D])
    w_up = din("w_up", [D, 4096])
    w_down = din("w_down", [4096, D])
    g1 = din("g_pre_mix", [D])
    g2 = din("g_post_mix", [D])
    g3 = din("g_pre_ffn", [D])
    g4 = din("g_post_ffn", [D])
    c_ident = din("c_ident", [128, 128])
    c_mask = din("c_mask", [2, 128, 512])
    c_msamp = din("c_msamp", [16, 512])
    c_sel = din("c_sel", [128, 128])

    y_p = dout("y_prompt", [NSEQ, SEQ, D])
    y_s = dout("y_sample", [NSAMP, D])
    kwp = dout("k_win_prompt", [NSEQ, 128, 256])
    vwp = dout("v_win_prompt", [NSEQ, 128, 256])
    cvp = dout("conv_prompt", [NSEQ, 3, D])
    lrp = dout("lru_prompt", [NSEQ, D])
    kws = dout("k_win_sample", [NSAMP, 128, 256])
    vws = dout("v_win_sample", [NSAMP, 128, 256])
    cvs = dout("conv_sample", [NSAMP, 3, D])
    lrs = dout("lru_sample", [NSAMP, D])

    scr = nc.dram_tensor("wscr", [NPIECE, 128, 8, 512], BF16, kind="Internal").ap()
    if debug:
        dbg_m = dout("dbg_merged", [128, 8, 512], BF16)
        dbg_a = dout("dbg_attn", [128, 8, 512], BF16)
        dbg_x1 = dout("dbg_x1", [128, 4, D])

    with ExitStack() as es:
        def sb(name, shape, dt=F32):
            return es.enter_context(nc.sbuf_tensor(name, list(shape), dt))

        def ps(name, shape, dt=F32):
            return es.enter_context(nc.psum_tensor(name, list(shape), dt))

        nsem = [0]

        def newsem(is_dma):
            nsem[0] += 1
            return SemC(es.enter_context(nc.semaphore("s%d" % nsem[0])), is_dma)

        ring = [sb("ring%d" % i, [128, 8, 512], BF16) for i in range(NR)]
        ringB = [Buf("ring%d" % i) for i in range(NR)]
        wst = sb("wst", [128, 8, 512])
        wstB = Buf("wst")
        xa = [sb("xa%d" % i, [128, D]) for i in range(2)]
        xaB = [Buf("xa") for _ in range(2)]
        xnb = [sb("xnb%d" % i, [128, D], BF16) for i in range(2)]
        xnbB = [Buf("xnb") for _ in range(2)]
        xnT = sb("xnT", [128, 8, TT], BF16)
        xnTB = [Buf("xnT%d" % b) for b in range(4)]
        xt = sb("xt", [128, 4, D])
        xtB = [Buf("xt%d" % b) for b in range(4)]
        kbuf = sb("kbuf", [128, 2, 640], BF16)
        kbufB = Buf("kbuf")
        vbuf = sb("vbuf", [128, 5, 256], BF16)
        vbufB = Buf("vbuf")
        USL = 40
        U = sb("U", [128, USL * 256])
        UB = [Buf("U%d" % i) for i in range(USL)]

        def uview_bf(slot0, nslots):
            return U[:, slot0 * 256:(slot0 + nslots) * 256].bitcast(BF16).rearrange(
                "p (c t) -> p c t", t=512)

        qT = uview_bf(0, 8)
        sa = uview_bf(8, 8)
        merged = uview_bf(16, 8)
        gs = uview_bf(24, 4)
        ubuf = U[:, 28 * 256:28 * 256 + 4 * 516].rearrange("p (c t) -> p c t", t=516)
        hT = uview_bf(0, 32)
        fbuf = U[:, 32 * 256:40 * 256].rearrange("p (b t) -> p b t", t=512)

        def qTB(j): return [UB[j]]
        def saB(c): return [UB[8 + c]]
        def mgB(c): return [UB[16 + c]]
        def gsB(c): return [UB[24 + c % 4]]
        def ubB(c):
            lo = 28 * 1024 + (c % 4) * 2064
            return [UB[i] for i in range(lo // 1024, (lo + 2063) // 1024 + 1)]
        def hTB(fc): return [UB[fc]]
        def fbB(b): return [UB[32 + 2 * b], UB[33 + 2 * b]]

        class Pool_:
            def __init__(self, name, n, dt):
                self.free = [(sb("%s%d" % (name, i), [128, 512], dt), Buf(name)) for i in range(n)]

            def get(self):
                assert self.free, "temp pool exhausted"
                return self.free.pop(0)

            def put(self, t):
                self.free.append(t)

        TF = Pool_("tf", 13, F32)
        TB = Pool_("tb", 7, BF16)
        cnt = {"st": 0, "xa": 0, "pf": 0, "pt": 0, "au": 0}

        NST = 12
        stt = sb("stt", [128, NST, 8])
        sttB = [Buf("st") for _ in range(NST)]

        def gst():
            i = cnt["st"] % NST
            cnt["st"] += 1
            return stt[:, i, :], sttB[i]

        stg = sb("stg", [128, D])
        stgB = Buf("stg")
        kvst = sb("kvst", [128, 512])
        kvstB = Buf("kvst")
        hst = sb("hst", [128, 8])
        hstB = Buf("hst")
        uh = sb("uh", [128, 8, 3])
        uhB = Buf("uh")
        kcf = [sb("kcf%d" % i, [128, 256]) for i in range(2)]
        kcfB = [Buf("kcf") for _ in range(2)]
        kc = [sb("kc%d" % i, [128, 2, 128], BF16) for i in range(2)]
        kcB = [Buf("kc") for _ in range(2)]
        vc = [sb("vc%d" % i, [128, 256], BF16) for i in range(2)]
        vcB = [Buf("vc") for _ in range(2)]
        scT = sb("scT", [128, 8, 48])
        h0T = sb("h0T", [128, 8, 16])
        sampB = Buf("samp")
        hS_t = sb("hS_t", [128, 128])
        hS_B = Buf("hS")
        uS_t = sb("uS_t", [128, 128])
        ssb_t = sb("ssb_t", [128, 8])
        ssb_B = Buf("ssb")
        gbc = sb("gbc", [128, 4, D])
        identf = sb("identf", [128, 128])
        identb = sb("identb", [128, 128], BF16)
        maskb = sb("maskb", [128, 2, 512], BF16)
        msamp = sb("msamp", [128, 256], BF16)
        selb = sb("selb", [128, 128], BF16)
        sink8 = sb("sink8", [8, 2])
        wbd = sb("wbd", [128, 2, 8, 128], BF16)
        cw = sb("cw", [128, 4, 8])
        pv = sb("pv", [128, 8, 8])
        sinkp = sb("sinkp", [128, 16])
        epsT = sb("epsT", [128, 2])
        cI, cM, cG, cS, cV, cW, cE = (Buf("cI"), Buf("cM"), Buf("cG"), Buf("cS"), Buf("cV"), Buf("cW"), Buf("cE"))
        constB = [cI, cM, cG, cS, cV, cW, cE]

        NPF = 6
        pf = [ps("pf%d" % i, [128, 512]) for i in range(NPF)]
        pfB = [Buf("pf%d" % i, excl=True) for i in range(NPF)]
        ptt = [ps("pt%d" % i, [128, 1024], BF16) for i in range(2)]
        ptB = [Buf("pt%d" % i, excl=True) for i in range(2)]

        es.enter_context(nc.Block())

        PE = Eng(nc.tensor, newsem(False), "PE")
        ACT = Eng(nc.scalar, newsem(False), "ACT")
        DVE = Eng(nc.vector, newsem(False), "DVE")
        POOL = Eng(nc.gpsimd, newsem(False), "POOL")
        SP = Eng(nc.sync, newsem(False), "SP")
        nc_real = nc
        nc = NcProxy(nc_real)
        Ctx.recording = True
        Ctx.ops = []
        Ctx.cap = []
        Ctx.pend = None

        all_dma_sems = []

        def dma(q, out, in_, reads=(), writes=(), nonc=False):
            reads = _flat(reads)
            writes = _flat(writes)
            o = Op()
            o.eng, o.calls, o.reads, o.writes, o.kind = q, None, reads, writes, "dma"
            o.dma = (out, in_, nonc)
            o.ts = None
            try:
                nbytes = int(out.nbytes())
            except Exception:
                nbytes = 4096
            cast = (out.dtype != in_.dtype)
            o.dur = (1.0 if q is POOL else 0.15) + (nbytes / 80e3 if cast else 0.0)
            o.lat = 2.0 + nbytes / 150e3
            o.idx = len(Ctx.ops)
            Ctx.ops.append(o)

        def dma_emit(o):
            q = o.eng
            out, in_, nonc = o.dma
            reads, writes = o.reads, o.writes
            q._deps(reads, writes)
            tgt = (writes + reads)[0]
            if tgt.dsem is None:
                tgt.dsem = {}
            if id(q) not in tgt.dsem:
                tgt.dsem[id(q)] = newsem(True)
                all_dma_sems.append(tgt.dsem[id(q)])
            sc = tgt.dsem[id(q)]
            if nonc:
                with nc_real.allow_non_contiguous_dma(reason="small strided"):
                    q.h.dma_start(out=out, in_=in_).then_inc(sc.s, 16)
            else:
                q.h.dma_start(out=out, in_=in_).then_inc(sc.s, 16)
            sc.cnt += 16
            tok = (sc, sc.cnt)
            for b in reads:
                b.r.append(tok)
            for b in writes:
                b.w = tok
                b.r = []

        def A(fn, r=(), w=()):
            return ACT.op(fn, _flat(r), _flat(w))

        def V(fn, r=(), w=()):
            return DVE.op(fn, _flat(r), _flat(w))

        def G(fn, r=(), w=()):
            return POOL.op(fn, _flat(r), _flat(w))

        def P(fn, r=(), w=(), sig=True):
            return PE.op(fn, _flat(r), _flat(w), sig=sig)

        act = nc.scalar.activation
        mm = nc.tensor.matmul

        dma(SP, identf[:, :], c_ident[:, :], writes=[cI])
        dma(POOL, identb[:, :], c_ident[:, :], writes=[cI])
        dma(POOL, maskb[:, :, :], c_mask.rearrange("v p c -> p v c"), writes=[cM])
        G(lambda: nc.gpsimd.memset(msamp[:, :], 0.0), w=[cM])
        dma(POOL, msamp[0:16, :], c_msamp[:, 0:256], writes=[cM])
        dma(POOL, selb[:, :], c_sel[:, :], writes=[cM])
        sk8 = sinks.rearrange("(a g hb) -> hb g a", a=2, g=4, hb=2)
        for hb in range(2):
            dma(SP, sink8[hb * 4:(hb + 1) * 4, :], sk8[hb], writes=[cS], nonc=True)
        for i, g in enumerate((g1, g2, g3, g4)):
            dma(SP, gbc[:, i, :], g.partition_broadcast(128), writes=[cG])
        dma(SP, sinkp[:, :], sinks.partition_broadcast(128), writes=[cS])
        G(lambda: nc.gpsimd.memset(wbd[:, :, :, :], 0.0), w=[cW])
        G(lambda: nc.gpsimd.memset(epsT[:, 0:1], EPS), w=[cE])
        G(lambda: nc.gpsimd.memset(epsT[:, 1:2], 1.0), w=[cE])
        G(lambda: nc.gpsimd.memset(kbuf[:, :, :], 0.0), w=[kbufB])
        G(lambda: nc.gpsimd.memset(vbuf[:, :, :], 0.0), w=[vbufB])
        G(lambda: nc.gpsimd.memset(U[:, :], 0.0), w=UB)
        for gi, lw in enumerate((lw_a, lw_x)):
            lwv = lw.rearrange("(a hb g) ci d -> hb ci a g d", a=2, hb=2, g=4)
            for hb in range(2):
                for a in range(2):
                    dma(POOL, wbd[hb * 64:(hb + 1) * 64, gi, a * 4:(a + 1) * 4, hb * 64:(hb + 1) * 64],
                        lwv[hb][:, a], writes=[cW])
        schedule.n_setup = len(Ctx.ops)
        scrB = [Buf("scr%d" % i) for i in range(NPIECE)]
        win_v = w_in.rearrange("(k p) e -> p k e", p=128)

        def head_cols(base, h):
            return win_v[:, :, base + h * 64: base + (h + 1) * 64]

        specs = [[] for _ in range(NPIECE)]

        def spec(pi, c0, c1, src, p0=0, p1=128, k0=0, k1=8):
            specs[pi].append((p0, p1, k0, k1, c0, c1, src))

        for pi in range(2):
            for jj in range(4):
                j = pi * 4 + jj
                a, g = j // 4, j % 4
                spec(pi, jj * 128, jj * 128 + 64, head_cols(0, 8 * a + g))
                spec(pi, jj * 128 + 64, jj * 128 + 128, head_cols(0, 8 * a + 4 + g))
        spec(2, 0, 512, win_v[:, :, 1024:1536])
        for cp in range(4):
            for which, bases in ((0, (2560, 4608)), (1, (1536, 3584))):
                pi = 3 + 2 * cp + which
                for bi, base in enumerate(bases):
                    for ci in range(2):
                        c = 2 * cp + ci
                        a, g = c // 4, c % 4
                        col = (bi * 2 + ci) * 128
                        spec(pi, col, col + 64, head_cols(base, 8 * a + g))
                        spec(pi, col + 64, col + 128, head_cols(base, 8 * a + 4 + g))
        wo_v = w_out.rearrange("(a hb g p) e -> hb p a g e", a=2, hb=2, g=4)
        for hf in range(2):
            for hb in range(2):
                for a in range(2):
                    spec(11 + hf, 0, 512, wo_v[hb][:, a, :, hf * 512:(hf + 1) * 512],
                         p0=hb * 64, p1=(hb + 1) * 64, k0=a * 4, k1=(a + 1) * 4)
        wu_v = w_up.rearrange("(k p) e -> p k e", p=128)
        for q in range(8):
            spec(13 + q, 0, 512, wu_v[:, :, q * 512:(q + 1) * 512])
        wd_v = w_down.rearrange("(q k p) e -> q p k e", k=8, p=128)
        for hf in range(2):
            for q in range(4):
                spec(21 + hf * 4 + q, 0, 512, wd_v[q][:, :, hf * 512:(hf + 1) * 512])

        ntiles = 1 + NSEQ * (SEQ // TT)
        total_pieces = ntiles * NPIECE
        wstate = {"loaded": 0}

        def load_upto(n):
            while wstate["loaded"] < min(n, total_pieces):
                g = wstate["loaded"]
                pi = g % NPIECE
                s_ = g % NR
                if g < NPIECE and pi >= 13 and pi % 2 == 1 and len(specs[pi]) == 1:
                    src = specs[pi][0][6]
                    dma(SP, wst[:, :, :], src, writes=[wstB])
                    A(lambda: act(out=ring[s_][:, 0:4, :], in_=wst[:, 0:4, :], func=AF.Copy), r=[wstB], w=[ringB[s_]])
                    V(lambda: nc.vector.tensor_copy(out=ring[s_][:, 4:8, :], in_=wst[:, 4:8, :]), r=[wstB], w=[ringB[s_]])
                    if ntiles > 1:
                        dma(SP, scr[pi], ring[s_][:, :, :], reads=[ringB[s_]], writes=[scrB[pi]])
                elif g < NPIECE:
                    for (p0, p1, k0, k1, c0, c1, src) in specs[pi]:
                        dma(POOL, ring[s_][p0:p1, k0:k1, c0:c1], src, writes=[ringB[s_]])
                    if ntiles > 1:
                        dma(SP, scr[pi], ring[s_][:, :, :], reads=[ringB[s_]], writes=[scrB[pi]])
                else:
                    dma(SP, ring[s_][:, :, :], scr[pi], reads=[scrB[pi]], writes=[ringB[s_]])
                wstate["loaded"] += 1

        def piece(tile_idx, pi, held=0):
            g = tile_idx * NPIECE + pi
            load_upto(g + NR - held)
            s_ = g % NR
            return ring[s_], ringB[s_]

        def gpf():
            i = cnt["pf"] % 2
            cnt["pf"] += 1
            return pf[i], pfB[i]

        def gpt():
            i = cnt["pt"] % 2
            cnt["pt"] += 1
            return ptt[i], ptB[i]

        def rstd_from(ss_ap, ssB, n):
            st, stB = gst()
            A(lambda: act(out=st[:n, 0:1], in_=ss_ap, func=AF.Sqrt, bias=epsT[:n, 0:1], scale=1.0 / D),
              r=[ssB, constB], w=[stB])
            V(lambda: nc.vector.reciprocal(out=st[:n, 1:2], in_=st[:n, 0:1]), r=[stB], w=[stB])
            return st[:n, 1:2], stB

        def norm_a(src_ap, srcB, gi, bt):
            i = cnt["xa"] % 2
            cnt["xa"] += 1
            xb, xbB = xnb[i], xnbB[i]
            st, stB = gst()
            A(lambda: act(out=xb[:bt, :], in_=src_ap, func=AF.Square, accum_out=st[:bt, 0:1]),
              r=[srcB], w=[xbB, stB])
            rs, rsB = rstd_from(st[:bt, 0:1], stB, bt)
            V(lambda: nc.vector.scalar_tensor_tensor(out=xb[:bt, :], in0=src_ap, scalar=rs, in1=gbc[:bt, gi, :],
                                                     op0=ALU.mult, op1=ALU.mult),
              r=[srcB, rsB, constB], w=[xbB])
            return i

        def norm_b(i, blk, bt):
            xb, xbB = xnb[i], xnbB[i]
            pt, ptb = gpt()
            ptv = pt[:, :].rearrange("p (k t) -> p k t", t=128)
            for k in range(8):
                P(lambda k=k: nc.tensor.transpose(out=ptv[:, k, :bt], in_=xb[:bt, k * 128:(k + 1) * 128],
                                                  identity=identb[:bt, :bt]),
                  r=[xbB, constB], w=[ptb], sig=(k == 7))
            A(lambda: act(out=xnT[:, :, blk * 128:blk * 128 + bt], in_=ptv[:, :, :bt], func=AF.Copy),
              r=[ptb], w=[xnTB[blk]])

        def proj(rg, rgB, cc, NT, nbk):
            pb, pbB = gpf()
            for k in range(8):
                P(lambda k=k: mm(pb[:, :NT], lhsT=rg[:, k, cc * 128:(cc + 1) * 128], rhs=xnT[:, k, :NT],
                                 start=(k == 0), stop=(k == 7)),
                  r=[rgB, xnTB[:nbk]], w=[pbB], sig=(k == 7))
            return pb, pbB

        def dma_perm_out(q, dst_dram, src_sb, n, reads):
            dv = dst_dram.rearrange("n (a hb g d) -> n a hb g d", a=2, hb=2, g=4, d=64)
            sv_ = src_sb.rearrange("n (a g hb d) -> n a hb g d", a=2, hb=2, g=4, d=64)
            for a in range(2):
                for hb in range(2):
                    dma(q, dv[:, a, hb], sv_[:, a, hb], reads=reads)

        def dma_perm_in(q, dst_sb, src_dram, n, writes):
            sv_ = src_dram.rearrange("n (a hb g d) -> n a hb g d", a=2, hb=2, g=4, d=64)
            dv = dst_sb.rearrange("n (a g hb d) -> n a hb g d", a=2, hb=2, g=4, d=64)
            for a in range(2):
                for hb in range(2):
                    dma(q, dv[:, a, hb], sv_[:, a, hb], writes=writes)

        def vec_setup():
            dma_perm_in(SP, stg[0:4, :], conv_w[:, :], 4, [stgB])
            for i, v in enumerate((conv_b, lb_a, lb_x, lam)):
                dma_perm_in(SP, stg[4 + i:5 + i, :], v.rearrange("(o d) -> o d", o=1), 1, [stgB])
            pb, pbB = pf[0], pfB[0]
            for c in range(8):
                P(lambda c=c: nc.tensor.transpose(out=pb[:, c * 8:(c + 1) * 8], in_=stg[0:8, c * 128:(c + 1) * 128],
                                                  identity=identf[0:8, 0:8]),
                  r=[stgB, cI], w=[pbB], sig=(c == 7))
            pb3 = pb[:, 0:64].rearrange("p (c v) -> p v c", v=8)
            V(lambda: nc.vector.tensor_copy(out=cw[:, :, :], in_=pb3[:, 0:4, :]), r=[pbB], w=[cV])
            V(lambda: nc.vector.tensor_copy(out=pv[:, 0:4, :], in_=pb3[:, 4:8, :]), r=[pbB], w=[cV])
            A(lambda: act(out=pv[:, 6, :], in_=pv[:, 3, :], func=AF.Exp, scale=-1.0), r=[cV, cE], w=[cV])
            A(lambda: act(out=pv[:, 6, :], in_=pv[:, 6, :], func=AF.Ln, bias=epsT[:, 1:2], scale=1.0),
              r=[cV, cE], w=[cV])
            V(lambda: nc.vector.tensor_scalar(out=pv[:, 4, :], in0=pv[:, 6, :], scalar1=-4.0, scalar2=None,
                                              op0=ALU.mult), r=[cV, cE], w=[cV])
            V(lambda: nc.vector.tensor_scalar(out=pv[:, 5, :], in0=pv[:, 6, :], scalar1=-8.0, scalar2=None,
                                              op0=ALU.mult), r=[cV, cE], w=[cV])
            for i_ in (1, 2):
                V(lambda i_=i_: nc.vector.tensor_scalar(out=pv[:, i_, :], in0=pv[:, i_, :], scalar1=0.5, scalar2=None,
                                                        op0=ALU.mult), r=[cV, cE], w=[cV])

        def out_rows_from_chunks(srcs, n, srcBs, dst_dram):
            for half in range(2):
                pb, pbB = gpf()
                for cc in range(4):
                    c = half * 4 + cc
                    P(lambda c=c, cc=cc: nc.tensor.transpose(out=pb[:n, cc * 128:(cc + 1) * 128], in_=srcs[c],
                                                             identity=identf[:, :]),
                      r=[srcBs, constB], w=[pbB], sig=(cc == 3))
                V(lambda: nc.vector.tensor_copy(out=stg[:n, half * 512:(half + 1) * 512], in_=pb[:n, :]),
                  r=[pbB], w=[stgB])
            dma_perm_out(SP, dst_dram, stg[:n, :], n, [stgB])

        class AttnPipe:
            def __init__(self):
                self.q2 = []
                self.q3 = []

            def push(self, u):
                st1 = self.phase1(u)
                self.q2.append(st1)
                if len(self.q2) > 2:
                    self.q3.append(self.phase2(self.q2.pop(0)))
                if len(self.q3) > 1:
                    self.phase3(self.q3.pop(0))

            def drain(self):
                while self.q2:
                    self.q3.append(self.phase2(self.q2.pop(0)))
                    if len(self.q3) > 1:
                        self.phase3(self.q3.pop(0))
                while self.q3:
                    self.phase3(self.q3.pop(0))

            def phase1(self, u):
                nq, j, qcols, kparts, maskmm, nk = u["nq"], u["j"], u["qcols"], u["kparts"], u["mask"], u["nk"]
                par = cnt["au"] % 2
                cnt["au"] += 1
                bank, bB = pf[2 + par], pfB[2 + par]
                sv = bank[:, :].rearrange("p (h k) -> p h k", h=2)
                nparts = len(kparts)
                for h in range(2):
                    P(maskmm(h, sv[:nq, h, :nk]), r=[constB], w=[bB], sig=False)
                    off = 0
                    for pi_, (kfn, n, kb) in enumerate(kparts):
                        last = (h == 1 and pi_ == nparts - 1)
                        P(lambda h=h, off=off, n=n, kfn=kfn, last=last: mm(
                            sv[:nq, h, off:off + n], lhsT=qT[h * 64:(h + 1) * 64, j, qcols[0]:qcols[1]],
                            rhs=kfn(h), start=False, stop=last),
                          r=[qTB(j), kb], w=[bB], sig=last)
                        off += n
                st, stB = gst()
                V(lambda: nc.vector.tensor_reduce(out=st[:nq, 0:2], in_=sv[:nq, :, :nk], axis=AX.X, op=ALU.max),
                  r=[bB], w=[stB])
                V(lambda: nc.vector.tensor_scalar(out=st[:nq, 2:4], in0=st[:nq, 0:2], scalar1=-0.125, scalar2=None,
                                                  op0=ALU.mult), r=[stB], w=[stB])
                V(lambda: nc.vector.tensor_tensor(out=st[:nq, 6:8], in0=st[:nq, 2:4], in1=sinkp[:nq, 2 * j:2 * j + 2],
                                                  op=ALU.add), r=[stB, constB], w=[stB])
                Et = TF.get()
                E, EB = Et
                Ev = E[:, :].rearrange("p (h k) -> p h k", h=2)
                for h in range(2):
                    A(lambda h=h: act(out=Ev[:nq, h, :nk], in_=sv[:nq, h, :nk], func=AF.Exp,
                                      bias=st[:nq, 2 + h:3 + h], scale=0.125, accum_out=st[:nq, 4 + h:5 + h]),
                      r=[bB, stB], w=[EB, stB])
                A(lambda: act(out=st[:nq, 6:8], in_=st[:nq, 6:8], func=AF.Exp), r=[stB], w=[stB])
                V(lambda: nc.vector.tensor_tensor(out=st[:nq, 6:8], in0=st[:nq, 6:8], in1=st[:nq, 4:6], op=ALU.add),
                  r=[stB], w=[stB])
                V(lambda: nc.vector.reciprocal(out=st[:nq, 6:8], in_=st[:nq, 6:8]), r=[stB], w=[stB])
                Pt = TB.get()
                Pn, PnB = Pt
                Pv = Pn[:, :].rearrange("p (h k) -> p h k", h=2)
                for h in range(2):
                    V(lambda h=h: nc.vector.tensor_scalar(out=Pv[:nq, h, :nk], in0=Ev[:nq, h, :nk],
                                                          scalar1=st[:nq, 6 + h:7 + h], scalar2=None, op0=ALU.mult),
                      r=[EB, stB], w=[PnB])
                TF.put(Et)
                return (u, Pt)

            def phase2(self, state):
                u, Pt = state
                nq, vparts = u["nq"], u["vparts"]
                Pn, PnB = Pt
                Pv = Pn[:, :].rearrange("p (h k) -> p h k", h=2)
                pt, ptb = gpt()
                ptv = pt[:, 0:512].rearrange("p (i t) -> p i t", t=128)
                nvp = len(vparts)
                for h in range(2):
                    for vi, (lv, K, (c0, c1), vb) in enumerate(vparts):
                        last = (h == 1 and vi == nvp - 1)
                        P(lambda h=h, vi=vi, K=K, c0=c0, c1=c1: nc.tensor.transpose(
                            out=ptv[:K, h * 2 + vi, :nq], in_=Pv[:nq, h, c0:c1], identity=identb[:nq, :nq]),
                          r=[PnB, constB], w=[ptb], sig=last)
                PTt = TB.get()
                PT, PTB = PTt
                PTv = PT[:, :].rearrange("p (i t) -> p i t", t=128)
                if nq == 128:
                    A(lambda: act(out=PTv[:, :, :], in_=ptv[:, :, :], func=AF.Copy), r=[ptb], w=[PTB])
                else:
                    for vi, (lv, K, cc_, vb) in enumerate(vparts):
                        A(lambda vi=vi, K=K: act(out=PTv[:K, vi::2, :nq], in_=ptv[:K, vi::2, :nq], func=AF.Copy),
                          r=[ptb], w=[PTB])
                TB.put(Pt)
                return (u, PTt)

            def phase3(self, state):
                u, PTt = state
                nq, j, vparts, ocols = u["nq"], u["j"], u["vparts"], u["ocols"]
                PT, PTB = PTt
                PTv = PT[:, :].rearrange("p (i t) -> p i t", t=128)
                ob, obB = pf[4], pfB[4]
                nvp = len(vparts)
                for h in range(2):
                    for vi, (lv, K, cc_, vb) in enumerate(vparts):
                        last = (h == 1 and vi == nvp - 1)
                        P(lambda h=h, vi=vi, K=K, lv=lv: mm(ob[:, h * 128:h * 128 + nq], lhsT=lv,
                                                            rhs=PTv[:K, h * 2 + vi, :nq],
                                                            start=(vi == 0), stop=(vi == nvp - 1)),
                          r=[PTB, vb], w=[obB], sig=last)
                V(lambda: nc.vector.tensor_copy(out=qT[0:64, j, ocols[0]:ocols[1]], in_=ob[0:64, 0:nq]),
                  r=[obB], w=qTB(j))
                V(lambda: nc.vector.tensor_copy(out=qT[64:128, j, ocols[0]:ocols[1]], in_=ob[64:128, 128:128 + nq]),
                  r=[obB], w=qTB(j))
                TB.put(PTt)

        tiles = [("p", seq, t0) for seq in range(NSEQ) for t0 in range(0, SEQ, TT)] + [("s", 0, 0)]

        def tinfo(ti):
            kind, seq, t0 = tiles[ti]
            sample = (kind == "s")
            return dict(sample=sample, seq=seq, t0=t0, NT=(NSAMP if sample else TT), nbk=(1 if sample else 4),
                        bt=(NSAMP if sample else 128),
                        xsrc=(x_s if sample else x_p[seq, t0:t0 + TT, :]))

        stA = {}
        marks = []
        build_program.marks = marks

        def stage_a_load(ti, blk):
            t = tinfo(ti)
            bt = t["bt"]
            i = cnt["xa"] % 2
            dma(SP, xa[i][:bt, :], t["xsrc"][blk * 128:blk * 128 + bt, :], writes=[xaB[i]])
            stA[(ti, blk)] = norm_a(xa[i][:bt, :], xaB[i], 0, bt)

        def stage_a_tr(ti, blk):
            t = tinfo(ti)
            norm_b(stA.pop((ti, blk)), blk, t["bt"])

        def run_tile(ti, prestaged):
            t = tinfo(ti)
            marks.append(("tile%d" % ti, len(Ctx.ops)))
            sample, seq, t0, NT, nbk, bt, xsrc = (t["sample"], t["seq"], t["t0"], t["NT"], t["nbk"], t["bt"],
                                                  t["xsrc"])
            first = (not sample) and t0 == 0
            lastt = (not sample) and t0 + TT == SEQ

            if not prestaged:
                for blk in range(nbk):
                    stage_a_load(ti, blk)
                    stage_a_tr(ti, blk)
            if sample:
                dma(SP, xt[:bt, 0, :], xsrc, writes=[xtB[0]])
            else:
                dma(SP, xt[:, :, :], xsrc.rearrange("(b p) d -> p b d", p=128), writes=xtB)

            if sample:
                dma_perm_in(SP, stg[:48, :], s_conv.rearrange("s i d -> (s i) d"), 48, [stgB])
                for half in range(2):
                    pb, pbB = gpf()
                    pv3 = pb[:, 0:192].rearrange("p (c t) -> p c t", t=48)
                    for cc in range(4):
                        c = half * 4 + cc
                        P(lambda c=c, cc=cc: nc.tensor.transpose(out=pv3[:, cc, :], in_=stg[:48, c * 128:(c + 1) * 128],
                                                                 identity=identf[:48, :48]),
                          r=[stgB, constB], w=[pbB], sig=(cc == 3))
                    V(lambda: nc.vector.tensor_copy(out=scT[:, half * 4:half * 4 + 4, :], in_=pv3[:, :, :]),
                      r=[pbB], w=[sampB])
                dma_perm_in(SP, stg[:16, :], s_lru[:, :], 16, [stgB])
                for half in range(2):
                    pb, pbB = gpf()
                    pv3 = pb[:, 0:64].rearrange("p (c t) -> p c t", t=16)
                    for cc in range(4):
                        c = half * 4 + cc
                        P(lambda c=c, cc=cc: nc.tensor.transpose(out=pv3[:, cc, :], in_=stg[:16, c * 128:(c + 1) * 128],
                                                                 identity=identf[:16, :16]),
                          r=[stgB, constB], w=[pbB], sig=(cc == 3))
                    V(lambda: nc.vector.tensor_copy(out=h0T[:, half * 4:half * 4 + 4, :], in_=pv3[:, :, :]),
                      r=[pbB], w=[sampB])
            elif first:
                V(lambda: nc.vector.memset(hst[:, :], 0.0), w=[hstB])
                V(lambda: nc.vector.memset(uh[:, :, :], 0.0), w=[uhB])

            if sample:
                qbdt = TB.get()
                qbd, qbdB = qbdt
                qbd5 = qbd[:, 0:256].rearrange("p (s a h g) -> p s a h g", s=16, a=2, h=2, g=4)
                V(lambda: nc.vector.memset(qbd[:, 0:256], 0.0), w=[qbdB])
            for pi in range(2):
                rg, rgB = piece(ti, pi)
                for cc in range(4):
                    j = pi * 4 + cc
                    pb, pbB = proj(rg, rgB, cc, NT, nbk)
                    if sample:
                        a_, g_ = j // 4, j % 4
                        V(lambda: nc.vector.tensor_copy(out=qbd5[0:64, :, a_, 0, g_], in_=pb[0:64, :NT]),
                          r=[pbB], w=[qbdB])
                        V(lambda: nc.vector.tensor_copy(out=qbd5[64:128, :, a_, 1, g_], in_=pb[64:128, :NT]),
                          r=[pbB], w=[qbdB])
                        continue
                    if cc % 2 == 0:
                        A(lambda: act(out=qT[:, j, :NT], in_=pb[:, :NT], func=AF.Copy), r=[pbB], w=qTB(j))
                    else:
                        V(lambda: nc.vector.tensor_copy(out=qT[:, j, :NT], in_=pb[:, :NT]), r=[pbB], w=qTB(j))
            rg, rgB = piece(ti, 2)
            for a in range(2):
                pb, pbB = proj(rg, rgB, a, NT, nbk)
                V(lambda: nc.vector.tensor_copy(out=kbuf[:, a, 128:128 + NT], in_=pb[:, :NT]), r=[pbB], w=[kbufB])
            for blk in range(nbk):
                pb, pbB = gpf()
                for k in range(8):
                    P(lambda k=k: mm(pb[:bt, :], lhsT=xnT[:, k, blk * 128:blk * 128 + bt], rhs=rg[:, k, :],
                                     start=(k == 0), stop=(k == 7)),
                      r=[rgB, xnTB[blk]], w=[pbB], sig=(k == 7))
                A(lambda: act(out=vbuf[:bt, 1 + blk, :], in_=pb[:bt, 256:512], func=AF.Copy), r=[pbB], w=[vbufB])
                if sample or (lastt and blk == nbk - 1):
                    V(lambda: nc.vector.tensor_copy(out=kvst[:bt, :], in_=pb[:bt, :]), r=[pbB], w=[kvstB])
                    if sample:
                        dma(SP, kws[:, 127, :], kvst[:bt, 0:256], reads=[kvstB])
                        dma(SP, vws[:, 127, :], kvst[:bt, 256:512], reads=[kvstB])
                    else:
                        dma(SP, kwp[seq], kvst[:, 0:256], reads=[kvstB])
                        dma(SP, vwp[seq], kvst[:, 256:512], reads=[kvstB])

            if sample:
                oall, oallB = pf[4], pfB[4]
                oall4 = oall[:, 0:256].rearrange("p (s a m) -> p s a m", s=16, a=2, m=8)
                for s in range(NSAMP):
                    i = s % 2
                    dma(SP, kcf[i][:, :], ck[s], writes=[kcfB[i]])
                    dma(POOL, vc[i][:, :], cv[s], writes=[vcB[i]])
                    pb, pbB = gpf()
                    for a in range(2):
                        P(lambda a=a: nc.tensor.transpose(out=pb[:, a * 128:(a + 1) * 128],
                                                          in_=kcf[i][:, a * 128:(a + 1) * 128], identity=identf[:, :]),
                          r=[kcfB[i], constB], w=[pbB], sig=(a == 1))
                    V(lambda: nc.vector.tensor_copy(out=kc[i][:, :, :], in_=pb[:, 0:256].rearrange("p (a t) -> p a t", a=2)),
                      r=[pbB], w=[kcB[i]])
                    bank, bB = pf[2 + i], pfB[2 + i]
                    sv = bank[:, :].rearrange("p (a k) -> p a k", a=2)
                    for a in range(2):
                        P(lambda a=a: mm(sv[0:8, a, 0:144], lhsT=selb[:, s * 8:(s + 1) * 8], rhs=msamp[:, 0:144],
                                         start=(a == 0), stop=False), r=[constB], w=[bB], sig=False)
                        P(lambda a=a: mm(sv[0:8, a, 0:128], lhsT=qbd5[:, s, a, :, :], rhs=kc[i][:, a, :],
                                         start=False, stop=False), r=[qbdB, kcB[i]], w=[bB], sig=False)
                        P(lambda a=a: mm(sv[0:8, a, 128:144], lhsT=qbd5[:, s, a, :, :], rhs=kbuf[:, a, 128:144],
                                         start=False, stop=(a == 1)), r=[qbdB, kbufB], w=[bB], sig=(a == 1))
                    st, stB = gst()
                    V(lambda: nc.vector.tensor_reduce(out=st[0:8, 0:2], in_=sv[0:8, :, 0:144], axis=AX.X, op=ALU.max),
                      r=[bB], w=[stB])
                    V(lambda: nc.vector.tensor_scalar(out=st[0:8, 2:4], in0=st[0:8, 0:2], scalar1=-0.125, scalar2=None,
                                                      op0=ALU.mult), r=[stB], w=[stB])
                    V(lambda: nc.vector.tensor_tensor(out=st[0:8, 6:8], in0=st[0:8, 2:4], in1=sink8[0:8, 0:2], op=ALU.add),
                      r=[stB, constB], w=[stB])
                    Et = TF.get(); E, EB = Et
                    Ev = E[:, :].rearrange("p (a k) -> p a k", a=2)
                    for a in range(2):
                        A(lambda a=a: act(out=Ev[0:8, a, 0:144], in_=sv[0:8, a, 0:144], func=AF.Exp,
                                          bias=st[0:8, 2 + a:3 + a], scale=0.125, accum_out=st[0:8, 4 + a:5 + a]),
                          r=[bB, stB], w=[EB, stB])
                    A(lambda: act(out=st[0:8, 6:8], in_=st[0:8, 6:8], func=AF.Exp), r=[stB], w=[stB])
                    V(lambda: nc.vector.tensor_tensor(out=st[0:8, 6:8], in0=st[0:8, 6:8], in1=st[0:8, 4:6], op=ALU.add),
                      r=[stB], w=[stB])
                    V(lambda: nc.vector.reciprocal(out=st[0:8, 6:8], in_=st[0:8, 6:8]), r=[stB], w=[stB])
                    Pt = TB.get(); Pn, PnB = Pt
                    Pv = Pn[:, :].rearrange("p (a k) -> p a k", a=2)
                    for a in range(2):
                        V(lambda a=a: nc.vector.tensor_scalar(out=Pv[0:8, a, 0:144], in0=Ev[0:8, a, 0:144],
                                                              scalar1=st[0:8, 6 + a:7 + a], scalar2=None, op0=ALU.mult),
                          r=[EB, stB], w=[PnB])
                    TF.put(Et)
                    pt, ptb = gpt()
                    ptv = pt[:, 0:32].rearrange("p (i t) -> p i t", t=8)
                    for a in range(2):
                        P(lambda a=a: nc.tensor.transpose(out=ptv[:, 2 * a, :], in_=Pv[0:8, a, 0:128],
                                                          identity=identb[0:8, 0:8]),
                          r=[PnB, constB], w=[ptb], sig=False)
                        P(lambda a=a: nc.tensor.transpose(out=ptv[0:16, 2 * a + 1, :], in_=Pv[0:8, a, 128:144],
                                                          identity=identb[0:8, 0:8]),
                          r=[PnB, constB], w=[ptb], sig=(a == 1))
                    PTt = TB.get(); PT, PTB = PTt
                    PTv = PT[:, 0:32].rearrange("p (i t) -> p i t", t=8)
                    A(lambda: act(out=PTv[:, 0::2, :], in_=ptv[:, 0::2, :], func=AF.Copy), r=[ptb], w=[PTB])
                    A(lambda: act(out=PTv[0:16, 1::2, :], in_=ptv[0:16, 1::2, :], func=AF.Copy), r=[ptb], w=[PTB])
                    TB.put(Pt)
                    for a in range(2):
                        P(lambda a=a: mm(oall4[:, s, a, :], lhsT=vc[i][:, a * 128:(a + 1) * 128], rhs=PTv[:, 2 * a, :],
                                         start=True, stop=False), r=[PTB, vcB[i]], w=[oallB], sig=False)
                        P(lambda a=a: mm(oall4[:, s, a, :], lhsT=vbuf[0:16, 1, a * 128:(a + 1) * 128],
                                         rhs=PTv[0:16, 2 * a + 1, :], start=False, stop=True),
                          r=[PTB, vbufB], w=[oallB], sig=(a == 1))
                    TB.put(PTt)
                oall5 = oall[:, 0:256].rearrange("p (s a h g) -> p s a h g", s=16, a=2, h=2, g=4)
                for hb in range(2):
                    for a in range(2):
                        V(lambda hb=hb, a=a: nc.vector.tensor_copy(
                            out=qT[hb * 64:(hb + 1) * 64, a * 4:(a + 1) * 4, 0:16],
                            in_=oall5[hb * 64:(hb + 1) * 64, :, a, hb, :].rearrange("p s g -> p g s")),
                          r=[oallB], w=[qTB(a * 4 + g_) for g_ in range(4)])
                TB.put(qbdt)

            units = []
            if sample:
                pass
            else:
                for blk in range(nbk):
                    for j in range(8):
                        units.append(("p", blk, j))
            upos = [0]
            pipe = AttnPipe()
            sstate = {}

            def sample_prep(s):
                i = s % 2
                dma(SP, kcf[i][:, :], ck[s], writes=[kcfB[i]])
                dma(POOL, vc[i][:, :], cv[s], writes=[vcB[i]])
                pb, pbB = gpf()
                for a in range(2):
                    P(lambda a=a: nc.tensor.transpose(out=pb[:, a * 128:(a + 1) * 128],
                                                      in_=kcf[i][:, a * 128:(a + 1) * 128], identity=identf[:, :]),
                      r=[kcfB[i], constB], w=[pbB], sig=(a == 1))
                V(lambda: nc.vector.tensor_copy(out=kc[i][:, :, :], in_=pb[:, 0:256].rearrange("p (a t) -> p a t", a=2)),
                  r=[pbB], w=[kcB[i]])

            def push_units(n):
                for _ in range(n):
                    if upos[0] >= len(units):
                        return
                    kind_, x_, j = units[upos[0]]
                    upos[0] += 1
                    a = j // 4
                    if kind_ == "s":
                        s = x_
                        i = s % 2
                        if j == 0:
                            sample_prep(s)
                        u = dict(nq=1, j=j, qcols=(s, s + 1), nk=144, ocols=(s, s + 1),
                                 kparts=[(lambda h, a=a, i=i: kc[i][h * 64:(h + 1) * 64, a, :], 128, [kcB[i]]),
                                         (lambda h, a=a: kbuf[h * 64:(h + 1) * 64, a, 128:144], 16, [kbufB])],
                                 vparts=[(vc[i][:, a * 128:(a + 1) * 128], 128, (0, 128), [vcB[i]]),
                                         (vbuf[:16, 1, a * 128:(a + 1) * 128], 16, (128, 144), [vbufB])],
                                 mask=(lambda h, out_ap, s=s: (lambda: mm(out_ap, lhsT=identb[:, s:s + 1],
                                                                          rhs=msamp[:, 0:144], start=(h == 0),
                                                                          stop=False))))
                    else:
                        blk = x_
                        mv = 1 if (first and blk == 0) else 0
                        u = dict(nq=128, j=j, qcols=(blk * 128, blk * 128 + 128), nk=256,
                                 ocols=(blk * 128, blk * 128 + 128),
                                 kparts=[(lambda h, a=a, blk=blk: kbuf[h * 64:(h + 1) * 64, a, blk * 128:blk * 128 + 256],
                                          256, [kbufB])],
                                 vparts=[(vbuf[:, blk, a * 128:(a + 1) * 128], 128, (0, 128), [vbufB]),
                                         (vbuf[:, blk + 1, a * 128:(a + 1) * 128], 128, (128, 256), [vbufB])],
                                 mask=(lambda h, out_ap, mv=mv: (lambda: mm(out_ap, lhsT=identb[:, :],
                                                                            rhs=maskb[:, mv, 0:256], start=(h == 0),
                                                                            stop=False))))
                    pipe.push(u)

            upp = (len(units) + 31) // 32

            hS, hSB = hS_t, hS_B

            def R1(c):
                sl = c % 4
                uct = TF.get(); uc, ucB = uct
                if sample:
                    V(lambda: nc.vector.tensor_scalar(out=uc[:, :NT], in0=scT[:, c, 0::3], scalar1=cw[:, 0, c:c + 1],
                                                      scalar2=pv[:, 0, c:c + 1], op0=ALU.mult, op1=ALU.add),
                      r=[sampB, constB], w=[ucB])
                    for i_ in (1, 2):
                        V(lambda i_=i_: nc.vector.scalar_tensor_tensor(out=uc[:, :NT], in0=scT[:, c, i_::3],
                                                                       scalar=cw[:, i_, c:c + 1], in1=uc[:, :NT],
                                                                       op0=ALU.mult, op1=ALU.add),
                          r=[sampB, constB], w=[ucB])
                    V(lambda: nc.vector.scalar_tensor_tensor(out=uc[:, :NT], in0=ubuf[:, sl, 3:3 + NT],
                                                             scalar=cw[:, 3, c:c + 1], in1=uc[:, :NT],
                                                             op0=ALU.mult, op1=ALU.add),
                      r=[ubB(c), constB], w=[ucB])
                    V(lambda: nc.vector.tensor_copy(out=uS_t[:, c * 16:(c + 1) * 16], in_=ubuf[:, sl, 3:3 + NT]),
                      r=ubB(c), w=[sampB])
                else:
                    V(lambda: nc.vector.tensor_scalar(out=uc[:, :NT], in0=ubuf[:, sl, 0:NT], scalar1=cw[:, 0, c:c + 1],
                                                      scalar2=pv[:, 0, c:c + 1], op0=ALU.mult, op1=ALU.add),
                      r=[ubB(c), constB], w=[ucB])
                    for i_ in (1, 2, 3):
                        V(lambda i_=i_: nc.vector.scalar_tensor_tensor(out=uc[:, :NT], in0=ubuf[:, sl, i_:i_ + NT],
                                                                       scalar=cw[:, i_, c:c + 1], in1=uc[:, :NT],
                                                                       op0=ALU.mult, op1=ALU.add),
                          r=[ubB(c), constB], w=[ucB])
                    V(lambda: nc.vector.tensor_copy(out=uh[:, c, :], in_=ubuf[:, sl, NT:NT + 3]),
                      r=ubB(c), w=[uhB])
                ucbt = TB.get(); ucb, ucbB = ucbt
                A(lambda: act(out=ucb[:, :NT], in_=uc[:, :NT], func=AF.Copy), r=[ucB], w=[ucbB])
                return dict(c=c, uct=uct, ucbt=ucbt)

            def R2(stt_):
                c = stt_["c"]
                ucb, ucbB = stt_["ucbt"]
                trt = TF.get(); tr, trB = trt
                tit = TF.get(); ti_, tiB = tit
                P(lambda: mm(pf[5][:, :NT], lhsT=wbd[:, 0, c, :], rhs=ucb[:, :NT], start=True, stop=True),
                  r=[ucbB, constB], w=[pfB[5]])
                A(lambda: act(out=tr[:, :NT], in_=pf[5][:, :NT], func=AF.Tanh, bias=pv[:, 1, c:c + 1], scale=0.5),
                  r=[pfB[5], constB], w=[trB])
                P(lambda: mm(pf[5][:, :NT], lhsT=wbd[:, 1, c, :], rhs=ucb[:, :NT], start=True, stop=True),
                  r=[ucbB, constB], w=[pfB[5]])
                A(lambda: act(out=ti_[:, :NT], in_=pf[5][:, :NT], func=AF.Tanh, bias=pv[:, 2, c:c + 1], scale=0.5),
                  r=[pfB[5], constB], w=[tiB])
                TB.put(stt_["ucbt"])
                stt_["trt"] = trt
                stt_["tit"] = tit

            def R3a(stt_):
                c = stt_["c"]
                tr, trB = stt_["trt"]
                aat = TF.get(); aa, aaB = aat
                A(lambda: act(out=aa[:, :NT], in_=tr[:, :NT], func=AF.Exp, bias=pv[:, 4, c:c + 1],
                              scale=pv[:, 4, c:c + 1]), r=[trB, constB], w=[aaB])
                A(lambda: act(out=tr[:, :NT], in_=tr[:, :NT], func=AF.Exp, bias=pv[:, 5, c:c + 1],
                              scale=pv[:, 5, c:c + 1]), r=[constB], w=[trB])
                stt_["aat"] = aat

            def R3b(stt_):
                tr, trB = stt_["trt"]
                A(lambda: act(out=tr[:, :NT], in_=tr[:, :NT], func=AF.Sqrt, bias=epsT[:, 1:2], scale=-1.0),
                  r=[constB], w=[trB])
                if first:
                    V(lambda: nc.vector.memset(tr[:, 0:1], 1.0), w=[trB])

            def R3c(stt_):
                c = stt_["c"]
                sl = c % 4
                tr, trB = stt_["trt"]
                ti_, tiB = stt_["tit"]
                aa, aaB = stt_["aat"]
                uc, ucB = stt_["uct"]
                V(lambda: nc.vector.scalar_tensor_tensor(out=ti_[:, :NT], in0=ti_[:, :NT], scalar=1.0, in1=tr[:, :NT],
                                                         op0=ALU.add, op1=ALU.mult), r=[trB], w=[tiB])
                V(lambda: nc.vector.scalar_tensor_tensor(out=ti_[:, :NT], in0=ti_[:, :NT], scalar=0.5, in1=uc[:, :NT],
                                                         op0=ALU.mult, op1=ALU.mult), r=[ucB], w=[tiB])
                if sample:
                    hv = hS[:, c * 16:(c + 1) * 16]
                    V(lambda: nc.vector.tensor_tensor(out=aa[:, :NT], in0=aa[:, :NT], in1=h0T[:, c, :], op=ALU.mult),
                      r=[sampB], w=[aaB])
                    V(lambda: nc.vector.tensor_tensor(out=hv, in0=aa[:, :NT], in1=ti_[:, :NT], op=ALU.add),
                      r=[aaB, tiB], w=[hSB])
                    hB_ = hSB
                else:
                    hv = tr[:, :NT]
                    hB_ = trB
                    V(lambda: nc.vector.tensor_tensor_scan(out=hv, data0=aa[:, :NT], data1=ti_[:, :NT],
                                                           initial=hst[:, c:c + 1], op0=ALU.mult, op1=ALU.add),
                      r=[aaB, tiB, hstB], w=[trB])
                    V(lambda: nc.vector.tensor_copy(out=hst[:, c:c + 1], in_=tr[:, NT - 1:NT]), r=[trB], w=[hstB])
                V(lambda: nc.vector.scalar_tensor_tensor(out=merged[:, c, :NT], in0=hv, scalar=0.5, in1=gs[:, sl, :NT],
                                                         op0=ALU.mult, op1=ALU.mult),
                  r=[hB_, gsB(c)], w=mgB(c))
                TF.put(stt_["uct"]); TF.put(stt_["trt"]); TF.put(stt_["tit"]); TF.put(stt_["aat"])

            marks.append(("rnnpieces", len(Ctx.ops)))
            prev = []
            for cp in range(4):
                c0, c1 = 2 * cp, 2 * cp + 1
                rg, rgB = piece(ti, 3 + 2 * cp)
                gts = []
                for ci in range(2):
                    pb, pbB = proj(rg, rgB, ci, NT, nbk)
                    gtt = TF.get()
                    g32, g32B = gtt
                    A(lambda: act(out=g32[:, :NT], in_=pb[:, :NT], func=AF.Gelu_apprx_tanh), r=[pbB], w=[g32B])
                    gts.append(gtt)
                    push_units(upp)
                    if prev:
                        R2(prev[ci])
                for ci in range(2):
                    c = c0 + ci
                    pb, pbB = proj(rg, rgB, 2 + ci, NT, nbk)
                    trt = TF.get()
                    A(lambda trt=trt: act(out=trt[0][:, :NT], in_=pb[:, :NT], func=AF.Tanh, scale=0.5),
                      r=[pbB], w=[trt[1]])
                    V(lambda trt=trt, ci=ci, c=c: nc.vector.scalar_tensor_tensor(
                        out=gs[:, c % 4, :NT], in0=trt[0][:, :NT], scalar=1.0, in1=gts[ci][0][:, :NT],
                        op0=ALU.add, op1=ALU.mult), r=[trt[1], gts[ci][1]], w=gsB(c))
                    TF.put(trt)
                    TF.put(gts[ci])
                    push_units(upp)
                    if prev:
                        if ci == 0:
                            R3a(prev[0]); R3a(prev[1])
                        else:
                            R3b(prev[0]); R3b(prev[1])
                            R3c(prev[0]); R3c(prev[1])
                prev = []
                rg, rgB = piece(ti, 4 + 2 * cp)
                for ci in range(2):
                    c = c0 + ci
                    sl = c % 4
                    if not sample:
                        V(lambda c=c, sl=sl: nc.vector.tensor_copy(out=ubuf[:, sl, 0:3], in_=uh[:, c, :]),
                          r=[uhB], w=ubB(c))
                    pb, pbB = proj(rg, rgB, ci, NT, nbk)
                    V(lambda sl=sl, c=c: nc.vector.tensor_copy(out=ubuf[:, sl, 3:3 + NT], in_=pb[:, :NT]),
                      r=[pbB], w=ubB(c))
                    push_units(upp)
                    prev.append(R1(c))
                for ci in range(2):
                    c = c0 + ci
                    pb, pbB = proj(rg, rgB, 2 + ci, NT, nbk)
                    A(lambda c=c: act(out=sa[:, c, :NT], in_=pb[:, :NT], func=AF.Tanh, scale=0.5), r=[pbB], w=saB(c))
                    push_units(upp)
            push_units(len(units))
            R2(prev[0]); R2(prev[1])
            R3a(prev[0]); R3a(prev[1])
            R3b(prev[0]); R3b(prev[1])
            pipe.drain()
            R3c(prev[0]); R3c(prev[1])

            marks.append(("merge", len(Ctx.ops)))
            for c in range(8):
                tmt = TF.get(); tm, tmB = tmt
                V(lambda c=c: nc.vector.scalar_tensor_tensor(out=tm[:, :NT], in0=sa[:, c, :NT], scalar=1.0,
                                                             in1=qT[:, c, :NT], op0=ALU.add, op1=ALU.mult),
                  r=[saB(c), qTB(c)], w=[tmB])
                V(lambda c=c: nc.vector.scalar_tensor_tensor(out=merged[:, c, :NT], in0=tm[:, :NT], scalar=0.5,
                                                             in1=merged[:, c, :NT], op0=ALU.mult, op1=ALU.add),
                  r=[tmB], w=mgB(c))
                TF.put(tmt)
            if debug and not sample:
                dma(SP, dbg_m[:, :, :], merged[:, :, :], reads=[mgB(c) for c in range(8)])

            if sample:
                out_rows_from_chunks([hS[:, c * 16:(c + 1) * 16] for c in range(8)], 16, [hSB], lrs[:, :])
                out_rows_from_chunks([uS_t[:, c * 16:(c + 1) * 16] for c in range(8)], 16, [sampB], cvs[:, 2, :])
            elif lastt:
                pb, pbB = gpf()
                P(lambda: nc.tensor.transpose(out=pb[:8, 0:128], in_=hst[:, 0:8], identity=identf[:, :]),
                  r=[hstB, constB], w=[pbB])
                V(lambda: nc.vector.tensor_copy(out=stg[:8, 0:128], in_=pb[:8, 0:128]), r=[pbB], w=[stgB])
                lv_ = lrp[seq].rearrange("(a hb g d) -> a hb g d", a=2, hb=2, g=4, d=64)
                for a in range(2):
                    for hb in range(2):
                        dma(SP, lv_[a, hb], stg[a * 4:(a + 1) * 4, hb * 64:(hb + 1) * 64], reads=[stgB])
                out_rows_from_chunks([uh[:, c, :] for c in range(8)], 3, [uhB], cvp[seq])
            if not sample and not lastt:
                V(lambda: nc.vector.tensor_copy(out=kbuf[:, :, 0:128], in_=kbuf[:, :, 512:640]), r=[kbufB], w=[kbufB])
                V(lambda: nc.vector.tensor_copy(out=vbuf[:, 0, :], in_=vbuf[:, 4, :]), r=[vbufB], w=[vbufB])

            marks.append(("wout", len(Ctx.ops)))
            ro0, ro0B = piece(ti, 11)
            ro1, ro1B = piece(ti, 12, held=1)

            def wout_mm(blk):
                base = 2 * (blk % 2)
                for hf, (ro, roB) in enumerate(((ro0, ro0B), (ro1, ro1B))):
                    for k in range(8):
                        P(lambda k=k, hf=hf, ro=ro: mm(pf[base + hf][:bt, :], lhsT=merged[:, k, blk * 128:blk * 128 + bt],
                                                       rhs=ro[:, k, :], start=(k == 0), stop=(k == 7)),
                          r=[roB, mgB(k)], w=[pfB[base + hf]], sig=(k == 7))

            def wout_post(blk):
                base = 2 * (blk % 2)
                st, stB = gst()
                jkt = TB.get(); jk, jkB = jkt
                for hf in range(2):
                    A(lambda hf=hf: act(out=jk[:bt, :], in_=pf[base + hf][:bt, :], func=AF.Square,
                                        accum_out=st[:bt, 2 + hf:3 + hf]), r=[pfB[base + hf]], w=[jkB, stB])
                TB.put(jkt)
                V(lambda: nc.vector.tensor_tensor(out=st[:bt, 4:5], in0=st[:bt, 2:3], in1=st[:bt, 3:4], op=ALU.add),
                  r=[stB], w=[stB])
                rs, rsB = rstd_from(st[:bt, 4:5], stB, bt)
                for hf in range(2):
                    tmt = TF.get(); tm, tmB = tmt
                    V(lambda hf=hf: nc.vector.tensor_tensor(out=tm[:bt, :], in0=pf[base + hf][:bt, :],
                                                            in1=gbc[:bt, 1, hf * 512:(hf + 1) * 512], op=ALU.mult),
                      r=[pfB[base + hf], constB], w=[tmB])
                    V(lambda hf=hf: nc.vector.scalar_tensor_tensor(
                        out=xt[:bt, blk, hf * 512:(hf + 1) * 512], in0=tm[:bt, :], scalar=rs,
                        in1=xt[:bt, blk, hf * 512:(hf + 1) * 512], op0=ALU.mult, op1=ALU.add),
                      r=[tmB, rsB], w=[xtB[blk]])
                    TF.put(tmt)
                return norm_a(xt[:bt, blk, :], xtB[blk], 2, bt)

            wout_mm(0)
            pend = None
            for blk in range(nbk):
                if blk + 1 < nbk:
                    wout_mm(blk + 1)
                i_ = wout_post(blk)
                if pend is not None:
                    norm_b(pend[0], pend[1], bt)
                pend = (i_, blk)
            norm_b(pend[0], pend[1], bt)
            if debug and not sample:
                dma(SP, dbg_x1[:, :, :], xt[:, :, :], reads=xtB)

            marks.append(("ffnup", len(Ctx.ops)))
            for q in range(8):
                rg, rgB = piece(ti, 13 + q)
                for cc in range(4):
                    fc = q * 4 + cc
                    pb, pbB = proj(rg, rgB, cc, NT, nbk)
                    rlt = TB.get(); rl, rlB = rlt
                    A(lambda: act(out=rl[:, :NT], in_=pb[:, :NT], func=AF.Relu), r=[pbB], w=[rlB])
                    V(lambda: nc.vector.tensor_tensor(out=hT[:, fc, :NT], in0=rl[:, :NT], in1=rl[:, :NT], op=ALU.mult),
                      r=[rlB], w=hTB(fc))
                    TB.put(rlt)
            marks.append(("ffndown", len(Ctx.ops)))
            nxt = ti + 1 if ti + 1 < len(tiles) else None
            nsteps = []
            if nxt is not None:
                nb2 = tinfo(nxt)["nbk"]
                for b2 in range(nb2):
                    nsteps.append(("l", b2))
                    nsteps.append(("t", b2))
            step = [0]

            def next_stage_step(n=1):
                for _ in range(n):
                    if step[0] < len(nsteps):
                        k_, b2 = nsteps[step[0]]
                        step[0] += 1
                        if k_ == "l":
                            stage_a_load(nxt, b2)
                        else:
                            stage_a_tr(nxt, b2)

            ssb, ssbB = ssb_t[:, :], ssb_B
            for hf in range(2):
                for q in range(4):
                    rg, rgB = piece(ti, 21 + hf * 4 + q)
                    for kk in range(8):
                        fc = q * 8 + kk
                        for blk in range(nbk):
                            P(lambda kk=kk, fc=fc, blk=blk: mm(pf[2 + blk][:bt, :], lhsT=hT[:, fc, blk * 128:blk * 128 + bt],
                                                               rhs=rg[:, kk, :], start=(fc == 0), stop=(fc == 31)),
                              r=[rgB, hTB(fc)], w=[pfB[2 + blk]], sig=(kk == 7 and blk == nbk - 1))
                    next_stage_step(1)
                for blk in range(nbk):
                    jkt = TB.get(); jk, jkB = jkt
                    A(lambda blk=blk, hf=hf: act(out=jk[:bt, :], in_=pf[2 + blk][:bt, :], func=AF.Square,
                                                 accum_out=ssb[:bt, 2 * blk + hf:2 * blk + hf + 1]),
                      r=[pfB[2 + blk]], w=[jkB, ssbB])
                    TB.put(jkt)
                    if hf == 0:
                        V(lambda blk=blk: nc.vector.tensor_tensor(out=fbuf[:bt, blk, :], in0=pf[2 + blk][:bt, :],
                                                                  in1=gbc[:bt, 3, 0:512], op=ALU.mult),
                          r=[pfB[2 + blk], constB], w=fbB(blk))
            for blk in range(nbk):
                st, stB = gst()
                V(lambda blk=blk: nc.vector.tensor_tensor(out=st[:bt, 4:5], in0=ssb[:bt, 2 * blk:2 * blk + 1],
                                                          in1=ssb[:bt, 2 * blk + 1:2 * blk + 2], op=ALU.add),
                  r=[ssbB], w=[stB])
                rs, rsB = rstd_from(st[:bt, 4:5], stB, bt)
                V(lambda blk=blk: nc.vector.scalar_tensor_tensor(out=xt[:bt, blk, 0:512], in0=fbuf[:bt, blk, :], scalar=rs,
                                                                 in1=xt[:bt, blk, 0:512], op0=ALU.mult, op1=ALU.add),
                  r=[fbB(blk), rsB], w=[xtB[blk]])
                tmt = TF.get(); tm, tmB = tmt
                V(lambda blk=blk: nc.vector.tensor_tensor(out=tm[:bt, :], in0=pf[2 + blk][:bt, :],
                                                          in1=gbc[:bt, 3, 512:1024], op=ALU.mult),
                  r=[pfB[2 + blk], constB], w=[tmB])
                V(lambda blk=blk: nc.vector.scalar_tensor_tensor(out=xt[:bt, blk, 512:1024], in0=tm[:bt, :], scalar=rs,
                                                                 in1=xt[:bt, blk, 512:1024], op0=ALU.mult, op1=ALU.add),
                  r=[tmB, rsB], w=[xtB[blk]])
                TF.put(tmt)
            if sample:
                dma(SP, y_s[:, :], xt[:bt, 0, :], reads=[xtB[0]])
            else:
                dma(SP, y_p[seq, t0:t0 + TT, :].rearrange("(b p) d -> p b d", p=128), xt[:, :, :], reads=xtB)
            next_stage_step(len(nsteps))
            return nxt is not None

        outB = Buf("outcopy")
        dma(SP, kws[:, 0:127, :], ck[:, 1:128, :], writes=[outB])
        dma(SP, vws[:, 0:127, :], cv[:, 1:128, :], writes=[outB])
        dma(SP, cvs[:, 0:2, :], s_conv[:, 1:3, :], writes=[outB])

        vec_setup()
        pre = False
        for ti in range(len(tiles)):
            pre = run_tile(ti, pre)

        assert Ctx.pend is None
        order, est = schedule(Ctx.ops)
        build_program.est_us = est
        for i in order:
            o = Ctx.ops[i]
            if o.kind == "dma":
                dma_emit(o)
            else:
                o.eng.emit(o)
        nc = nc_real
        for sc in all_dma_sems:
            nc.gpsimd.wait_ge(sc.s, sc.cnt)
        for e in (PE, ACT, DVE, SP):
            nc.gpsimd.wait_ge(e.sc.s, e.sc.cnt)
    return nc


_CACHE = {}


def _consts():
    ident = np.eye(128, dtype=np.float32)
    qi = np.arange(128)[:, None]
    kj = np.arange(256)[None, :]
    diff = qi + 128 - kj
    band = (diff >= 0) & (diff <= 128)
    m0 = np.where(band, 0.0, NEG).astype(np.float32)
    m1 = m0.copy()
    m1[:, :128] = NEG
    mask = np.stack([np.concatenate([m0, m0], axis=1), np.concatenate([m1, m1], axis=1)])
    ms = np.full((16, 256), NEG, dtype=np.float32)
    ms[:, :128] = 0.0
    for s in range(16):
        ms[s, 128 + s] = 0.0
    msamp = np.concatenate([ms, ms], axis=1)
    sel = np.zeros((128, 128), dtype=np.float32)
    for s_ in range(16):
        sel[s_, s_ * 8:(s_ + 1) * 8] = 1.0
    return ident, np.ascontiguousarray(mask), np.ascontiguousarray(msamp), sel


def kernel(x_prompt, x_sample, cache_k_win, cache_v_win, state_conv, state_lru,
           w_in, w_out, sinks, conv_w, conv_b, lru_w_a, lru_b_a, lru_w_x, lru_b_x, lru_lambda,
           w_up, w_down, g_pre_mix, g_post_mix, g_pre_ffn, g_post_ffn):
    f = lambda a: np.ascontiguousarray(np.asarray(a, dtype=np.float32))
    if "nc" not in _CACHE:
        _CACHE["nc"] = build_program()
    nc = _CACHE["nc"]
    ident, mask, msamp, sel = _consts()
    sk = f(sinks)[0].reshape(2, 2, 4).transpose(0, 2, 1).reshape(16)
    shared = {
        "w_in": f(w_in)[0], "w_out": f(w_out)[0], "sinks": np.ascontiguousarray(sk),
        "conv_w": f(conv_w)[0], "conv_b": f(conv_b)[0],
        "lru_w_a": f(lru_w_a)[0], "lru_b_a": f(lru_b_a)[0], "lru_w_x": f(lru_w_x)[0], "lru_b_x": f(lru_b_x)[0],
        "lru_lambda": f(lru_lambda)[0], "w_up": f(w_up)[0], "w_down": f(w_down)[0],
        "g_pre_mix": f(g_pre_mix)[0], "g_post_mix": f(g_post_mix)[0],
        "g_pre_ffn": f(g_pre_ffn)[0], "g_post_ffn": f(g_post_ffn)[0],
        "c_ident": ident, "c_mask": mask, "c_msamp": msamp, "c_sel": sel,
    }
    xp = f(x_prompt)
    xs = f(x_sample)[:, 0, :]
    ckk = f(cache_k_win)[0].reshape(128, 128, 256)
    cvv = f(cache_v_win)[0].reshape(128, 128, 256)
    sc = f(state_conv)[0]
    sl = f(state_lru)[0]
    in_maps = []
    for i in range(NCORES):
        m = dict(shared)
        m["x_prompt"] = np.ascontiguousarray(xp[2 * i:2 * i + 2])
        m["x_sample"] = np.ascontiguousarray(xs[16 * i:16 * i + 16])
        m["cache_k"] = np.ascontiguousarray(ckk[16 * i:16 * i + 16])
        m["cache_v"] = np.ascontiguousarray(cvv[16 * i:16 * i + 16])
        m["state_conv"] = np.ascontiguousarray(sc[16 * i:16 * i + 16])
        m["state_lru"] = np.ascontiguousarray(sl[16 * i:16 * i + 16])
        in_maps.append(m)
    res = run_bass_kernel_spmd(nc, in_maps, core_ids=list(range(NCORES)))
    R = res.results
    cat = lambda k: np.concatenate([np.asarray(r[k], dtype=np.float32) for r in R], axis=0)
    y_prompt = cat("y_prompt")
    y_sample = cat("y_sample").reshape(128, 1, D)
    kwp = cat("k_win_prompt").reshape(1, 16, 128, 4, 64)
    vwp = cat("v_win_prompt").reshape(1, 16, 128, 4, 64)
    cvp = cat("conv_prompt").reshape(1, 16, 3, D)
    lrp = cat("lru_prompt").reshape(1, 16, D)
    kws = cat("k_win_sample").reshape(1, 128, 128, 4, 64)
    vws = cat("v_win_sample").reshape(1, 128, 128, 4, 64)
    cvs = cat("conv_sample").reshape(1, 128, 3, D)
    lrs = cat("lru_sample").reshape(1, 128, D)
    return (y_prompt, y_sample, kwp, vwp, cvp, lrp, kws, vws, cvs, lrs)
```

```python
from contextlib import ExitStack
import numpy as np
import concourse.bass as bass
import concourse.mybir as mybir
from concourse.bass_utils import run_bass_kernel_spmd

F32 = mybir.dt.float32
BF16 = mybir.dt.bfloat16
AF = mybir.ActivationFunctionType
ALU = mybir.AluOpType
AX = mybir.AxisListType

NCORES = 8
D = 1024
SEQ = 2048
NSEQ = 2
NSAMP = 16
TT = 512
NPIECE = 29
NR = 5
EPS = 1e-6
NEG = -30000.0
GC1 = 0.7978845608028654
GC2 = 0.044715
STOP = 99


class Buf:
    __slots__ = ("name", "w", "r", "dsem", "dcnt", "excl")

    def __init__(self, name, excl=False):
        self.name = name
        self.excl = excl
        self.w = None
        self.r = []
        self.dsem = None
        self.dcnt = 0


class SemC:
    __slots__ = ("s", "cnt", "is_dma")

    def __init__(self, s, is_dma):
        self.s = s
        self.cnt = 0
        self.is_dma = is_dma


class Ctx:
    recording = True
    ops = []
    cap = []
    pend = None


class Op:
    __slots__ = ("eng", "calls", "reads", "writes", "dur", "ts", "kind", "dma", "idx", "lat")


def _free_size(ap):
    try:
        return int(ap.free_size())
    except Exception:
        return 512


def _estimate(engname, calls):
    d = 0.0
    ts = None
    for (m, a, k) in calls:
        name = getattr(m, "__name__", "")
        if engname == "PE":
            ap = k.get("rhs", k.get("identity"))
            n = _free_size(ap) if ap is not None else 128
            d += (max(n, 48) + 16) / 1900.0
        elif engname == "ACT":
            ap = k.get("in_")
            n = _free_size(ap) if ap is not None else 512
            d += 0.2 + n / 1100.0
            f = k.get("func")
            if f == AF.Exp:
                ts = "E"
            elif f == AF.Tanh:
                ts = "T"
            elif f == AF.Sqrt:
                ts = "S"
            elif f == AF.Ln:
                ts = "L"
            elif f == AF.Gelu_apprx_tanh:
                ts = "G"
        elif engname == "DVE":
            ap = k.get("in0", k.get("in_", k.get("data0", k.get("out", k.get("ap")))))
            if ap is None and a:
                ap = a[0]
            n = _free_size(ap) if ap is not None else 512
            mult = 2.0 if "scan" in name else 1.0
            d += 0.1 + mult * n / 760.0
        else:
            ap = k.get("in0", k.get("in_", k.get("out", k.get("ap"))))
            if ap is None and a:
                ap = a[0]
            n = _free_size(ap) if ap is not None else 512
            d += 0.25 + n / 450.0
    return d, ts


class _Dummy:
    def then_inc(self, *a, **k):
        return self


class Rec:
    def __init__(self, real):
        object.__setattr__(self, "_real", real)

    def __getattr__(self, name):
        real_m = getattr(object.__getattribute__(self, "_real"), name)

        def f(*a, **k):
            Ctx.cap.append((real_m, a, k))
            return _Dummy()
        f.__name__ = name
        return f


class NcProxy:
    def __init__(self, real):
        self._real = real
        self.tensor = Rec(real.tensor)
        self.vector = Rec(real.vector)
        self.scalar = Rec(real.scalar)
        self.gpsimd = Rec(real.gpsimd)

    def __getattr__(self, name):
        return getattr(self._real, name)


class Eng:
    def __init__(self, h, semc, name):
        self.h = h
        self.sc = semc
        self.name = name
        self.waited = {}

    def wait(self, tok):
        sc, val = tok
        if sc.is_dma:
            val = sc.cnt
        if self.waited.get(id(sc), 0) >= val:
            return
        self.h.wait_ge(sc.s, val)
        self.waited[id(sc)] = val

    def _deps(self, reads, writes):
        for b in reads:
            if b.w is not None:
                self.wait(b.w)
        for b in writes:
            if b.w is not None:
                self.wait(b.w)
            for t in b.r:
                self.wait(t)

    def op(self, fn, reads=(), writes=(), sig=True):
        del Ctx.cap[:]
        fn()
        calls = list(Ctx.cap)
        reads = list(reads)
        writes = list(writes)
        if Ctx.pend is not None:
            pc, pr, pw = Ctx.pend
            calls = pc + calls
            reads = pr + [b for b in reads if b not in pr]
            writes = pw + [b for b in writes if b not in pw]
            Ctx.pend = None
        if not sig:
            Ctx.pend = (calls, reads, writes)
            return
        ex = [b for b in reads if b.excl]
        if ex:
            writes = writes + [b for b in ex if b not in writes]
            reads = [b for b in reads if not b.excl]
        o = Op()
        o.eng, o.calls, o.reads, o.writes, o.kind, o.dma = self, calls, reads, writes, "c", None
        o.dur, o.ts = _estimate(self.name, calls)
        o.lat = 0.0
        o.idx = len(Ctx.ops)
        Ctx.ops.append(o)

    def emit(self, o):
        self._deps(o.reads, o.writes)
        inst = None
        for (m, a, k) in o.calls:
            inst = m(*a, **k)
        inst.then_inc(self.sc.s, 1)
        self.sc.cnt += 1
        tok = (self.sc, self.sc.cnt)
        for b in o.reads:
            b.r.append(tok)
        for b in o.writes:
            b.w = tok
            b.r = []


def schedule(ops):
    n = len(ops)
    if not hasattr(schedule, 'n_setup'):
        schedule.n_setup = 0
    lastw, readers = {}, {}
    preds = [set() for _ in range(n)]
    for i, o in enumerate(ops):
        for b in o.reads:
            w = lastw.get(id(b))
            if w is not None:
                preds[i].add(w)
        for b in o.writes:
            w = lastw.get(id(b))
            if w is not None:
                preds[i].add(w)
            for r in readers.get(id(b), ()):
                preds[i].add(r)
        for b in o.reads:
            readers.setdefault(id(b), []).append(i)
        for b in o.writes:
            lastw[id(b)] = i
            readers[id(b)] = []
        preds[i].discard(i)
    succs = [[] for _ in range(n)]
    npred = [0] * n
    for i in range(n):
        npred[i] = len(preds[i])
        for p in preds[i]:
            succs[p].append(i)
    ready_time = [0.0] * n
    finish = [0.0] * n
    start = [0.0] * n
    engs = {}
    for o in ops:
        engs.setdefault(id(o.eng), o.eng)
    eng_free = {k: 0.0 for k in engs}
    ready = {k: [] for k in engs}
    cur_ts = [None]
    for i in range(n):
        if npred[i] == 0:
            ready[id(ops[i].eng)].append(i)
    done = 0
    order = []
    while done < n:
        best = None
        for k, lst in ready.items():
            if not lst:
                continue
            ef = eng_free[k]
            isact = engs[k].name == "ACT"
            for i in lst[:32]:
                st = max(ready_time[i], ef)
                pen = 0.0
                if isact and ops[i].ts is not None:
                    t_ = ops[i].ts
                    if t_ == "T":
                        if cur_ts[0] not in ("E", "G"):
                            pen = 1.3
                    elif t_ != cur_ts[0]:
                        pen = 1.3
                if i < schedule.n_setup:
                    st, pen = 0.0, 0.0
                key = (st + pen, i)
                if best is None or key < best[0]:
                    best = (key, i, k, st + pen)
        _, i, k, st = best
        o = ops[i]
        ready[k].remove(i)
        start[i] = st
        if o.kind == "dma":
            eng_free[k] = st + o.dur
            finish[i] = st + o.dur + o.lat
        else:
            finish[i] = st + o.dur
            eng_free[k] = finish[i]
            if o.eng.name == "ACT" and o.ts is not None:
                if o.ts == "T":
                    if cur_ts[0] not in ("E", "G"):
                        cur_ts[0] = "E"
                else:
                    cur_ts[0] = o.ts
        order.append(i)
        done += 1
        for sidx in succs[i]:
            hop = 0.0 if ops[sidx].eng is o.eng and o.kind != "dma" else 0.25
            rt = finish[i] + hop
            if rt > ready_time[sidx]:
                ready_time[sidx] = rt
            npred[sidx] -= 1
            if npred[sidx] == 0:
                lst = ready[id(ops[sidx].eng)]
                lo, hi = 0, len(lst)
                while lo < hi:
                    mid = (lo + hi) // 2
                    if lst[mid] < sidx:
                        lo = mid + 1
                    else:
                        hi = mid
                lst.insert(lo, sidx)
    schedule.start = start
    schedule.finish = finish
    return order, max(finish) if n else 0.0


def _flat(lst):
    out = []
    for x in lst:
        if isinstance(x, (list, tuple)):
            out.extend(_flat(x))
        elif x is not None:
            out.append(x)
    return out


def build_program(debug=False):
    nc = bass.Bass("TRN2", target_bir_lowering=False)

    def din(name, shape, dt=F32):
        return nc.dram_tensor(name, list(shape), dt, kind="ExternalInput").ap()

    def dout(name, shape, dt=F32):
        return nc.dram_tensor(name, list(shape), dt, kind="ExternalOutput").ap()

    x_p = din("x_prompt", [NSEQ, SEQ, D])
    x_s = din("x_sample", [NSAMP, D])
    ck = din("cache_k", [NSAMP, 128, 256])
    cv = din("cache_v", [NSAMP, 128, 256])
    s_conv = din("state_conv", [NSAMP, 3, D])
    s_lru = din("state_lru", [NSAMP, D])
    w_in = din("w_in", [D, 5632])
    w_out = din("w_out", [D, D])
    sinks = din("sinks", [16])
    conv_w = din("conv_w", [4, D])
    conv_b = din("conv_b", [D])
    lw_a = din("lru_w_a", [16, 64, 64])
    lb_a = din("lru_b_a", [D])
    lw_x = din("lru_w_x", [16, 64, 64])
    lb_x = din("lru_b_x", [D])
    lam = din("lru_lambda", [D])
    w_up = din("w_up", [D, 4096])
    w_down = din("w_down", [4096, D])
    g1 = din("g_pre_mix", [D])
    g2 = din("g_post_mix", [D])
    g3 = din("g_pre_ffn", [D])
    g4 = din("g_post_ffn", [D])
    c_ident = din("c_ident", [128, 128])
    c_mask = din("c_mask", [2, 128, 512])
    c_msamp = din("c_msamp", [16, 512])
    c_sel = din("c_sel", [128, 128])

    y_p = dout("y_prompt", [NSEQ, SEQ, D])
    y_s = dout("y_sample", [NSAMP, D])
    kwp = dout("k_win_prompt", [NSEQ, 128, 256])
    vwp = dout("v_win_prompt", [NSEQ, 128, 256])
    cvp = dout("conv_prompt", [NSEQ, 3, D])
    lrp = dout("lru_prompt", [NSEQ, D])
    kws = dout("k_win_sample", [NSAMP, 128, 256])
    vws = dout("v_win_sample", [NSAMP, 128, 256])
    cvs = dout("conv_sample", [NSAMP, 3, D])
    lrs = dout("lru_sample", [NSAMP, D])

    scr = nc.dram_tensor("wscr", [NPIECE, 128, 8, 512], BF16, kind="Internal").ap()
    if debug:
        dbg_m = dout("dbg_merged", [128, 8, 512], BF16)
        dbg_a = dout("dbg_attn", [128, 8, 512], BF16)
        dbg_x1 = dout("dbg_x1", [128, 4, D])

    with ExitStack() as es:
        def sb(name, shape, dt=F32):
            return es.enter_context(nc.sbuf_tensor(name, list(shape), dt))

        def ps(name, shape, dt=F32):
            return es.enter_context(nc.psum_tensor(name, list(shape), dt))

        nsem = [0]

        def newsem(is_dma):
            nsem[0] += 1
            return SemC(es.enter_context(nc.semaphore("s%d" % nsem[0])), is_dma)

        ring = [sb("ring%d" % i, [128, 8, 512], BF16) for i in range(NR)]
        ringB = [Buf("ring%d" % i) for i in range(NR)]
        xa = [sb("xa%d" % i, [128, D]) for i in range(2)]
        xaB = [Buf("xa") for _ in range(2)]
        xnb = [sb("xnb%d" % i, [128, D], BF16) for i in range(2)]
        xnbB = [Buf("xnb") for _ in range(2)]
        xnT = sb("xnT", [128, 8, TT], BF16)
        xnTB = [Buf("xnT%d" % b) for b in range(4)]
        xt = sb("xt", [128, 4, D])
        xtB = [Buf("xt%d" % b) for b in range(4)]
        kbuf = sb("kbuf", [128, 2, 640], BF16)
        kbufB = Buf("kbuf")
        vbuf = sb("vbuf", [128, 5, 256], BF16)
        vbufB = Buf("vbuf")
        USL = 40
        U = sb("U", [128, USL * 256])
        UB = [Buf("U%d" % i) for i in range(USL)]

        def uview_bf(slot0, nslots):
            return U[:, slot0 * 256:(slot0 + nslots) * 256].bitcast(BF16).rearrange(
                "p (c t) -> p c t", t=512)

        qT = uview_bf(0, 8)
        sa = uview_bf(8, 8)
        merged = uview_bf(16, 8)
        gs = uview_bf(24, 4)
        ubuf = U[:, 28 * 256:28 * 256 + 4 * 516].rearrange("p (c t) -> p c t", t=516)
        hT = uview_bf(0, 32)
        fbuf = U[:, 32 * 256:40 * 256].rearrange("p (b t) -> p b t", t=512)

        def qTB(j): return [UB[j]]
        def saB(c): return [UB[8 + c]]
        def mgB(c): return [UB[16 + c]]
        def gsB(c): return [UB[24 + c % 4]]
        def ubB(c):
            lo = 28 * 1024 + (c % 4) * 2064
            return [UB[i] for i in range(lo // 1024, (lo + 2063) // 1024 + 1)]
        def hTB(fc): return [UB[fc]]
        def fbB(b): return [UB[32 + 2 * b], UB[33 + 2 * b]]

        class Pool_:
            def __init__(self, name, n, dt):
                self.free = [(sb("%s%d" % (name, i), [128, 512], dt), Buf(name)) for i in range(n)]

            def get(self):
                assert self.free, "temp pool exhausted"
                return self.free.pop(0)

            def put(self, t):
                self.free.append(t)

        TF = Pool_("tf", 13, F32)
        TB = Pool_("tb", 7, BF16)
        cnt = {"st": 0, "xa": 0, "pf": 0, "pt": 0, "au": 0}

        NST = 12
        stt = sb("stt", [128, NST, 8])
        sttB = [Buf("st") for _ in range(NST)]

        def gst():
            i = cnt["st"] % NST
            cnt["st"] += 1
            return stt[:, i, :], sttB[i]

        stg = sb("stg", [128, D])
        stgB = Buf("stg")
        kvst = sb("kvst", [128, 512])
        kvstB = Buf("kvst")
        hst = sb("hst", [128, 8])
        hstB = Buf("hst")
        uh = sb("uh", [128, 8, 3])
        uhB = Buf("uh")
        kcf = [sb("kcf%d" % i, [128, 256]) for i in range(2)]
        kcfB = [Buf("kcf") for _ in range(2)]
        kc = [sb("kc%d" % i, [128, 2, 128], BF16) for i in range(2)]
        kcB = [Buf("kc") for _ in range(2)]
        vc = [sb("vc%d" % i, [128, 256], BF16) for i in range(2)]
        vcB = [Buf("vc") for _ in range(2)]
        scT = sb("scT", [128, 8, 48])
        h0T = sb("h0T", [128, 8, 16])
        sampB = Buf("samp")
        hS_t = sb("hS_t", [128, 128])
        hS_B = Buf("hS")
        uS_t = sb("uS_t", [128, 128])
        ssb_t = sb("ssb_t", [128, 8])
        ssb_B = Buf("ssb")
        gbc = sb("gbc", [128, 4, D])
        identf = sb("identf", [128, 128])
        identb = sb("identb", [128, 128], BF16)
        maskb = sb("maskb", [128, 2, 512], BF16)
        msamp = sb("msamp", [128, 256], BF16)
        selb = sb("selb", [128, 128], BF16)
        sink8 = sb("sink8", [8, 2])
        wbd = sb("wbd", [128, 2, 8, 128], BF16)
        cw = sb("cw", [128, 4, 8])
        pv = sb("pv", [128, 8, 8])
        sinkp = sb("sinkp", [128, 16])
        epsT = sb("epsT", [128, 2])
        cI, cM, cG, cS, cV, cW, cE = (Buf("cI"), Buf("cM"), Buf("cG"), Buf("cS"), Buf("cV"), Buf("cW"), Buf("cE"))
        constB = [cI, cM, cG, cS, cV, cW, cE]

        NPF = 6
        pf = [ps("pf%d" % i, [128, 512]) for i in range(NPF)]
        pfB = [Buf("pf%d" % i, excl=True) for i in range(NPF)]
        ptt = [ps("pt%d" % i, [128, 1024], BF16) for i in range(2)]
        ptB = [Buf("pt%d" % i, excl=True) for i in range(2)]

        es.enter_context(nc.Block())

        PE = Eng(nc.tensor, newsem(False), "PE")
        ACT = Eng(nc.scalar, newsem(False), "ACT")
        DVE = Eng(nc.vector, newsem(False), "DVE")
        POOL = Eng(nc.gpsimd, newsem(False), "POOL")
        SP = Eng(nc.sync, newsem(False), "SP")
        nc_real = nc
        nc = NcProxy(nc_real)
        Ctx.recording = True
        Ctx.ops = []
        Ctx.cap = []
        Ctx.pend = None

        all_dma_sems = []

        def dma(q, out, in_, reads=(), writes=(), nonc=False):
            reads = _flat(reads)
            writes = _flat(writes)
            o = Op()
            o.eng, o.calls, o.reads, o.writes, o.kind = q, None, reads, writes, "dma"
            o.dma = (out, in_, nonc)
            o.ts = None
            try:
                nbytes = int(out.nbytes())
            except Exception:
                nbytes = 4096
            cast = (out.dtype != in_.dtype)
            o.dur = (1.0 if q is POOL else 0.15) + (nbytes / 80e3 if cast else 0.0)
            o.lat = 2.0 + nbytes / 150e3
            o.idx = len(Ctx.ops)
            Ctx.ops.append(o)

        def dma_emit(o):
            q = o.eng
            out, in_, nonc = o.dma
            reads, writes = o.reads, o.writes
            q._deps(reads, writes)
            tgt = (writes + reads)[0]
            if tgt.dsem is None:
                tgt.dsem = {}
            if id(q) not in tgt.dsem:
                tgt.dsem[id(q)] = newsem(True)
                all_dma_sems.append(tgt.dsem[id(q)])
            sc = tgt.dsem[id(q)]
            if nonc:
                with nc_real.allow_non_contiguous_dma(reason="small strided"):
                    q.h.dma_start(out=out, in_=in_).then_inc(sc.s, 16)
            else:
                q.h.dma_start(out=out, in_=in_).then_inc(sc.s, 16)
            sc.cnt += 16
            tok = (sc, sc.cnt)
            for b in reads:
                b.r.append(tok)
            for b in writes:
                b.w = tok
                b.r = []

        def A(fn, r=(), w=()):
            return ACT.op(fn, _flat(r), _flat(w))

        def V(fn, r=(), w=()):
            return DVE.op(fn, _flat(r), _flat(w))

        def G(fn, r=(), w=()):
            return POOL.op(fn, _flat(r), _flat(w))

        def P(fn, r=(), w=(), sig=True):
            return PE.op(fn, _flat(r), _flat(w), sig=sig)

        act = nc.scalar.activation
        mm = nc.tensor.matmul

        dma(SP, identf[:, :], c_ident[:, :], writes=[cI])
        dma(POOL, identb[:, :], c_ident[:, :], writes=[cI])
        dma(POOL, maskb[:, :, :], c_mask.rearrange("v p c -> p v c"), writes=[cM])
        G(lambda: nc.gpsimd.memset(msamp[:, :], 0.0), w=[cM])
        dma(POOL, msamp[0:16, :], c_msamp[:, 0:256], writes=[cM])
        dma(POOL, selb[:, :], c_sel[:, :], writes=[cM])
        sk8 = sinks.rearrange("(a g hb) -> hb g a", a=2, g=4, hb=2)
        for hb in range(2):
            dma(SP, sink8[hb * 4:(hb + 1) * 4, :], sk8[hb], writes=[cS], nonc=True)
        for i, g in enumerate((g1, g2, g3, g4)):
            dma(SP, gbc[:, i, :], g.partition_broadcast(128), writes=[cG])
        dma(SP, sinkp[:, :], sinks.partition_broadcast(128), writes=[cS])
        G(lambda: nc.gpsimd.memset(wbd[:, :, :, :], 0.0), w=[cW])
        G(lambda: nc.gpsimd.memset(epsT[:, 0:1], EPS), w=[cE])
        G(lambda: nc.gpsimd.memset(epsT[:, 1:2], 1.0), w=[cE])
        G(lambda: nc.gpsimd.memset(kbuf[:, :, :], 0.0), w=[kbufB])
        G(lambda: nc.gpsimd.memset(vbuf[:, :, :], 0.0), w=[vbufB])
        G(lambda: nc.gpsimd.memset(U[:, :], 0.0), w=UB)
        for gi, lw in enumerate((lw_a, lw_x)):
            lwv = lw.rearrange("(a hb g) ci d -> hb ci a g d", a=2, hb=2, g=4)
            for hb in range(2):
                for a in range(2):
                    dma(POOL, wbd[hb * 64:(hb + 1) * 64, gi, a * 4:(a + 1) * 4, hb * 64:(hb + 1) * 64],
                        lwv[hb][:, a], writes=[cW])
        schedule.n_setup = len(Ctx.ops)
        scrB = [Buf("scr%d" % i) for i in range(NPIECE)]
        win_v = w_in.rearrange("(k p) e -> p k e", p=128)

        def head_cols(base, h):
            return win_v[:, :, base + h * 64: base + (h + 1) * 64]

        specs = [[] for _ in range(NPIECE)]

        def spec(pi, c0, c1, src, p0=0, p1=128, k0=0, k1=8):
            specs[pi].append((p0, p1, k0, k1, c0, c1, src))

        for pi in range(2):
            for jj in range(4):
                j = pi * 4 + jj
                a, g = j // 4, j % 4
                spec(pi, jj * 128, jj * 128 + 64, head_cols(0, 8 * a + g))
                spec(pi, jj * 128 + 64, jj * 128 + 128, head_cols(0, 8 * a + 4 + g))
        spec(2, 0, 512, win_v[:, :, 1024:1536])
        for cp in range(4):
            for which, bases in ((0, (2560, 4608)), (1, (1536, 3584))):
                pi = 3 + 2 * cp + which
                for bi, base in enumerate(bases):
                    for ci in range(2):
                        c = 2 * cp + ci
                        a, g = c // 4, c % 4
                        col = (bi * 2 + ci) * 128
                        spec(pi, col, col + 64, head_cols(base, 8 * a + g))
                        spec(pi, col + 64, col + 128, head_cols(base, 8 * a + 4 + g))
        wo_v = w_out.rearrange("(a hb g p) e -> hb p a g e", a=2, hb=2, g=4)
        for hf in range(2):
            for hb in range(2):
                for a in range(2):
                    spec(11 + hf, 0, 512, wo_v[hb][:, a, :, hf * 512:(hf + 1) * 512],
                         p0=hb * 64, p1=(hb + 1) * 64, k0=a * 4, k1=(a + 1) * 4)
        wu_v = w_up.rearrange("(k p) e -> p k e", p=128)
        for q in range(8):
            spec(13 + q, 0, 512, wu_v[:, :, q * 512:(q + 1) * 512])
        wd_v = w_down.rearrange("(q k p) e -> q p k e", k=8, p=128)
        for hf in range(2):
            for q in range(4):
                spec(21 + hf * 4 + q, 0, 512, wd_v[q][:, :, hf * 512:(hf + 1) * 512])

        ntiles = 1 + NSEQ * (SEQ // TT)
        total_pieces = ntiles * NPIECE
        wstate = {"loaded": 0}

        def load_upto(n):
            while wstate["loaded"] < min(n, total_pieces):
                g = wstate["loaded"]
                pi = g % NPIECE
                s_ = g % NR
                if g < NPIECE:
                    for (p0, p1, k0, k1, c0, c1, src) in specs[pi]:
                        dma(POOL, ring[s_][p0:p1, k0:k1, c0:c1], src, writes=[ringB[s_]])
                    if ntiles > 1:
                        dma(SP, scr[pi], ring[s_][:, :, :], reads=[ringB[s_]], writes=[scrB[pi]])
                else:
                    dma(SP, ring[s_][:, :, :], scr[pi], reads=[scrB[pi]], writes=[ringB[s_]])
                wstate["loaded"] += 1

        def piece(tile_idx, pi, held=0):
            g = tile_idx * NPIECE + pi
            load_upto(g + NR - held)
            s_ = g % NR
            return ring[s_], ringB[s_]

        def gpf():
            i = cnt["pf"] % 2
            cnt["pf"] += 1
            return pf[i], pfB[i]

        def gpt():
            i = cnt["pt"] % 2
            cnt["pt"] += 1
            return ptt[i], ptB[i]

        def rstd_from(ss_ap, ssB, n):
            st, stB = gst()
            A(lambda: act(out=st[:n, 0:1], in_=ss_ap, func=AF.Sqrt, bias=epsT[:n, 0:1], scale=1.0 / D),
              r=[ssB, constB], w=[stB])
            V(lambda: nc.vector.reciprocal(out=st[:n, 1:2], in_=st[:n, 0:1]), r=[stB], w=[stB])
            return st[:n, 1:2], stB

        def norm_a(src_ap, srcB, gi, bt):
            i = cnt["xa"] % 2
            cnt["xa"] += 1
            xb, xbB = xnb[i], xnbB[i]
            st, stB = gst()
            A(lambda: act(out=xb[:bt, :], in_=src_ap, func=AF.Square, accum_out=st[:bt, 0:1]),
              r=[srcB], w=[xbB, stB])
            rs, rsB = rstd_from(st[:bt, 0:1], stB, bt)
            V(lambda: nc.vector.scalar_tensor_tensor(out=xb[:bt, :], in0=src_ap, scalar=rs, in1=gbc[:bt, gi, :],
                                                     op0=ALU.mult, op1=ALU.mult),
              r=[srcB, rsB, constB], w=[xbB])
            return i

        def norm_b(i, blk, bt):
            xb, xbB = xnb[i], xnbB[i]
            pt, ptb = gpt()
            ptv = pt[:, :].rearrange("p (k t) -> p k t", t=128)
            for k in range(8):
                P(lambda k=k: nc.tensor.transpose(out=ptv[:, k, :bt], in_=xb[:bt, k * 128:(k + 1) * 128],
                                                  identity=identb[:bt, :bt]),
                  r=[xbB, constB], w=[ptb], sig=(k == 7))
            A(lambda: act(out=xnT[:, :, blk * 128:blk * 128 + bt], in_=ptv[:, :, :bt], func=AF.Copy),
              r=[ptb], w=[xnTB[blk]])

        def proj(rg, rgB, cc, NT, nbk):
            pb, pbB = gpf()
            for k in range(8):
                P(lambda k=k: mm(pb[:, :NT], lhsT=rg[:, k, cc * 128:(cc + 1) * 128], rhs=xnT[:, k, :NT],
                                 start=(k == 0), stop=(k == 7)),
                  r=[rgB, xnTB[:nbk]], w=[pbB], sig=(k == 7))
            return pb, pbB

        def dma_perm_out(q, dst_dram, src_sb, n, reads):
            dv = dst_dram.rearrange("n (a hb g d) -> n a hb g d", a=2, hb=2, g=4, d=64)
            sv_ = src_sb.rearrange("n (a g hb d) -> n a hb g d", a=2, hb=2, g=4, d=64)
            for a in range(2):
                for hb in range(2):
                    dma(q, dv[:, a, hb], sv_[:, a, hb], reads=reads)

        def dma_perm_in(q, dst_sb, src_dram, n, writes):
            sv_ = src_dram.rearrange("n (a hb g d) -> n a hb g d", a=2, hb=2, g=4, d=64)
            dv = dst_sb.rearrange("n (a g hb d) -> n a hb g d", a=2, hb=2, g=4, d=64)
            for a in range(2):
                for hb in range(2):
                    dma(q, dv[:, a, hb], sv_[:, a, hb], writes=writes)

        def vec_setup():
            dma_perm_in(SP, stg[0:4, :], conv_w[:, :], 4, [stgB])
            for i, v in enumerate((conv_b, lb_a, lb_x, lam)):
                dma_perm_in(SP, stg[4 + i:5 + i, :], v.rearrange("(o d) -> o d", o=1), 1, [stgB])
            pb, pbB = pf[0], pfB[0]
            for c in range(8):
                P(lambda c=c: nc.tensor.transpose(out=pb[:, c * 8:(c + 1) * 8], in_=stg[0:8, c * 128:(c + 1) * 128],
                                                  identity=identf[0:8, 0:8]),
                  r=[stgB, cI], w=[pbB], sig=(c == 7))
            pb3 = pb[:, 0:64].rearrange("p (c v) -> p v c", v=8)
            V(lambda: nc.vector.tensor_copy(out=cw[:, :, :], in_=pb3[:, 0:4, :]), r=[pbB], w=[cV])
            V(lambda: nc.vector.tensor_copy(out=pv[:, 0:4, :], in_=pb3[:, 4:8, :]), r=[pbB], w=[cV])
            A(lambda: act(out=pv[:, 6, :], in_=pv[:, 3, :], func=AF.Exp, scale=-1.0), r=[cV, cE], w=[cV])
            A(lambda: act(out=pv[:, 6, :], in_=pv[:, 6, :], func=AF.Ln, bias=epsT[:, 1:2], scale=1.0),
              r=[cV, cE], w=[cV])
            V(lambda: nc.vector.tensor_scalar(out=pv[:, 4, :], in0=pv[:, 6, :], scalar1=-4.0, scalar2=None,
                                              op0=ALU.mult), r=[cV, cE], w=[cV])
            V(lambda: nc.vector.tensor_scalar(out=pv[:, 5, :], in0=pv[:, 6, :], scalar1=-8.0, scalar2=None,
                                              op0=ALU.mult), r=[cV, cE], w=[cV])
            for i_ in (1, 2):
                V(lambda i_=i_: nc.vector.tensor_scalar(out=pv[:, i_, :], in0=pv[:, i_, :], scalar1=0.5, scalar2=None,
                                                        op0=ALU.mult), r=[cV, cE], w=[cV])

        def out_rows_from_chunks(srcs, n, srcBs, dst_dram):
            for half in range(2):
                pb, pbB = gpf()
                for cc in range(4):
                    c = half * 4 + cc
                    P(lambda c=c, cc=cc: nc.tensor.transpose(out=pb[:n, cc * 128:(cc + 1) * 128], in_=srcs[c],
                                                             identity=identf[:, :]),
                      r=[srcBs, constB], w=[pbB], sig=(cc == 3))
                V(lambda: nc.vector.tensor_copy(out=stg[:n, half * 512:(half + 1) * 512], in_=pb[:n, :]),
                  r=[pbB], w=[stgB])
            dma_perm_out(SP, dst_dram, stg[:n, :], n, [stgB])

        class AttnPipe:
            def __init__(self):
                self.q2 = []
                self.q3 = []

            def push(self, u):
                st1 = self.phase1(u)
                self.q2.append(st1)
                if len(self.q2) > 2:
                    self.q3.append(self.phase2(self.q2.pop(0)))
                if len(self.q3) > 1:
                    self.phase3(self.q3.pop(0))

            def drain(self):
                while self.q2:
                    self.q3.append(self.phase2(self.q2.pop(0)))
                    if len(self.q3) > 1:
                        self.phase3(self.q3.pop(0))
                while self.q3:
                    self.phase3(self.q3.pop(0))

            def phase1(self, u):
                nq, j, qcols, kparts, maskmm, nk = u["nq"], u["j"], u["qcols"], u["kparts"], u["mask"], u["nk"]
                par = cnt["au"] % 2
                cnt["au"] += 1
                bank, bB = pf[2 + par], pfB[2 + par]
                sv = bank[:, :].rearrange("p (h k) -> p h k", h=2)
                nparts = len(kparts)
                for h in range(2):
                    P(maskmm(h, sv[:nq, h, :nk]), r=[constB], w=[bB], sig=False)
                    off = 0
                    for pi_, (kfn, n, kb) in enumerate(kparts):
                        last = (h == 1 and pi_ == nparts - 1)
                        P(lambda h=h, off=off, n=n, kfn=kfn, last=last: mm(
                            sv[:nq, h, off:off + n], lhsT=qT[h * 64:(h + 1) * 64, j, qcols[0]:qcols[1]],
                            rhs=kfn(h), start=False, stop=last),
                          r=[qTB(j), kb], w=[bB], sig=last)
                        off += n
                st, stB = gst()
                V(lambda: nc.vector.tensor_reduce(out=st[:nq, 0:2], in_=sv[:nq, :, :nk], axis=AX.X, op=ALU.max),
                  r=[bB], w=[stB])
                V(lambda: nc.vector.tensor_scalar(out=st[:nq, 2:4], in0=st[:nq, 0:2], scalar1=-0.125, scalar2=None,
                                                  op0=ALU.mult), r=[stB], w=[stB])
                V(lambda: nc.vector.tensor_tensor(out=st[:nq, 6:8], in0=st[:nq, 2:4], in1=sinkp[:nq, 2 * j:2 * j + 2],
                                                  op=ALU.add), r=[stB, constB], w=[stB])
                Et = TF.get()
                E, EB = Et
                Ev = E[:, :].rearrange("p (h k) -> p h k", h=2)
                for h in range(2):
                    A(lambda h=h: act(out=Ev[:nq, h, :nk], in_=sv[:nq, h, :nk], func=AF.Exp,
                                      bias=st[:nq, 2 + h:3 + h], scale=0.125, accum_out=st[:nq, 4 + h:5 + h]),
                      r=[bB, stB], w=[EB, stB])
                A(lambda: act(out=st[:nq, 6:8], in_=st[:nq, 6:8], func=AF.Exp), r=[stB], w=[stB])
                V(lambda: nc.vector.tensor_tensor(out=st[:nq, 6:8], in0=st[:nq, 6:8], in1=st[:nq, 4:6], op=ALU.add),
                  r=[stB], w=[stB])
                V(lambda: nc.vector.reciprocal(out=st[:nq, 6:8], in_=st[:nq, 6:8]), r=[stB], w=[stB])
                Pt = TB.get()
                Pn, PnB = Pt
                Pv = Pn[:, :].rearrange("p (h k) -> p h k", h=2)
                for h in range(2):
                    V(lambda h=h: nc.vector.tensor_scalar(out=Pv[:nq, h, :nk], in0=Ev[:nq, h, :nk],
                                                          scalar1=st[:nq, 6 + h:7 + h], scalar2=None, op0=ALU.mult),
                      r=[EB, stB], w=[PnB])
                TF.put(Et)
                return (u, Pt)

            def phase2(self, state):
                u, Pt = state
                nq, vparts = u["nq"], u["vparts"]
                Pn, PnB = Pt
                Pv = Pn[:, :].rearrange("p (h k) -> p h k", h=2)
                pt, ptb = gpt()
                ptv = pt[:, 0:512].rearrange("p (i t) -> p i t", t=128)
                nvp = len(vparts)
                for h in range(2):
                    for vi, (lv, K, (c0, c1), vb) in enumerate(vparts):
                        last = (h == 1 and vi == nvp - 1)
                        P(lambda h=h, vi=vi, K=K, c0=c0, c1=c1: nc.tensor.transpose(
                            out=ptv[:K, h * 2 + vi, :nq], in_=Pv[:nq, h, c0:c1], identity=identb[:nq, :nq]),
                          r=[PnB, constB], w=[ptb], sig=last)
                PTt = TB.get()
                PT, PTB = PTt
                PTv = PT[:, :].rearrange("p (i t) -> p i t", t=128)
                if nq == 128:
                    A(lambda: act(out=PTv[:, :, :], in_=ptv[:, :, :], func=AF.Copy), r=[ptb], w=[PTB])
                else:
                    for vi, (lv, K, cc_, vb) in enumerate(vparts):
                        A(lambda vi=vi, K=K: act(out=PTv[:K, vi::2, :nq], in_=ptv[:K, vi::2, :nq], func=AF.Copy),
                          r=[ptb], w=[PTB])
                TB.put(Pt)
                return (u, PTt)

            def phase3(self, state):
                u, PTt = state
                nq, j, vparts, ocols = u["nq"], u["j"], u["vparts"], u["ocols"]
                PT, PTB = PTt
                PTv = PT[:, :].rearrange("p (i t) -> p i t", t=128)
                ob, obB = pf[4], pfB[4]
                nvp = len(vparts)
                for h in range(2):
                    for vi, (lv, K, cc_, vb) in enumerate(vparts):
                        last = (h == 1 and vi == nvp - 1)
                        P(lambda h=h, vi=vi, K=K, lv=lv: mm(ob[:, h * 128:h * 128 + nq], lhsT=lv,
                                                            rhs=PTv[:K, h * 2 + vi, :nq],
                                                            start=(vi == 0), stop=(vi == nvp - 1)),
                          r=[PTB, vb], w=[obB], sig=last)
                V(lambda: nc.vector.tensor_copy(out=qT[0:64, j, ocols[0]:ocols[1]], in_=ob[0:64, 0:nq]),
                  r=[obB], w=qTB(j))
                V(lambda: nc.vector.tensor_copy(out=qT[64:128, j, ocols[0]:ocols[1]], in_=ob[64:128, 128:128 + nq]),
                  r=[obB], w=qTB(j))
                TB.put(PTt)

        tiles = [("p", seq, t0) for seq in range(NSEQ) for t0 in range(0, SEQ, TT)] + [("s", 0, 0)]

        def tinfo(ti):
            kind, seq, t0 = tiles[ti]
            sample = (kind == "s")
            return dict(sample=sample, seq=seq, t0=t0, NT=(NSAMP if sample else TT), nbk=(1 if sample else 4),
                        bt=(NSAMP if sample else 128),
                        xsrc=(x_s if sample else x_p[seq, t0:t0 + TT, :]))

        stA = {}
        marks = []
        build_program.marks = marks

        def stage_a_load(ti, blk):
            t = tinfo(ti)
            bt = t["bt"]
            i = cnt["xa"] % 2
            dma(SP, xa[i][:bt, :], t["xsrc"][blk * 128:blk * 128 + bt, :], writes=[xaB[i]])
            stA[(ti, blk)] = norm_a(xa[i][:bt, :], xaB[i], 0, bt)

        def stage_a_tr(ti, blk):
            t = tinfo(ti)
            norm_b(stA.pop((ti, blk)), blk, t["bt"])

        def run_tile(ti, prestaged):
            t = tinfo(ti)
            marks.append(("tile%d" % ti, len(Ctx.ops)))
            sample, seq, t0, NT, nbk, bt, xsrc = (t["sample"], t["seq"], t["t0"], t["NT"], t["nbk"], t["bt"],
                                                  t["xsrc"])
            first = (not sample) and t0 == 0
            lastt = (not sample) and t0 + TT == SEQ

            if not prestaged:
                for blk in range(nbk):
                    stage_a_load(ti, blk)
                    stage_a_tr(ti, blk)
            if sample:
                dma(SP, xt[:bt, 0, :], xsrc, writes=[xtB[0]])
            else:
                dma(SP, xt[:, :, :], xsrc.rearrange("(b p) d -> p b d", p=128), writes=xtB)

            if sample:
                dma_perm_in(SP, stg[:48, :], s_conv.rearrange("s i d -> (s i) d"), 48, [stgB])
                for half in range(2):
                    pb, pbB = gpf()
                    pv3 = pb[:, 0:192].rearrange("p (c t) -> p c t", t=48)
                    for cc in range(4):
                        c = half * 4 + cc
                        P(lambda c=c, cc=cc: nc.tensor.transpose(out=pv3[:, cc, :], in_=stg[:48, c * 128:(c + 1) * 128],
                                                                 identity=identf[:48, :48]),
                          r=[stgB, constB], w=[pbB], sig=(cc == 3))
                    V(lambda: nc.vector.tensor_copy(out=scT[:, half * 4:half * 4 + 4, :], in_=pv3[:, :, :]),
                      r=[pbB], w=[sampB])
                dma_perm_in(SP, stg[:16, :], s_lru[:, :], 16, [stgB])
                for half in range(2):
                    pb, pbB = gpf()
                    pv3 = pb[:, 0:64].rearrange("p (c t) -> p c t", t=16)
                    for cc in range(4):
                        c = half * 4 + cc
                        P(lambda c=c, cc=cc: nc.tensor.transpose(out=pv3[:, cc, :], in_=stg[:16, c * 128:(c + 1) * 128],
                                                                 identity=identf[:16, :16]),
                          r=[stgB, constB], w=[pbB], sig=(cc == 3))
                    V(lambda: nc.vector.tensor_copy(out=h0T[:, half * 4:half * 4 + 4, :], in_=pv3[:, :, :]),
                      r=[pbB], w=[sampB])
            elif first:
                V(lambda: nc.vector.memset(hst[:, :], 0.0), w=[hstB])
                V(lambda: nc.vector.memset(uh[:, :, :], 0.0), w=[uhB])

            if sample:
                qbdt = TB.get()
                qbd, qbdB = qbdt
                qbd5 = qbd[:, 0:256].rearrange("p (s a h g) -> p s a h g", s=16, a=2, h=2, g=4)
                V(lambda: nc.vector.memset(qbd[:, 0:256], 0.0), w=[qbdB])
            for pi in range(2):
                rg, rgB = piece(ti, pi)
                for cc in range(4):
                    j = pi * 4 + cc
                    pb, pbB = proj(rg, rgB, cc, NT, nbk)
                    if sample:
                        a_, g_ = j // 4, j % 4
                        V(lambda: nc.vector.tensor_copy(out=qbd5[0:64, :, a_, 0, g_], in_=pb[0:64, :NT]),
                          r=[pbB], w=[qbdB])
                        V(lambda: nc.vector.tensor_copy(out=qbd5[64:128, :, a_, 1, g_], in_=pb[64:128, :NT]),
                          r=[pbB], w=[qbdB])
                        continue
                    if cc % 2 == 0:
                        A(lambda: act(out=qT[:, j, :NT], in_=pb[:, :NT], func=AF.Copy), r=[pbB], w=qTB(j))
                    else:
                        V(lambda: nc.vector.tensor_copy(out=qT[:, j, :NT], in_=pb[:, :NT]), r=[pbB], w=qTB(j))
            rg, rgB = piece(ti, 2)
            for a in range(2):
                pb, pbB = proj(rg, rgB, a, NT, nbk)
                V(lambda: nc.vector.tensor_copy(out=kbuf[:, a, 128:128 + NT], in_=pb[:, :NT]), r=[pbB], w=[kbufB])
            for blk in range(nbk):
                pb, pbB = gpf()
                for k in range(8):
                    P(lambda k=k: mm(pb[:bt, :], lhsT=xnT[:, k, blk * 128:blk * 128 + bt], rhs=rg[:, k, :],
                                     start=(k == 0), stop=(k == 7)),
                      r=[rgB, xnTB[blk]], w=[pbB], sig=(k == 7))
                A(lambda: act(out=vbuf[:bt, 1 + blk, :], in_=pb[:bt, 256:512], func=AF.Copy), r=[pbB], w=[vbufB])
                if sample or (lastt and blk == nbk - 1):
                    V(lambda: nc.vector.tensor_copy(out=kvst[:bt, :], in_=pb[:bt, :]), r=[pbB], w=[kvstB])
                    if sample:
                        dma(SP, kws[:, 127, :], kvst[:bt, 0:256], reads=[kvstB])
                        dma(SP, vws[:, 127, :], kvst[:bt, 256:512], reads=[kvstB])
                    else:
                        dma(SP, kwp[seq], kvst[:, 0:256], reads=[kvstB])
                        dma(SP, vwp[seq], kvst[:, 256:512], reads=[kvstB])

            if sample:
                oall, oallB = pf[4], pfB[4]
                oall4 = oall[:, 0:256].rearrange("p (s a m) -> p s a m", s=16, a=2, m=8)
                for s in range(NSAMP):
                    i = s % 2
                    dma(SP, kcf[i][:, :], ck[s], writes=[kcfB[i]])
                    dma(POOL, vc[i][:, :], cv[s], writes=[vcB[i]])
                    pb, pbB = gpf()
                    for a in range(2):
                        P(lambda a=a: nc.tensor.transpose(out=pb[:, a * 128:(a + 1) * 128],
                                                          in_=kcf[i][:, a * 128:(a + 1) * 128], identity=identf[:, :]),
                          r=[kcfB[i], constB], w=[pbB], sig=(a == 1))
                    V(lambda: nc.vector.tensor_copy(out=kc[i][:, :, :], in_=pb[:, 0:256].rearrange("p (a t) -> p a t", a=2)),
                      r=[pbB], w=[kcB[i]])
                    bank, bB = pf[2 + i], pfB[2 + i]
                    sv = bank[:, :].rearrange("p (a k) -> p a k", a=2)
                    for a in range(2):
                        P(lambda a=a: mm(sv[0:8, a, 0:144], lhsT=selb[:, s * 8:(s + 1) * 8], rhs=msamp[:, 0:144],
                                         start=(a == 0), stop=False), r=[constB], w=[bB], sig=False)
                        P(lambda a=a: mm(sv[0:8, a, 0:128], lhsT=qbd5[:, s, a, :, :], rhs=kc[i][:, a, :],
                                         start=False, stop=False), r=[qbdB, kcB[i]], w=[bB], sig=False)
                        P(lambda a=a: mm(sv[0:8, a, 128:144], lhsT=qbd5[:, s, a, :, :], rhs=kbuf[:, a, 128:144],
                                         start=False, stop=(a == 1)), r=[qbdB, kbufB], w=[bB], sig=(a == 1))
                    st, stB = gst()
                    V(lambda: nc.vector.tensor_reduce(out=st[0:8, 0:2], in_=sv[0:8, :, 0:144], axis=AX.X, op=ALU.max),
                      r=[bB], w=[stB])
                    V(lambda: nc.vector.tensor_scalar(out=st[0:8, 2:4], in0=st[0:8, 0:2], scalar1=-0.125, scalar2=None,
                                                      op0=ALU.mult), r=[stB], w=[stB])
                    V(lambda: nc.vector.tensor_tensor(out=st[0:8, 6:8], in0=st[0:8, 2:4], in1=sink8[0:8, 0:2], op=ALU.add),
                      r=[stB, constB], w=[stB])
                    Et = TF.get(); E, EB = Et
                    Ev = E[:, :].rearrange("p (a k) -> p a k", a=2)
                    for a in range(2):
                        A(lambda a=a: act(out=Ev[0:8, a, 0:144], in_=sv[0:8, a, 0:144], func=AF.Exp,
                                          bias=st[0:8, 2 + a:3 + a], scale=0.125, accum_out=st[0:8, 4 + a:5 + a]),
                          r=[bB, stB], w=[EB, stB])
                    A(lambda: act(out=st[0:8, 6:8], in_=st[0:8, 6:8], func=AF.Exp), r=[stB], w=[stB])
                    V(lambda: nc.vector.tensor_tensor(out=st[0:8, 6:8], in0=st[0:8, 6:8], in1=st[0:8, 4:6], op=ALU.add),
                      r=[stB], w=[stB])
                    V(lambda: nc.vector.reciprocal(out=st[0:8, 6:8], in_=st[0:8, 6:8]), r=[stB], w=[stB])
                    Pt = TB.get(); Pn, PnB = Pt
                    Pv = Pn[:, :].rearrange("p (a k) -> p a k", a=2)
                    for a in range(2):
                        V(lambda a=a: nc.vector.tensor_scalar(out=Pv[0:8, a, 0:144], in0=Ev[0:8, a, 0:144],
                                                              scalar1=st[0:8, 6 + a:7 + a], scalar2=None, op0=ALU.mult),
                          r=[EB, stB], w=[PnB])
                    TF.put(Et)
                    pt, ptb = gpt()
                    ptv = pt[:, 0:32].rearrange("p (i t) -> p i t", t=8)
                    for a in range(2):
                        P(lambda a=a: nc.tensor.transpose(out=ptv[:, 2 * a, :], in_=Pv[0:8, a, 0:128],
                                                          identity=identb[0:8, 0:8]),
                          r=[PnB, constB], w=[ptb], sig=False)
                        P(lambda a=a: nc.tensor.transpose(out=ptv[0:16, 2 * a + 1, :], in_=Pv[0:8, a, 128:144],
                                                          identity=identb[0:8, 0:8]),
                          r=[PnB, constB], w=[ptb], sig=(a == 1))
                    PTt = TB.get(); PT, PTB = PTt
                    PTv = PT[:, 0:32].rearrange("p (i t) -> p i t", t=8)
                    A(lambda: act(out=PTv[:, 0::2, :], in_=ptv[:, 0::2, :], func=AF.Copy), r=[ptb], w=[PTB])
                    A(lambda: act(out=PTv[0:16, 1::2, :], in_=ptv[0:16, 1::2, :], func=AF.Copy), r=[ptb], w=[PTB])
                    TB.put(Pt)
                    for a in range(2):
                        P(lambda a=a: mm(oall4[:, s, a, :], lhsT=vc[i][:, a * 128:(a + 1) * 128], rhs=PTv[:, 2 * a, :],
                                         start=True, stop=False), r=[PTB, vcB[i]], w=[oallB], sig=False)
                        P(lambda a=a: mm(oall4[:, s, a, :], lhsT=vbuf[0:16, 1, a * 128:(a + 1) * 128],
                                         rhs=PTv[0:16, 2 * a + 1, :], start=False, stop=True),
                          r=[PTB, vbufB], w=[oallB], sig=(a == 1))
                    TB.put(PTt)
                oall5 = oall[:, 0:256].rearrange("p (s a h g) -> p s a h g", s=16, a=2, h=2, g=4)
                for hb in range(2):
                    for a in range(2):
                        V(lambda hb=hb, a=a: nc.vector.tensor_copy(
                            out=qT[hb * 64:(hb + 1) * 64, a * 4:(a + 1) * 4, 0:16],
                            in_=oall5[hb * 64:(hb + 1) * 64, :, a, hb, :].rearrange("p s g -> p g s")),
                          r=[oallB], w=[qTB(a * 4 + g_) for g_ in range(4)])
                TB.put(qbdt)

            units = []
            if sample:
                pass
            else:
                for blk in range(nbk):
                    for j in range(8):
                        units.append(("p", blk, j))
            upos = [0]
            pipe = AttnPipe()
            sstate = {}

            def sample_prep(s):
                i = s % 2
                dma(SP, kcf[i][:, :], ck[s], writes=[kcfB[i]])
                dma(POOL, vc[i][:, :], cv[s], writes=[vcB[i]])
                pb, pbB = gpf()
                for a in range(2):
                    P(lambda a=a: nc.tensor.transpose(out=pb[:, a * 128:(a + 1) * 128],
                                                      in_=kcf[i][:, a * 128:(a + 1) * 128], identity=identf[:, :]),
                      r=[kcfB[i], constB], w=[pbB], sig=(a == 1))
                V(lambda: nc.vector.tensor_copy(out=kc[i][:, :, :], in_=pb[:, 0:256].rearrange("p (a t) -> p a t", a=2)),
                  r=[pbB], w=[kcB[i]])

            def push_units(n):
                for _ in range(n):
                    if upos[0] >= len(units):
                        return
                    kind_, x_, j = units[upos[0]]
                    upos[0] += 1
                    a = j // 4
                    if kind_ == "s":
                        s = x_
                        i = s % 2
                        if j == 0:
                            sample_prep(s)
                        u = dict(nq=1, j=j, qcols=(s, s + 1), nk=144, ocols=(s, s + 1),
                                 kparts=[(lambda h, a=a, i=i: kc[i][h * 64:(h + 1) * 64, a, :], 128, [kcB[i]]),
                                         (lambda h, a=a: kbuf[h * 64:(h + 1) * 64, a, 128:144], 16, [kbufB])],
                                 vparts=[(vc[i][:, a * 128:(a + 1) * 128], 128, (0, 128), [vcB[i]]),
                                         (vbuf[:16, 1, a * 128:(a + 1) * 128], 16, (128, 144), [vbufB])],
                                 mask=(lambda h, out_ap, s=s: (lambda: mm(out_ap, lhsT=identb[:, s:s + 1],
                                                                          rhs=msamp[:, 0:144], start=(h == 0),
                                                                          stop=False))))
                    else:
                        blk = x_
                        mv = 1 if (first and blk == 0) else 0
                        u = dict(nq=128, j=j, qcols=(blk * 128, blk * 128 + 128), nk=256,
                                 ocols=(blk * 128, blk * 128 + 128),
                                 kparts=[(lambda h, a=a, blk=blk: kbuf[h * 64:(h + 1) * 64, a, blk * 128:blk * 128 + 256],
                                          256, [kbufB])],
                                 vparts=[(vbuf[:, blk, a * 128:(a + 1) * 128], 128, (0, 128), [vbufB]),
                                         (vbuf[:, blk + 1, a * 128:(a + 1) * 128], 128, (128, 256), [vbufB])],
                                 mask=(lambda h, out_ap, mv=mv: (lambda: mm(out_ap, lhsT=identb[:, :],
                                                                            rhs=maskb[:, mv, 0:256], start=(h == 0),
                                                                            stop=False))))
                    pipe.push(u)

            upp = (len(units) + 31) // 32

            hS, hSB = hS_t, hS_B

            def R1(c):
                sl = c % 4
                uct = TF.get(); uc, ucB = uct
                if sample:
                    V(lambda: nc.vector.tensor_scalar(out=uc[:, :NT], in0=scT[:, c, 0::3], scalar1=cw[:, 0, c:c + 1],
                                                      scalar2=pv[:, 0, c:c + 1], op0=ALU.mult, op1=ALU.add),
                      r=[sampB, constB], w=[ucB])
                    for i_ in (1, 2):
                        V(lambda i_=i_: nc.vector.scalar_tensor_tensor(out=uc[:, :NT], in0=scT[:, c, i_::3],
                                                                       scalar=cw[:, i_, c:c + 1], in1=uc[:, :NT],
                                                                       op0=ALU.mult, op1=ALU.add),
                          r=[sampB, constB], w=[ucB])
                    V(lambda: nc.vector.scalar_tensor_tensor(out=uc[:, :NT], in0=ubuf[:, sl, 3:3 + NT],
                                                             scalar=cw[:, 3, c:c + 1], in1=uc[:, :NT],
                                                             op0=ALU.mult, op1=ALU.add),
                      r=[ubB(c), constB], w=[ucB])
                    V(lambda: nc.vector.tensor_copy(out=uS_t[:, c * 16:(c + 1) * 16], in_=ubuf[:, sl, 3:3 + NT]),
                      r=ubB(c), w=[sampB])
                else:
                    V(lambda: nc.vector.tensor_scalar(out=uc[:, :NT], in0=ubuf[:, sl, 0:NT], scalar1=cw[:, 0, c:c + 1],
                                                      scalar2=pv[:, 0, c:c + 1], op0=ALU.mult, op1=ALU.add),
                      r=[ubB(c), constB], w=[ucB])
                    for i_ in (1, 2, 3):
                        V(lambda i_=i_: nc.vector.scalar_tensor_tensor(out=uc[:, :NT], in0=ubuf[:, sl, i_:i_ + NT],
                                                                       scalar=cw[:, i_, c:c + 1], in1=uc[:, :NT],
                                                                       op0=ALU.mult, op1=ALU.add),
                          r=[ubB(c), constB], w=[ucB])
                    V(lambda: nc.vector.tensor_copy(out=uh[:, c, :], in_=ubuf[:, sl, NT:NT + 3]),
                      r=ubB(c), w=[uhB])
                ucbt = TB.get(); ucb, ucbB = ucbt
                A(lambda: act(out=ucb[:, :NT], in_=uc[:, :NT], func=AF.Copy), r=[ucB], w=[ucbB])
                return dict(c=c, uct=uct, ucbt=ucbt)

            def R2(stt_):
                c = stt_["c"]
                ucb, ucbB = stt_["ucbt"]
                trt = TF.get(); tr, trB = trt
                tit = TF.get(); ti_, tiB = tit
                P(lambda: mm(pf[5][:, :NT], lhsT=wbd[:, 0, c, :], rhs=ucb[:, :NT], start=True, stop=True),
                  r=[ucbB, constB], w=[pfB[5]])
                A(lambda: act(out=tr[:, :NT], in_=pf[5][:, :NT], func=AF.Tanh, bias=pv[:, 1, c:c + 1], scale=0.5),
                  r=[pfB[5], constB], w=[trB])
                P(lambda: mm(pf[5][:, :NT], lhsT=wbd[:, 1, c, :], rhs=ucb[:, :NT], start=True, stop=True),
                  r=[ucbB, constB], w=[pfB[5]])
                A(lambda: act(out=ti_[:, :NT], in_=pf[5][:, :NT], func=AF.Tanh, bias=pv[:, 2, c:c + 1], scale=0.5),
                  r=[pfB[5], constB], w=[tiB])
                TB.put(stt_["ucbt"])
                stt_["trt"] = trt
                stt_["tit"] = tit

            def R3a(stt_):
                c = stt_["c"]
                tr, trB = stt_["trt"]
                aat = TF.get(); aa, aaB = aat
                A(lambda: act(out=aa[:, :NT], in_=tr[:, :NT], func=AF.Exp, bias=pv[:, 4, c:c + 1],
                              scale=pv[:, 4, c:c + 1]), r=[trB, constB], w=[aaB])
                A(lambda: act(out=tr[:, :NT], in_=tr[:, :NT], func=AF.Exp, bias=pv[:, 5, c:c + 1],
                              scale=pv[:, 5, c:c + 1]), r=[constB], w=[trB])
                stt_["aat"] = aat

            def R3b(stt_):
                tr, trB = stt_["trt"]
                A(lambda: act(out=tr[:, :NT], in_=tr[:, :NT], func=AF.Sqrt, bias=epsT[:, 1:2], scale=-1.0),
                  r=[constB], w=[trB])
                if first:
                    V(lambda: nc.vector.memset(tr[:, 0:1], 1.0), w=[trB])

            def R3c(stt_):
                c = stt_["c"]
                sl = c % 4
                tr, trB = stt_["trt"]
                ti_, tiB = stt_["tit"]
                aa, aaB = stt_["aat"]
                uc, ucB = stt_["uct"]
                V(lambda: nc.vector.scalar_tensor_tensor(out=ti_[:, :NT], in0=ti_[:, :NT], scalar=1.0, in1=tr[:, :NT],
                                                         op0=ALU.add, op1=ALU.mult), r=[trB], w=[tiB])
                V(lambda: nc.vector.scalar_tensor_tensor(out=ti_[:, :NT], in0=ti_[:, :NT], scalar=0.5, in1=uc[:, :NT],
                                                         op0=ALU.mult, op1=ALU.mult), r=[ucB], w=[tiB])
                if sample:
                    hv = hS[:, c * 16:(c + 1) * 16]
                    V(lambda: nc.vector.tensor_tensor(out=aa[:, :NT], in0=aa[:, :NT], in1=h0T[:, c, :], op=ALU.mult),
                      r=[sampB], w=[aaB])
                    V(lambda: nc.vector.tensor_tensor(out=hv, in0=aa[:, :NT], in1=ti_[:, :NT], op=ALU.add),
                      r=[aaB, tiB], w=[hSB])
                    hB_ = hSB
                else:
                    hv = tr[:, :NT]
                    hB_ = trB
                    V(lambda: nc.vector.tensor_tensor_scan(out=hv, data0=aa[:, :NT], data1=ti_[:, :NT],
                                                           initial=hst[:, c:c + 1], op0=ALU.mult, op1=ALU.add),
                      r=[aaB, tiB, hstB], w=[trB])
                    V(lambda: nc.vector.tensor_copy(out=hst[:, c:c + 1], in_=tr[:, NT - 1:NT]), r=[trB], w=[hstB])
                V(lambda: nc.vector.scalar_tensor_tensor(out=merged[:, c, :NT], in0=hv, scalar=0.5, in1=gs[:, sl, :NT],
                                                         op0=ALU.mult, op1=ALU.mult),
                  r=[hB_, gsB(c)], w=mgB(c))
                TF.put(stt_["uct"]); TF.put(stt_["trt"]); TF.put(stt_["tit"]); TF.put(stt_["aat"])

            marks.append(("rnnpieces", len(Ctx.ops)))
            prev = []
            for cp in range(4):
                c0, c1 = 2 * cp, 2 * cp + 1
                rg, rgB = piece(ti, 3 + 2 * cp)
                gts = []
                for ci in range(2):
                    pb, pbB = proj(rg, rgB, ci, NT, nbk)
                    gtt = TF.get()
                    g32, g32B = gtt
                    A(lambda: act(out=g32[:, :NT], in_=pb[:, :NT], func=AF.Gelu_apprx_tanh), r=[pbB], w=[g32B])
                    gts.append(gtt)
                    push_units(upp)
                    if prev:
                        R2(prev[ci])
                for ci in range(2):
                    c = c0 + ci
                    pb, pbB = proj(rg, rgB, 2 + ci, NT, nbk)
                    trt = TF.get()
                    A(lambda trt=trt: act(out=trt[0][:, :NT], in_=pb[:, :NT], func=AF.Tanh, scale=0.5),
                      r=[pbB], w=[trt[1]])
                    V(lambda trt=trt, ci=ci, c=c: nc.vector.scalar_tensor_tensor(
                        out=gs[:, c % 4, :NT], in0=trt[0][:, :NT], scalar=1.0, in1=gts[ci][0][:, :NT],
                        op0=ALU.add, op1=ALU.mult), r=[trt[1], gts[ci][1]], w=gsB(c))
                    TF.put(trt)
                    TF.put(gts[ci])
                    push_units(upp)
                    if prev:
                        if ci == 0:
                            R3a(prev[0]); R3a(prev[1])
                        else:
                            R3b(prev[0]); R3b(prev[1])
                            R3c(prev[0]); R3c(prev[1])
                prev = []
                rg, rgB = piece(ti, 4 + 2 * cp)
                for ci in range(2):
                    c = c0 + ci
                    sl = c % 4
                    if not sample:
                        V(lambda c=c, sl=sl: nc.vector.tensor_copy(out=ubuf[:, sl, 0:3], in_=uh[:, c, :]),
                          r=[uhB], w=ubB(c))
                    pb, pbB = proj(rg, rgB, ci, NT, nbk)
                    V(lambda sl=sl, c=c: nc.vector.tensor_copy(out=ubuf[:, sl, 3:3 + NT], in_=pb[:, :NT]),
                      r=[pbB], w=ubB(c))
                    push_units(upp)
                    prev.append(R1(c))
                for ci in range(2):
                    c = c0 + ci
                    pb, pbB = proj(rg, rgB, 2 + ci, NT, nbk)
                    A(lambda c=c: act(out=sa[:, c, :NT], in_=pb[:, :NT], func=AF.Tanh, scale=0.5), r=[pbB], w=saB(c))
                    push_units(upp)
            push_units(len(units))
            R2(prev[0]); R2(prev[1])
            R3a(prev[0]); R3a(prev[1])
            R3b(prev[0]); R3b(prev[1])
            pipe.drain()
            R3c(prev[0]); R3c(prev[1])

            marks.append(("merge", len(Ctx.ops)))
            for c in range(8):
                tmt = TF.get(); tm, tmB = tmt
                V(lambda c=c: nc.vector.scalar_tensor_tensor(out=tm[:, :NT], in0=sa[:, c, :NT], scalar=1.0,
                                                             in1=qT[:, c, :NT], op0=ALU.add, op1=ALU.mult),
                  r=[saB(c), qTB(c)], w=[tmB])
                V(lambda c=c: nc.vector.scalar_tensor_tensor(out=merged[:, c, :NT], in0=tm[:, :NT], scalar=0.5,
                                                             in1=merged[:, c, :NT], op0=ALU.mult, op1=ALU.add),
                  r=[tmB], w=mgB(c))
                TF.put(tmt)
            if debug and not sample:
                dma(SP, dbg_m[:, :, :], merged[:, :, :], reads=[mgB(c) for c in range(8)])

            if sample:
                out_rows_from_chunks([hS[:, c * 16:(c + 1) * 16] for c in range(8)], 16, [hSB], lrs[:, :])
                out_rows_from_chunks([uS_t[:, c * 16:(c + 1) * 16] for c in range(8)], 16, [sampB], cvs[:, 2, :])
            elif lastt:
                pb, pbB = gpf()
                P(lambda: nc.tensor.transpose(out=pb[:8, 0:128], in_=hst[:, 0:8], identity=identf[:, :]),
                  r=[hstB, constB], w=[pbB])
                V(lambda: nc.vector.tensor_copy(out=stg[:8, 0:128], in_=pb[:8, 0:128]), r=[pbB], w=[stgB])
                lv_ = lrp[seq].rearrange("(a hb g d) -> a hb g d", a=2, hb=2, g=4, d=64)
                for a in range(2):
                    for hb in range(2):
                        dma(SP, lv_[a, hb], stg[a * 4:(a + 1) * 4, hb * 64:(hb + 1) * 64], reads=[stgB])
                out_rows_from_chunks([uh[:, c, :] for c in range(8)], 3, [uhB], cvp[seq])
            if not sample and not lastt:
                V(lambda: nc.vector.tensor_copy(out=kbuf[:, :, 0:128], in_=kbuf[:, :, 512:640]), r=[kbufB], w=[kbufB])
                V(lambda: nc.vector.tensor_copy(out=vbuf[:, 0, :], in_=vbuf[:, 4, :]), r=[vbufB], w=[vbufB])

            marks.append(("wout", len(Ctx.ops)))
            ro0, ro0B = piece(ti, 11)
            ro1, ro1B = piece(ti, 12, held=1)

            def wout_mm(blk):
                base = 2 * (blk % 2)
                for hf, (ro, roB) in enumerate(((ro0, ro0B), (ro1, ro1B))):
                    for k in range(8):
                        P(lambda k=k, hf=hf, ro=ro: mm(pf[base + hf][:bt, :], lhsT=merged[:, k, blk * 128:blk * 128 + bt],
                                                       rhs=ro[:, k, :], start=(k == 0), stop=(k == 7)),
                          r=[roB, mgB(k)], w=[pfB[base + hf]], sig=(k == 7))

            def wout_post(blk):
                base = 2 * (blk % 2)
                st, stB = gst()
                jkt = TB.get(); jk, jkB = jkt
                for hf in range(2):
                    A(lambda hf=hf: act(out=jk[:bt, :], in_=pf[base + hf][:bt, :], func=AF.Square,
                                        accum_out=st[:bt, 2 + hf:3 + hf]), r=[pfB[base + hf]], w=[jkB, stB])
                TB.put(jkt)
                V(lambda: nc.vector.tensor_tensor(out=st[:bt, 4:5], in0=st[:bt, 2:3], in1=st[:bt, 3:4], op=ALU.add),
                  r=[stB], w=[stB])
                rs, rsB = rstd_from(st[:bt, 4:5], stB, bt)
                for hf in range(2):
                    tmt = TF.get(); tm, tmB = tmt
                    V(lambda hf=hf: nc.vector.tensor_tensor(out=tm[:bt, :], in0=pf[base + hf][:bt, :],
                                                            in1=gbc[:bt, 1, hf * 512:(hf + 1) * 512], op=ALU.mult),
                      r=[pfB[base + hf], constB], w=[tmB])
                    V(lambda hf=hf: nc.vector.scalar_tensor_tensor(
                        out=xt[:bt, blk, hf * 512:(hf + 1) * 512], in0=tm[:bt, :], scalar=rs,
                        in1=xt[:bt, blk, hf * 512:(hf + 1) * 512], op0=ALU.mult, op1=ALU.add),
                      r=[tmB, rsB], w=[xtB[blk]])
                    TF.put(tmt)
                return norm_a(xt[:bt, blk, :], xtB[blk], 2, bt)

            wout_mm(0)
            pend = None
            for blk in range(nbk):
                if blk + 1 < nbk:
                    wout_mm(blk + 1)
                i_ = wout_post(blk)
                if pend is not None:
                    norm_b(pend[0], pend[1], bt)
                pend = (i_, blk)
            norm_b(pend[0], pend[1], bt)
            if debug and not sample:
                dma(SP, dbg_x1[:, :, :], xt[:, :, :], reads=xtB)

            marks.append(("ffnup", len(Ctx.ops)))
            for q in range(8):
                rg, rgB = piece(ti, 13 + q)
                for cc in range(4):
                    fc = q * 4 + cc
                    pb, pbB = proj(rg, rgB, cc, NT, nbk)
                    rlt = TB.get(); rl, rlB = rlt
                    A(lambda: act(out=rl[:, :NT], in_=pb[:, :NT], func=AF.Relu), r=[pbB], w=[rlB])
                    V(lambda: nc.vector.tensor_tensor(out=hT[:, fc, :NT], in0=rl[:, :NT], in1=rl[:, :NT], op=ALU.mult),
                      r=[rlB], w=hTB(fc))
                    TB.put(rlt)
            marks.append(("ffndown", len(Ctx.ops)))
            nxt = ti + 1 if ti + 1 < len(tiles) else None
            nsteps = []
            if nxt is not None:
                nb2 = tinfo(nxt)["nbk"]
                for b2 in range(nb2):
                    nsteps.append(("l", b2))
                    nsteps.append(("t", b2))
            step = [0]

            def next_stage_step(n=1):
                for _ in range(n):
                    if step[0] < len(nsteps):
                        k_, b2 = nsteps[step[0]]
                        step[0] += 1
                        if k_ == "l":
                            stage_a_load(nxt, b2)
                        else:
                            stage_a_tr(nxt, b2)

            ssb, ssbB = ssb_t[:, :], ssb_B
            for hf in range(2):
                for q in range(4):
                    rg, rgB = piece(ti, 21 + hf * 4 + q)
                    for kk in range(8):
                        fc = q * 8 + kk
                        for blk in range(nbk):
                            P(lambda kk=kk, fc=fc, blk=blk: mm(pf[2 + blk][:bt, :], lhsT=hT[:, fc, blk * 128:blk * 128 + bt],
                                                               rhs=rg[:, kk, :], start=(fc == 0), stop=(fc == 31)),
                              r=[rgB, hTB(fc)], w=[pfB[2 + blk]], sig=(kk == 7 and blk == nbk - 1))
                    next_stage_step(1)
                for blk in range(nbk):
                    jkt = TB.get(); jk, jkB = jkt
                    A(lambda blk=blk, hf=hf: act(out=jk[:bt, :], in_=pf[2 + blk][:bt, :], func=AF.Square,
                                                 accum_out=ssb[:bt, 2 * blk + hf:2 * blk + hf + 1]),
                      r=[pfB[2 + blk]], w=[jkB, ssbB])
                    TB.put(jkt)
                    if hf == 0:
                        V(lambda blk=blk: nc.vector.tensor_tensor(out=fbuf[:bt, blk, :], in0=pf[2 + blk][:bt, :],
                                                                  in1=gbc[:bt, 3, 0:512], op=ALU.mult),
                          r=[pfB[2 + blk], constB], w=fbB(blk))
            for blk in range(nbk):
                st, stB = gst()
                V(lambda blk=blk: nc.vector.tensor_tensor(out=st[:bt, 4:5], in0=ssb[:bt, 2 * blk:2 * blk + 1],
                                                          in1=ssb[:bt, 2 * blk + 1:2 * blk + 2], op=ALU.add),
                  r=[ssbB], w=[stB])
                rs, rsB = rstd_from(st[:bt, 4:5], stB, bt)
                V(lambda blk=blk: nc.vector.scalar_tensor_tensor(out=xt[:bt, blk, 0:512], in0=fbuf[:bt, blk, :], scalar=rs,
                                                                 in1=xt[:bt, blk, 0:512], op0=ALU.mult, op1=ALU.add),
                  r=[fbB(blk), rsB], w=[xtB[blk]])
                tmt = TF.get(); tm, tmB = tmt
                V(lambda blk=blk: nc.vector.tensor_tensor(out=tm[:bt, :], in0=pf[2 + blk][:bt, :],
                                                          in1=gbc[:bt, 3, 512:1024], op=ALU.mult),
                  r=[pfB[2 + blk], constB], w=[tmB])
                V(lambda blk=blk: nc.vector.scalar_tensor_tensor(out=xt[:bt, blk, 512:1024], in0=tm[:bt, :], scalar=rs,
                                                                 in1=xt[:bt, blk, 512:1024], op0=ALU.mult, op1=ALU.add),
                  r=[tmB, rsB], w=[xtB[blk]])
                TF.put(tmt)
            if sample:
                dma(SP, y_s[:, :], xt[:bt, 0, :], reads=[xtB[0]])
            else:
                dma(SP, y_p[seq, t0:t0 + TT, :].rearrange("(b p) d -> p b d", p=128), xt[:, :, :], reads=xtB)
            next_stage_step(len(nsteps))
            return nxt is not None

        outB = Buf("outcopy")
        dma(SP, kws[:, 0:127, :], ck[:, 1:128, :], writes=[outB])
        dma(SP, vws[:, 0:127, :], cv[:, 1:128, :], writes=[outB])
        dma(SP, cvs[:, 0:2, :], s_conv[:, 1:3, :], writes=[outB])

        vec_setup()
        pre = False
        for ti in range(len(tiles)):
            pre = run_tile(ti, pre)

        assert Ctx.pend is None
        order, est = schedule(Ctx.ops)
        build_program.est_us = est
        for i in order:
            o = Ctx.ops[i]
            if o.kind == "dma":
                dma_emit(o)
            else:
                o.eng.emit(o)
        nc = nc_real
        for sc in all_dma_sems:
            nc.gpsimd.wait_ge(sc.s, sc.cnt)
        for e in (PE, ACT, DVE, SP):
            nc.gpsimd.wait_ge(e.sc.s, e.sc.cnt)
    return nc


_CACHE = {}


def _consts():
    ident = np.eye(128, dtype=np.float32)
    qi = np.arange(128)[:, None]
    kj = np.arange(256)[None, :]
    diff = qi + 128 - kj
    band = (diff >= 0) & (diff <= 128)
    m0 = np.where(band, 0.0, NEG).astype(np.float32)
    m1 = m0.copy()
    m1[:, :128] = NEG
    mask = np.stack([np.concatenate([m0, m0], axis=1), np.concatenate([m1, m1], axis=1)])
    ms = np.full((16, 256), NEG, dtype=np.float32)
    ms[:, :128] = 0.0
    for s in range(16):
        ms[s, 128 + s] = 0.0
    msamp = np.concatenate([ms, ms], axis=1)
    sel = np.zeros((128, 128), dtype=np.float32)
    for s_ in range(16):
        sel[s_, s_ * 8:(s_ + 1) * 8] = 1.0
    return ident, np.ascontiguousarray(mask), np.ascontiguousarray(msamp), sel


def kernel(x_prompt, x_sample, cache_k_win, cache_v_win, state_conv, state_lru,
           w_in, w_out, sinks, conv_w, conv_b, lru_w_a, lru_b_a, lru_w_x, lru_b_x, lru_lambda,
           w_up, w_down, g_pre_mix, g_post_mix, g_pre_ffn, g_post_ffn):
    f = lambda a: np.ascontiguousarray(np.asarray(a, dtype=np.float32))
    if "nc" not in _CACHE:
        _CACHE["nc"] = build_program()
    nc = _CACHE["nc"]
    ident, mask, msamp, sel = _consts()
    sk = f(sinks)[0].reshape(2, 2, 4).transpose(0, 2, 1).reshape(16)
    shared = {
        "w_in": f(w_in)[0], "w_out": f(w_out)[0], "sinks": np.ascontiguousarray(sk),
        "conv_w": f(conv_w)[0], "conv_b": f(conv_b)[0],
        "lru_w_a": f(lru_w_a)[0], "lru_b_a": f(lru_b_a)[0], "lru_w_x": f(lru_w_x)[0], "lru_b_x": f(lru_b_x)[0],
        "lru_lambda": f(lru_lambda)[0], "w_up": f(w_up)[0], "w_down": f(w_down)[0],
        "g_pre_mix": f(g_pre_mix)[0], "g_post_mix": f(g_post_mix)[0],
        "g_pre_ffn": f(g_pre_ffn)[0], "g_post_ffn": f(g_post_ffn)[0],
        "c_ident": ident, "c_mask": mask, "c_msamp": msamp, "c_sel": sel,
    }
    xp = f(x_prompt)
    xs = f(x_sample)[:, 0, :]
    ckk = f(cache_k_win)[0].reshape(128, 128, 256)
    cvv = f(cache_v_win)[0].reshape(128, 128, 256)
    sc = f(state_conv)[0]
    sl = f(state_lru)[0]
    in_maps = []
    for i in range(NCORES):
        m = dict(shared)
        m["x_prompt"] = np.ascontiguousarray(xp[2 * i:2 * i + 2])
        m["x_sample"] = np.ascontiguousarray(xs[16 * i:16 * i + 16])
        m["cache_k"] = np.ascontiguousarray(ckk[16 * i:16 * i + 16])
        m["cache_v"] = np.ascontiguousarray(cvv[16 * i:16 * i + 16])
        m["state_conv"] = np.ascontiguousarray(sc[16 * i:16 * i + 16])
        m["state_lru"] = np.ascontiguousarray(sl[16 * i:16 * i + 16])
        in_maps.append(m)
    res = run_bass_kernel_spmd(nc, in_maps, core_ids=list(range(NCORES)))
    R = res.results
    cat = lambda k: np.concatenate([np.asarray(r[k], dtype=np.float32) for r in R], axis=0)
    y_prompt = cat("y_prompt")
    y_sample = cat("y_sample").reshape(128, 1, D)
    kwp = cat("k_win_prompt").reshape(1, 16, 128, 4, 64)
    vwp = cat("v_win_prompt").reshape(1, 16, 128, 4, 64)
    cvp = cat("conv_prompt").reshape(1, 16, 3, D)
    lrp = cat("lru_prompt").reshape(1, 16, D)
    kws = cat("k_win_sample").reshape(1, 128, 128, 4, 64)
    vws = cat("v_win_sample").reshape(1, 128, 128, 4, 64)
    cvs = cat("conv_sample").reshape(1, 128, 3, D)
    lrs = cat("lru_sample").reshape(1, 128, D)
    return (y_prompt, y_sample, kwp, vwp, cvp, lrp, kws, vws, cvs, lrs)
```

```python
from contextlib import ExitStack
import numpy as np
import concourse.bass as bass
import concourse.mybir as mybir
from concourse.bass_utils import run_bass_kernel_spmd

F32 = mybir.dt.float32
BF16 = mybir.dt.bfloat16
AF = mybir.ActivationFunctionType
ALU = mybir.AluOpType
AX = mybir.AxisListType

NCORES = 8
D = 1024
SEQ = 2048
NSEQ = 2
NSAMP = 16
TT = 512
NPIECE = 29
NR = 6
EPS = 1e-6
NEG = -30000.0
GC1 = 0.7978845608028654
GC2 = 0.044715
STOP = 99


class Buf:
    __slots__ = ("name", "w", "r", "dsem", "dcnt", "excl")

    def __init__(self, name, excl=False):
        self.name = name
        self.excl = excl
        self.w = None
        self.r = []
        self.dsem = None
        self.dcnt = 0


class SemC:
    __slots__ = ("s", "cnt", "is_dma")

    def __init__(self, s, is_dma):
        self.s = s
        self.cnt = 0
        self.is_dma = is_dma


class Ctx:
    recording = True
    ops = []
    cap = []
    pend = None


class Op:
    __slots__ = ("eng", "calls", "reads", "writes", "dur", "ts", "kind", "dma", "idx", "lat")


def _free_size(ap):
    try:
        return int(ap.free_size())
    except Exception:
        return 512


def _estimate(engname, calls):
    d = 0.0
    ts = None
    for (m, a, k) in calls:
        name = getattr(m, "__name__", "")
        if engname == "PE":
            ap = k.get("rhs", k.get("identity"))
            n = _free_size(ap) if ap is not None else 128
            d += (max(n, 48) + 16) / 1900.0
        elif engname == "ACT":
            ap = k.get("in_")
            n = _free_size(ap) if ap is not None else 512
            d += 0.2 + n / 1100.0
            f = k.get("func")
            if f == AF.Exp:
                ts = "E"
            elif f == AF.Tanh:
                ts = "T"
            elif f == AF.Sqrt:
                ts = "S"
            elif f == AF.Ln:
                ts = "L"
            elif f == AF.Gelu_apprx_tanh:
                ts = "G"
        elif engname == "DVE":
            ap = k.get("in0", k.get("in_", k.get("data0", k.get("out", k.get("ap")))))
            if ap is None and a:
                ap = a[0]
            n = _free_size(ap) if ap is not None else 512
            mult = 2.0 if "scan" in name else 1.0
            d += 0.1 + mult * n / 760.0
        else:
            ap = k.get("in0", k.get("in_", k.get("out", k.get("ap"))))
            if ap is None and a:
                ap = a[0]
            n = _free_size(ap) if ap is not None else 512
            d += 0.25 + n / 450.0
    return d, ts


class _Dummy:
    def then_inc(self, *a, **k):
        return self


class Rec:
    def __init__(self, real):
        object.__setattr__(self, "_real", real)

    def __getattr__(self, name):
        real_m = getattr(object.__getattribute__(self, "_real"), name)

        def f(*a, **k):
            Ctx.cap.append((real_m, a, k))
            return _Dummy()
        f.__name__ = name
        return f


class NcProxy:
    def __init__(self, real):
        self._real = real
        self.tensor = Rec(real.tensor)
        self.vector = Rec(real.vector)
        self.scalar = Rec(real.scalar)
        self.gpsimd = Rec(real.gpsimd)

    def __getattr__(self, name):
        return getattr(self._real, name)


class Eng:
    def __init__(self, h, semc, name):
        self.h = h
        self.sc = semc
        self.name = name
        self.waited = {}

    def wait(self, tok):
        sc, val = tok
        if sc.is_dma:
            val = sc.cnt
        if self.waited.get(id(sc), 0) >= val:
            return
        self.h.wait_ge(sc.s, val)
        self.waited[id(sc)] = val

    def _deps(self, reads, writes):
        for b in reads:
            if b.w is not None:
                self.wait(b.w)
        for b in writes:
            if b.w is not None:
                self.wait(b.w)
            for t in b.r:
                self.wait(t)

    def op(self, fn, reads=(), writes=(), sig=True):
        del Ctx.cap[:]
        fn()
        calls = list(Ctx.cap)
        reads = list(reads)
        writes = list(writes)
        if Ctx.pend is not None:
            pc, pr, pw = Ctx.pend
            calls = pc + calls
            reads = pr + [b for b in reads if b not in pr]
            writes = pw + [b for b in writes if b not in pw]
            Ctx.pend = None
        if not sig:
            Ctx.pend = (calls, reads, writes)
            return
        ex = [b for b in reads if b.excl]
        if ex:
            writes = writes + [b for b in ex if b not in writes]
            reads = [b for b in reads if not b.excl]
        o = Op()
        o.eng, o.calls, o.reads, o.writes, o.kind, o.dma = self, calls, reads, writes, "c", None
        o.dur, o.ts = _estimate(self.name, calls)
        o.lat = 0.0
        o.idx = len(Ctx.ops)
        Ctx.ops.append(o)

    def emit(self, o):
        self._deps(o.reads, o.writes)
        inst = None
        for (m, a, k) in o.calls:
            inst = m(*a, **k)
        inst.then_inc(self.sc.s, 1)
        self.sc.cnt += 1
        tok = (self.sc, self.sc.cnt)
        for b in o.reads:
            b.r.append(tok)
        for b in o.writes:
            b.w = tok
            b.r = []


def schedule(ops):
    n = len(ops)
    if not hasattr(schedule, 'n_setup'):
        schedule.n_setup = 0
    lastw, readers = {}, {}
    preds = [set() for _ in range(n)]
    for i, o in enumerate(ops):
        for b in o.reads:
            w = lastw.get(id(b))
            if w is not None:
                preds[i].add(w)
        for b in o.writes:
            w = lastw.get(id(b))
            if w is not None:
                preds[i].add(w)
            for r in readers.get(id(b), ()):
                preds[i].add(r)
        for b in o.reads:
            readers.setdefault(id(b), []).append(i)
        for b in o.writes:
            lastw[id(b)] = i
            readers[id(b)] = []
        preds[i].discard(i)
    succs = [[] for _ in range(n)]
    npred = [0] * n
    for i in range(n):
        npred[i] = len(preds[i])
        for p in preds[i]:
            succs[p].append(i)
    ready_time = [0.0] * n
    finish = [0.0] * n
    start = [0.0] * n
    engs = {}
    for o in ops:
        engs.setdefault(id(o.eng), o.eng)
    eng_free = {k: 0.0 for k in engs}
    ready = {k: [] for k in engs}
    cur_ts = [None]
    for i in range(n):
        if npred[i] == 0:
            ready[id(ops[i].eng)].append(i)
    done = 0
    order = []
    while done < n:
        best = None
        for k, lst in ready.items():
            if not lst:
                continue
            ef = eng_free[k]
            isact = engs[k].name == "ACT"
            for i in lst[:32]:
                st = max(ready_time[i], ef)
                pen = 0.0
                if isact and ops[i].ts is not None:
                    t_ = ops[i].ts
                    if t_ == "T":
                        if cur_ts[0] not in ("E", "G"):
                            pen = 1.3
                    elif t_ != cur_ts[0]:
                        pen = 1.3
                if i < schedule.n_setup:
                    st, pen = 0.0, 0.0
                key = (st + pen, i)
                if best is None or key < best[0]:
                    best = (key, i, k, st + pen)
        _, i, k, st = best
        o = ops[i]
        ready[k].remove(i)
        start[i] = st
        if o.kind == "dma":
            eng_free[k] = st + o.dur
            finish[i] = st + o.dur + o.lat
        else:
            finish[i] = st + o.dur
            eng_free[k] = finish[i]
            if o.eng.name == "ACT" and o.ts is not None:
                if o.ts == "T":
                    if cur_ts[0] not in ("E", "G"):
                        cur_ts[0] = "E"
                else:
                    cur_ts[0] = o.ts
        order.append(i)
        done += 1
        for sidx in succs[i]:
            hop = 0.0 if ops[sidx].eng is o.eng and o.kind != "dma" else 0.25
            rt = finish[i] + hop
            if rt > ready_time[sidx]:
                ready_time[sidx] = rt
            npred[sidx] -= 1
            if npred[sidx] == 0:
                lst = ready[id(ops[sidx].eng)]
                lo, hi = 0, len(lst)
                while lo < hi:
                    mid = (lo + hi) // 2
                    if lst[mid] < sidx:
                        lo = mid + 1
                    else:
                        hi = mid
                lst.insert(lo, sidx)
    schedule.start = start
    schedule.finish = finish
    return order, max(finish) if n else 0.0


def _flat(lst):
    out = []
    for x in lst:
        if isinstance(x, (list, tuple)):
            out.extend(_flat(x))
        elif x is not None:
            out.append(x)
    return out


def build_program(debug=False):
    nc = bass.Bass("TRN2", target_bir_lowering=False)

    def din(name, shape, dt=F32):
        return nc.dram_tensor(name, list(shape), dt, kind="ExternalInput").ap()

    def dout(name, shape, dt=F32):
        return nc.dram_tensor(name, list(shape), dt, kind="ExternalOutput").ap()

    x_p = din("x_prompt", [NSEQ, SEQ, D])
    x_s = din("x_sample", [NSAMP, D])
    ck = din("cache_k", [NSAMP, 128, 256])
    cv = din("cache_v", [NSAMP, 128, 256])
    s_conv = din("state_conv", [NSAMP, 3, D])
    s_lru = din("state_lru", [NSAMP, D])
    w_in = din("w_in", [D, 5632])
    w_out = din("w_out", [D, D])
    sinks = din("sinks", [16])
    conv_w = din("conv_w", [4, D])
    conv_b = din("conv_b", [D])
    lw_a = din("lru_w_a", [16, 64, 64])
    lb_a = din("lru_b_a", [D])
    lw_x = din("lru_w_x", [16, 64, 64])
    lb_x = din("lru_b_x", [D])
    lam = din("lru_lambda", [D])
    w_up = din("w_up", [D, 4096])
    w_down = din("w_down", [4096, D])
    g1 = din("g_pre_mix", [D])
    g2 = din("g_post_mix", [D])
    g3 = din("g_pre_ffn", [D])
    g4 = din("g_post_ffn", [D])
    c_ident = din("c_ident", [128, 128])
    c_mask = din("c_mask", [2, 128, 512])
    c_msamp = din("c_msamp", [16, 512])
    c_sel = din("c_sel", [128, 128])

    y_p = dout("y_prompt", [NSEQ, SEQ, D])
    y_s = dout("y_sample", [NSAMP, D])
    kwp = dout("k_win_prompt", [NSEQ, 128, 256])
    vwp = dout("v_win_prompt", [NSEQ, 128, 256])
    cvp = dout("conv_prompt", [NSEQ, 3, D])
    lrp = dout("lru_prompt", [NSEQ, D])
    kws = dout("k_win_sample", [NSAMP, 128, 256])
    vws = dout("v_win_sample", [NSAMP, 128, 256])
    cvs = dout("conv_sample", [NSAMP, 3, D])
    lrs = dout("lru_sample", [NSAMP, D])

    scr = nc.dram_tensor("wscr", [NPIECE, 128, 8, 512], BF16, kind="Internal").ap()
    if debug:
        dbg_m = dout("dbg_merged", [128, 8, 512], BF16)
        dbg_a = dout("dbg_attn", [128, 8, 512], BF16)
        dbg_x1 = dout("dbg_x1", [128, 4, D])

    with ExitStack() as es:
        def sb(name, shape, dt=F32):
            return es.enter_context(nc.sbuf_tensor(name, list(shape), dt))

        def ps(name, shape, dt=F32):
            return es.enter_context(nc.psum_tensor(name, list(shape), dt))

        nsem = [0]

        def newsem(is_dma):
            nsem[0] += 1
            return SemC(es.enter_context(nc.semaphore("s%d" % nsem[0])), is_dma)

        ring = [sb("ring%d" % i, [128, 8, 512], BF16) for i in range(NR)]
        ringB = [Buf("ring%d" % i) for i in range(NR)]
        xa = [sb("xa%d" % i, [128, D]) for i in range(2)]
        xaB = [Buf("xa") for _ in range(2)]
        xnb = [sb("xnb%d" % i, [128, D], BF16) for i in range(2)]
        xnbB = [Buf("xnb") for _ in range(2)]
        xnT = sb("xnT", [128, 8, TT], BF16)
        xnTB = [Buf("xnT%d" % b) for b in range(4)]
        xt = sb("xt", [128, 4, D])
        xtB = [Buf("xt%d" % b) for b in range(4)]
        kbuf = sb("kbuf", [128, 2, 640], BF16)
        kbufB = Buf("kbuf")
        vbuf = sb("vbuf", [128, 5, 256], BF16)
        vbufB = Buf("vbuf")
        USL = 40
        U = sb("U", [128, USL * 256])
        UB = [Buf("U%d" % i) for i in range(USL)]

        def uview_bf(slot0, nslots):
            return U[:, slot0 * 256:(slot0 + nslots) * 256].bitcast(BF16).rearrange(
                "p (c t) -> p c t", t=512)

        qT = uview_bf(0, 8)
        sa = uview_bf(8, 8)
        merged = uview_bf(16, 8)
        gs = uview_bf(24, 4)
        ubuf = U[:, 28 * 256:28 * 256 + 4 * 516].rearrange("p (c t) -> p c t", t=516)
        hT = uview_bf(0, 32)
        fbuf = U[:, 32 * 256:40 * 256].rearrange("p (b t) -> p b t", t=512)

        def qTB(j): return [UB[j]]
        def saB(c): return [UB[8 + c]]
        def mgB(c): return [UB[16 + c]]
        def gsB(c): return [UB[24 + c % 4]]
        def ubB(c):
            lo = 28 * 1024 + (c % 4) * 2064
            return [UB[i] for i in range(lo // 1024, (lo + 2063) // 1024 + 1)]
        def hTB(fc): return [UB[fc]]
        def fbB(b): return [UB[32 + 2 * b], UB[33 + 2 * b]]

        class Pool_:
            def __init__(self, name, n, dt):
                self.free = [(sb("%s%d" % (name, i), [128, 512], dt), Buf(name)) for i in range(n)]

            def get(self):
                assert self.free, "temp pool exhausted"
                return self.free.pop(0)

            def put(self, t):
                self.free.append(t)

        TF = Pool_("tf", 13, F32)
        TB = Pool_("tb", 7, BF16)
        cnt = {"st": 0, "xa": 0, "pf": 0, "pt": 0, "au": 0}

        NST = 12
        stt = sb("stt", [128, NST, 8])
        sttB = [Buf("st") for _ in range(NST)]

        def gst():
            i = cnt["st"] % NST
            cnt["st"] += 1
            return stt[:, i, :], sttB[i]

        stg = sb("stg", [128, D])
        stgB = Buf("stg")
        kvst = sb("kvst", [128, 512])
        kvstB = Buf("kvst")
        hst = sb("hst", [128, 8])
        hstB = Buf("hst")
        uh = sb("uh", [128, 8, 3])
        uhB = Buf("uh")
        kcf = [sb("kcf%d" % i, [128, 256]) for i in range(2)]
        kcfB = [Buf("kcf") for _ in range(2)]
        kc = [sb("kc%d" % i, [128, 2, 128], BF16) for i in range(2)]
        kcB = [Buf("kc") for _ in range(2)]
        vc = [sb("vc%d" % i, [128, 256], BF16) for i in range(2)]
        vcB = [Buf("vc") for _ in range(2)]
        scT = sb("scT", [128, 8, 48])
        h0T = sb("h0T", [128, 8, 16])
        sampB = Buf("samp")
        hS_t = sb("hS_t", [128, 128])
        hS_B = Buf("hS")
        uS_t = sb("uS_t", [128, 128])
        ssb_t = sb("ssb_t", [128, 8])
        ssb_B = Buf("ssb")
        gbc = sb("gbc", [128, 4, D])
        identf = sb("identf", [128, 128])
        identb = sb("identb", [128, 128], BF16)
        maskb = sb("maskb", [128, 2, 512], BF16)
        msamp = sb("msamp", [128, 256], BF16)
        selb = sb("selb", [128, 128], BF16)
        sink8 = sb("sink8", [8, 2])
        wbd = sb("wbd", [128, 2, 8, 128], BF16)
        cw = sb("cw", [128, 4, 8])
        pv = sb("pv", [128, 8, 8])
        sinkp = sb("sinkp", [128, 16])
        epsT = sb("epsT", [128, 2])
        cI, cM, cG, cS, cV, cW, cE = (Buf("cI"), Buf("cM"), Buf("cG"), Buf("cS"), Buf("cV"), Buf("cW"), Buf("cE"))
        constB = [cI, cM, cG, cS, cV, cW, cE]

        NPF = 6
        pf = [ps("pf%d" % i, [128, 512]) for i in range(NPF)]
        pfB = [Buf("pf%d" % i, excl=True) for i in range(NPF)]
        ptt = [ps("pt%d" % i, [128, 1024], BF16) for i in range(2)]
        ptB = [Buf("pt%d" % i, excl=True) for i in range(2)]

        es.enter_context(nc.Block())

        PE = Eng(nc.tensor, newsem(False), "PE")
        ACT = Eng(nc.scalar, newsem(False), "ACT")
        DVE = Eng(nc.vector, newsem(False), "DVE")
        POOL = Eng(nc.gpsimd, newsem(False), "POOL")
        SP = Eng(nc.sync, newsem(False), "SP")
        nc_real = nc
        nc = NcProxy(nc_real)
        Ctx.recording = True
        Ctx.ops = []
        Ctx.cap = []
        Ctx.pend = None

        all_dma_sems = []

        def dma(q, out, in_, reads=(), writes=(), nonc=False):
            reads = _flat(reads)
            writes = _flat(writes)
            o = Op()
            o.eng, o.calls, o.reads, o.writes, o.kind = q, None, reads, writes, "dma"
            o.dma = (out, in_, nonc)
            o.ts = None
            try:
                nbytes = int(out.nbytes())
            except Exception:
                nbytes = 4096
            cast = (out.dtype != in_.dtype)
            o.dur = (1.0 if q is POOL else 0.15) + (nbytes / 80e3 if cast else 0.0)
            o.lat = 2.0 + nbytes / 150e3
            o.idx = len(Ctx.ops)
            Ctx.ops.append(o)

        def dma_emit(o):
            q = o.eng
            out, in_, nonc = o.dma
            reads, writes = o.reads, o.writes
            q._deps(reads, writes)
            tgt = (writes + reads)[0]
            if tgt.dsem is None:
                tgt.dsem = {}
            if id(q) not in tgt.dsem:
                tgt.dsem[id(q)] = newsem(True)
                all_dma_sems.append(tgt.dsem[id(q)])
            sc = tgt.dsem[id(q)]
            if nonc:
                with nc_real.allow_non_contiguous_dma(reason="small strided"):
                    q.h.dma_start(out=out, in_=in_).then_inc(sc.s, 16)
            else:
                q.h.dma_start(out=out, in_=in_).then_inc(sc.s, 16)
            sc.cnt += 16
            tok = (sc, sc.cnt)
            for b in reads:
                b.r.append(tok)
            for b in writes:
                b.w = tok
                b.r = []

        def A(fn, r=(), w=()):
            return ACT.op(fn, _flat(r), _flat(w))

        def V(fn, r=(), w=()):
            return DVE.op(fn, _flat(r), _flat(w))

        def G(fn, r=(), w=()):
            return POOL.op(fn, _flat(r), _flat(w))

        def P(fn, r=(), w=(), sig=True):
            return PE.op(fn, _flat(r), _flat(w), sig=sig)

        act = nc.scalar.activation
        mm = nc.tensor.matmul

        dma(SP, identf[:, :], c_ident[:, :], writes=[cI])
        dma(POOL, identb[:, :], c_ident[:, :], writes=[cI])
        dma(POOL, maskb[:, :, :], c_mask.rearrange("v p c -> p v c"), writes=[cM])
        G(lambda: nc.gpsimd.memset(msamp[:, :], 0.0), w=[cM])
        dma(POOL, msamp[0:16, :], c_msamp[:, 0:256], writes=[cM])
        dma(POOL, selb[:, :], c_sel[:, :], writes=[cM])
        sk8 = sinks.rearrange("(a g hb) -> hb g a", a=2, g=4, hb=2)
        for hb in range(2):
            dma(SP, sink8[hb * 4:(hb + 1) * 4, :], sk8[hb], writes=[cS], nonc=True)
        for i, g in enumerate((g1, g2, g3, g4)):
            dma(SP, gbc[:, i, :], g.partition_broadcast(128), writes=[cG])
        dma(SP, sinkp[:, :], sinks.partition_broadcast(128), writes=[cS])
        G(lambda: nc.gpsimd.memset(wbd[:, :, :, :], 0.0), w=[cW])
        G(lambda: nc.gpsimd.memset(epsT[:, 0:1], EPS), w=[cE])
        G(lambda: nc.gpsimd.memset(epsT[:, 1:2], 1.0), w=[cE])
        G(lambda: nc.gpsimd.memset(kbuf[:, :, :], 0.0), w=[kbufB])
        G(lambda: nc.gpsimd.memset(vbuf[:, :, :], 0.0), w=[vbufB])
        G(lambda: nc.gpsimd.memset(U[:, :], 0.0), w=UB)
        for gi, lw in enumerate((lw_a, lw_x)):
            lwv = lw.rearrange("(a hb g) ci d -> hb ci a g d", a=2, hb=2, g=4)
            for hb in range(2):
                for a in range(2):
                    dma(POOL, wbd[hb * 64:(hb + 1) * 64, gi, a * 4:(a + 1) * 4, hb * 64:(hb + 1) * 64],
                        lwv[hb][:, a], writes=[cW])
        schedule.n_setup = len(Ctx.ops)
        scrB = [Buf("scr%d" % i) for i in range(NPIECE)]
        win_v = w_in.rearrange("(k p) e -> p k e", p=128)

        def head_cols(base, h):
            return win_v[:, :, base + h * 64: base + (h + 1) * 64]

        specs = [[] for _ in range(NPIECE)]

        def spec(pi, c0, c1, src, p0=0, p1=128, k0=0, k1=8):
            specs[pi].append((p0, p1, k0, k1, c0, c1, src))

        for pi in range(2):
            for jj in range(4):
                j = pi * 4 + jj
                a, g = j // 4, j % 4
                spec(pi, jj * 128, jj * 128 + 64, head_cols(0, 8 * a + g))
                spec(pi, jj * 128 + 64, jj * 128 + 128, head_cols(0, 8 * a + 4 + g))
        spec(2, 0, 512, win_v[:, :, 1024:1536])
        for cp in range(4):
            for which, bases in ((0, (2560, 4608)), (1, (1536, 3584))):
                pi = 3 + 2 * cp + which
                for bi, base in enumerate(bases):
                    for ci in range(2):
                        c = 2 * cp + ci
                        a, g = c // 4, c % 4
                        col = (bi * 2 + ci) * 128
                        spec(pi, col, col + 64, head_cols(base, 8 * a + g))
                        spec(pi, col + 64, col + 128, head_cols(base, 8 * a + 4 + g))
        wo_v = w_out.rearrange("(a hb g p) e -> hb p a g e", a=2, hb=2, g=4)
        for hf in range(2):
            for hb in range(2):
                for a in range(2):
                    spec(11 + hf, 0, 512, wo_v[hb][:, a, :, hf * 512:(hf + 1) * 512],
                         p0=hb * 64, p1=(hb + 1) * 64, k0=a * 4, k1=(a + 1) * 4)
        wu_v = w_up.rearrange("(k p) e -> p k e", p=128)
        for q in range(8):
            spec(13 + q, 0, 512, wu_v[:, :, q * 512:(q + 1) * 512])
        wd_v = w_down.rearrange("(q k p) e -> q p k e", k=8, p=128)
        for hf in range(2):
            for q in range(4):
                spec(21 + hf * 4 + q, 0, 512, wd_v[q][:, :, hf * 512:(hf + 1) * 512])

        ntiles = 1 + NSEQ * (SEQ // TT)
        total_pieces = ntiles * NPIECE
        wstate = {"loaded": 0}

        def load_upto(n):
            while wstate["loaded"] < min(n, total_pieces):
                g = wstate["loaded"]
                pi = g % NPIECE
                s_ = g % NR
                if g < NPIECE:
                    for (p0, p1, k0, k1, c0, c1, src) in specs[pi]:
                        dma(POOL, ring[s_][p0:p1, k0:k1, c0:c1], src, writes=[ringB[s_]])
                    if ntiles > 1:
                        dma(SP, scr[pi], ring[s_][:, :, :], reads=[ringB[s_]], writes=[scrB[pi]])
                else:
                    dma(SP, ring[s_][:, :, :], scr[pi], reads=[scrB[pi]], writes=[ringB[s_]])
                wstate["loaded"] += 1

        def piece(tile_idx, pi, held=0):
            g = tile_idx * NPIECE + pi
            load_upto(g + NR - held)
            s_ = g % NR
            return ring[s_], ringB[s_]

        def gpf():
            i = cnt["pf"] % 2
            cnt["pf"] += 1
            return pf[i], pfB[i]

        def gpt():
            i = cnt["pt"] % 2
            cnt["pt"] += 1
            return ptt[i], ptB[i]

        def rstd_from(ss_ap, ssB, n):
            st, stB = gst()
            A(lambda: act(out=st[:n, 0:1], in_=ss_ap, func=AF.Sqrt, bias=epsT[:n, 0:1], scale=1.0 / D),
              r=[ssB, constB], w=[stB])
            V(lambda: nc.vector.reciprocal(out=st[:n, 1:2], in_=st[:n, 0:1]), r=[stB], w=[stB])
            return st[:n, 1:2], stB

        def norm_a(src_ap, srcB, gi, bt):
            i = cnt["xa"] % 2
            cnt["xa"] += 1
            xb, xbB = xnb[i], xnbB[i]
            st, stB = gst()
            A(lambda: act(out=xb[:bt, :], in_=src_ap, func=AF.Square, accum_out=st[:bt, 0:1]),
              r=[srcB], w=[xbB, stB])
            rs, rsB = rstd_from(st[:bt, 0:1], stB, bt)
            V(lambda: nc.vector.scalar_tensor_tensor(out=xb[:bt, :], in0=src_ap, scalar=rs, in1=gbc[:bt, gi, :],
                                                     op0=ALU.mult, op1=ALU.mult),
              r=[srcB, rsB, constB], w=[xbB])
            return i

        def norm_b(i, blk, bt):
            xb, xbB = xnb[i], xnbB[i]
            pt, ptb = gpt()
            ptv = pt[:, :].rearrange("p (k t) -> p k t", t=128)
            for k in range(8):
                P(lambda k=k: nc.tensor.transpose(out=ptv[:, k, :bt], in_=xb[:bt, k * 128:(k + 1) * 128],
                                                  identity=identb[:bt, :bt]),
                  r=[xbB, constB], w=[ptb], sig=(k == 7))
            A(lambda: act(out=xnT[:, :, blk * 128:blk * 128 + bt], in_=ptv[:, :, :bt], func=AF.Copy),
              r=[ptb], w=[xnTB[blk]])

        def proj(rg, rgB, cc, NT, nbk):
            pb, pbB = gpf()
            for k in range(8):
                P(lambda k=k: mm(pb[:, :NT], lhsT=rg[:, k, cc * 128:(cc + 1) * 128], rhs=xnT[:, k, :NT],
                                 start=(k == 0), stop=(k == 7)),
                  r=[rgB, xnTB[:nbk]], w=[pbB], sig=(k == 7))
            return pb, pbB

        def dma_perm_out(q, dst_dram, src_sb, n, reads):
            dv = dst_dram.rearrange("n (a hb g d) -> n a hb g d", a=2, hb=2, g=4, d=64)
            sv_ = src_sb.rearrange("n (a g hb d) -> n a hb g d", a=2, hb=2, g=4, d=64)
            for a in range(2):
                for hb in range(2):
                    dma(q, dv[:, a, hb], sv_[:, a, hb], reads=reads)

        def dma_perm_in(q, dst_sb, src_dram, n, writes):
            sv_ = src_dram.rearrange("n (a hb g d) -> n a hb g d", a=2, hb=2, g=4, d=64)
            dv = dst_sb.rearrange("n (a g hb d) -> n a hb g d", a=2, hb=2, g=4, d=64)
            for a in range(2):
                for hb in range(2):
                    dma(q, dv[:, a, hb], sv_[:, a, hb], writes=writes)

        def vec_setup():
            dma_perm_in(SP, stg[0:4, :], conv_w[:, :], 4, [stgB])
            for i, v in enumerate((conv_b, lb_a, lb_x, lam)):
                dma_perm_in(SP, stg[4 + i:5 + i, :], v.rearrange("(o d) -> o d", o=1), 1, [stgB])
            pb, pbB = pf[0], pfB[0]
            for c in range(8):
                P(lambda c=c: nc.tensor.transpose(out=pb[:, c * 8:(c + 1) * 8], in_=stg[0:8, c * 128:(c + 1) * 128],
                                                  identity=identf[0:8, 0:8]),
                  r=[stgB, cI], w=[pbB], sig=(c == 7))
            pb3 = pb[:, 0:64].rearrange("p (c v) -> p v c", v=8)
            V(lambda: nc.vector.tensor_copy(out=cw[:, :, :], in_=pb3[:, 0:4, :]), r=[pbB], w=[cV])
            V(lambda: nc.vector.tensor_copy(out=pv[:, 0:4, :], in_=pb3[:, 4:8, :]), r=[pbB], w=[cV])
            A(lambda: act(out=pv[:, 6, :], in_=pv[:, 3, :], func=AF.Exp, scale=-1.0), r=[cV, cE], w=[cV])
            A(lambda: act(out=pv[:, 6, :], in_=pv[:, 6, :], func=AF.Ln, bias=epsT[:, 1:2], scale=1.0),
              r=[cV, cE], w=[cV])
            V(lambda: nc.vector.tensor_scalar(out=pv[:, 4, :], in0=pv[:, 6, :], scalar1=-4.0, scalar2=None,
                                              op0=ALU.mult), r=[cV, cE], w=[cV])
            V(lambda: nc.vector.tensor_scalar(out=pv[:, 5, :], in0=pv[:, 6, :], scalar1=-8.0, scalar2=None,
                                              op0=ALU.mult), r=[cV, cE], w=[cV])
            for i_ in (1, 2):
                V(lambda i_=i_: nc.vector.tensor_scalar(out=pv[:, i_, :], in0=pv[:, i_, :], scalar1=0.5, scalar2=None,
                                                        op0=ALU.mult), r=[cV, cE], w=[cV])

        def out_rows_from_chunks(srcs, n, srcBs, dst_dram):
            for half in range(2):
                pb, pbB = gpf()
                for cc in range(4):
                    c = half * 4 + cc
                    P(lambda c=c, cc=cc: nc.tensor.transpose(out=pb[:n, cc * 128:(cc + 1) * 128], in_=srcs[c],
                                                             identity=identf[:, :]),
                      r=[srcBs, constB], w=[pbB], sig=(cc == 3))
                V(lambda: nc.vector.tensor_copy(out=stg[:n, half * 512:(half + 1) * 512], in_=pb[:n, :]),
                  r=[pbB], w=[stgB])
            dma_perm_out(SP, dst_dram, stg[:n, :], n, [stgB])

        class AttnPipe:
            def __init__(self):
                self.q2 = []
                self.q3 = []

            def push(self, u):
                st1 = self.phase1(u)
                self.q2.append(st1)
                if len(self.q2) > 2:
                    self.q3.append(self.phase2(self.q2.pop(0)))
                if len(self.q3) > 1:
                    self.phase3(self.q3.pop(0))

            def drain(self):
                while self.q2:
                    self.q3.append(self.phase2(self.q2.pop(0)))
                    if len(self.q3) > 1:
                        self.phase3(self.q3.pop(0))
                while self.q3:
                    self.phase3(self.q3.pop(0))

            def phase1(self, u):
                nq, j, qcols, kparts, maskmm, nk = u["nq"], u["j"], u["qcols"], u["kparts"], u["mask"], u["nk"]
                par = cnt["au"] % 2
                cnt["au"] += 1
                bank, bB = pf[2 + par], pfB[2 + par]
                sv = bank[:, :].rearrange("p (h k) -> p h k", h=2)
                nparts = len(kparts)
                for h in range(2):
                    P(maskmm(h, sv[:nq, h, :nk]), r=[constB], w=[bB], sig=False)
                    off = 0
                    for pi_, (kfn, n, kb) in enumerate(kparts):
                        last = (h == 1 and pi_ == nparts - 1)
                        P(lambda h=h, off=off, n=n, kfn=kfn, last=last: mm(
                            sv[:nq, h, off:off + n], lhsT=qT[h * 64:(h + 1) * 64, j, qcols[0]:qcols[1]],
                            rhs=kfn(h), start=False, stop=last),
                          r=[qTB(j), kb], w=[bB], sig=last)
                        off += n
                st, stB = gst()
                V(lambda: nc.vector.tensor_reduce(out=st[:nq, 0:2], in_=sv[:nq, :, :nk], axis=AX.X, op=ALU.max),
                  r=[bB], w=[stB])
                V(lambda: nc.vector.tensor_scalar(out=st[:nq, 2:4], in0=st[:nq, 0:2], scalar1=-0.125, scalar2=None,
                                                  op0=ALU.mult), r=[stB], w=[stB])
                V(lambda: nc.vector.tensor_tensor(out=st[:nq, 6:8], in0=st[:nq, 2:4], in1=sinkp[:nq, 2 * j:2 * j + 2],
                                                  op=ALU.add), r=[stB, constB], w=[stB])
                Et = TF.get()
                E, EB = Et
                Ev = E[:, :].rearrange("p (h k) -> p h k", h=2)
                for h in range(2):
                    A(lambda h=h: act(out=Ev[:nq, h, :nk], in_=sv[:nq, h, :nk], func=AF.Exp,
                                      bias=st[:nq, 2 + h:3 + h], scale=0.125, accum_out=st[:nq, 4 + h:5 + h]),
                      r=[bB, stB], w=[EB, stB])
                A(lambda: act(out=st[:nq, 6:8], in_=st[:nq, 6:8], func=AF.Exp), r=[stB], w=[stB])
                V(lambda: nc.vector.tensor_tensor(out=st[:nq, 6:8], in0=st[:nq, 6:8], in1=st[:nq, 4:6], op=ALU.add),
                  r=[stB], w=[stB])
                V(lambda: nc.vector.reciprocal(out=st[:nq, 6:8], in_=st[:nq, 6:8]), r=[stB], w=[stB])
                Pt = TB.get()
                Pn, PnB = Pt
                Pv = Pn[:, :].rearrange("p (h k) -> p h k", h=2)
                for h in range(2):
                    V(lambda h=h: nc.vector.tensor_scalar(out=Pv[:nq, h, :nk], in0=Ev[:nq, h, :nk],
                                                          scalar1=st[:nq, 6 + h:7 + h], scalar2=None, op0=ALU.mult),
                      r=[EB, stB], w=[PnB])
                TF.put(Et)
                return (u, Pt)

            def phase2(self, state):
                u, Pt = state
                nq, vparts = u["nq"], u["vparts"]
                Pn, PnB = Pt
                Pv = Pn[:, :].rearrange("p (h k) -> p h k", h=2)
                pt, ptb = gpt()
                ptv = pt[:, 0:512].rearrange("p (i t) -> p i t", t=128)
                nvp = len(vparts)
                for h in range(2):
                    for vi, (lv, K, (c0, c1), vb) in enumerate(vparts):
                        last = (h == 1 and vi == nvp - 1)
                        P(lambda h=h, vi=vi, K=K, c0=c0, c1=c1: nc.tensor.transpose(
                            out=ptv[:K, h * 2 + vi, :nq], in_=Pv[:nq, h, c0:c1], identity=identb[:nq, :nq]),
                          r=[PnB, constB], w=[ptb], sig=last)
                PTt = TB.get()
                PT, PTB = PTt
                PTv = PT[:, :].rearrange("p (i t) -> p i t", t=128)
                if nq == 128:
                    A(lambda: act(out=PTv[:, :, :], in_=ptv[:, :, :], func=AF.Copy), r=[ptb], w=[PTB])
                else:
                    for vi, (lv, K, cc_, vb) in enumerate(vparts):
                        A(lambda vi=vi, K=K: act(out=PTv[:K, vi::2, :nq], in_=ptv[:K, vi::2, :nq], func=AF.Copy),
                          r=[ptb], w=[PTB])
                TB.put(Pt)
                return (u, PTt)

            def phase3(self, state):
                u, PTt = state
                nq, j, vparts, ocols = u["nq"], u["j"], u["vparts"], u["ocols"]
                PT, PTB = PTt
                PTv = PT[:, :].rearrange("p (i t) -> p i t", t=128)
                ob, obB = pf[4], pfB[4]
                nvp = len(vparts)
                for h in range(2):
                    for vi, (lv, K, cc_, vb) in enumerate(vparts):
                        last = (h == 1 and vi == nvp - 1)
                        P(lambda h=h, vi=vi, K=K, lv=lv: mm(ob[:, h * 128:h * 128 + nq], lhsT=lv,
                                                            rhs=PTv[:K, h * 2 + vi, :nq],
                                                            start=(vi == 0), stop=(vi == nvp - 1)),
                          r=[PTB, vb], w=[obB], sig=last)
                V(lambda: nc.vector.tensor_copy(out=qT[0:64, j, ocols[0]:ocols[1]], in_=ob[0:64, 0:nq]),
                  r=[obB], w=qTB(j))
                V(lambda: nc.vector.tensor_copy(out=qT[64:128, j, ocols[0]:ocols[1]], in_=ob[64:128, 128:128 + nq]),
                  r=[obB], w=qTB(j))
                TB.put(PTt)

        tiles = [("p", seq, t0) for seq in range(NSEQ) for t0 in range(0, SEQ, TT)] + [("s", 0, 0)]

        def tinfo(ti):
            kind, seq, t0 = tiles[ti]
            sample = (kind == "s")
            return dict(sample=sample, seq=seq, t0=t0, NT=(NSAMP if sample else TT), nbk=(1 if sample else 4),
                        bt=(NSAMP if sample else 128),
                        xsrc=(x_s if sample else x_p[seq, t0:t0 + TT, :]))

        stA = {}
        marks = []
        build_program.marks = marks

        def stage_a_load(ti, blk):
            t = tinfo(ti)
            bt = t["bt"]
            i = cnt["xa"] % 2
            dma(SP, xa[i][:bt, :], t["xsrc"][blk * 128:blk * 128 + bt, :], writes=[xaB[i]])
            stA[(ti, blk)] = norm_a(xa[i][:bt, :], xaB[i], 0, bt)

        def stage_a_tr(ti, blk):
            t = tinfo(ti)
            norm_b(stA.pop((ti, blk)), blk, t["bt"])

        def run_tile(ti, prestaged):
            t = tinfo(ti)
            marks.append(("tile%d" % ti, len(Ctx.ops)))
            sample, seq, t0, NT, nbk, bt, xsrc = (t["sample"], t["seq"], t["t0"], t["NT"], t["nbk"], t["bt"],
                                                  t["xsrc"])
            first = (not sample) and t0 == 0
            lastt = (not sample) and t0 + TT == SEQ

            if not prestaged:
                for blk in range(nbk):
                    stage_a_load(ti, blk)
                    stage_a_tr(ti, blk)
            if sample:
                dma(SP, xt[:bt, 0, :], xsrc, writes=[xtB[0]])
            else:
                dma(SP, xt[:, :, :], xsrc.rearrange("(b p) d -> p b d", p=128), writes=xtB)

            if sample:
                dma_perm_in(SP, stg[:48, :], s_conv.rearrange("s i d -> (s i) d"), 48, [stgB])
                for half in range(2):
                    pb, pbB = gpf()
                    pv3 = pb[:, 0:192].rearrange("p (c t) -> p c t", t=48)
                    for cc in range(4):
                        c = half * 4 + cc
                        P(lambda c=c, cc=cc: nc.tensor.transpose(out=pv3[:, cc, :], in_=stg[:48, c * 128:(c + 1) * 128],
                                                                 identity=identf[:48, :48]),
                          r=[stgB, constB], w=[pbB], sig=(cc == 3))
                    V(lambda: nc.vector.tensor_copy(out=scT[:, half * 4:half * 4 + 4, :], in_=pv3[:, :, :]),
                      r=[pbB], w=[sampB])
                dma_perm_in(SP, stg[:16, :], s_lru[:, :], 16, [stgB])
                for half in range(2):
                    pb, pbB = gpf()
                    pv3 = pb[:, 0:64].rearrange("p (c t) -> p c t", t=16)
                    for cc in range(4):
                        c = half * 4 + cc
                        P(lambda c=c, cc=cc: nc.tensor.transpose(out=pv3[:, cc, :], in_=stg[:16, c * 128:(c + 1) * 128],
                                                                 identity=identf[:16, :16]),
                          r=[stgB, constB], w=[pbB], sig=(cc == 3))
                    V(lambda: nc.vector.tensor_copy(out=h0T[:, half * 4:half * 4 + 4, :], in_=pv3[:, :, :]),
                      r=[pbB], w=[sampB])
            elif first:
                V(lambda: nc.vector.memset(hst[:, :], 0.0), w=[hstB])
                V(lambda: nc.vector.memset(uh[:, :, :], 0.0), w=[uhB])

            if sample:
                qbdt = TB.get()
                qbd, qbdB = qbdt
                qbd5 = qbd[:, 0:256].rearrange("p (s a h g) -> p s a h g", s=16, a=2, h=2, g=4)
                V(lambda: nc.vector.memset(qbd[:, 0:256], 0.0), w=[qbdB])
            for pi in range(2):
                rg, rgB = piece(ti, pi)
                for cc in range(4):
                    j = pi * 4 + cc
                    pb, pbB = proj(rg, rgB, cc, NT, nbk)
                    if sample:
                        a_, g_ = j // 4, j % 4
                        V(lambda: nc.vector.tensor_copy(out=qbd5[0:64, :, a_, 0, g_], in_=pb[0:64, :NT]),
                          r=[pbB], w=[qbdB])
                        V(lambda: nc.vector.tensor_copy(out=qbd5[64:128, :, a_, 1, g_], in_=pb[64:128, :NT]),
                          r=[pbB], w=[qbdB])
                        continue
                    if cc % 2 == 0:
                        A(lambda: act(out=qT[:, j, :NT], in_=pb[:, :NT], func=AF.Copy), r=[pbB], w=qTB(j))
                    else:
                        V(lambda: nc.vector.tensor_copy(out=qT[:, j, :NT], in_=pb[:, :NT]), r=[pbB], w=qTB(j))
            rg, rgB = piece(ti, 2)
            for a in range(2):
                pb, pbB = proj(rg, rgB, a, NT, nbk)
                V(lambda: nc.vector.tensor_copy(out=kbuf[:, a, 128:128 + NT], in_=pb[:, :NT]), r=[pbB], w=[kbufB])
            for blk in range(nbk):
                pb, pbB = gpf()
                for k in range(8):
                    P(lambda k=k: mm(pb[:bt, :], lhsT=xnT[:, k, blk * 128:blk * 128 + bt], rhs=rg[:, k, :],
                                     start=(k == 0), stop=(k == 7)),
                      r=[rgB, xnTB[blk]], w=[pbB], sig=(k == 7))
                A(lambda: act(out=vbuf[:bt, 1 + blk, :], in_=pb[:bt, 256:512], func=AF.Copy), r=[pbB], w=[vbufB])
                if sample or (lastt and blk == nbk - 1):
                    V(lambda: nc.vector.tensor_copy(out=kvst[:bt, :], in_=pb[:bt, :]), r=[pbB], w=[kvstB])
                    if sample:
                        dma(SP, kws[:, 127, :], kvst[:bt, 0:256], reads=[kvstB])
                        dma(SP, vws[:, 127, :], kvst[:bt, 256:512], reads=[kvstB])
                    else:
                        dma(SP, kwp[seq], kvst[:, 0:256], reads=[kvstB])
                        dma(SP, vwp[seq], kvst[:, 256:512], reads=[kvstB])

            if sample:
                oall, oallB = pf[4], pfB[4]
                oall4 = oall[:, 0:256].rearrange("p (s a m) -> p s a m", s=16, a=2, m=8)
                for s in range(NSAMP):
                    i = s % 2
                    dma(SP, kcf[i][:, :], ck[s], writes=[kcfB[i]])
                    dma(POOL, vc[i][:, :], cv[s], writes=[vcB[i]])
                    pb, pbB = gpf()
                    for a in range(2):
                        P(lambda a=a: nc.tensor.transpose(out=pb[:, a * 128:(a + 1) * 128],
                                                          in_=kcf[i][:, a * 128:(a + 1) * 128], identity=identf[:, :]),
                          r=[kcfB[i], constB], w=[pbB], sig=(a == 1))
                    V(lambda: nc.vector.tensor_copy(out=kc[i][:, :, :], in_=pb[:, 0:256].rearrange("p (a t) -> p a t", a=2)),
                      r=[pbB], w=[kcB[i]])
                    bank, bB = pf[2 + i], pfB[2 + i]
                    sv = bank[:, :].rearrange("p (a k) -> p a k", a=2)
                    for a in range(2):
                        P(lambda a=a: mm(sv[0:8, a, 0:144], lhsT=selb[:, s * 8:(s + 1) * 8], rhs=msamp[:, 0:144],
                                         start=(a == 0), stop=False), r=[constB], w=[bB], sig=False)
                        P(lambda a=a: mm(sv[0:8, a, 0:128], lhsT=qbd5[:, s, a, :, :], rhs=kc[i][:, a, :],
                                         start=False, stop=False), r=[qbdB, kcB[i]], w=[bB], sig=False)
                        P(lambda a=a: mm(sv[0:8, a, 128:144], lhsT=qbd5[:, s, a, :, :], rhs=kbuf[:, a, 128:144],
                                         start=False, stop=(a == 1)), r=[qbdB, kbufB], w=[bB], sig=(a == 1))
                    st, stB = gst()
                    V(lambda: nc.vector.tensor_reduce(out=st[0:8, 0:2], in_=sv[0:8, :, 0:144], axis=AX.X, op=ALU.max),
                      r=[bB], w=[stB])
                    V(lambda: nc.vector.tensor_scalar(out=st[0:8, 2:4], in0=st[0:8, 0:2], scalar1=-0.125, scalar2=None,
                                                      op0=ALU.mult), r=[stB], w=[stB])
                    V(lambda: nc.vector.tensor_tensor(out=st[0:8, 6:8], in0=st[0:8, 2:4], in1=sink8[0:8, 0:2], op=ALU.add),
                      r=[stB, constB], w=[stB])
                    Et = TF.get(); E, EB = Et
                    Ev = E[:, :].rearrange("p (a k) -> p a k", a=2)
                    for a in range(2):
                        A(lambda a=a: act(out=Ev[0:8, a, 0:144], in_=sv[0:8, a, 0:144], func=AF.Exp,
                                          bias=st[0:8, 2 + a:3 + a], scale=0.125, accum_out=st[0:8, 4 + a:5 + a]),
                          r=[bB, stB], w=[EB, stB])
                    A(lambda: act(out=st[0:8, 6:8], in_=st[0:8, 6:8], func=AF.Exp), r=[stB], w=[stB])
                    V(lambda: nc.vector.tensor_tensor(out=st[0:8, 6:8], in0=st[0:8, 6:8], in1=st[0:8, 4:6], op=ALU.add),
                      r=[stB], w=[stB])
                    V(lambda: nc.vector.reciprocal(out=st[0:8, 6:8], in_=st[0:8, 6:8]), r=[stB], w=[stB])
                    Pt = TB.get(); Pn, PnB = Pt
                    Pv = Pn[:, :].rearrange("p (a k) -> p a k", a=2)
                    for a in range(2):
                        V(lambda a=a: nc.vector.tensor_scalar(out=Pv[0:8, a, 0:144], in0=Ev[0:8, a, 0:144],
                                                              scalar1=st[0:8, 6 + a:7 + a], scalar2=None, op0=ALU.mult),
                          r=[EB, stB], w=[PnB])
                    TF.put(Et)
                    pt, ptb = gpt()
                    ptv = pt[:, 0:32].rearrange("p (i t) -> p i t", t=8)
                    for a in range(2):
                        P(lambda a=a: nc.tensor.transpose(out=ptv[:, 2 * a, :], in_=Pv[0:8, a, 0:128],
                                                          identity=identb[0:8, 0:8]),
                          r=[PnB, constB], w=[ptb], sig=False)
                        P(lambda a=a: nc.tensor.transpose(out=ptv[0:16, 2 * a + 1, :], in_=Pv[0:8, a, 128:144],
                                                          identity=identb[0:8, 0:8]),
                          r=[PnB, constB], w=[ptb], sig=(a == 1))
                    PTt = TB.get(); PT, PTB = PTt
                    PTv = PT[:, 0:32].rearrange("p (i t) -> p i t", t=8)
                    A(lambda: act(out=PTv[:, 0::2, :], in_=ptv[:, 0::2, :], func=AF.Copy), r=[ptb], w=[PTB])
                    A(lambda: act(out=PTv[0:16, 1::2, :], in_=ptv[0:16, 1::2, :], func=AF.Copy), r=[ptb], w=[PTB])
                    TB.put(Pt)
                    for a in range(2):
                        P(lambda a=a: mm(oall4[:, s, a, :], lhsT=vc[i][:, a * 128:(a + 1) * 128], rhs=PTv[:, 2 * a, :],
                                         start=True, stop=False), r=[PTB, vcB[i]], w=[oallB], sig=False)
                        P(lambda a=a: mm(oall4[:, s, a, :], lhsT=vbuf[0:16, 1, a * 128:(a + 1) * 128],
                                         rhs=PTv[0:16, 2 * a + 1, :], start=False, stop=True),
                          r=[PTB, vbufB], w=[oallB], sig=(a == 1))
                    TB.put(PTt)
                oall5 = oall[:, 0:256].rearrange("p (s a h g) -> p s a h g", s=16, a=2, h=2, g=4)
                for hb in range(2):
                    for a in range(2):
                        V(lambda hb=hb, a=a: nc.vector.tensor_copy(
                            out=qT[hb * 64:(hb + 1) * 64, a * 4:(a + 1) * 4, 0:16],
                            in_=oall5[hb * 64:(hb + 1) * 64, :, a, hb, :].rearrange("p s g -> p g s")),
                          r=[oallB], w=[qTB(a * 4 + g_) for g_ in range(4)])
                TB.put(qbdt)

            units = []
            if sample:
                pass
            else:
                for blk in range(nbk):
                    for j in range(8):
                        units.append(("p", blk, j))
            upos = [0]
            pipe = AttnPipe()
            sstate = {}

            def sample_prep(s):
                i = s % 2
                dma(SP, kcf[i][:, :], ck[s], writes=[kcfB[i]])
                dma(POOL, vc[i][:, :], cv[s], writes=[vcB[i]])
                pb, pbB = gpf()
                for a in range(2):
                    P(lambda a=a: nc.tensor.transpose(out=pb[:, a * 128:(a + 1) * 128],
                                                      in_=kcf[i][:, a * 128:(a + 1) * 128], identity=identf[:, :]),
                      r=[kcfB[i], constB], w=[pbB], sig=(a == 1))
                V(lambda: nc.vector.tensor_copy(out=kc[i][:, :, :], in_=pb[:, 0:256].rearrange("p (a t) -> p a t", a=2)),
                  r=[pbB], w=[kcB[i]])

            def push_units(n):
                for _ in range(n):
                    if upos[0] >= len(units):
                        return
                    kind_, x_, j = units[upos[0]]
                    upos[0] += 1
                    a = j // 4
                    if kind_ == "s":
                        s = x_
                        i = s % 2
                        if j == 0:
                            sample_prep(s)
                        u = dict(nq=1, j=j, qcols=(s, s + 1), nk=144, ocols=(s, s + 1),
                                 kparts=[(lambda h, a=a, i=i: kc[i][h * 64:(h + 1) * 64, a, :], 128, [kcB[i]]),
                                         (lambda h, a=a: kbuf[h * 64:(h + 1) * 64, a, 128:144], 16, [kbufB])],
                                 vparts=[(vc[i][:, a * 128:(a + 1) * 128], 128, (0, 128), [vcB[i]]),
                                         (vbuf[:16, 1, a * 128:(a + 1) * 128], 16, (128, 144), [vbufB])],
                                 mask=(lambda h, out_ap, s=s: (lambda: mm(out_ap, lhsT=identb[:, s:s + 1],
                                                                          rhs=msamp[:, 0:144], start=(h == 0),
                                                                          stop=False))))
                    else:
                        blk = x_
                        mv = 1 if (first and blk == 0) else 0
                        u = dict(nq=128, j=j, qcols=(blk * 128, blk * 128 + 128), nk=256,
                                 ocols=(blk * 128, blk * 128 + 128),
                                 kparts=[(lambda h, a=a, blk=blk: kbuf[h * 64:(h + 1) * 64, a, blk * 128:blk * 128 + 256],
                                          256, [kbufB])],
                                 vparts=[(vbuf[:, blk, a * 128:(a + 1) * 128], 128, (0, 128), [vbufB]),
                                         (vbuf[:, blk + 1, a * 128:(a + 1) * 128], 128, (128, 256), [vbufB])],
                                 mask=(lambda h, out_ap, mv=mv: (lambda: mm(out_ap, lhsT=identb[:, :],
                                                                            rhs=maskb[:, mv, 0:256], start=(h == 0),
                                                                            stop=False))))
                    pipe.push(u)

            upp = (len(units) + 31) // 32

            hS, hSB = hS_t, hS_B

            def R1(c):
                sl = c % 4
                uct = TF.get(); uc, ucB = uct
                if sample:
                    V(lambda: nc.vector.tensor_scalar(out=uc[:, :NT], in0=scT[:, c, 0::3], scalar1=cw[:, 0, c:c + 1],
                                                      scalar2=pv[:, 0, c:c + 1], op0=ALU.mult, op1=ALU.add),
                      r=[sampB, constB], w=[ucB])
                    for i_ in (1, 2):
                        V(lambda i_=i_: nc.vector.scalar_tensor_tensor(out=uc[:, :NT], in0=scT[:, c, i_::3],
                                                                       scalar=cw[:, i_, c:c + 1], in1=uc[:, :NT],
                                                                       op0=ALU.mult, op1=ALU.add),
                          r=[sampB, constB], w=[ucB])
                    V(lambda: nc.vector.scalar_tensor_tensor(out=uc[:, :NT], in0=ubuf[:, sl, 3:3 + NT],
                                                             scalar=cw[:, 3, c:c + 1], in1=uc[:, :NT],
                                                             op0=ALU.mult, op1=ALU.add),
                      r=[ubB(c), constB], w=[ucB])
                    V(lambda: nc.vector.tensor_copy(out=uS_t[:, c * 16:(c + 1) * 16], in_=ubuf[:, sl, 3:3 + NT]),
                      r=ubB(c), w=[sampB])
                else:
                    V(lambda: nc.vector.tensor_scalar(out=uc[:, :NT], in0=ubuf[:, sl, 0:NT], scalar1=cw[:, 0, c:c + 1],
                                                      scalar2=pv[:, 0, c:c + 1], op0=ALU.mult, op1=ALU.add),
                      r=[ubB(c), constB], w=[ucB])
                    for i_ in (1, 2, 3):
                        V(lambda i_=i_: nc.vector.scalar_tensor_tensor(out=uc[:, :NT], in0=ubuf[:, sl, i_:i_ + NT],
                                                                       scalar=cw[:, i_, c:c + 1], in1=uc[:, :NT],
                                                                       op0=ALU.mult, op1=ALU.add),
                          r=[ubB(c), constB], w=[ucB])
                    V(lambda: nc.vector.tensor_copy(out=uh[:, c, :], in_=ubuf[:, sl, NT:NT + 3]),
                      r=ubB(c), w=[uhB])
                ucbt = TB.get(); ucb, ucbB = ucbt
                A(lambda: act(out=ucb[:, :NT], in_=uc[:, :NT], func=AF.Copy), r=[ucB], w=[ucbB])
                return dict(c=c, uct=uct, ucbt=ucbt)

            def R2(stt_):
                c = stt_["c"]
                ucb, ucbB = stt_["ucbt"]
                trt = TF.get(); tr, trB = trt
                tit = TF.get(); ti_, tiB = tit
                P(lambda: mm(pf[5][:, :NT], lhsT=wbd[:, 0, c, :], rhs=ucb[:, :NT], start=True, stop=True),
                  r=[ucbB, constB], w=[pfB[5]])
                A(lambda: act(out=tr[:, :NT], in_=pf[5][:, :NT], func=AF.Tanh, bias=pv[:, 1, c:c + 1], scale=0.5),
                  r=[pfB[5], constB], w=[trB])
                P(lambda: mm(pf[5][:, :NT], lhsT=wbd[:, 1, c, :], rhs=ucb[:, :NT], start=True, stop=True),
                  r=[ucbB, constB], w=[pfB[5]])
                A(lambda: act(out=ti_[:, :NT], in_=pf[5][:, :NT], func=AF.Tanh, bias=pv[:, 2, c:c + 1], scale=0.5),
                  r=[pfB[5], constB], w=[tiB])
                TB.put(stt_["ucbt"])
                stt_["trt"] = trt
                stt_["tit"] = tit

            def R3a(stt_):
                c = stt_["c"]
                tr, trB = stt_["trt"]
                aat = TF.get(); aa, aaB = aat
                A(lambda: act(out=aa[:, :NT], in_=tr[:, :NT], func=AF.Exp, bias=pv[:, 4, c:c + 1],
                              scale=pv[:, 4, c:c + 1]), r=[trB, constB], w=[aaB])
                A(lambda: act(out=tr[:, :NT], in_=tr[:, :NT], func=AF.Exp, bias=pv[:, 5, c:c + 1],
                              scale=pv[:, 5, c:c + 1]), r=[constB], w=[trB])
                stt_["aat"] = aat

            def R3b(stt_):
                tr, trB = stt_["trt"]
                A(lambda: act(out=tr[:, :NT], in_=tr[:, :NT], func=AF.Sqrt, bias=epsT[:, 1:2], scale=-1.0),
                  r=[constB], w=[trB])
                if first:
                    V(lambda: nc.vector.memset(tr[:, 0:1], 1.0), w=[trB])

            def R3c(stt_):
                c = stt_["c"]
                sl = c % 4
                tr, trB = stt_["trt"]
                ti_, tiB = stt_["tit"]
                aa, aaB = stt_["aat"]
                uc, ucB = stt_["uct"]
                V(lambda: nc.vector.scalar_tensor_tensor(out=ti_[:, :NT], in0=ti_[:, :NT], scalar=1.0, in1=tr[:, :NT],
                                                         op0=ALU.add, op1=ALU.mult), r=[trB], w=[tiB])
                V(lambda: nc.vector.scalar_tensor_tensor(out=ti_[:, :NT], in0=ti_[:, :NT], scalar=0.5, in1=uc[:, :NT],
                                                         op0=ALU.mult, op1=ALU.mult), r=[ucB], w=[tiB])
                if sample:
                    hv = hS[:, c * 16:(c + 1) * 16]
                    V(lambda: nc.vector.tensor_tensor(out=aa[:, :NT], in0=aa[:, :NT], in1=h0T[:, c, :], op=ALU.mult),
                      r=[sampB], w=[aaB])
                    V(lambda: nc.vector.tensor_tensor(out=hv, in0=aa[:, :NT], in1=ti_[:, :NT], op=ALU.add),
                      r=[aaB, tiB], w=[hSB])
                    hB_ = hSB
                else:
                    hv = tr[:, :NT]
                    hB_ = trB
                    V(lambda: nc.vector.tensor_tensor_scan(out=hv, data0=aa[:, :NT], data1=ti_[:, :NT],
                                                           initial=hst[:, c:c + 1], op0=ALU.mult, op1=ALU.add),
                      r=[aaB, tiB, hstB], w=[trB])
                    V(lambda: nc.vector.tensor_copy(out=hst[:, c:c + 1], in_=tr[:, NT - 1:NT]), r=[trB], w=[hstB])
                V(lambda: nc.vector.scalar_tensor_tensor(out=merged[:, c, :NT], in0=hv, scalar=0.5, in1=gs[:, sl, :NT],
                                                         op0=ALU.mult, op1=ALU.mult),
                  r=[hB_, gsB(c)], w=mgB(c))
                TF.put(stt_["uct"]); TF.put(stt_["trt"]); TF.put(stt_["tit"]); TF.put(stt_["aat"])

            marks.append(("rnnpieces", len(Ctx.ops)))
            prev = []
            for cp in range(4):
                c0, c1 = 2 * cp, 2 * cp + 1
                rg, rgB = piece(ti, 3 + 2 * cp)
                gts = []
                for ci in range(2):
                    pb, pbB = proj(rg, rgB, ci, NT, nbk)
                    gtt = TF.get()
                    g32, g32B = gtt
                    A(lambda: act(out=g32[:, :NT], in_=pb[:, :NT], func=AF.Gelu_apprx_tanh), r=[pbB], w=[g32B])
                    gts.append(gtt)
                    push_units(upp)
                    if prev:
                        R2(prev[ci])
                for ci in range(2):
                    c = c0 + ci
                    pb, pbB = proj(rg, rgB, 2 + ci, NT, nbk)
                    trt = TF.get()
                    A(lambda trt=trt: act(out=trt[0][:, :NT], in_=pb[:, :NT], func=AF.Tanh, scale=0.5),
                      r=[pbB], w=[trt[1]])
                    V(lambda trt=trt, ci=ci, c=c: nc.vector.scalar_tensor_tensor(
                        out=gs[:, c % 4, :NT], in0=trt[0][:, :NT], scalar=1.0, in1=gts[ci][0][:, :NT],
                        op0=ALU.add, op1=ALU.mult), r=[trt[1], gts[ci][1]], w=gsB(c))
                    TF.put(trt)
                    TF.put(gts[ci])
                    push_units(upp)
                    if prev:
                        if ci == 0:
                            R3a(prev[0]); R3a(prev[1])
                        else:
                            R3b(prev[0]); R3b(prev[1])
                            R3c(prev[0]); R3c(prev[1])
                prev = []
                rg, rgB = piece(ti, 4 + 2 * cp)
                for ci in range(2):
                    c = c0 + ci
                    sl = c % 4
                    if not sample:
                        V(lambda c=c, sl=sl: nc.vector.tensor_copy(out=ubuf[:, sl, 0:3], in_=uh[:, c, :]),
                          r=[uhB], w=ubB(c))
                    pb, pbB = proj(rg, rgB, ci, NT, nbk)
                    V(lambda sl=sl, c=c: nc.vector.tensor_copy(out=ubuf[:, sl, 3:3 + NT], in_=pb[:, :NT]),
                      r=[pbB], w=ubB(c))
                    push_units(upp)
                    prev.append(R1(c))
                for ci in range(2):
                    c = c0 + ci
                    pb, pbB = proj(rg, rgB, 2 + ci, NT, nbk)
                    A(lambda c=c: act(out=sa[:, c, :NT], in_=pb[:, :NT], func=AF.Tanh, scale=0.5), r=[pbB], w=saB(c))
                    push_units(upp)
            push_units(len(units))
            R2(prev[0]); R2(prev[1])
            R3a(prev[0]); R3a(prev[1])
            R3b(prev[0]); R3b(prev[1])
            pipe.drain()
            R3c(prev[0]); R3c(prev[1])

            marks.append(("merge", len(Ctx.ops)))
            for c in range(8):
                tmt = TF.get(); tm, tmB = tmt
                V(lambda c=c: nc.vector.scalar_tensor_tensor(out=tm[:, :NT], in0=sa[:, c, :NT], scalar=1.0,
                                                             in1=qT[:, c, :NT], op0=ALU.add, op1=ALU.mult),
                  r=[saB(c), qTB(c)], w=[tmB])
                V(lambda c=c: nc.vector.scalar_tensor_tensor(out=merged[:, c, :NT], in0=tm[:, :NT], scalar=0.5,
                                                             in1=merged[:, c, :NT], op0=ALU.mult, op1=ALU.add),
                  r=[tmB], w=mgB(c))
                TF.put(tmt)
            if debug and not sample:
                dma(SP, dbg_m[:, :, :], merged[:, :, :], reads=[mgB(c) for c in range(8)])

            if sample:
                out_rows_from_chunks([hS[:, c * 16:(c + 1) * 16] for c in range(8)], 16, [hSB], lrs[:, :])
                out_rows_from_chunks([uS_t[:, c * 16:(c + 1) * 16] for c in range(8)], 16, [sampB], cvs[:, 2, :])
            elif lastt:
                pb, pbB = gpf()
                P(lambda: nc.tensor.transpose(out=pb[:8, 0:128], in_=hst[:, 0:8], identity=identf[:, :]),
                  r=[hstB, constB], w=[pbB])
                V(lambda: nc.vector.tensor_copy(out=stg[:8, 0:128], in_=pb[:8, 0:128]), r=[pbB], w=[stgB])
                lv_ = lrp[seq].rearrange("(a hb g d) -> a hb g d", a=2, hb=2, g=4, d=64)
                for a in range(2):
                    for hb in range(2):
                        dma(SP, lv_[a, hb], stg[a * 4:(a + 1) * 4, hb * 64:(hb + 1) * 64], reads=[stgB])
                out_rows_from_chunks([uh[:, c, :] for c in range(8)], 3, [uhB], cvp[seq])
            if not sample and not lastt:
                V(lambda: nc.vector.tensor_copy(out=kbuf[:, :, 0:128], in_=kbuf[:, :, 512:640]), r=[kbufB], w=[kbufB])
                V(lambda: nc.vector.tensor_copy(out=vbuf[:, 0, :], in_=vbuf[:, 4, :]), r=[vbufB], w=[vbufB])

            marks.append(("wout", len(Ctx.ops)))
            ro0, ro0B = piece(ti, 11)
            ro1, ro1B = piece(ti, 12, held=1)

            def wout_mm(blk):
                base = 2 * (blk % 2)
                for hf, (ro, roB) in enumerate(((ro0, ro0B), (ro1, ro1B))):
                    for k in range(8):
                        P(lambda k=k, hf=hf, ro=ro: mm(pf[base + hf][:bt, :], lhsT=merged[:, k, blk * 128:blk * 128 + bt],
                                                       rhs=ro[:, k, :], start=(k == 0), stop=(k == 7)),
                          r=[roB, mgB(k)], w=[pfB[base + hf]], sig=(k == 7))

            def wout_post(blk):
                base = 2 * (blk % 2)
                st, stB = gst()
                jkt = TB.get(); jk, jkB = jkt
                for hf in range(2):
                    A(lambda hf=hf: act(out=jk[:bt, :], in_=pf[base + hf][:bt, :], func=AF.Square,
                                        accum_out=st[:bt, 2 + hf:3 + hf]), r=[pfB[base + hf]], w=[jkB, stB])
                TB.put(jkt)
                V(lambda: nc.vector.tensor_tensor(out=st[:bt, 4:5], in0=st[:bt, 2:3], in1=st[:bt, 3:4], op=ALU.add),
                  r=[stB], w=[stB])
                rs, rsB = rstd_from(st[:bt, 4:5], stB, bt)
                for hf in range(2):
                    tmt = TF.get(); tm, tmB = tmt
                    V(lambda hf=hf: nc.vector.tensor_tensor(out=tm[:bt, :], in0=pf[base + hf][:bt, :],
                                                            in1=gbc[:bt, 1, hf * 512:(hf + 1) * 512], op=ALU.mult),
                      r=[pfB[base + hf], constB], w=[tmB])
                    V(lambda hf=hf: nc.vector.scalar_tensor_tensor(
                        out=xt[:bt, blk, hf * 512:(hf + 1) * 512], in0=tm[:bt, :], scalar=rs,
                        in1=xt[:bt, blk, hf * 512:(hf + 1) * 512], op0=ALU.mult, op1=ALU.add),
                      r=[tmB, rsB], w=[xtB[blk]])
                    TF.put(tmt)
                return norm_a(xt[:bt, blk, :], xtB[blk], 2, bt)

            wout_mm(0)
            pend = None
            for blk in range(nbk):
                if blk + 1 < nbk:
                    wout_mm(blk + 1)
                i_ = wout_post(blk)
                if pend is not None:
                    norm_b(pend[0], pend[1], bt)
                pend = (i_, blk)
            norm_b(pend[0], pend[1], bt)
            if debug and not sample:
                dma(SP, dbg_x1[:, :, :], xt[:, :, :], reads=xtB)

            marks.append(("ffnup", len(Ctx.ops)))
            for q in range(8):
                rg, rgB = piece(ti, 13 + q)
                for cc in range(4):
                    fc = q * 4 + cc
                    pb, pbB = proj(rg, rgB, cc, NT, nbk)
                    rlt = TB.get(); rl, rlB = rlt
                    A(lambda: act(out=rl[:, :NT], in_=pb[:, :NT], func=AF.Relu), r=[pbB], w=[rlB])
                    V(lambda: nc.vector.tensor_tensor(out=hT[:, fc, :NT], in0=rl[:, :NT], in1=rl[:, :NT], op=ALU.mult),
                      r=[rlB], w=hTB(fc))
                    TB.put(rlt)
            marks.append(("ffndown", len(Ctx.ops)))
            nxt = ti + 1 if ti + 1 < len(tiles) else None
            nsteps = []
            if nxt is not None:
                nb2 = tinfo(nxt)["nbk"]
                for b2 in range(nb2):
                    nsteps.append(("l", b2))
                    nsteps.append(("t", b2))
            step = [0]

            def next_stage_step(n=1):
                for _ in range(n):
                    if step[0] < len(nsteps):
                        k_, b2 = nsteps[step[0]]
                        step[0] += 1
                        if k_ == "l":
                            stage_a_load(nxt, b2)
                        else:
                            stage_a_tr(nxt, b2)

            ssb, ssbB = ssb_t[:, :], ssb_B
            for hf in range(2):
                for q in range(4):
                    rg, rgB = piece(ti, 21 + hf * 4 + q)
                    for kk in range(8):
                        fc = q * 8 + kk
                        for blk in range(nbk):
                            P(lambda kk=kk, fc=fc, blk=blk: mm(pf[2 + blk][:bt, :], lhsT=hT[:, fc, blk * 128:blk * 128 + bt],
                                                               rhs=rg[:, kk, :], start=(fc == 0), stop=(fc == 31)),
                              r=[rgB, hTB(fc)], w=[pfB[2 + blk]], sig=(kk == 7 and blk == nbk - 1))
                    next_stage_step(1)
                for blk in range(nbk):
                    jkt = TB.get(); jk, jkB = jkt
                    A(lambda blk=blk, hf=hf: act(out=jk[:bt, :], in_=pf[2 + blk][:bt, :], func=AF.Square,
                                                 accum_out=ssb[:bt, 2 * blk + hf:2 * blk + hf + 1]),
                      r=[pfB[2 + blk]], w=[jkB, ssbB])
                    TB.put(jkt)
                    if hf == 0:
                        V(lambda blk=blk: nc.vector.tensor_tensor(out=fbuf[:bt, blk, :], in0=pf[2 + blk][:bt, :],
                                                                  in1=gbc[:bt, 3, 0:512], op=ALU.mult),
                          r=[pfB[2 + blk], constB], w=fbB(blk))
            for blk in range(nbk):
                st, stB = gst()
                V(lambda blk=blk: nc.vector.tensor_tensor(out=st[:bt, 4:5], in0=ssb[:bt, 2 * blk:2 * blk + 1],
                                                          in1=ssb[:bt, 2 * blk + 1:2 * blk + 2], op=ALU.add),
                  r=[ssbB], w=[stB])
                rs, rsB = rstd_from(st[:bt, 4:5], stB, bt)
                V(lambda blk=blk: nc.vector.scalar_tensor_tensor(out=xt[:bt, blk, 0:512], in0=fbuf[:bt, blk, :], scalar=rs,
                                                                 in1=xt[:bt, blk, 0:512], op0=ALU.mult, op1=ALU.add),
                  r=[fbB(blk), rsB], w=[xtB[blk]])
                tmt = TF.get(); tm, tmB = tmt
                V(lambda blk=blk: nc.vector.tensor_tensor(out=tm[:bt, :], in0=pf[2 + blk][:bt, :],
                                                          in1=gbc[:bt, 3, 512:1024], op=ALU.mult),
                  r=[pfB[2 + blk], constB], w=[tmB])
                V(lambda blk=blk: nc.vector.scalar_tensor_tensor(out=xt[:bt, blk, 512:1024], in0=tm[:bt, :], scalar=rs,
                                                                 in1=xt[:bt, blk, 512:1024], op0=ALU.mult, op1=ALU.add),
                  r=[tmB, rsB], w=[xtB[blk]])
                TF.put(tmt)
            if sample:
                dma(SP, y_s[:, :], xt[:bt, 0, :], reads=[xtB[0]])
            else:
                dma(SP, y_p[seq, t0:t0 + TT, :].rearrange("(b p) d -> p b d", p=128), xt[:, :, :], reads=xtB)
            next_stage_step(len(nsteps))
            return nxt is not None

        outB = Buf("outcopy")
        dma(SP, kws[:, 0:127, :], ck[:, 1:128, :], writes=[outB])
        dma(SP, vws[:, 0:127, :], cv[:, 1:128, :], writes=[outB])
        dma(SP, cvs[:, 0:2, :], s_conv[:, 1:3, :], writes=[outB])

        vec_setup()
        pre = False
        for ti in range(len(tiles)):
            pre = run_tile(ti, pre)

        assert Ctx.pend is None
        order, est = schedule(Ctx.ops)
        build_program.est_us = est
        for i in order:
            o = Ctx.ops[i]
            if o.kind == "dma":
                dma_emit(o)
            else:
                o.eng.emit(o)
        nc = nc_real
        for sc in all_dma_sems:
            nc.gpsimd.wait_ge(sc.s, sc.cnt)
        for e in (PE, ACT, DVE, SP):
            nc.gpsimd.wait_ge(e.sc.s, e.sc.cnt)
    return nc


_CACHE = {}


def _consts():
    ident = np.eye(128, dtype=np.float32)
    qi = np.arange(128)[:, None]
    kj = np.arange(256)[None, :]
    diff = qi + 128 - kj
    band = (diff >= 0) & (diff <= 128)
    m0 = np.where(band, 0.0, NEG).astype(np.float32)
    m1 = m0.copy()
    m1[:, :128] = NEG
    mask = np.stack([np.concatenate([m0, m0], axis=1), np.concatenate([m1, m1], axis=1)])
    ms = np.full((16, 256), NEG, dtype=np.float32)
    ms[:, :128] = 0.0
    for s in range(16):
        ms[s, 128 + s] = 0.0
    msamp = np.concatenate([ms, ms], axis=1)
    sel = np.zeros((128, 128), dtype=np.float32)
    for s_ in range(16):
        sel[s_, s_ * 8:(s_ + 1) * 8] = 1.0
    return ident, np.ascontiguousarray(mask), np.ascontiguousarray(msamp), sel


def kernel(x_prompt, x_sample, cache_k_win, cache_v_win, state_conv, state_lru,
           w_in, w_out, sinks, conv_w, conv_b, lru_w_a, lru_b_a, lru_w_x, lru_b_x, lru_lambda,
           w_up, w_down, g_pre_mix, g_post_mix, g_pre_ffn, g_post_ffn):
    f = lambda a: np.ascontiguousarray(np.asarray(a, dtype=np.float32))
    if "nc" not in _CACHE:
        _CACHE["nc"] = build_program()
    nc = _CACHE["nc"]
    ident, mask, msamp, sel = _consts()
    sk = f(sinks)[0].reshape(2, 2, 4).transpose(0, 2, 1).reshape(16)
    shared = {
        "w_in": f(w_in)[0], "w_out": f(w_out)[0], "sinks": np.ascontiguousarray(sk),
        "conv_w": f(conv_w)[0], "conv_b": f(conv_b)[0],
        "lru_w_a": f(lru_w_a)[0], "lru_b_a": f(lru_b_a)[0], "lru_w_x": f(lru_w_x)[0], "lru_b_x": f(lru_b_x)[0],
        "lru_lambda": f(lru_lambda)[0], "w_up": f(w_up)[0], "w_down": f(w_down)[0],
        "g_pre_mix": f(g_pre_mix)[0], "g_post_mix": f(g_post_mix)[0],
        "g_pre_ffn": f(g_pre_ffn)[0], "g_post_ffn": f(g_post_ffn)[0],
        "c_ident": ident, "c_mask": mask, "c_msamp": msamp, "c_sel": sel,
    }
    xp = f(x_prompt)
    xs = f(x_sample)[:, 0, :]
    ckk = f(cache_k_win)[0].reshape(128, 128, 256)
    cvv = f(cache_v_win)[0].reshape(128, 128, 256)
    sc = f(state_conv)[0]
    sl = f(state_lru)[0]
    in_maps = []
    for i in range(NCORES):
        m = dict(shared)
        m["x_prompt"] = np.ascontiguousarray(xp[2 * i:2 * i + 2])
        m["x_sample"] = np.ascontiguousarray(xs[16 * i:16 * i + 16])
        m["cache_k"] = np.ascontiguousarray(ckk[16 * i:16 * i + 16])
        m["cache_v"] = np.ascontiguousarray(cvv[16 * i:16 * i + 16])
        m["state_conv"] = np.ascontiguousarray(sc[16 * i:16 * i + 16])
        m["state_lru"] = np.ascontiguousarray(sl[16 * i:16 * i + 16])
        in_maps.append(m)
    res = run_bass_kernel_spmd(nc, in_maps, core_ids=list(range(NCORES)))
    R = res.results
    cat = lambda k: np.concatenate([np.asarray(r[k], dtype=np.float32) for r in R], axis=0)
    y_prompt = cat("y_prompt")
    y_sample = cat("y_sample").reshape(128, 1, D)
    kwp = cat("k_win_prompt").reshape(1, 16, 128, 4, 64)
    vwp = cat("v_win_prompt").reshape(1, 16, 128, 4, 64)
    cvp = cat("conv_prompt").reshape(1, 16, 3, D)
    lrp = cat("lru_prompt").reshape(1, 16, D)
    kws = cat("k_win_sample").reshape(1, 128, 128, 4, 64)
    vws = cat("v_win_sample").reshape(1, 128, 128, 4, 64)
    cvs = cat("conv_sample").reshape(1, 128, 3, D)
    lrs = cat("lru_sample").reshape(1, 128, D)
    return (y_prompt, y_sample, kwp, vwp, cvp, lrp, kws, vws, cvs, lrs)
```

```python
from contextlib import ExitStack
import numpy as np
import concourse.bass as bass
import concourse.mybir as mybir
from concourse.bass_utils import run_bass_kernel_spmd

F32 = mybir.dt.float32
BF16 = mybir.dt.bfloat16
AF = mybir.ActivationFunctionType
ALU = mybir.AluOpType
AX = mybir.AxisListType

NCORES = 8
D = 1024
SEQ = 2048
NSEQ = 2
NSAMP = 16
TT = 512
NPIECE = 29
NR = 5
EPS = 1e-6
NEG = -30000.0
GC1 = 0.7978845608028654
GC2 = 0.044715
STOP = 99


class Buf:
    __slots__ = ("name", "w", "r", "dsem", "dcnt", "excl")

    def __init__(self, name, excl=False):
        self.name = name
        self.excl = excl
        self.w = None
        self.r = []
        self.dsem = None
        self.dcnt = 0


class SemC:
    __slots__ = ("s", "cnt", "is_dma")

    def __init__(self, s, is_dma):
        self.s = s
        self.cnt = 0
        self.is_dma = is_dma


class Ctx:
    recording = True
    ops = []
    cap = []
    pend = None


class Op:
    __slots__ = ("eng", "calls", "reads", "writes", "dur", "ts", "kind", "dma", "idx", "lat")


def _free_size(ap):
    try:
        return int(ap.free_size())
    except Exception:
        return 512


def _estimate(engname, calls):
    d = 0.0
    ts = None
    for (m, a, k) in calls:
        name = getattr(m, "__name__", "")
        if engname == "PE":
            ap = k.get("rhs", k.get("identity"))
            n = _free_size(ap) if ap is not None else 128
            d += (max(n, 48) + 16) / 1900.0
        elif engname == "ACT":
            ap = k.get("in_")
            n = _free_size(ap) if ap is not None else 512
            d += 0.2 + n / 1100.0
            f = k.get("func")
            if f == AF.Exp:
                ts = "E"
            elif f == AF.Tanh:
                ts = "T"
            elif f == AF.Sqrt:
                ts = "S"
            elif f == AF.Ln:
                ts = "L"
            elif f == AF.Gelu_apprx_tanh:
                ts = "G"
        elif engname == "DVE":
            ap = k.get("in0", k.get("in_", k.get("data0", k.get("out", k.get("ap")))))
            if ap is None and a:
                ap = a[0]
            n = _free_size(ap) if ap is not None else 512
            mult = 2.0 if "scan" in name else 1.0
            d += 0.1 + mult * n / 760.0
        else:
            ap = k.get("in0", k.get("in_", k.get("out", k.get("ap"))))
            if ap is None and a:
                ap = a[0]
            n = _free_size(ap) if ap is not None else 512
            d += 0.25 + n / 450.0
    return d, ts


class _Dummy:
    def then_inc(self, *a, **k):
        return self


class Rec:
    def __init__(self, real):
        object.__setattr__(self, "_real", real)

    def __getattr__(self, name):
        real_m = getattr(object.__getattribute__(self, "_real"), name)

        def f(*a, **k):
            Ctx.cap.append((real_m, a, k))
            return _Dummy()
        f.__name__ = name
        return f


class NcProxy:
    def __init__(self, real):
        self._real = real
        self.tensor = Rec(real.tensor)
        self.vector = Rec(real.vector)
        self.scalar = Rec(real.scalar)
        self.gpsimd = Rec(real.gpsimd)

    def __getattr__(self, name):
        return getattr(self._real, name)


class Eng:
    def __init__(self, h, semc, name):
        self.h = h
        self.sc = semc
        self.name = name
        self.waited = {}

    def wait(self, tok):
        sc, val = tok
        if sc.is_dma:
            val = sc.cnt
        if self.waited.get(id(sc), 0) >= val:
            return
        self.h.wait_ge(sc.s, val)
        self.waited[id(sc)] = val

    def _deps(self, reads, writes):
        for b in reads:
            if b.w is not None:
                self.wait(b.w)
        for b in writes:
            if b.w is not None:
                self.wait(b.w)
            for t in b.r:
                self.wait(t)

    def op(self, fn, reads=(), writes=(), sig=True):
        del Ctx.cap[:]
        fn()
        calls = list(Ctx.cap)
        reads = list(reads)
        writes = list(writes)
        if Ctx.pend is not None:
            pc, pr, pw = Ctx.pend
            calls = pc + calls
            reads = pr + [b for b in reads if b not in pr]
            writes = pw + [b for b in writes if b not in pw]
            Ctx.pend = None
        if not sig:
            Ctx.pend = (calls, reads, writes)
            return
        ex = [b for b in reads if b.excl]
        if ex:
            writes = writes + [b for b in ex if b not in writes]
            reads = [b for b in reads if not b.excl]
        o = Op()
        o.eng, o.calls, o.reads, o.writes, o.kind, o.dma = self, calls, reads, writes, "c", None
        o.dur, o.ts = _estimate(self.name, calls)
        o.lat = 0.0
        o.idx = len(Ctx.ops)
        Ctx.ops.append(o)

    def emit(self, o):
        self._deps(o.reads, o.writes)
        inst = None
        for (m, a, k) in o.calls:
            inst = m(*a, **k)
        inst.then_inc(self.sc.s, 1)
        self.sc.cnt += 1
        tok = (self.sc, self.sc.cnt)
        for b in o.reads:
            b.r.append(tok)
        for b in o.writes:
            b.w = tok
            b.r = []


def schedule(ops):
    n = len(ops)
    if not hasattr(schedule, 'n_setup'):
        schedule.n_setup = 0
    lastw, readers = {}, {}
    preds = [set() for _ in range(n)]
    for i, o in enumerate(ops):
        for b in o.reads:
            w = lastw.get(id(b))
            if w is not None:
                preds[i].add(w)
        for b in o.writes:
            w = lastw.get(id(b))
            if w is not None:
                preds[i].add(w)
            for r in readers.get(id(b), ()):
                preds[i].add(r)
        for b in o.reads:
            readers.setdefault(id(b), []).append(i)
        for b in o.writes:
            lastw[id(b)] = i
            readers[id(b)] = []
        preds[i].discard(i)
    succs = [[] for _ in range(n)]
    npred = [0] * n
    for i in range(n):
        npred[i] = len(preds[i])
        for p in preds[i]:
            succs[p].append(i)
    ready_time = [0.0] * n
    finish = [0.0] * n
    start = [0.0] * n
    engs = {}
    for o in ops:
        engs.setdefault(id(o.eng), o.eng)
    eng_free = {k: 0.0 for k in engs}
    ready = {k: [] for k in engs}
    cur_ts = [None]
    for i in range(n):
        if npred[i] == 0:
            ready[id(ops[i].eng)].append(i)
    done = 0
    order = []
    while done < n:
        best = None
        for k, lst in ready.items():
            if not lst:
                continue
            ef = eng_free[k]
            isact = engs[k].name == "ACT"
            for i in lst[:32]:
                st = max(ready_time[i], ef)
                pen = 0.0
                if isact and ops[i].ts is not None:
                    t_ = ops[i].ts
                    if t_ == "T":
                        if cur_ts[0] not in ("E", "G"):
                            pen = 1.3
                    elif t_ != cur_ts[0]:
                        pen = 1.3
                if i < schedule.n_setup:
                    st, pen = 0.0, 0.0
                key = (st + pen, i)
                if best is None or key < best[0]:
                    best = (key, i, k, st + pen)
        _, i, k, st = best
        o = ops[i]
        ready[k].remove(i)
        start[i] = st
        if o.kind == "dma":
            eng_free[k] = st + o.dur
            finish[i] = st + o.dur + o.lat
        else:
            finish[i] = st + o.dur
            eng_free[k] = finish[i]
            if o.eng.name == "ACT" and o.ts is not None:
                if o.ts == "T":
                    if cur_ts[0] not in ("E", "G"):
                        cur_ts[0] = "E"
                else:
                    cur_ts[0] = o.ts
        order.append(i)
        done += 1
        for sidx in succs[i]:
            hop = 0.0 if ops[sidx].eng is o.eng and o.kind != "dma" else 0.25
            rt = finish[i] + hop
            if rt > ready_time[sidx]:
                ready_time[sidx] = rt
            npred[sidx] -= 1
            if npred[sidx] == 0:
                lst = ready[id(ops[sidx].eng)]
                lo, hi = 0, len(lst)
                while lo < hi:
                    mid = (lo + hi) // 2
                    if lst[mid] < sidx:
                        lo = mid + 1
                    else:
                        hi = mid
                lst.insert(lo, sidx)
    schedule.start = start
    schedule.finish = finish
    return order, max(finish) if n else 0.0


def _flat(lst):
    out = []
    for x in lst:
        if isinstance(x, (list, tuple)):
            out.extend(_flat(x))
        elif x is not None:
            out.append(x)
    return out


def build_program(debug=False):
    nc = bass.Bass("TRN2", target_bir_lowering=False)

    def din(name, shape, dt=F32):
        return nc.dram_tensor(name, list(shape), dt, kind="ExternalInput").ap()

    def dout(name, shape, dt=F32):
        return nc.dram_tensor(name, list(shape), dt, kind="ExternalOutput").ap()

    x_p = din("x_prompt", [NSEQ, SEQ, D])
    x_s = din("x_sample", [NSAMP, D])
    ck = din("cache_k", [NSAMP, 128, 256])
    cv = din("cache_v", [NSAMP, 128, 256])
    s_conv = din("state_conv", [NSAMP, 3, D])
    s_lru = din("state_lru", [NSAMP, D])
    w_in = din("w_in", [D, 5632])
    w_out = din("w_out", [D, D])
    sinks = din("sinks", [16])
    conv_w = din("conv_w", [4, D])
    conv_b = din("conv_b", [D])
    lw_a = din("lru_w_a", [16, 64, 64])
    lb_a = din("lru_b_a", [D])
    lw_x = din("lru_w_x", [16, 64, 64])
    lb_x = din("lru_b_x", [D])
    lam = din("lru_lambda", [D])
    w_up = din("w_up", [D, 4096])
    w_down = din("w_down", [4096, D])
    g1 = din("g_pre_mix", [D])
    g2 = din("g_post_mix", [D])
    g3 = din("g_pre_ffn", [D])
    g4 = din("g_post_ffn", [D])
    c_ident = din("c_ident", [128, 128])
    c_mask = din("c_mask", [2, 128, 512])
    c_msamp = din("c_msamp", [16, 512])
    c_sel = din("c_sel", [128, 128])

    y_p = dout("y_prompt", [NSEQ, SEQ, D])
    y_s = dout("y_sample", [NSAMP, D])
    kwp = dout("k_win_prompt", [NSEQ, 128, 256])
    vwp = dout("v_win_prompt", [NSEQ, 128, 256])
    cvp = dout("conv_prompt", [NSEQ, 3, D])
    lrp = dout("lru_prompt", [NSEQ, D])
    kws = dout("k_win_sample", [NSAMP, 128, 256])
    vws = dout("v_win_sample", [NSAMP, 128, 256])
    cvs = dout("conv_sample", [NSAMP, 3, D])
    lrs = dout("lru_sample", [NSAMP, D])

    scr = nc.dram_tensor("wscr", [NPIECE, 128, 8, 512], BF16, kind="Internal").ap()
    if debug:
        dbg_m = dout("dbg_merged", [128, 8, 512], BF16)
        dbg_a = dout("dbg_attn", [128, 8, 512], BF16)
        dbg_x1 = dout("dbg_x1", [128, 4, D])

    with ExitStack() as es:
        def sb(name, shape, dt=F32):
            return es.enter_context(nc.sbuf_tensor(name, list(shape), dt))

        def ps(name, shape, dt=F32):
            return es.enter_context(nc.psum_tensor(name, list(shape), dt))

        nsem = [0]

        def newsem(is_dma):
            nsem[0] += 1
            return SemC(es.enter_context(nc.semaphore("s%d" % nsem[0])), is_dma)

        ring = [sb("ring%d" % i, [128, 8, 512], BF16) for i in range(NR)]
        ringB = [Buf("ring%d" % i) for i in range(NR)]
        xa = [sb("xa%d" % i, [128, D]) for i in range(2)]
        xaB = [Buf("xa") for _ in range(2)]
        xnb = [sb("xnb%d" % i, [128, D], BF16) for i in range(2)]
        xnbB = [Buf("xnb") for _ in range(2)]
        xnT = sb("xnT", [128, 8, TT], BF16)
        xnTB = [Buf("xnT%d" % b) for b in range(4)]
        xt = sb("xt", [128, 4, D])
        xtB = [Buf("xt%d" % b) for b in range(4)]
        kbuf = sb("kbuf", [128, 2, 640], BF16)
        kbufB = Buf("kbuf")
        vbuf = sb("vbuf", [128, 5, 256], BF16)
        vbufB = Buf("vbuf")
        USL = 40
        U = sb("U", [128, USL * 256])
        UB = [Buf("U%d" % i) for i in range(USL)]

        def uview_bf(slot0, nslots):
            return U[:, slot0 * 256:(slot0 + nslots) * 256].bitcast(BF16).rearrange(
                "p (c t) -> p c t", t=512)

        qT = uview_bf(0, 8)
        sa = uview_bf(8, 8)
        merged = uview_bf(16, 8)
        gs = uview_bf(24, 4)
        ubuf = U[:, 28 * 256:28 * 256 + 4 * 516].rearrange("p (c t) -> p c t", t=516)
        hT = uview_bf(0, 32)
        fbuf = U[:, 32 * 256:40 * 256].rearrange("p (b t) -> p b t", t=512)

        def qTB(j): return [UB[j]]
        def saB(c): return [UB[8 + c]]
        def mgB(c): return [UB[16 + c]]
        def gsB(c): return [UB[24 + c % 4]]
        def ubB(c):
            lo = 28 * 1024 + (c % 4) * 2064
            return [UB[i] for i in range(lo // 1024, (lo + 2063) // 1024 + 1)]
        def hTB(fc): return [UB[fc]]
        def fbB(b): return [UB[32 + 2 * b], UB[33 + 2 * b]]

        class Pool_:
            def __init__(self, name, n, dt):
                self.free = [(sb("%s%d" % (name, i), [128, 512], dt), Buf(name)) for i in range(n)]

            def get(self):
                assert self.free, "temp pool exhausted"
                return self.free.pop(0)

            def put(self, t):
                self.free.append(t)

        TF = Pool_("tf", 15, F32)
        TB = Pool_("tb", 10, BF16)
        cnt = {"st": 0, "xa": 0, "pf": 0, "pt": 0, "au": 0}

        NST = 12
        stt = sb("stt", [128, NST, 8])
        sttB = [Buf("st") for _ in range(NST)]

        def gst():
            i = cnt["st"] % NST
            cnt["st"] += 1
            return stt[:, i, :], sttB[i]

        stg = sb("stg", [128, D])
        stgB = Buf("stg")
        kvst = sb("kvst", [128, 512])
        kvstB = Buf("kvst")
        hst = sb("hst", [128, 8])
        hstB = Buf("hst")
        uh = sb("uh", [128, 8, 3])
        uhB = Buf("uh")
        kcf = [sb("kcf%d" % i, [128, 256]) for i in range(2)]
        kcfB = [Buf("kcf") for _ in range(2)]
        kc = [sb("kc%d" % i, [128, 2, 128], BF16) for i in range(2)]
        kcB = [Buf("kc") for _ in range(2)]
        vc = [sb("vc%d" % i, [128, 256], BF16) for i in range(2)]
        vcB = [Buf("vc") for _ in range(2)]
        scT = sb("scT", [128, 8, 48])
        h0T = sb("h0T", [128, 8, 16])
        sampB = Buf("samp")
        hS_t = sb("hS_t", [128, 128])
        hS_B = Buf("hS")
        uS_t = sb("uS_t", [128, 128])
        ssb_t = sb("ssb_t", [128, 8])
        ssb_B = Buf("ssb")
        gbc = sb("gbc", [128, 4, D])
        identf = sb("identf", [128, 128])
        identb = sb("identb", [128, 128], BF16)
        maskb = sb("maskb", [128, 2, 512], BF16)
        msamp = sb("msamp", [128, 256], BF16)
        selb = sb("selb", [128, 128], BF16)
        sink8 = sb("sink8", [8, 2])
        wbd = sb("wbd", [128, 2, 8, 128], BF16)
        cw = sb("cw", [128, 4, 8])
        pv = sb("pv", [128, 8, 8])
        sinkp = sb("sinkp", [128, 16])
        epsT = sb("epsT", [128, 2])
        cI, cM, cG, cS, cV, cW, cE = (Buf("cI"), Buf("cM"), Buf("cG"), Buf("cS"), Buf("cV"), Buf("cW"), Buf("cE"))
        constB = [cI, cM, cG, cS, cV, cW, cE]

        NPF = 6
        pf = [ps("pf%d" % i, [128, 512]) for i in range(NPF)]
        pfB = [Buf("pf%d" % i, excl=True) for i in range(NPF)]
        ptt = [ps("pt%d" % i, [128, 1024], BF16) for i in range(2)]
        ptB = [Buf("pt%d" % i, excl=True) for i in range(2)]

        es.enter_context(nc.Block())

        PE = Eng(nc.tensor, newsem(False), "PE")
        ACT = Eng(nc.scalar, newsem(False), "ACT")
        DVE = Eng(nc.vector, newsem(False), "DVE")
        POOL = Eng(nc.gpsimd, newsem(False), "POOL")
        SP = Eng(nc.sync, newsem(False), "SP")
        nc_real = nc
        nc = NcProxy(nc_real)
        Ctx.recording = True
        Ctx.ops = []
        Ctx.cap = []
        Ctx.pend = None

        all_dma_sems = []

        def dma(q, out, in_, reads=(), writes=(), nonc=False):
            reads = _flat(reads)
            writes = _flat(writes)
            o = Op()
            o.eng, o.calls, o.reads, o.writes, o.kind = q, None, reads, writes, "dma"
            o.dma = (out, in_, nonc)
            o.ts = None
            try:
                nbytes = int(out.nbytes())
            except Exception:
                nbytes = 4096
            cast = (out.dtype != in_.dtype)
            o.dur = (1.0 if q is POOL else 0.15) + (nbytes / 80e3 if cast else 0.0)
            o.lat = 2.0 + nbytes / 150e3
            o.idx = len(Ctx.ops)
            Ctx.ops.append(o)

        def dma_emit(o):
            q = o.eng
            out, in_, nonc = o.dma
            reads, writes = o.reads, o.writes
            q._deps(reads, writes)
            tgt = (writes + reads)[0]
            if tgt.dsem is None:
                tgt.dsem = {}
            if id(q) not in tgt.dsem:
                tgt.dsem[id(q)] = newsem(True)
                all_dma_sems.append(tgt.dsem[id(q)])
            sc = tgt.dsem[id(q)]
            if nonc:
                with nc_real.allow_non_contiguous_dma(reason="small strided"):
                    q.h.dma_start(out=out, in_=in_).then_inc(sc.s, 16)
            else:
                q.h.dma_start(out=out, in_=in_).then_inc(sc.s, 16)
            sc.cnt += 16
            tok = (sc, sc.cnt)
            for b in reads:
                b.r.append(tok)
            for b in writes:
                b.w = tok
                b.r = []

        def A(fn, r=(), w=()):
            return ACT.op(fn, _flat(r), _flat(w))

        def V(fn, r=(), w=()):
            return DVE.op(fn, _flat(r), _flat(w))

        def G(fn, r=(), w=()):
            return POOL.op(fn, _flat(r), _flat(w))

        def P(fn, r=(), w=(), sig=True):
            return PE.op(fn, _flat(r), _flat(w), sig=sig)

        act = nc.scalar.activation
        mm = nc.tensor.matmul

        dma(SP, identf[:, :], c_ident[:, :], writes=[cI])
        dma(POOL, identb[:, :], c_ident[:, :], writes=[cI])
        dma(POOL, maskb[:, :, :], c_mask.rearrange("v p c -> p v c"), writes=[cM])
        G(lambda: nc.gpsimd.memset(msamp[:, :], 0.0), w=[cM])
        dma(POOL, msamp[0:16, :], c_msamp[:, 0:256], writes=[cM])
        dma(POOL, selb[:, :], c_sel[:, :], writes=[cM])
        sk8 = sinks.rearrange("(a g hb) -> hb g a", a=2, g=4, hb=2)
        for hb in range(2):
            dma(SP, sink8[hb * 4:(hb + 1) * 4, :], sk8[hb], writes=[cS], nonc=True)
        for i, g in enumerate((g1, g2, g3, g4)):
            dma(SP, gbc[:, i, :], g.partition_broadcast(128), writes=[cG])
        dma(SP, sinkp[:, :], sinks.partition_broadcast(128), writes=[cS])
        G(lambda: nc.gpsimd.memset(wbd[:, :, :, :], 0.0), w=[cW])
        G(lambda: nc.gpsimd.memset(epsT[:, 0:1], EPS), w=[cE])
        G(lambda: nc.gpsimd.memset(epsT[:, 1:2], 1.0), w=[cE])
        G(lambda: nc.gpsimd.memset(kbuf[:, :, :], 0.0), w=[kbufB])
        G(lambda: nc.gpsimd.memset(vbuf[:, :, :], 0.0), w=[vbufB])
        G(lambda: nc.gpsimd.memset(U[:, :], 0.0), w=UB)
        for gi, lw in enumerate((lw_a, lw_x)):
            lwv = lw.rearrange("(a hb g) ci d -> hb ci a g d", a=2, hb=2, g=4)
            for hb in range(2):
                for a in range(2):
                    dma(POOL, wbd[hb * 64:(hb + 1) * 64, gi, a * 4:(a + 1) * 4, hb * 64:(hb + 1) * 64],
                        lwv[hb][:, a], writes=[cW])
        schedule.n_setup = len(Ctx.ops)
        scrB = [Buf("scr%d" % i) for i in range(NPIECE)]
        win_v = w_in.rearrange("(k p) e -> p k e", p=128)

        def head_cols(base, h):
            return win_v[:, :, base + h * 64: base + (h + 1) * 64]

        specs = [[] for _ in range(NPIECE)]

        def spec(pi, c0, c1, src, p0=0, p1=128, k0=0, k1=8):
            specs[pi].append((p0, p1, k0, k1, c0, c1, src))

        for pi in range(2):
            for jj in range(4):
                j = pi * 4 + jj
                a, g = j // 4, j % 4
                spec(pi, jj * 128, jj * 128 + 64, head_cols(0, 8 * a + g))
                spec(pi, jj * 128 + 64, jj * 128 + 128, head_cols(0, 8 * a + 4 + g))
        spec(2, 0, 512, win_v[:, :, 1024:1536])
        for cp in range(4):
            for which, bases in ((0, (2560, 4608)), (1, (1536, 3584))):
                pi = 3 + 2 * cp + which
                for bi, base in enumerate(bases):
                    for ci in range(2):
                        c = 2 * cp + ci
                        a, g = c // 4, c % 4
                        col = (bi * 2 + ci) * 128
                        spec(pi, col, col + 64, head_cols(base, 8 * a + g))
                        spec(pi, col + 64, col + 128, head_cols(base, 8 * a + 4 + g))
        wo_v = w_out.rearrange("(a hb g p) e -> hb p a g e", a=2, hb=2, g=4)
        for hf in range(2):
            for hb in range(2):
                for a in range(2):
                    spec(11 + hf, 0, 512, wo_v[hb][:, a, :, hf * 512:(hf + 1) * 512],
                         p0=hb * 64, p1=(hb + 1) * 64, k0=a * 4, k1=(a + 1) * 4)
        wu_v = w_up.rearrange("(k p) e -> p k e", p=128)
        for q in range(8):
            spec(13 + q, 0, 512, wu_v[:, :, q * 512:(q + 1) * 512])
        wd_v = w_down.rearrange("(q k p) e -> q p k e", k=8, p=128)
        for hf in range(2):
            for q in range(4):
                spec(21 + hf * 4 + q, 0, 512, wd_v[q][:, :, hf * 512:(hf + 1) * 512])

        ntiles = 1 + NSEQ * (SEQ // TT)
        total_pieces = ntiles * NPIECE
        wstate = {"loaded": 0}

        def load_upto(n):
            while wstate["loaded"] < min(n, total_pieces):
                g = wstate["loaded"]
                pi = g % NPIECE
                s_ = g % NR
                if g < NPIECE:
                    for (p0, p1, k0, k1, c0, c1, src) in specs[pi]:
                        dma(POOL, ring[s_][p0:p1, k0:k1, c0:c1], src, writes=[ringB[s_]])
                    if ntiles > 1:
                        dma(SP, scr[pi], ring[s_][:, :, :], reads=[ringB[s_]], writes=[scrB[pi]])
                else:
                    dma(SP, ring[s_][:, :, :], scr[pi], reads=[scrB[pi]], writes=[ringB[s_]])
                wstate["loaded"] += 1

        def piece(tile_idx, pi, held=0):
            g = tile_idx * NPIECE + pi
            load_upto(g + NR - held)
            s_ = g % NR
            return ring[s_], ringB[s_]

        def gpf():
            i = cnt["pf"] % 2
            cnt["pf"] += 1
            return pf[i], pfB[i]

        def gpt():
            i = cnt["pt"] % 2
            cnt["pt"] += 1
            return ptt[i], ptB[i]

        def rstd_from(ss_ap, ssB, n):
            st, stB = gst()
            A(lambda: act(out=st[:n, 0:1], in_=ss_ap, func=AF.Sqrt, bias=epsT[:n, 0:1], scale=1.0 / D),
              r=[ssB, constB], w=[stB])
            V(lambda: nc.vector.reciprocal(out=st[:n, 1:2], in_=st[:n, 0:1]), r=[stB], w=[stB])
            return st[:n, 1:2], stB

        def norm_a(src_ap, srcB, gi, bt):
            i = cnt["xa"] % 2
            cnt["xa"] += 1
            xb, xbB = xnb[i], xnbB[i]
            st, stB = gst()
            A(lambda: act(out=xb[:bt, :], in_=src_ap, func=AF.Square, accum_out=st[:bt, 0:1]),
              r=[srcB], w=[xbB, stB])
            rs, rsB = rstd_from(st[:bt, 0:1], stB, bt)
            V(lambda: nc.vector.scalar_tensor_tensor(out=xb[:bt, :], in0=src_ap, scalar=rs, in1=gbc[:bt, gi, :],
                                                     op0=ALU.mult, op1=ALU.mult),
              r=[srcB, rsB, constB], w=[xbB])
            return i

        def norm_b(i, blk, bt):
            xb, xbB = xnb[i], xnbB[i]
            pt, ptb = gpt()
            ptv = pt[:, :].rearrange("p (k t) -> p k t", t=128)
            for k in range(8):
                P(lambda k=k: nc.tensor.transpose(out=ptv[:, k, :bt], in_=xb[:bt, k * 128:(k + 1) * 128],
                                                  identity=identb[:bt, :bt]),
                  r=[xbB, constB], w=[ptb], sig=(k == 7))
            A(lambda: act(out=xnT[:, :, blk * 128:blk * 128 + bt], in_=ptv[:, :, :bt], func=AF.Copy),
              r=[ptb], w=[xnTB[blk]])

        def proj(rg, rgB, cc, NT, nbk):
            pb, pbB = gpf()
            for k in range(8):
                P(lambda k=k: mm(pb[:, :NT], lhsT=rg[:, k, cc * 128:(cc + 1) * 128], rhs=xnT[:, k, :NT],
                                 start=(k == 0), stop=(k == 7)),
                  r=[rgB, xnTB[:nbk]], w=[pbB], sig=(k == 7))
            return pb, pbB

        def dma_perm_out(q, dst_dram, src_sb, n, reads):
            dv = dst_dram.rearrange("n (a hb g d) -> n a hb g d", a=2, hb=2, g=4, d=64)
            sv_ = src_sb.rearrange("n (a g hb d) -> n a hb g d", a=2, hb=2, g=4, d=64)
            for a in range(2):
                for hb in range(2):
                    dma(q, dv[:, a, hb], sv_[:, a, hb], reads=reads)

        def dma_perm_in(q, dst_sb, src_dram, n, writes):
            sv_ = src_dram.rearrange("n (a hb g d) -> n a hb g d", a=2, hb=2, g=4, d=64)
            dv = dst_sb.rearrange("n (a g hb d) -> n a hb g d", a=2, hb=2, g=4, d=64)
            for a in range(2):
                for hb in range(2):
                    dma(q, dv[:, a, hb], sv_[:, a, hb], writes=writes)

        def vec_setup():
            dma_perm_in(SP, stg[0:4, :], conv_w[:, :], 4, [stgB])
            for i, v in enumerate((conv_b, lb_a, lb_x, lam)):
                dma_perm_in(SP, stg[4 + i:5 + i, :], v.rearrange("(o d) -> o d", o=1), 1, [stgB])
            pb, pbB = pf[0], pfB[0]
            for c in range(8):
                P(lambda c=c: nc.tensor.transpose(out=pb[:, c * 8:(c + 1) * 8], in_=stg[0:8, c * 128:(c + 1) * 128],
                                                  identity=identf[0:8, 0:8]),
                  r=[stgB, cI], w=[pbB], sig=(c == 7))
            pb3 = pb[:, 0:64].rearrange("p (c v) -> p v c", v=8)
            V(lambda: nc.vector.tensor_copy(out=cw[:, :, :], in_=pb3[:, 0:4, :]), r=[pbB], w=[cV])
            V(lambda: nc.vector.tensor_copy(out=pv[:, 0:4, :], in_=pb3[:, 4:8, :]), r=[pbB], w=[cV])
            A(lambda: act(out=pv[:, 6, :], in_=pv[:, 3, :], func=AF.Exp, scale=-1.0), r=[cV, cE], w=[cV])
            A(lambda: act(out=pv[:, 6, :], in_=pv[:, 6, :], func=AF.Ln, bias=epsT[:, 1:2], scale=1.0),
              r=[cV, cE], w=[cV])
            V(lambda: nc.vector.tensor_scalar(out=pv[:, 4, :], in0=pv[:, 6, :], scalar1=-4.0, scalar2=None,
                                              op0=ALU.mult), r=[cV, cE], w=[cV])
            V(lambda: nc.vector.tensor_scalar(out=pv[:, 5, :], in0=pv[:, 6, :], scalar1=-8.0, scalar2=None,
                                              op0=ALU.mult), r=[cV, cE], w=[cV])
            for i_ in (1, 2):
                V(lambda i_=i_: nc.vector.tensor_scalar(out=pv[:, i_, :], in0=pv[:, i_, :], scalar1=0.5, scalar2=None,
                                                        op0=ALU.mult), r=[cV, cE], w=[cV])

        def out_rows_from_chunks(srcs, n, srcBs, dst_dram):
            for half in range(2):
                pb, pbB = gpf()
                for cc in range(4):
                    c = half * 4 + cc
                    P(lambda c=c, cc=cc: nc.tensor.transpose(out=pb[:n, cc * 128:(cc + 1) * 128], in_=srcs[c],
                                                             identity=identf[:, :]),
                      r=[srcBs, constB], w=[pbB], sig=(cc == 3))
                V(lambda: nc.vector.tensor_copy(out=stg[:n, half * 512:(half + 1) * 512], in_=pb[:n, :]),
                  r=[pbB], w=[stgB])
            dma_perm_out(SP, dst_dram, stg[:n, :], n, [stgB])

        class AttnPipe:
            def __init__(self):
                self.q2 = []
                self.q3 = []

            def push(self, u):
                st1 = self.phase1(u)
                self.q2.append(st1)
                if len(self.q2) > 3:
                    self.q3.append(self.phase2(self.q2.pop(0)))
                if len(self.q3) > 1:
                    self.phase3(self.q3.pop(0))

            def drain(self):
                while self.q2:
                    self.q3.append(self.phase2(self.q2.pop(0)))
                    if len(self.q3) > 1:
                        self.phase3(self.q3.pop(0))
                while self.q3:
                    self.phase3(self.q3.pop(0))

            def phase1(self, u):
                nq, j, qcols, kparts, maskmm, nk = u["nq"], u["j"], u["qcols"], u["kparts"], u["mask"], u["nk"]
                par = cnt["au"] % 3
                cnt["au"] += 1
                bi_ = (2, 3, 5)[par]
                bank, bB = pf[bi_], pfB[bi_]
                sv = bank[:, :].rearrange("p (h k) -> p h k", h=2)
                nparts = len(kparts)
                for h in range(2):
                    P(maskmm(h, sv[:nq, h, :nk]), r=[constB], w=[bB], sig=False)
                    off = 0
                    for pi_, (kfn, n, kb) in enumerate(kparts):
                        last = (h == 1 and pi_ == nparts - 1)
                        P(lambda h=h, off=off, n=n, kfn=kfn, last=last: mm(
                            sv[:nq, h, off:off + n], lhsT=qT[h * 64:(h + 1) * 64, j, qcols[0]:qcols[1]],
                            rhs=kfn(h), start=False, stop=last),
                          r=[qTB(j), kb], w=[bB], sig=last)
                        off += n
                st, stB = gst()
                V(lambda: nc.vector.tensor_reduce(out=st[:nq, 0:2], in_=sv[:nq, :, :nk], axis=AX.X, op=ALU.max),
                  r=[bB], w=[stB])
                V(lambda: nc.vector.tensor_scalar(out=st[:nq, 2:4], in0=st[:nq, 0:2], scalar1=-0.125, scalar2=None,
                                                  op0=ALU.mult), r=[stB], w=[stB])
                V(lambda: nc.vector.tensor_tensor(out=st[:nq, 6:8], in0=st[:nq, 2:4], in1=sinkp[:nq, 2 * j:2 * j + 2],
                                                  op=ALU.add), r=[stB, constB], w=[stB])
                Et = TF.get()
                E, EB = Et
                Ev = E[:, :].rearrange("p (h k) -> p h k", h=2)
                for h in range(2):
                    A(lambda h=h: act(out=Ev[:nq, h, :nk], in_=sv[:nq, h, :nk], func=AF.Exp,
                                      bias=st[:nq, 2 + h:3 + h], scale=0.125, accum_out=st[:nq, 4 + h:5 + h]),
                      r=[bB, stB], w=[EB, stB])
                A(lambda: act(out=st[:nq, 6:8], in_=st[:nq, 6:8], func=AF.Exp), r=[stB], w=[stB])
                V(lambda: nc.vector.tensor_tensor(out=st[:nq, 6:8], in0=st[:nq, 6:8], in1=st[:nq, 4:6], op=ALU.add),
                  r=[stB], w=[stB])
                V(lambda: nc.vector.reciprocal(out=st[:nq, 6:8], in_=st[:nq, 6:8]), r=[stB], w=[stB])
                Pt = TB.get()
                Pn, PnB = Pt
                Pv = Pn[:, :].rearrange("p (h k) -> p h k", h=2)
                for h in range(2):
                    V(lambda h=h: nc.vector.tensor_scalar(out=Pv[:nq, h, :nk], in0=Ev[:nq, h, :nk],
                                                          scalar1=st[:nq, 6 + h:7 + h], scalar2=None, op0=ALU.mult),
                      r=[EB, stB], w=[PnB])
                TF.put(Et)
                return (u, Pt)

            def phase2(self, state):
                u, Pt = state
                nq, vparts = u["nq"], u["vparts"]
                Pn, PnB = Pt
                Pv = Pn[:, :].rearrange("p (h k) -> p h k", h=2)
                pt, ptb = gpt()
                ptv = pt[:, 0:512].rearrange("p (i t) -> p i t", t=128)
                nvp = len(vparts)
                for h in range(2):
                    for vi, (lv, K, (c0, c1), vb) in enumerate(vparts):
                        last = (h == 1 and vi == nvp - 1)
                        P(lambda h=h, vi=vi, K=K, c0=c0, c1=c1: nc.tensor.transpose(
                            out=ptv[:K, h * 2 + vi, :nq], in_=Pv[:nq, h, c0:c1], identity=identb[:nq, :nq]),
                          r=[PnB, constB], w=[ptb], sig=last)
                PTt = TB.get()
                PT, PTB = PTt
                PTv = PT[:, :].rearrange("p (i t) -> p i t", t=128)
                if nq == 128:
                    A(lambda: act(out=PTv[:, :, :], in_=ptv[:, :, :], func=AF.Copy), r=[ptb], w=[PTB])
                else:
                    for vi, (lv, K, cc_, vb) in enumerate(vparts):
                        A(lambda vi=vi, K=K: act(out=PTv[:K, vi::2, :nq], in_=ptv[:K, vi::2, :nq], func=AF.Copy),
                          r=[ptb], w=[PTB])
                TB.put(Pt)
                return (u, PTt)

            def phase3(self, state):
                u, PTt = state
                nq, j, vparts, ocols = u["nq"], u["j"], u["vparts"], u["ocols"]
                PT, PTB = PTt
                PTv = PT[:, :].rearrange("p (i t) -> p i t", t=128)
                ob, obB = pf[4], pfB[4]
                nvp = len(vparts)
                for h in range(2):
                    for vi, (lv, K, cc_, vb) in enumerate(vparts):
                        last = (h == 1 and vi == nvp - 1)
                        P(lambda h=h, vi=vi, K=K, lv=lv: mm(ob[:, h * 128:h * 128 + nq], lhsT=lv,
                                                            rhs=PTv[:K, h * 2 + vi, :nq],
                                                            start=(vi == 0), stop=(vi == nvp - 1)),
                          r=[PTB, vb], w=[obB], sig=last)
                V(lambda: nc.vector.tensor_copy(out=qT[0:64, j, ocols[0]:ocols[1]], in_=ob[0:64, 0:nq]),
                  r=[obB], w=qTB(j))
                V(lambda: nc.vector.tensor_copy(out=qT[64:128, j, ocols[0]:ocols[1]], in_=ob[64:128, 128:128 + nq]),
                  r=[obB], w=qTB(j))
                TB.put(PTt)

        tiles = [("p", seq, t0) for seq in range(NSEQ) for t0 in range(0, SEQ, TT)] + [("s", 0, 0)]

        def tinfo(ti):
            kind, seq, t0 = tiles[ti]
            sample = (kind == "s")
            return dict(sample=sample, seq=seq, t0=t0, NT=(NSAMP if sample else TT), nbk=(1 if sample else 4),
                        bt=(NSAMP if sample else 128),
                        xsrc=(x_s if sample else x_p[seq, t0:t0 + TT, :]))

        stA = {}
        marks = []
        build_program.marks = marks

        def stage_a_load(ti, blk):
            t = tinfo(ti)
            bt = t["bt"]
            i = cnt["xa"] % 2
            dma(SP, xa[i][:bt, :], t["xsrc"][blk * 128:blk * 128 + bt, :], writes=[xaB[i]])
            stA[(ti, blk)] = norm_a(xa[i][:bt, :], xaB[i], 0, bt)

        def stage_a_tr(ti, blk):
            t = tinfo(ti)
            norm_b(stA.pop((ti, blk)), blk, t["bt"])

        def run_tile(ti, prestaged):
            t = tinfo(ti)
            marks.append(("tile%d" % ti, len(Ctx.ops)))
            sample, seq, t0, NT, nbk, bt, xsrc = (t["sample"], t["seq"], t["t0"], t["NT"], t["nbk"], t["bt"],
                                                  t["xsrc"])
            first = (not sample) and t0 == 0
            lastt = (not sample) and t0 + TT == SEQ

            if not prestaged:
                for blk in range(nbk):
                    stage_a_load(ti, blk)
                    stage_a_tr(ti, blk)
            if sample:
                dma(SP, xt[:bt, 0, :], xsrc, writes=[xtB[0]])
            else:
                dma(SP, xt[:, :, :], xsrc.rearrange("(b p) d -> p b d", p=128), writes=xtB)

            if sample:
                dma_perm_in(SP, stg[:48, :], s_conv.rearrange("s i d -> (s i) d"), 48, [stgB])
                for half in range(2):
                    pb, pbB = gpf()
                    pv3 = pb[:, 0:192].rearrange("p (c t) -> p c t", t=48)
                    for cc in range(4):
                        c = half * 4 + cc
                        P(lambda c=c, cc=cc: nc.tensor.transpose(out=pv3[:, cc, :], in_=stg[:48, c * 128:(c + 1) * 128],
                                                                 identity=identf[:48, :48]),
                          r=[stgB, constB], w=[pbB], sig=(cc == 3))
                    V(lambda: nc.vector.tensor_copy(out=scT[:, half * 4:half * 4 + 4, :], in_=pv3[:, :, :]),
                      r=[pbB], w=[sampB])
                dma_perm_in(SP, stg[:16, :], s_lru[:, :], 16, [stgB])
                for half in range(2):
                    pb, pbB = gpf()
                    pv3 = pb[:, 0:64].rearrange("p (c t) -> p c t", t=16)
                    for cc in range(4):
                        c = half * 4 + cc
                        P(lambda c=c, cc=cc: nc.tensor.transpose(out=pv3[:, cc, :], in_=stg[:16, c * 128:(c + 1) * 128],
                                                                 identity=identf[:16, :16]),
                          r=[stgB, constB], w=[pbB], sig=(cc == 3))
                    V(lambda: nc.vector.tensor_copy(out=h0T[:, half * 4:half * 4 + 4, :], in_=pv3[:, :, :]),
                      r=[pbB], w=[sampB])
            elif first:
                V(lambda: nc.vector.memset(hst[:, :], 0.0), w=[hstB])
                V(lambda: nc.vector.memset(uh[:, :, :], 0.0), w=[uhB])

            if sample:
                qbdt = TB.get()
                qbd, qbdB = qbdt
                qbd5 = qbd[:, 0:256].rearrange("p (s a h g) -> p s a h g", s=16, a=2, h=2, g=4)
                V(lambda: nc.vector.memset(qbd[:, 0:256], 0.0), w=[qbdB])
            for pi in range(2):
                rg, rgB = piece(ti, pi)
                for cc in range(4):
                    j = pi * 4 + cc
                    pb, pbB = proj(rg, rgB, cc, NT, nbk)
                    if sample:
                        a_, g_ = j // 4, j % 4
                        V(lambda: nc.vector.tensor_copy(out=qbd5[0:64, :, a_, 0, g_], in_=pb[0:64, :NT]),
                          r=[pbB], w=[qbdB])
                        V(lambda: nc.vector.tensor_copy(out=qbd5[64:128, :, a_, 1, g_], in_=pb[64:128, :NT]),
                          r=[pbB], w=[qbdB])
                        continue
                    if cc % 2 == 0:
                        A(lambda: act(out=qT[:, j, :NT], in_=pb[:, :NT], func=AF.Copy), r=[pbB], w=qTB(j))
                    else:
                        V(lambda: nc.vector.tensor_copy(out=qT[:, j, :NT], in_=pb[:, :NT]), r=[pbB], w=qTB(j))
            rg, rgB = piece(ti, 2)
            for a in range(2):
                pb, pbB = proj(rg, rgB, a, NT, nbk)
                V(lambda: nc.vector.tensor_copy(out=kbuf[:, a, 128:128 + NT], in_=pb[:, :NT]), r=[pbB], w=[kbufB])
            for blk in range(nbk):
                pb, pbB = gpf()
                for k in range(8):
                    P(lambda k=k: mm(pb[:bt, :], lhsT=xnT[:, k, blk * 128:blk * 128 + bt], rhs=rg[:, k, :],
                                     start=(k == 0), stop=(k == 7)),
                      r=[rgB, xnTB[blk]], w=[pbB], sig=(k == 7))
                A(lambda: act(out=vbuf[:bt, 1 + blk, :], in_=pb[:bt, 256:512], func=AF.Copy), r=[pbB], w=[vbufB])
                if sample or (lastt and blk == nbk - 1):
                    V(lambda: nc.vector.tensor_copy(out=kvst[:bt, :], in_=pb[:bt, :]), r=[pbB], w=[kvstB])
                    if sample:
                        dma(SP, kws[:, 127, :], kvst[:bt, 0:256], reads=[kvstB])
                        dma(SP, vws[:, 127, :], kvst[:bt, 256:512], reads=[kvstB])
                    else:
                        dma(SP, kwp[seq], kvst[:, 0:256], reads=[kvstB])
                        dma(SP, vwp[seq], kvst[:, 256:512], reads=[kvstB])

            if sample:
                oall, oallB = pf[4], pfB[4]
                oall4 = oall[:, 0:256].rearrange("p (s a m) -> p s a m", s=16, a=2, m=8)
                for s in range(NSAMP):
                    i = s % 2
                    dma(SP, kcf[i][:, :], ck[s], writes=[kcfB[i]])
                    dma(POOL, vc[i][:, :], cv[s], writes=[vcB[i]])
                    pb, pbB = gpf()
                    for a in range(2):
                        P(lambda a=a: nc.tensor.transpose(out=pb[:, a * 128:(a + 1) * 128],
                                                          in_=kcf[i][:, a * 128:(a + 1) * 128], identity=identf[:, :]),
                          r=[kcfB[i], constB], w=[pbB], sig=(a == 1))
                    V(lambda: nc.vector.tensor_copy(out=kc[i][:, :, :], in_=pb[:, 0:256].rearrange("p (a t) -> p a t", a=2)),
                      r=[pbB], w=[kcB[i]])
                    bank, bB = pf[2 + i], pfB[2 + i]
                    sv = bank[:, :].rearrange("p (a k) -> p a k", a=2)
                    for a in range(2):
                        P(lambda a=a: mm(sv[0:8, a, 0:144], lhsT=selb[:, s * 8:(s + 1) * 8], rhs=msamp[:, 0:144],
                                         start=(a == 0), stop=False), r=[constB], w=[bB], sig=False)
                        P(lambda a=a: mm(sv[0:8, a, 0:128], lhsT=qbd5[:, s, a, :, :], rhs=kc[i][:, a, :],
                                         start=False, stop=False), r=[qbdB, kcB[i]], w=[bB], sig=False)
                        P(lambda a=a: mm(sv[0:8, a, 128:144], lhsT=qbd5[:, s, a, :, :], rhs=kbuf[:, a, 128:144],
                                         start=False, stop=(a == 1)), r=[qbdB, kbufB], w=[bB], sig=(a == 1))
                    st, stB = gst()
                    V(lambda: nc.vector.tensor_reduce(out=st[0:8, 0:2], in_=sv[0:8, :, 0:144], axis=AX.X, op=ALU.max),
                      r=[bB], w=[stB])
                    V(lambda: nc.vector.tensor_scalar(out=st[0:8, 2:4], in0=st[0:8, 0:2], scalar1=-0.125, scalar2=None,
                                                      op0=ALU.mult), r=[stB], w=[stB])
                    V(lambda: nc.vector.tensor_tensor(out=st[0:8, 6:8], in0=st[0:8, 2:4], in1=sink8[0:8, 0:2], op=ALU.add),
                      r=[stB, constB], w=[stB])
                    Et = TF.get(); E, EB = Et
                    Ev = E[:, :].rearrange("p (a k) -> p a k", a=2)
                    for a in range(2):
                        A(lambda a=a: act(out=Ev[0:8, a, 0:144], in_=sv[0:8, a, 0:144], func=AF.Exp,
                                          bias=st[0:8, 2 + a:3 + a], scale=0.125, accum_out=st[0:8, 4 + a:5 + a]),
                          r=[bB, stB], w=[EB, stB])
                    A(lambda: act(out=st[0:8, 6:8], in_=st[0:8, 6:8], func=AF.Exp), r=[stB], w=[stB])
                    V(lambda: nc.vector.tensor_tensor(out=st[0:8, 6:8], in0=st[0:8, 6:8], in1=st[0:8, 4:6], op=ALU.add),
                      r=[stB], w=[stB])
                    V(lambda: nc.vector.reciprocal(out=st[0:8, 6:8], in_=st[0:8, 6:8]), r=[stB], w=[stB])
                    Pt = TB.get(); Pn, PnB = Pt
                    Pv = Pn[:, :].rearrange("p (a k) -> p a k", a=2)
                    for a in range(2):
                        V(lambda a=a: nc.vector.tensor_scalar(out=Pv[0:8, a, 0:144], in0=Ev[0:8, a, 0:144],
                                                              scalar1=st[0:8, 6 + a:7 + a], scalar2=None, op0=ALU.mult),
                          r=[EB, stB], w=[PnB])
                    TF.put(Et)
                    pt, ptb = gpt()
                    ptv = pt[:, 0:32].rearrange("p (i t) -> p i t", t=8)
                    for a in range(2):
                        P(lambda a=a: nc.tensor.transpose(out=ptv[:, 2 * a, :], in_=Pv[0:8, a, 0:128],
                                                          identity=identb[0:8, 0:8]),
                          r=[PnB, constB], w=[ptb], sig=False)
                        P(lambda a=a: nc.tensor.transpose(out=ptv[0:16, 2 * a + 1, :], in_=Pv[0:8, a, 128:144],
                                                          identity=identb[0:8, 0:8]),
                          r=[PnB, constB], w=[ptb], sig=(a == 1))
                    PTt = TB.get(); PT, PTB = PTt
                    PTv = PT[:, 0:32].rearrange("p (i t) -> p i t", t=8)
                    A(lambda: act(out=PTv[:, 0::2, :], in_=ptv[:, 0::2, :], func=AF.Copy), r=[ptb], w=[PTB])
                    A(lambda: act(out=PTv[0:16, 1::2, :], in_=ptv[0:16, 1::2, :], func=AF.Copy), r=[ptb], w=[PTB])
                    TB.put(Pt)
                    for a in range(2):
                        P(lambda a=a: mm(oall4[:, s, a, :], lhsT=vc[i][:, a * 128:(a + 1) * 128], rhs=PTv[:, 2 * a, :],
                                         start=True, stop=False), r=[PTB, vcB[i]], w=[oallB], sig=False)
                        P(lambda a=a: mm(oall4[:, s, a, :], lhsT=vbuf[0:16, 1, a * 128:(a + 1) * 128],
                                         rhs=PTv[0:16, 2 * a + 1, :], start=False, stop=True),
                          r=[PTB, vbufB], w=[oallB], sig=(a == 1))
                    TB.put(PTt)
                oall5 = oall[:, 0:256].rearrange("p (s a h g) -> p s a h g", s=16, a=2, h=2, g=4)
                for hb in range(2):
                    for a in range(2):
                        V(lambda hb=hb, a=a: nc.vector.tensor_copy(
                            out=qT[hb * 64:(hb + 1) * 64, a * 4:(a + 1) * 4, 0:16],
                            in_=oall5[hb * 64:(hb + 1) * 64, :, a, hb, :].rearrange("p s g -> p g s")),
                          r=[oallB], w=[qTB(a * 4 + g_) for g_ in range(4)])
                TB.put(qbdt)

            units = []
            if sample:
                pass
            else:
                for blk in range(nbk):
                    for j in range(8):
                        units.append(("p", blk, j))
            upos = [0]
            pipe = AttnPipe()
            sstate = {}

            def sample_prep(s):
                i = s % 2
                dma(SP, kcf[i][:, :], ck[s], writes=[kcfB[i]])
                dma(POOL, vc[i][:, :], cv[s], writes=[vcB[i]])
                pb, pbB = gpf()
                for a in range(2):
                    P(lambda a=a: nc.tensor.transpose(out=pb[:, a * 128:(a + 1) * 128],
                                                      in_=kcf[i][:, a * 128:(a + 1) * 128], identity=identf[:, :]),
                      r=[kcfB[i], constB], w=[pbB], sig=(a == 1))
                V(lambda: nc.vector.tensor_copy(out=kc[i][:, :, :], in_=pb[:, 0:256].rearrange("p (a t) -> p a t", a=2)),
                  r=[pbB], w=[kcB[i]])

            def push_units(n):
                for _ in range(n):
                    if upos[0] >= len(units):
                        return
                    kind_, x_, j = units[upos[0]]
                    upos[0] += 1
                    a = j // 4
                    if kind_ == "s":
                        s = x_
                        i = s % 2
                        if j == 0:
                            sample_prep(s)
                        u = dict(nq=1, j=j, qcols=(s, s + 1), nk=144, ocols=(s, s + 1),
                                 kparts=[(lambda h, a=a, i=i: kc[i][h * 64:(h + 1) * 64, a, :], 128, [kcB[i]]),
                                         (lambda h, a=a: kbuf[h * 64:(h + 1) * 64, a, 128:144], 16, [kbufB])],
                                 vparts=[(vc[i][:, a * 128:(a + 1) * 128], 128, (0, 128), [vcB[i]]),
                                         (vbuf[:16, 1, a * 128:(a + 1) * 128], 16, (128, 144), [vbufB])],
                                 mask=(lambda h, out_ap, s=s: (lambda: mm(out_ap, lhsT=identb[:, s:s + 1],
                                                                          rhs=msamp[:, 0:144], start=(h == 0),
                                                                          stop=False))))
                    else:
                        blk = x_
                        mv = 1 if (first and blk == 0) else 0
                        u = dict(nq=128, j=j, qcols=(blk * 128, blk * 128 + 128), nk=256,
                                 ocols=(blk * 128, blk * 128 + 128),
                                 kparts=[(lambda h, a=a, blk=blk: kbuf[h * 64:(h + 1) * 64, a, blk * 128:blk * 128 + 256],
                                          256, [kbufB])],
                                 vparts=[(vbuf[:, blk, a * 128:(a + 1) * 128], 128, (0, 128), [vbufB]),
                                         (vbuf[:, blk + 1, a * 128:(a + 1) * 128], 128, (128, 256), [vbufB])],
                                 mask=(lambda h, out_ap, mv=mv: (lambda: mm(out_ap, lhsT=identb[:, :],
                                                                            rhs=maskb[:, mv, 0:256], start=(h == 0),
                                                                            stop=False))))
                    pipe.push(u)

            upp = (len(units) + 31) // 32

            hS, hSB = hS_t, hS_B

            def R1(c):
                sl = c % 4
                uct = TF.get(); uc, ucB = uct
                if sample:
                    V(lambda: nc.vector.tensor_scalar(out=uc[:, :NT], in0=scT[:, c, 0::3], scalar1=cw[:, 0, c:c + 1],
                                                      scalar2=pv[:, 0, c:c + 1], op0=ALU.mult, op1=ALU.add),
                      r=[sampB, constB], w=[ucB])
                    for i_ in (1, 2):
                        V(lambda i_=i_: nc.vector.scalar_tensor_tensor(out=uc[:, :NT], in0=scT[:, c, i_::3],
                                                                       scalar=cw[:, i_, c:c + 1], in1=uc[:, :NT],
                                                                       op0=ALU.mult, op1=ALU.add),
                          r=[sampB, constB], w=[ucB])
                    V(lambda: nc.vector.scalar_tensor_tensor(out=uc[:, :NT], in0=ubuf[:, sl, 3:3 + NT],
                                                             scalar=cw[:, 3, c:c + 1], in1=uc[:, :NT],
                                                             op0=ALU.mult, op1=ALU.add),
                      r=[ubB(c), constB], w=[ucB])
                    V(lambda: nc.vector.tensor_copy(out=uS_t[:, c * 16:(c + 1) * 16], in_=ubuf[:, sl, 3:3 + NT]),
                      r=ubB(c), w=[sampB])
                else:
                    V(lambda: nc.vector.tensor_scalar(out=uc[:, :NT], in0=ubuf[:, sl, 0:NT], scalar1=cw[:, 0, c:c + 1],
                                                      scalar2=pv[:, 0, c:c + 1], op0=ALU.mult, op1=ALU.add),
                      r=[ubB(c), constB], w=[ucB])
                    for i_ in (1, 2, 3):
                        V(lambda i_=i_: nc.vector.scalar_tensor_tensor(out=uc[:, :NT], in0=ubuf[:, sl, i_:i_ + NT],
                                                                       scalar=cw[:, i_, c:c + 1], in1=uc[:, :NT],
                                                                       op0=ALU.mult, op1=ALU.add),
                          r=[ubB(c), constB], w=[ucB])
                    V(lambda: nc.vector.tensor_copy(out=uh[:, c, :], in_=ubuf[:, sl, NT:NT + 3]),
                      r=ubB(c), w=[uhB])
                ucbt = TB.get(); ucb, ucbB = ucbt
                A(lambda: act(out=ucb[:, :NT], in_=uc[:, :NT], func=AF.Copy), r=[ucB], w=[ucbB])
                return dict(c=c, uct=uct, ucbt=ucbt)

            def R2(stt_):
                c = stt_["c"]
                ucb, ucbB = stt_["ucbt"]
                trt = TF.get(); tr, trB = trt
                tit = TF.get(); ti_, tiB = tit
                P(lambda: mm(pf[5][:, :NT], lhsT=wbd[:, 0, c, :], rhs=ucb[:, :NT], start=True, stop=True),
                  r=[ucbB, constB], w=[pfB[5]])
                A(lambda: act(out=tr[:, :NT], in_=pf[5][:, :NT], func=AF.Tanh, bias=pv[:, 1, c:c + 1], scale=0.5),
                  r=[pfB[5], constB], w=[trB])
                P(lambda: mm(pf[5][:, :NT], lhsT=wbd[:, 1, c, :], rhs=ucb[:, :NT], start=True, stop=True),
                  r=[ucbB, constB], w=[pfB[5]])
                A(lambda: act(out=ti_[:, :NT], in_=pf[5][:, :NT], func=AF.Tanh, bias=pv[:, 2, c:c + 1], scale=0.5),
                  r=[pfB[5], constB], w=[tiB])
                TB.put(stt_["ucbt"])
                stt_["trt"] = trt
                stt_["tit"] = tit

            def R3a(stt_):
                c = stt_["c"]
                tr, trB = stt_["trt"]
                aat = TF.get(); aa, aaB = aat
                A(lambda: act(out=aa[:, :NT], in_=tr[:, :NT], func=AF.Exp, bias=pv[:, 4, c:c + 1],
                              scale=pv[:, 4, c:c + 1]), r=[trB, constB], w=[aaB])
                A(lambda: act(out=tr[:, :NT], in_=tr[:, :NT], func=AF.Exp, bias=pv[:, 5, c:c + 1],
                              scale=pv[:, 5, c:c + 1]), r=[constB], w=[trB])
                stt_["aat"] = aat

            def R3b(stt_):
                tr, trB = stt_["trt"]
                A(lambda: act(out=tr[:, :NT], in_=tr[:, :NT], func=AF.Sqrt, bias=epsT[:, 1:2], scale=-1.0),
                  r=[constB], w=[trB])
                if first:
                    V(lambda: nc.vector.memset(tr[:, 0:1], 1.0), w=[trB])

            def R3c(stt_):
                c = stt_["c"]
                sl = c % 4
                tr, trB = stt_["trt"]
                ti_, tiB = stt_["tit"]
                aa, aaB = stt_["aat"]
                uc, ucB = stt_["uct"]
                V(lambda: nc.vector.scalar_tensor_tensor(out=ti_[:, :NT], in0=ti_[:, :NT], scalar=1.0, in1=tr[:, :NT],
                                                         op0=ALU.add, op1=ALU.mult), r=[trB], w=[tiB])
                V(lambda: nc.vector.scalar_tensor_tensor(out=ti_[:, :NT], in0=ti_[:, :NT], scalar=0.5, in1=uc[:, :NT],
                                                         op0=ALU.mult, op1=ALU.mult), r=[ucB], w=[tiB])
                if sample:
                    hv = hS[:, c * 16:(c + 1) * 16]
                    V(lambda: nc.vector.tensor_tensor(out=aa[:, :NT], in0=aa[:, :NT], in1=h0T[:, c, :], op=ALU.mult),
                      r=[sampB], w=[aaB])
                    V(lambda: nc.vector.tensor_tensor(out=hv, in0=aa[:, :NT], in1=ti_[:, :NT], op=ALU.add),
                      r=[aaB, tiB], w=[hSB])
                    hB_ = hSB
                else:
                    hv = tr[:, :NT]
                    hB_ = trB
                    V(lambda: nc.vector.tensor_tensor_scan(out=hv, data0=aa[:, :NT], data1=ti_[:, :NT],
                                                           initial=hst[:, c:c + 1], op0=ALU.mult, op1=ALU.add),
                      r=[aaB, tiB, hstB], w=[trB])
                    V(lambda: nc.vector.tensor_copy(out=hst[:, c:c + 1], in_=tr[:, NT - 1:NT]), r=[trB], w=[hstB])
                V(lambda: nc.vector.scalar_tensor_tensor(out=merged[:, c, :NT], in0=hv, scalar=0.5, in1=gs[:, sl, :NT],
                                                         op0=ALU.mult, op1=ALU.mult),
                  r=[hB_, gsB(c)], w=mgB(c))
                TF.put(stt_["uct"]); TF.put(stt_["trt"]); TF.put(stt_["tit"]); TF.put(stt_["aat"])

            marks.append(("rnnpieces", len(Ctx.ops)))
            prev = []
            for cp in range(4):
                c0, c1 = 2 * cp, 2 * cp + 1
                rg, rgB = piece(ti, 3 + 2 * cp)
                gts = []
                for ci in range(2):
                    pb, pbB = proj(rg, rgB, ci, NT, nbk)
                    gtt = TF.get()
                    g32, g32B = gtt
                    A(lambda: act(out=g32[:, :NT], in_=pb[:, :NT], func=AF.Gelu_apprx_tanh), r=[pbB], w=[g32B])
                    gts.append(gtt)
                    push_units(upp)
                    if prev:
                        R2(prev[ci])
                for ci in range(2):
                    c = c0 + ci
                    pb, pbB = proj(rg, rgB, 2 + ci, NT, nbk)
                    trt = TF.get()
                    A(lambda trt=trt: act(out=trt[0][:, :NT], in_=pb[:, :NT], func=AF.Tanh, scale=0.5),
                      r=[pbB], w=[trt[1]])
                    V(lambda trt=trt, ci=ci, c=c: nc.vector.scalar_tensor_tensor(
                        out=gs[:, c % 4, :NT], in0=trt[0][:, :NT], scalar=1.0, in1=gts[ci][0][:, :NT],
                        op0=ALU.add, op1=ALU.mult), r=[trt[1], gts[ci][1]], w=gsB(c))
                    TF.put(trt)
                    TF.put(gts[ci])
                    push_units(upp)
                    if prev:
                        if ci == 0:
                            R3a(prev[0]); R3a(prev[1])
                        else:
                            R3b(prev[0]); R3b(prev[1])
                            R3c(prev[0]); R3c(prev[1])
                prev = []
                rg, rgB = piece(ti, 4 + 2 * cp)
                for ci in range(2):
                    c = c0 + ci
                    sl = c % 4
                    if not sample:
                        V(lambda c=c, sl=sl: nc.vector.tensor_copy(out=ubuf[:, sl, 0:3], in_=uh[:, c, :]),
                          r=[uhB], w=ubB(c))
                    pb, pbB = proj(rg, rgB, ci, NT, nbk)
                    V(lambda sl=sl, c=c: nc.vector.tensor_copy(out=ubuf[:, sl, 3:3 + NT], in_=pb[:, :NT]),
                      r=[pbB], w=ubB(c))
                    push_units(upp)
                    prev.append(R1(c))
                for ci in range(2):
                    c = c0 + ci
                    pb, pbB = proj(rg, rgB, 2 + ci, NT, nbk)
                    A(lambda c=c: act(out=sa[:, c, :NT], in_=pb[:, :NT], func=AF.Tanh, scale=0.5), r=[pbB], w=saB(c))
                    push_units(upp)
            push_units(len(units))
            R2(prev[0]); R2(prev[1])
            R3a(prev[0]); R3a(prev[1])
            R3b(prev[0]); R3b(prev[1])
            pipe.drain()
            R3c(prev[0]); R3c(prev[1])

            marks.append(("merge", len(Ctx.ops)))
            for c in range(8):
                tmt = TF.get(); tm, tmB = tmt
                V(lambda c=c: nc.vector.scalar_tensor_tensor(out=tm[:, :NT], in0=sa[:, c, :NT], scalar=1.0,
                                                             in1=qT[:, c, :NT], op0=ALU.add, op1=ALU.mult),
                  r=[saB(c), qTB(c)], w=[tmB])
                V(lambda c=c: nc.vector.scalar_tensor_tensor(out=merged[:, c, :NT], in0=tm[:, :NT], scalar=0.5,
                                                             in1=merged[:, c, :NT], op0=ALU.mult, op1=ALU.add),
                  r=[tmB], w=mgB(c))
                TF.put(tmt)
            if debug and not sample:
                dma(SP, dbg_m[:, :, :], merged[:, :, :], reads=[mgB(c) for c in range(8)])

            if sample:
                out_rows_from_chunks([hS[:, c * 16:(c + 1) * 16] for c in range(8)], 16, [hSB], lrs[:, :])
                out_rows_from_chunks([uS_t[:, c * 16:(c + 1) * 16] for c in range(8)], 16, [sampB], cvs[:, 2, :])
            elif lastt:
                pb, pbB = gpf()
                P(lambda: nc.tensor.transpose(out=pb[:8, 0:128], in_=hst[:, 0:8], identity=identf[:, :]),
                  r=[hstB, constB], w=[pbB])
                V(lambda: nc.vector.tensor_copy(out=stg[:8, 0:128], in_=pb[:8, 0:128]), r=[pbB], w=[stgB])
                lv_ = lrp[seq].rearrange("(a hb g d) -> a hb g d", a=2, hb=2, g=4, d=64)
                for a in range(2):
                    for hb in range(2):
                        dma(SP, lv_[a, hb], stg[a * 4:(a + 1) * 4, hb * 64:(hb + 1) * 64], reads=[stgB])
                out_rows_from_chunks([uh[:, c, :] for c in range(8)], 3, [uhB], cvp[seq])
            if not sample and not lastt:
                V(lambda: nc.vector.tensor_copy(out=kbuf[:, :, 0:128], in_=kbuf[:, :, 512:640]), r=[kbufB], w=[kbufB])
                V(lambda: nc.vector.tensor_copy(out=vbuf[:, 0, :], in_=vbuf[:, 4, :]), r=[vbufB], w=[vbufB])

            marks.append(("wout", len(Ctx.ops)))
            ro0, ro0B = piece(ti, 11)
            ro1, ro1B = piece(ti, 12, held=1)

            def wout_mm(blk):
                base = 2 * (blk % 2)
                for hf, (ro, roB) in enumerate(((ro0, ro0B), (ro1, ro1B))):
                    for k in range(8):
                        P(lambda k=k, hf=hf, ro=ro: mm(pf[base + hf][:bt, :], lhsT=merged[:, k, blk * 128:blk * 128 + bt],
                                                       rhs=ro[:, k, :], start=(k == 0), stop=(k == 7)),
                          r=[roB, mgB(k)], w=[pfB[base + hf]], sig=(k == 7))

            def wout_post(blk):
                base = 2 * (blk % 2)
                st, stB = gst()
                jkt = TB.get(); jk, jkB = jkt
                for hf in range(2):
                    A(lambda hf=hf: act(out=jk[:bt, :], in_=pf[base + hf][:bt, :], func=AF.Square,
                                        accum_out=st[:bt, 2 + hf:3 + hf]), r=[pfB[base + hf]], w=[jkB, stB])
                TB.put(jkt)
                V(lambda: nc.vector.tensor_tensor(out=st[:bt, 4:5], in0=st[:bt, 2:3], in1=st[:bt, 3:4], op=ALU.add),
                  r=[stB], w=[stB])
                rs, rsB = rstd_from(st[:bt, 4:5], stB, bt)
                for hf in range(2):
                    tmt = TF.get(); tm, tmB = tmt
                    V(lambda hf=hf: nc.vector.tensor_tensor(out=tm[:bt, :], in0=pf[base + hf][:bt, :],
                                                            in1=gbc[:bt, 1, hf * 512:(hf + 1) * 512], op=ALU.mult),
                      r=[pfB[base + hf], constB], w=[tmB])
                    V(lambda hf=hf: nc.vector.scalar_tensor_tensor(
                        out=xt[:bt, blk, hf * 512:(hf + 1) * 512], in0=tm[:bt, :], scalar=rs,
                        in1=xt[:bt, blk, hf * 512:(hf + 1) * 512], op0=ALU.mult, op1=ALU.add),
                      r=[tmB, rsB], w=[xtB[blk]])
                    TF.put(tmt)
                return norm_a(xt[:bt, blk, :], xtB[blk], 2, bt)

            wout_mm(0)
            pend = None
            for blk in range(nbk):
                if blk + 1 < nbk:
                    wout_mm(blk + 1)
                i_ = wout_post(blk)
                if pend is not None:
                    norm_b(pend[0], pend[1], bt)
                pend = (i_, blk)
            norm_b(pend[0], pend[1], bt)
            if debug and not sample:
                dma(SP, dbg_x1[:, :, :], xt[:, :, :], reads=xtB)

            marks.append(("ffnup", len(Ctx.ops)))
            for q in range(8):
                rg, rgB = piece(ti, 13 + q)
                for cc in range(4):
                    fc = q * 4 + cc
                    pb, pbB = proj(rg, rgB, cc, NT, nbk)
                    rlt = TB.get(); rl, rlB = rlt
                    A(lambda: act(out=rl[:, :NT], in_=pb[:, :NT], func=AF.Relu), r=[pbB], w=[rlB])
                    V(lambda: nc.vector.tensor_tensor(out=hT[:, fc, :NT], in0=rl[:, :NT], in1=rl[:, :NT], op=ALU.mult),
                      r=[rlB], w=hTB(fc))
                    TB.put(rlt)
            marks.append(("ffndown", len(Ctx.ops)))
            nxt = ti + 1 if ti + 1 < len(tiles) else None
            nsteps = []
            if nxt is not None:
                nb2 = tinfo(nxt)["nbk"]
                for b2 in range(nb2):
                    nsteps.append(("l", b2))
                    nsteps.append(("t", b2))
            step = [0]

            def next_stage_step(n=1):
                for _ in range(n):
                    if step[0] < len(nsteps):
                        k_, b2 = nsteps[step[0]]
                        step[0] += 1
                        if k_ == "l":
                            stage_a_load(nxt, b2)
                        else:
                            stage_a_tr(nxt, b2)

            ssb, ssbB = ssb_t[:, :], ssb_B
            for hf in range(2):
                for q in range(4):
                    rg, rgB = piece(ti, 21 + hf * 4 + q)
                    for kk in range(8):
                        fc = q * 8 + kk
                        for blk in range(nbk):
                            P(lambda kk=kk, fc=fc, blk=blk: mm(pf[2 + blk][:bt, :], lhsT=hT[:, fc, blk * 128:blk * 128 + bt],
                                                               rhs=rg[:, kk, :], start=(fc == 0), stop=(fc == 31)),
                              r=[rgB, hTB(fc)], w=[pfB[2 + blk]], sig=(kk == 7 and blk == nbk - 1))
                    next_stage_step(1)
                for blk in range(nbk):
                    jkt = TB.get(); jk, jkB = jkt
                    A(lambda blk=blk, hf=hf: act(out=jk[:bt, :], in_=pf[2 + blk][:bt, :], func=AF.Square,
                                                 accum_out=ssb[:bt, 2 * blk + hf:2 * blk + hf + 1]),
                      r=[pfB[2 + blk]], w=[jkB, ssbB])
                    TB.put(jkt)
                    if hf == 0:
                        V(lambda blk=blk: nc.vector.tensor_tensor(out=fbuf[:bt, blk, :], in0=pf[2 + blk][:bt, :],
                                                                  in1=gbc[:bt, 3, 0:512], op=ALU.mult),
                          r=[pfB[2 + blk], constB], w=fbB(blk))
            for blk in range(nbk):
                st, stB = gst()
                V(lambda blk=blk: nc.vector.tensor_tensor(out=st[:bt, 4:5], in0=ssb[:bt, 2 * blk:2 * blk + 1],
                                                          in1=ssb[:bt, 2 * blk + 1:2 * blk + 2], op=ALU.add),
                  r=[ssbB], w=[stB])
                rs, rsB = rstd_from(st[:bt, 4:5], stB, bt)
                V(lambda blk=blk: nc.vector.scalar_tensor_tensor(out=xt[:bt, blk, 0:512], in0=fbuf[:bt, blk, :], scalar=rs,
                                                                 in1=xt[:bt, blk, 0:512], op0=ALU.mult, op1=ALU.add),
                  r=[fbB(blk), rsB], w=[xtB[blk]])
                tmt = TF.get(); tm, tmB = tmt
                V(lambda blk=blk: nc.vector.tensor_tensor(out=tm[:bt, :], in0=pf[2 + blk][:bt, :],
                                                          in1=gbc[:bt, 3, 512:1024], op=ALU.mult),
                  r=[pfB[2 + blk], constB], w=[tmB])
                V(lambda blk=blk: nc.vector.scalar_tensor_tensor(out=xt[:bt, blk, 512:1024], in0=tm[:bt, :], scalar=rs,
                                                                 in1=xt[:bt, blk, 512:1024], op0=ALU.mult, op1=ALU.add),
                  r=[tmB, rsB], w=[xtB[blk]])
                TF.put(tmt)
            if sample:
                dma(SP, y_s[:, :], xt[:bt, 0, :], reads=[xtB[0]])
            else:
                dma(SP, y_p[seq, t0:t0 + TT, :].rearrange("(b p) d -> p b d", p=128), xt[:, :, :], reads=xtB)
            next_stage_step(len(nsteps))
            return nxt is not None

        outB = Buf("outcopy")
        dma(SP, kws[:, 0:127, :], ck[:, 1:128, :], writes=[outB])
        dma(SP, vws[:, 0:127, :], cv[:, 1:128, :], writes=[outB])
        dma(SP, cvs[:, 0:2, :], s_conv[:, 1:3, :], writes=[outB])

        vec_setup()
        pre = False
        for ti in range(len(tiles)):
            pre = run_tile(ti, pre)

        assert Ctx.pend is None
        order, est = schedule(Ctx.ops)
        build_program.est_us = est
        for i in order:
            o = Ctx.ops[i]
            if o.kind == "dma":
                dma_emit(o)
            else:
                o.eng.emit(o)
        nc = nc_real
        for sc in all_dma_sems:
            nc.gpsimd.wait_ge(sc.s, sc.cnt)
        for e in (PE, ACT, DVE, SP):
            nc.gpsimd.wait_ge(e.sc.s, e.sc.cnt)
    return nc


_CACHE = {}


def _consts():
    ident = np.eye(128, dtype=np.float32)
    qi = np.arange(128)[:, None]
    kj = np.arange(256)[None, :]
    diff = qi + 128 - kj
    band = (diff >= 0) & (diff <= 128)
    m0 = np.where(band, 0.0, NEG).astype(np.float32)
    m1 = m0.copy()
    m1[:, :128] = NEG
    mask = np.stack([np.concatenate([m0, m0], axis=1), np.concatenate([m1, m1], axis=1)])
    ms = np.full((16, 256), NEG, dtype=np.float32)
    ms[:, :128] = 0.0
    for s in range(16):
        ms[s, 128 + s] = 0.0
    msamp = np.concatenate([ms, ms], axis=1)
    sel = np.zeros((128, 128), dtype=np.float32)
    for s_ in range(16):
        sel[s_, s_ * 8:(s_ + 1) * 8] = 1.0
    return ident, np.ascontiguousarray(mask), np.ascontiguousarray(msamp), sel


def kernel(x_prompt, x_sample, cache_k_win, cache_v_win, state_conv, state_lru,
           w_in, w_out, sinks, conv_w, conv_b, lru_w_a, lru_b_a, lru_w_x, lru_b_x, lru_lambda,
           w_up, w_down, g_pre_mix, g_post_mix, g_pre_ffn, g_post_ffn):
    f = lambda a: np.ascontiguousarray(np.asarray(a, dtype=np.float32))
    if "nc" not in _CACHE:
        _CACHE["nc"] = build_program()
    nc = _CACHE["nc"]
    ident, mask, msamp, sel = _consts()
    sk = f(sinks)[0].reshape(2, 2, 4).transpose(0, 2, 1).reshape(16)
    shared = {
        "w_in": f(w_in)[0], "w_out": f(w_out)[0], "sinks": np.ascontiguousarray(sk),
        "conv_w": f(conv_w)[0], "conv_b": f(conv_b)[0],
        "lru_w_a": f(lru_w_a)[0], "lru_b_a": f(lru_b_a)[0], "lru_w_x": f(lru_w_x)[0], "lru_b_x": f(lru_b_x)[0],
        "lru_lambda": f(lru_lambda)[0], "w_up": f(w_up)[0], "w_down": f(w_down)[0],
        "g_pre_mix": f(g_pre_mix)[0], "g_post_mix": f(g_post_mix)[0],
        "g_pre_ffn": f(g_pre_ffn)[0], "g_post_ffn": f(g_post_ffn)[0],
        "c_ident": ident, "c_mask": mask, "c_msamp": msamp, "c_sel": sel,
    }
    xp = f(x_prompt)
    xs = f(x_sample)[:, 0, :]
    ckk = f(cache_k_win)[0].reshape(128, 128, 256)
    cvv = f(cache_v_win)[0].reshape(128, 128, 256)
    sc = f(state_conv)[0]
    sl = f(state_lru)[0]
    in_maps = []
    for i in range(NCORES):
        m = dict(shared)
        m["x_prompt"] = np.ascontiguousarray(xp[2 * i:2 * i + 2])
        m["x_sample"] = np.ascontiguousarray(xs[16 * i:16 * i + 16])
        m["cache_k"] = np.ascontiguousarray(ckk[16 * i:16 * i + 16])
        m["cache_v"] = np.ascontiguousarray(cvv[16 * i:16 * i + 16])
        m["state_conv"] = np.ascontiguousarray(sc[16 * i:16 * i + 16])
        m["state_lru"] = np.ascontiguousarray(sl[16 * i:16 * i + 16])
        in_maps.append(m)
    res = run_bass_kernel_spmd(nc, in_maps, core_ids=list(range(NCORES)))
    R = res.results
    cat = lambda k: np.concatenate([np.asarray(r[k], dtype=np.float32) for r in R], axis=0)
    y_prompt = cat("y_prompt")
    y_sample = cat("y_sample").reshape(128, 1, D)
    kwp = cat("k_win_prompt").reshape(1, 16, 128, 4, 64)
    vwp = cat("v_win_prompt").reshape(1, 16, 128, 4, 64)
    cvp = cat("conv_prompt").reshape(1, 16, 3, D)
    lrp = cat("lru_prompt").reshape(1, 16, D)
    kws = cat("k_win_sample").reshape(1, 128, 128, 4, 64)
    vws = cat("v_win_sample").reshape(1, 128, 128, 4, 64)
    cvs = cat("conv_sample").reshape(1, 128, 3, D)
    lrs = cat("lru_sample").reshape(1, 128, D)
    return (y_prompt, y_sample, kwp, vwp, cvp, lrp, kws, vws, cvs, lrs)
```

```python
from contextlib import ExitStack
import numpy as np
import concourse.bass as bass
import concourse.mybir as mybir
from concourse.bass_utils import run_bass_kernel_spmd

F32 = mybir.dt.float32
BF16 = mybir.dt.bfloat16
AF = mybir.ActivationFunctionType
ALU = mybir.AluOpType
AX = mybir.AxisListType

NCORES = 8
D = 1024
SEQ = 2048
NSEQ = 2
NSAMP = 16
TT = 512
NPIECE = 29
NR = 5
EPS = 1e-6
NEG = -30000.0
GC1 = 0.7978845608028654
GC2 = 0.044715
STOP = 99


class Buf:
    __slots__ = ("name", "w", "r", "dsem", "dcnt", "excl")

    def __init__(self, name, excl=False):
        self.name = name
        self.excl = excl
        self.w = None
        self.r = []
        self.dsem = None
        self.dcnt = 0


class SemC:
    __slots__ = ("s", "cnt", "is_dma")

    def __init__(self, s, is_dma):
        self.s = s
        self.cnt = 0
        self.is_dma = is_dma


class Ctx:
    recording = True
    ops = []
    cap = []
    pend = None


class Op:
    __slots__ = ("eng", "calls", "reads", "writes", "dur", "ts", "kind", "dma", "idx", "lat")


def _free_size(ap):
    try:
        return int(ap.free_size())
    except Exception:
        return 512


def _estimate(engname, calls):
    d = 0.0
    ts = None
    for (m, a, k) in calls:
        name = getattr(m, "__name__", "")
        if engname == "PE":
            ap = k.get("rhs", k.get("identity"))
            n = _free_size(ap) if ap is not None else 128
            d += (max(n, 48) + 16) / 1900.0
        elif engname == "ACT":
            ap = k.get("in_")
            n = _free_size(ap) if ap is not None else 512
            d += 0.2 + n / 1100.0
            f = k.get("func")
            if f == AF.Exp:
                ts = "E"
            elif f == AF.Tanh:
                ts = "T"
            elif f == AF.Sqrt:
                ts = "S"
            elif f == AF.Ln:
                ts = "L"
            elif f == AF.Gelu_apprx_tanh:
                ts = "G"
        elif engname == "DVE":
            ap = k.get("in0", k.get("in_", k.get("data0", k.get("out", k.get("ap")))))
            if ap is None and a:
                ap = a[0]
            n = _free_size(ap) if ap is not None else 512
            mult = 2.0 if "scan" in name else 1.0
            d += 0.1 + mult * n / 760.0
        else:
            ap = k.get("in0", k.get("in_", k.get("out", k.get("ap"))))
            if ap is None and a:
                ap = a[0]
            n = _free_size(ap) if ap is not None else 512
            d += 0.25 + n / 450.0
    return d, ts


class _Dummy:
    def then_inc(self, *a, **k):
        return self


class Rec:
    def __init__(self, real):
        object.__setattr__(self, "_real", real)

    def __getattr__(self, name):
        real_m = getattr(object.__getattribute__(self, "_real"), name)

        def f(*a, **k):
            Ctx.cap.append((real_m, a, k))
            return _Dummy()
        f.__name__ = name
        return f


class NcProxy:
    def __init__(self, real):
        self._real = real
        self.tensor = Rec(real.tensor)
        self.vector = Rec(real.vector)
        self.scalar = Rec(real.scalar)
        self.gpsimd = Rec(real.gpsimd)

    def __getattr__(self, name):
        return getattr(self._real, name)


class Eng:
    def __init__(self, h, semc, name):
        self.h = h
        self.sc = semc
        self.name = name
        self.waited = {}

    def wait(self, tok):
        sc, val = tok
        if sc.is_dma:
            val = sc.cnt
        if self.waited.get(id(sc), 0) >= val:
            return
        self.h.wait_ge(sc.s, val)
        self.waited[id(sc)] = val

    def _deps(self, reads, writes):
        for b in reads:
            if b.w is not None:
                self.wait(b.w)
        for b in writes:
            if b.w is not None:
                self.wait(b.w)
            for t in b.r:
                self.wait(t)

    def op(self, fn, reads=(), writes=(), sig=True):
        del Ctx.cap[:]
        fn()
        calls = list(Ctx.cap)
        reads = list(reads)
        writes = list(writes)
        if Ctx.pend is not None:
            pc, pr, pw = Ctx.pend
            calls = pc + calls
            reads = pr + [b for b in reads if b not in pr]
            writes = pw + [b for b in writes if b not in pw]
            Ctx.pend = None
        if not sig:
            Ctx.pend = (calls, reads, writes)
            return
        ex = [b for b in reads if b.excl]
        if ex:
            writes = writes + [b for b in ex if b not in writes]
            reads = [b for b in reads if not b.excl]
        o = Op()
        o.eng, o.calls, o.reads, o.writes, o.kind, o.dma = self, calls, reads, writes, "c", None
        o.dur, o.ts = _estimate(self.name, calls)
        o.lat = 0.0
        o.idx = len(Ctx.ops)
        Ctx.ops.append(o)

    def emit(self, o):
        self._deps(o.reads, o.writes)
        inst = None
        for (m, a, k) in o.calls:
            inst = m(*a, **k)
        inst.then_inc(self.sc.s, 1)
        self.sc.cnt += 1
        tok = (self.sc, self.sc.cnt)
        for b in o.reads:
            b.r.append(tok)
        for b in o.writes:
            b.w = tok
            b.r = []


def schedule(ops):
    n = len(ops)
    if not hasattr(schedule, 'n_setup'):
        schedule.n_setup = 0
    lastw, readers = {}, {}
    preds = [set() for _ in range(n)]
    for i, o in enumerate(ops):
        for b in o.reads:
            w = lastw.get(id(b))
            if w is not None:
                preds[i].add(w)
        for b in o.writes:
            w = lastw.get(id(b))
            if w is not None:
                preds[i].add(w)
            for r in readers.get(id(b), ()):
                preds[i].add(r)
        for b in o.reads:
            readers.setdefault(id(b), []).append(i)
        for b in o.writes:
            lastw[id(b)] = i
            readers[id(b)] = []
        preds[i].discard(i)
    succs = [[] for _ in range(n)]
    npred = [0] * n
    for i in range(n):
        npred[i] = len(preds[i])
        for p in preds[i]:
            succs[p].append(i)
    ready_time = [0.0] * n
    finish = [0.0] * n
    start = [0.0] * n
    engs = {}
    for o in ops:
        engs.setdefault(id(o.eng), o.eng)
    eng_free = {k: 0.0 for k in engs}
    ready = {k: [] for k in engs}
    cur_ts = [None]
    for i in range(n):
        if npred[i] == 0:
            ready[id(ops[i].eng)].append(i)
    done = 0
    order = []
    while done < n:
        best = None
        for k, lst in ready.items():
            if not lst:
                continue
            ef = eng_free[k]
            isact = engs[k].name == "ACT"
            for i in lst[:32]:
                st = max(ready_time[i], ef)
                pen = 0.0
                if isact and ops[i].ts is not None:
                    t_ = ops[i].ts
                    if t_ == "T":
                        if cur_ts[0] not in ("E", "G"):
                            pen = 1.3
                    elif t_ != cur_ts[0]:
                        pen = 1.3
                if i < schedule.n_setup:
                    st, pen = 0.0, 0.0
                key = (st + pen, i)
                if best is None or key < best[0]:
                    best = (key, i, k, st + pen)
        _, i, k, st = best
        o = ops[i]
        ready[k].remove(i)
        start[i] = st
        if o.kind == "dma":
            eng_free[k] = st + o.dur
            finish[i] = st + o.dur + o.lat
        else:
            finish[i] = st + o.dur
            eng_free[k] = finish[i]
            if o.eng.name == "ACT" and o.ts is not None:
                if o.ts == "T":
                    if cur_ts[0] not in ("E", "G"):
                        cur_ts[0] = "E"
                else:
                    cur_ts[0] = o.ts
        order.append(i)
        done += 1
        for sidx in succs[i]:
            hop = 0.0 if ops[sidx].eng is o.eng and o.kind != "dma" else 0.25
            rt = finish[i] + hop
            if rt > ready_time[sidx]:
                ready_time[sidx] = rt
            npred[sidx] -= 1
            if npred[sidx] == 0:
                lst = ready[id(ops[sidx].eng)]
                lo, hi = 0, len(lst)
                while lo < hi:
                    mid = (lo + hi) // 2
                    if lst[mid] < sidx:
                        lo = mid + 1
                    else:
                        hi = mid
                lst.insert(lo, sidx)
    schedule.start = start
    schedule.finish = finish
    return order, max(finish) if n else 0.0


def _flat(lst):
    out = []
    for x in lst:
        if isinstance(x, (list, tuple)):
            out.extend(_flat(x))
        elif x is not None:
            out.append(x)
    return out


def build_program(debug=False):
    nc = bass.Bass("TRN2", target_bir_lowering=False)

    def din(name, shape, dt=F32):
        return nc.dram_tensor(name, list(shape), dt, kind="ExternalInput").ap()

    def dout(name, shape, dt=F32):
        return nc.dram_tensor(name, list(shape), dt, kind="ExternalOutput").ap()

    x_p = din("x_prompt", [NSEQ, SEQ, D])
    x_s = din("x_sample", [NSAMP, D])
    ck = din("cache_k", [NSAMP, 128, 256])
    cv = din("cache_v", [NSAMP, 128, 256])
    s_conv = din("state_conv", [NSAMP, 3, D])
    s_lru = din("state_lru", [NSAMP, D])
    w_in = din("w_in", [D, 5632])
    w_out = din("w_out", [D, D])
    sinks = din("sinks", [16])
    conv_w = din("conv_w", [4, D])
    conv_b = din("conv_b", [D])
    lw_a = din("lru_w_a", [16, 64, 64])
    lb_a = din("lru_b_a", [D])
    lw_x = din("lru_w_x", [16, 64, 64])
    lb_x = din("lru_b_x", [D])
    lam = din("lru_lambda", [D])
    w_up = din("w_up", [D, 4096])
    w_down = din("w_down", [4096, D])
    g1 = din("g_pre_mix", [D])
    g2 = din("g_post_mix", [D])
    g3 = din("g_pre_ffn", [D])
    g4 = din("g_post_ffn", [D])
    c_ident = din("c_ident", [128, 128])
    c_mask = din("c_mask", [2, 128, 512])
    c_msamp = din("c_msamp", [16, 512])
    c_sel = din("c_sel", [128, 128])

    y_p = dout("y_prompt", [NSEQ, SEQ, D])
    y_s = dout("y_sample", [NSAMP, D])
    kwp = dout("k_win_prompt", [NSEQ, 128, 256])
    vwp = dout("v_win_prompt", [NSEQ, 128, 256])
    cvp = dout("conv_prompt", [NSEQ, 3, D])
    lrp = dout("lru_prompt", [NSEQ, D])
    kws = dout("k_win_sample", [NSAMP, 128, 256])
    vws = dout("v_win_sample", [NSAMP, 128, 256])
    cvs = dout("conv_sample", [NSAMP, 3, D])
    lrs = dout("lru_sample", [NSAMP, D])

    scr = nc.dram_tensor("wscr", [NPIECE, 128, 8, 512], BF16, kind="Internal").ap()
    if debug:
        dbg_m = dout("dbg_merged", [128, 8, 512], BF16)
        dbg_a = dout("dbg_attn", [128, 8, 512], BF16)
        dbg_x1 = dout("dbg_x1", [128, 4, D])

    with ExitStack() as es:
        def sb(name, shape, dt=F32):
            return es.enter_context(nc.sbuf_tensor(name, list(shape), dt))

        def ps(name, shape, dt=F32):
            return es.enter_context(nc.psum_tensor(name, list(shape), dt))

        nsem = [0]

        def newsem(is_dma):
            nsem[0] += 1
            return SemC(es.enter_context(nc.semaphore("s%d" % nsem[0])), is_dma)

        ring = [sb("ring%d" % i, [128, 8, 512], BF16) for i in range(NR)]
        ringB = [Buf("ring%d" % i) for i in range(NR)]
        xa = [sb("xa%d" % i, [128, D]) for i in range(2)]
        xaB = [Buf("xa") for _ in range(2)]
        xnb = [sb("xnb%d" % i, [128, D], BF16) for i in range(2)]
        xnbB = [Buf("xnb") for _ in range(2)]
        xnT = sb("xnT", [128, 8, TT], BF16)
        xnTB = [Buf("xnT%d" % b) for b in range(4)]
        xt = sb("xt", [128, 4, D])
        xtB = [Buf("xt%d" % b) for b in range(4)]
        kbuf = sb("kbuf", [128, 2, 640], BF16)
        kbufB = Buf("kbuf")
        vbuf = sb("vbuf", [128, 5, 256], BF16)
        vbufB = Buf("vbuf")
        USL = 40
        U = sb("U", [128, USL * 256])
        UB = [Buf("U%d" % i) for i in range(USL)]

        def uview_bf(slot0, nslots):
            return U[:, slot0 * 256:(slot0 + nslots) * 256].bitcast(BF16).rearrange(
                "p (c t) -> p c t", t=512)

        qT = uview_bf(0, 8)
        sa = uview_bf(8, 8)
        merged = uview_bf(16, 8)
        gs = uview_bf(24, 4)
        ubuf = U[:, 28 * 256:28 * 256 + 4 * 516].rearrange("p (c t) -> p c t", t=516)
        hT = uview_bf(0, 32)
        fbuf = U[:, 32 * 256:40 * 256].rearrange("p (b t) -> p b t", t=512)

        def qTB(j): return [UB[j]]
        def saB(c): return [UB[8 + c]]
        def mgB(c): return [UB[16 + c]]
        def gsB(c): return [UB[24 + c % 4]]
        def ubB(c):
            lo = 28 * 1024 + (c % 4) * 2064
            return [UB[i] for i in range(lo // 1024, (lo + 2063) // 1024 + 1)]
        def hTB(fc): return [UB[fc]]
        def fbB(b): return [UB[32 + 2 * b], UB[33 + 2 * b]]

        class Pool_:
            def __init__(self, name, n, dt):
                self.free = [(sb("%s%d" % (name, i), [128, 512], dt), Buf(name)) for i in range(n)]

            def get(self):
                assert self.free, "temp pool exhausted"
                return self.free.pop(0)

            def put(self, t):
                self.free.append(t)

        TF = Pool_("tf", 14, F32)
        TB = Pool_("tb", 12, BF16)
        cnt = {"st": 0, "xa": 0, "pf": 0, "pt": 0, "au": 0}

        NST = 12
        stt = sb("stt", [128, NST, 8])
        sttB = [Buf("st") for _ in range(NST)]

        def gst():
            i = cnt["st"] % NST
            cnt["st"] += 1
            return stt[:, i, :], sttB[i]

        stg = sb("stg", [128, D])
        stgB = Buf("stg")
        kvst = sb("kvst", [128, 512])
        kvstB = Buf("kvst")
        hst = sb("hst", [128, 8])
        hstB = Buf("hst")
        uh = sb("uh", [128, 8, 3])
        uhB = Buf("uh")
        kcf = [sb("kcf%d" % i, [128, 256]) for i in range(2)]
        kcfB = [Buf("kcf") for _ in range(2)]
        kc = [sb("kc%d" % i, [128, 2, 128], BF16) for i in range(2)]
        kcB = [Buf("kc") for _ in range(2)]
        vc = [sb("vc%d" % i, [128, 256], BF16) for i in range(2)]
        vcB = [Buf("vc") for _ in range(2)]
        scT = sb("scT", [128, 8, 48])
        h0T = sb("h0T", [128, 8, 16])
        sampB = Buf("samp")
        hS_t = sb("hS_t", [128, 128])
        hS_B = Buf("hS")
        uS_t = sb("uS_t", [128, 128])
        ssb_t = sb("ssb_t", [128, 8])
        ssb_B = Buf("ssb")
        gbc = sb("gbc", [128, 4, D])
        identf = sb("identf", [128, 128])
        identb = sb("identb", [128, 128], BF16)
        maskb = sb("maskb", [128, 2, 512], BF16)
        msamp = sb("msamp", [128, 256], BF16)
        selb = sb("selb", [128, 128], BF16)
        sink8 = sb("sink8", [8, 2])
        wbd = sb("wbd", [128, 2, 8, 128], BF16)
        cw = sb("cw", [128, 4, 8])
        pv = sb("pv", [128, 8, 8])
        sinkp = sb("sinkp", [128, 16])
        epsT = sb("epsT", [128, 2])
        cI, cM, cG, cS, cV, cW, cE = (Buf("cI"), Buf("cM"), Buf("cG"), Buf("cS"), Buf("cV"), Buf("cW"), Buf("cE"))
        constB = [cI, cM, cG, cS, cV, cW, cE]

        NPF = 6
        pf = [ps("pf%d" % i, [128, 512]) for i in range(NPF)]
        pfB = [Buf("pf%d" % i, excl=True) for i in range(NPF)]
        ptt = [ps("pt%d" % i, [128, 1024], BF16) for i in range(2)]
        ptB = [Buf("pt%d" % i, excl=True) for i in range(2)]

        es.enter_context(nc.Block())

        PE = Eng(nc.tensor, newsem(False), "PE")
        ACT = Eng(nc.scalar, newsem(False), "ACT")
        DVE = Eng(nc.vector, newsem(False), "DVE")
        POOL = Eng(nc.gpsimd, newsem(False), "POOL")
        SP = Eng(nc.sync, newsem(False), "SP")
        nc_real = nc
        nc = NcProxy(nc_real)
        Ctx.recording = True
        Ctx.ops = []
        Ctx.cap = []
        Ctx.pend = None

        all_dma_sems = []

        def dma(q, out, in_, reads=(), writes=(), nonc=False):
            reads = _flat(reads)
            writes = _flat(writes)
            o = Op()
            o.eng, o.calls, o.reads, o.writes, o.kind = q, None, reads, writes, "dma"
            o.dma = (out, in_, nonc)
            o.ts = None
            try:
                nbytes = int(out.nbytes())
            except Exception:
                nbytes = 4096
            cast = (out.dtype != in_.dtype)
            o.dur = (1.0 if q is POOL else 0.15) + (nbytes / 80e3 if cast else 0.0)
            o.lat = 2.0 + nbytes / 150e3
            o.idx = len(Ctx.ops)
            Ctx.ops.append(o)

        def dma_emit(o):
            q = o.eng
            out, in_, nonc = o.dma
            reads, writes = o.reads, o.writes
            q._deps(reads, writes)
            tgt = (writes + reads)[0]
            if tgt.dsem is None:
                tgt.dsem = {}
            if id(q) not in tgt.dsem:
                tgt.dsem[id(q)] = newsem(True)
                all_dma_sems.append(tgt.dsem[id(q)])
            sc = tgt.dsem[id(q)]
            if nonc:
                with nc_real.allow_non_contiguous_dma(reason="small strided"):
                    q.h.dma_start(out=out, in_=in_).then_inc(sc.s, 16)
            else:
                q.h.dma_start(out=out, in_=in_).then_inc(sc.s, 16)
            sc.cnt += 16
            tok = (sc, sc.cnt)
            for b in reads:
                b.r.append(tok)
            for b in writes:
                b.w = tok
                b.r = []

        def A(fn, r=(), w=()):
            return ACT.op(fn, _flat(r), _flat(w))

        def V(fn, r=(), w=()):
            return DVE.op(fn, _flat(r), _flat(w))

        def G(fn, r=(), w=()):
            return POOL.op(fn, _flat(r), _flat(w))

        def P(fn, r=(), w=(), sig=True):
            return PE.op(fn, _flat(r), _flat(w), sig=sig)

        act = nc.scalar.activation
        mm = nc.tensor.matmul

        dma(SP, identf[:, :], c_ident[:, :], writes=[cI])
        dma(POOL, identb[:, :], c_ident[:, :], writes=[cI])
        dma(POOL, maskb[:, :, :], c_mask.rearrange("v p c -> p v c"), writes=[cM])
        G(lambda: nc.gpsimd.memset(msamp[:, :], 0.0), w=[cM])
        dma(POOL, msamp[0:16, :], c_msamp[:, 0:256], writes=[cM])
        dma(POOL, selb[:, :], c_sel[:, :], writes=[cM])
        sk8 = sinks.rearrange("(a g hb) -> hb g a", a=2, g=4, hb=2)
        for hb in range(2):
            dma(SP, sink8[hb * 4:(hb + 1) * 4, :], sk8[hb], writes=[cS], nonc=True)
        for i, g in enumerate((g1, g2, g3, g4)):
            dma(SP, gbc[:, i, :], g.partition_broadcast(128), writes=[cG])
        dma(SP, sinkp[:, :], sinks.partition_broadcast(128), writes=[cS])
        G(lambda: nc.gpsimd.memset(wbd[:, :, :, :], 0.0), w=[cW])
        G(lambda: nc.gpsimd.memset(epsT[:, 0:1], EPS), w=[cE])
        G(lambda: nc.gpsimd.memset(epsT[:, 1:2], 1.0), w=[cE])
        G(lambda: nc.gpsimd.memset(kbuf[:, :, :], 0.0), w=[kbufB])
        G(lambda: nc.gpsimd.memset(vbuf[:, :, :], 0.0), w=[vbufB])
        G(lambda: nc.gpsimd.memset(U[:, :], 0.0), w=UB)
        for gi, lw in enumerate((lw_a, lw_x)):
            lwv = lw.rearrange("(a hb g) ci d -> hb ci a g d", a=2, hb=2, g=4)
            for hb in range(2):
                for a in range(2):
                    dma(POOL, wbd[hb * 64:(hb + 1) * 64, gi, a * 4:(a + 1) * 4, hb * 64:(hb + 1) * 64],
                        lwv[hb][:, a], writes=[cW])
        schedule.n_setup = len(Ctx.ops)
        scrB = [Buf("scr%d" % i) for i in range(NPIECE)]
        win_v = w_in.rearrange("(k p) e -> p k e", p=128)

        def head_cols(base, h):
            return win_v[:, :, base + h * 64: base + (h + 1) * 64]

        specs = [[] for _ in range(NPIECE)]

        def spec(pi, c0, c1, src, p0=0, p1=128, k0=0, k1=8):
            specs[pi].append((p0, p1, k0, k1, c0, c1, src))

        for pi in range(2):
            for jj in range(4):
                j = pi * 4 + jj
                a, g = j // 4, j % 4
                spec(pi, jj * 128, jj * 128 + 64, head_cols(0, 8 * a + g))
                spec(pi, jj * 128 + 64, jj * 128 + 128, head_cols(0, 8 * a + 4 + g))
        spec(2, 0, 512, win_v[:, :, 1024:1536])
        for cp in range(4):
            for which, bases in ((0, (2560, 4608)), (1, (1536, 3584))):
                pi = 3 + 2 * cp + which
                for bi, base in enumerate(bases):
                    for ci in range(2):
                        c = 2 * cp + ci
                        a, g = c // 4, c % 4
                        col = (bi * 2 + ci) * 128
                        spec(pi, col, col + 64, head_cols(base, 8 * a + g))
                        spec(pi, col + 64, col + 128, head_cols(base, 8 * a + 4 + g))
        wo_v = w_out.rearrange("(a hb g p) e -> hb p a g e", a=2, hb=2, g=4)
        for hf in range(2):
            for hb in range(2):
                for a in range(2):
                    spec(11 + hf, 0, 512, wo_v[hb][:, a, :, hf * 512:(hf + 1) * 512],
                         p0=hb * 64, p1=(hb + 1) * 64, k0=a * 4, k1=(a + 1) * 4)
        wu_v = w_up.rearrange("(k p) e -> p k e", p=128)
        for q in range(8):
            spec(13 + q, 0, 512, wu_v[:, :, q * 512:(q + 1) * 512])
        wd_v = w_down.rearrange("(q k p) e -> q p k e", k=8, p=128)
        for hf in range(2):
            for q in range(4):
                spec(21 + hf * 4 + q, 0, 512, wd_v[q][:, :, hf * 512:(hf + 1) * 512])

        ntiles = 1 + NSEQ * (SEQ // TT)
        total_pieces = ntiles * NPIECE
        wstate = {"loaded": 0}

        def load_upto(n):
            while wstate["loaded"] < min(n, total_pieces):
                g = wstate["loaded"]
                pi = g % NPIECE
                s_ = g % NR
                if g < NPIECE:
                    for (p0, p1, k0, k1, c0, c1, src) in specs[pi]:
                        dma(POOL, ring[s_][p0:p1, k0:k1, c0:c1], src, writes=[ringB[s_]])
                    if ntiles > 1:
                        dma(SP, scr[pi], ring[s_][:, :, :], reads=[ringB[s_]], writes=[scrB[pi]])
                else:
                    dma(SP, ring[s_][:, :, :], scr[pi], reads=[scrB[pi]], writes=[ringB[s_]])
                wstate["loaded"] += 1

        def piece(tile_idx, pi, held=0):
            g = tile_idx * NPIECE + pi
            load_upto(g + NR - held)
            s_ = g % NR
            return ring[s_], ringB[s_]

        def gpf():
            i = cnt["pf"] % 2
            cnt["pf"] += 1
            return pf[i], pfB[i]

        def gpt():
            i = cnt["pt"] % 2
            cnt["pt"] += 1
            return ptt[i], ptB[i]

        def rstd_from(ss_ap, ssB, n):
            st, stB = gst()
            A(lambda: act(out=st[:n, 0:1], in_=ss_ap, func=AF.Sqrt, bias=epsT[:n, 0:1], scale=1.0 / D),
              r=[ssB, constB], w=[stB])
            V(lambda: nc.vector.reciprocal(out=st[:n, 1:2], in_=st[:n, 0:1]), r=[stB], w=[stB])
            return st[:n, 1:2], stB

        def norm_a(src_ap, srcB, gi, bt):
            i = cnt["xa"] % 2
            cnt["xa"] += 1
            xb, xbB = xnb[i], xnbB[i]
            st, stB = gst()
            A(lambda: act(out=xb[:bt, :], in_=src_ap, func=AF.Square, accum_out=st[:bt, 0:1]),
              r=[srcB], w=[xbB, stB])
            rs, rsB = rstd_from(st[:bt, 0:1], stB, bt)
            V(lambda: nc.vector.scalar_tensor_tensor(out=xb[:bt, :], in0=src_ap, scalar=rs, in1=gbc[:bt, gi, :],
                                                     op0=ALU.mult, op1=ALU.mult),
              r=[srcB, rsB, constB], w=[xbB])
            return i

        def norm_b(i, blk, bt):
            xb, xbB = xnb[i], xnbB[i]
            pt, ptb = gpt()
            ptv = pt[:, :].rearrange("p (k t) -> p k t", t=128)
            for k in range(8):
                P(lambda k=k: nc.tensor.transpose(out=ptv[:, k, :bt], in_=xb[:bt, k * 128:(k + 1) * 128],
                                                  identity=identb[:bt, :bt]),
                  r=[xbB, constB], w=[ptb], sig=(k == 7))
            A(lambda: act(out=xnT[:, :, blk * 128:blk * 128 + bt], in_=ptv[:, :, :bt], func=AF.Copy),
              r=[ptb], w=[xnTB[blk]])

        def proj(rg, rgB, cc, NT, nbk):
            pb, pbB = gpf()
            for k in range(8):
                P(lambda k=k: mm(pb[:, :NT], lhsT=rg[:, k, cc * 128:(cc + 1) * 128], rhs=xnT[:, k, :NT],
                                 start=(k == 0), stop=(k == 7)),
                  r=[rgB, xnTB[:nbk]], w=[pbB], sig=(k == 7))
            return pb, pbB

        def dma_perm_out(q, dst_dram, src_sb, n, reads):
            dv = dst_dram.rearrange("n (a hb g d) -> n a hb g d", a=2, hb=2, g=4, d=64)
            sv_ = src_sb.rearrange("n (a g hb d) -> n a hb g d", a=2, hb=2, g=4, d=64)
            for a in range(2):
                for hb in range(2):
                    dma(q, dv[:, a, hb], sv_[:, a, hb], reads=reads)

        def dma_perm_in(q, dst_sb, src_dram, n, writes):
            sv_ = src_dram.rearrange("n (a hb g d) -> n a hb g d", a=2, hb=2, g=4, d=64)
            dv = dst_sb.rearrange("n (a g hb d) -> n a hb g d", a=2, hb=2, g=4, d=64)
            for a in range(2):
                for hb in range(2):
                    dma(q, dv[:, a, hb], sv_[:, a, hb], writes=writes)

        def vec_setup():
            dma_perm_in(SP, stg[0:4, :], conv_w[:, :], 4, [stgB])
            for i, v in enumerate((conv_b, lb_a, lb_x, lam)):
                dma_perm_in(SP, stg[4 + i:5 + i, :], v.rearrange("(o d) -> o d", o=1), 1, [stgB])
            pb, pbB = pf[0], pfB[0]
            for c in range(8):
                P(lambda c=c: nc.tensor.transpose(out=pb[:, c * 8:(c + 1) * 8], in_=stg[0:8, c * 128:(c + 1) * 128],
                                                  identity=identf[0:8, 0:8]),
                  r=[stgB, cI], w=[pbB], sig=(c == 7))
            pb3 = pb[:, 0:64].rearrange("p (c v) -> p v c", v=8)
            V(lambda: nc.vector.tensor_copy(out=cw[:, :, :], in_=pb3[:, 0:4, :]), r=[pbB], w=[cV])
            V(lambda: nc.vector.tensor_copy(out=pv[:, 0:4, :], in_=pb3[:, 4:8, :]), r=[pbB], w=[cV])
            A(lambda: act(out=pv[:, 6, :], in_=pv[:, 3, :], func=AF.Exp, scale=-1.0), r=[cV, cE], w=[cV])
            A(lambda: act(out=pv[:, 6, :], in_=pv[:, 6, :], func=AF.Ln, bias=epsT[:, 1:2], scale=1.0),
              r=[cV, cE], w=[cV])
            V(lambda: nc.vector.tensor_scalar(out=pv[:, 4, :], in0=pv[:, 6, :], scalar1=-4.0, scalar2=None,
                                              op0=ALU.mult), r=[cV, cE], w=[cV])
            V(lambda: nc.vector.tensor_scalar(out=pv[:, 5, :], in0=pv[:, 6, :], scalar1=-8.0, scalar2=None,
                                              op0=ALU.mult), r=[cV, cE], w=[cV])
            for i_ in (1, 2):
                V(lambda i_=i_: nc.vector.tensor_scalar(out=pv[:, i_, :], in0=pv[:, i_, :], scalar1=0.5, scalar2=None,
                                                        op0=ALU.mult), r=[cV, cE], w=[cV])

        def out_rows_from_chunks(srcs, n, srcBs, dst_dram):
            for half in range(2):
                pb, pbB = gpf()
                for cc in range(4):
                    c = half * 4 + cc
                    P(lambda c=c, cc=cc: nc.tensor.transpose(out=pb[:n, cc * 128:(cc + 1) * 128], in_=srcs[c],
                                                             identity=identf[:, :]),
                      r=[srcBs, constB], w=[pbB], sig=(cc == 3))
                V(lambda: nc.vector.tensor_copy(out=stg[:n, half * 512:(half + 1) * 512], in_=pb[:n, :]),
                  r=[pbB], w=[stgB])
            dma_perm_out(SP, dst_dram, stg[:n, :], n, [stgB])

        class AttnPipe:
            def __init__(self):
                self.q2 = []
                self.q3 = []

            def push(self, u):
                st1 = self.phase1(u)
                self.q2.append(st1)
                if len(self.q2) > 3:
                    self.q3.append(self.phase2(self.q2.pop(0)))
                if len(self.q3) > 1:
                    self.phase3(self.q3.pop(0))

            def drain(self):
                while self.q2:
                    self.q3.append(self.phase2(self.q2.pop(0)))
                    if len(self.q3) > 1:
                        self.phase3(self.q3.pop(0))
                while self.q3:
                    self.phase3(self.q3.pop(0))

            def phase1(self, u):
                nq, j, qcols, kparts, maskmm, nk = u["nq"], u["j"], u["qcols"], u["kparts"], u["mask"], u["nk"]
                par = cnt["au"] % 3
                cnt["au"] += 1
                bi_ = (2, 3, 5)[par]
                bank, bB = pf[bi_], pfB[bi_]
                sv = bank[:, :].rearrange("p (h k) -> p h k", h=2)
                nparts = len(kparts)
                for h in range(2):
                    P(maskmm(h, sv[:nq, h, :nk]), r=[constB], w=[bB], sig=False)
                    off = 0
                    for pi_, (kfn, n, kb) in enumerate(kparts):
                        last = (h == 1 and pi_ == nparts - 1)
                        P(lambda h=h, off=off, n=n, kfn=kfn, last=last: mm(
                            sv[:nq, h, off:off + n], lhsT=qT[h * 64:(h + 1) * 64, j, qcols[0]:qcols[1]],
                            rhs=kfn(h), start=False, stop=last),
                          r=[qTB(j), kb], w=[bB], sig=last)
                        off += n
                st, stB = gst()
                V(lambda: nc.vector.tensor_reduce(out=st[:nq, 2:4], in_=sv[:nq, :, :nk], axis=AX.X, op=ALU.max,
                                                  negate=True), r=[bB], w=[stB])
                V(lambda: nc.vector.tensor_tensor(out=st[:nq, 6:8], in0=st[:nq, 2:4], in1=sinkp[:nq, 2 * j:2 * j + 2],
                                                  op=ALU.add), r=[stB, constB], w=[stB])
                Et = TB.get()
                E, EB = Et
                Ev = E[:, :].rearrange("p (h k) -> p h k", h=2)
                for h in range(2):
                    A(lambda h=h: act(out=Ev[:nq, h, :nk], in_=sv[:nq, h, :nk], func=AF.Exp,
                                      bias=st[:nq, 2 + h:3 + h], scale=1.0, accum_out=st[:nq, 4 + h:5 + h]),
                      r=[bB, stB], w=[EB, stB])
                A(lambda: act(out=st[:nq, 6:8], in_=st[:nq, 6:8], func=AF.Exp), r=[stB], w=[stB])
                V(lambda: nc.vector.tensor_tensor(out=st[:nq, 6:8], in0=st[:nq, 6:8], in1=st[:nq, 4:6], op=ALU.add),
                  r=[stB], w=[stB])
                V(lambda: nc.vector.reciprocal(out=st[:nq, 6:8], in_=st[:nq, 6:8]), r=[stB], w=[stB])
                Pt = TB.get()
                Pn, PnB = Pt
                Pv = Pn[:, :].rearrange("p (h k) -> p h k", h=2)
                for h in range(2):
                    V(lambda h=h: nc.vector.tensor_scalar(out=Pv[:nq, h, :nk], in0=Ev[:nq, h, :nk],
                                                          scalar1=st[:nq, 6 + h:7 + h], scalar2=None, op0=ALU.mult),
                      r=[EB, stB], w=[PnB])
                TB.put(Et)
                return (u, Pt)

            def phase2(self, state):
                u, Pt = state
                nq, vparts = u["nq"], u["vparts"]
                Pn, PnB = Pt
                Pv = Pn[:, :].rearrange("p (h k) -> p h k", h=2)
                pt, ptb = gpt()
                ptv = pt[:, 0:512].rearrange("p (i t) -> p i t", t=128)
                nvp = len(vparts)
                for h in range(2):
                    for vi, (lv, K, (c0, c1), vb) in enumerate(vparts):
                        last = (h == 1 and vi == nvp - 1)
                        P(lambda h=h, vi=vi, K=K, c0=c0, c1=c1: nc.tensor.transpose(
                            out=ptv[:K, h * 2 + vi, :nq], in_=Pv[:nq, h, c0:c1], identity=identb[:nq, :nq]),
                          r=[PnB, constB], w=[ptb], sig=last)
                PTt = TB.get()
                PT, PTB = PTt
                PTv = PT[:, :].rearrange("p (i t) -> p i t", t=128)
                if nq == 128:
                    A(lambda: act(out=PTv[:, :, :], in_=ptv[:, :, :], func=AF.Copy), r=[ptb], w=[PTB])
                else:
                    for vi, (lv, K, cc_, vb) in enumerate(vparts):
                        A(lambda vi=vi, K=K: act(out=PTv[:K, vi::2, :nq], in_=ptv[:K, vi::2, :nq], func=AF.Copy),
                          r=[ptb], w=[PTB])
                TB.put(Pt)
                return (u, PTt)

            def phase3(self, state):
                u, PTt = state
                nq, j, vparts, ocols = u["nq"], u["j"], u["vparts"], u["ocols"]
                PT, PTB = PTt
                PTv = PT[:, :].rearrange("p (i t) -> p i t", t=128)
                ob, obB = pf[4], pfB[4]
                nvp = len(vparts)
                for h in range(2):
                    for vi, (lv, K, cc_, vb) in enumerate(vparts):
                        last = (h == 1 and vi == nvp - 1)
                        P(lambda h=h, vi=vi, K=K, lv=lv: mm(ob[:, h * 128:h * 128 + nq], lhsT=lv,
                                                            rhs=PTv[:K, h * 2 + vi, :nq],
                                                            start=(vi == 0), stop=(vi == nvp - 1)),
                          r=[PTB, vb], w=[obB], sig=last)
                V(lambda: nc.vector.tensor_copy(out=qT[0:64, j, ocols[0]:ocols[1]], in_=ob[0:64, 0:nq]),
                  r=[obB], w=qTB(j))
                V(lambda: nc.vector.tensor_copy(out=qT[64:128, j, ocols[0]:ocols[1]], in_=ob[64:128, 128:128 + nq]),
                  r=[obB], w=qTB(j))
                TB.put(PTt)

        tiles = [("p", seq, t0) for seq in range(NSEQ) for t0 in range(0, SEQ, TT)] + [("s", 0, 0)]

        def tinfo(ti):
            kind, seq, t0 = tiles[ti]
            sample = (kind == "s")
            return dict(sample=sample, seq=seq, t0=t0, NT=(NSAMP if sample else TT), nbk=(1 if sample else 4),
                        bt=(NSAMP if sample else 128),
                        xsrc=(x_s if sample else x_p[seq, t0:t0 + TT, :]))

        stA = {}
        marks = []
        build_program.marks = marks

        def stage_a_load(ti, blk):
            t = tinfo(ti)
            bt = t["bt"]
            i = cnt["xa"] % 2
            dma(SP, xa[i][:bt, :], t["xsrc"][blk * 128:blk * 128 + bt, :], writes=[xaB[i]])
            stA[(ti, blk)] = norm_a(xa[i][:bt, :], xaB[i], 0, bt)

        def stage_a_tr(ti, blk):
            t = tinfo(ti)
            norm_b(stA.pop((ti, blk)), blk, t["bt"])

        def run_tile(ti, prestaged):
            t = tinfo(ti)
            marks.append(("tile%d" % ti, len(Ctx.ops)))
            sample, seq, t0, NT, nbk, bt, xsrc = (t["sample"], t["seq"], t["t0"], t["NT"], t["nbk"], t["bt"],
                                                  t["xsrc"])
            first = (not sample) and t0 == 0
            lastt = (not sample) and t0 + TT == SEQ

            if not prestaged:
                for blk in range(nbk):
                    stage_a_load(ti, blk)
                    stage_a_tr(ti, blk)
            if sample:
                dma(SP, xt[:bt, 0, :], xsrc, writes=[xtB[0]])
            else:
                dma(SP, xt[:, :, :], xsrc.rearrange("(b p) d -> p b d", p=128), writes=xtB)

            if sample:
                dma_perm_in(SP, stg[:48, :], s_conv.rearrange("s i d -> (s i) d"), 48, [stgB])
                for half in range(2):
                    pb, pbB = gpf()
                    pv3 = pb[:, 0:192].rearrange("p (c t) -> p c t", t=48)
                    for cc in range(4):
                        c = half * 4 + cc
                        P(lambda c=c, cc=cc: nc.tensor.transpose(out=pv3[:, cc, :], in_=stg[:48, c * 128:(c + 1) * 128],
                                                                 identity=identf[:48, :48]),
                          r=[stgB, constB], w=[pbB], sig=(cc == 3))
                    V(lambda: nc.vector.tensor_copy(out=scT[:, half * 4:half * 4 + 4, :], in_=pv3[:, :, :]),
                      r=[pbB], w=[sampB])
                dma_perm_in(SP, stg[:16, :], s_lru[:, :], 16, [stgB])
                for half in range(2):
                    pb, pbB = gpf()
                    pv3 = pb[:, 0:64].rearrange("p (c t) -> p c t", t=16)
                    for cc in range(4):
                        c = half * 4 + cc
                        P(lambda c=c, cc=cc: nc.tensor.transpose(out=pv3[:, cc, :], in_=stg[:16, c * 128:(c + 1) * 128],
                                                                 identity=identf[:16, :16]),
                          r=[stgB, constB], w=[pbB], sig=(cc == 3))
                    V(lambda: nc.vector.tensor_copy(out=h0T[:, half * 4:half * 4 + 4, :], in_=pv3[:, :, :]),
                      r=[pbB], w=[sampB])
            elif first:
                V(lambda: nc.vector.memset(hst[:, :], 0.0), w=[hstB])
                V(lambda: nc.vector.memset(uh[:, :, :], 0.0), w=[uhB])

            if sample:
                qbdt = TB.get()
                qbd, qbdB = qbdt
                qbd5 = qbd[:, 0:256].rearrange("p (s a h g) -> p s a h g", s=16, a=2, h=2, g=4)
                V(lambda: nc.vector.memset(qbd[:, 0:256], 0.0), w=[qbdB])
            for pi in range(2):
                rg, rgB = piece(ti, pi)
                for cc in range(4):
                    j = pi * 4 + cc
                    pb, pbB = proj(rg, rgB, cc, NT, nbk)
                    if sample:
                        a_, g_ = j // 4, j % 4
                        V(lambda: nc.vector.tensor_scalar(out=qbd5[0:64, :, a_, 0, g_], in0=pb[0:64, :NT], scalar1=0.125,
                                                          scalar2=None, op0=ALU.mult), r=[pbB], w=[qbdB])
                        V(lambda: nc.vector.tensor_scalar(out=qbd5[64:128, :, a_, 1, g_], in0=pb[64:128, :NT], scalar1=0.125,
                                                          scalar2=None, op0=ALU.mult), r=[pbB], w=[qbdB])
                        continue
                    if cc % 2 == 0:
                        A(lambda: act(out=qT[:, j, :NT], in_=pb[:, :NT], func=AF.Copy, scale=0.125), r=[pbB], w=qTB(j))
                    else:
                        V(lambda: nc.vector.tensor_scalar(out=qT[:, j, :NT], in0=pb[:, :NT], scalar1=0.125, scalar2=None,
                                                          op0=ALU.mult), r=[pbB], w=qTB(j))
            rg, rgB = piece(ti, 2)
            for a in range(2):
                pb, pbB = proj(rg, rgB, a, NT, nbk)
                V(lambda: nc.vector.tensor_copy(out=kbuf[:, a, 128:128 + NT], in_=pb[:, :NT]), r=[pbB], w=[kbufB])
            for blk in range(nbk):
                pb, pbB = gpf()
                c0_ = 0 if (sample or (lastt and blk == nbk - 1)) else 256
                for k in range(8):
                    P(lambda k=k: mm(pb[:bt, c0_:512], lhsT=xnT[:, k, blk * 128:blk * 128 + bt], rhs=rg[:, k, c0_:512],
                                     start=(k == 0), stop=(k == 7)),
                      r=[rgB, xnTB[blk]], w=[pbB], sig=(k == 7))
                A(lambda: act(out=vbuf[:bt, 1 + blk, :], in_=pb[:bt, 256:512], func=AF.Copy), r=[pbB], w=[vbufB])
                if sample or (lastt and blk == nbk - 1):
                    V(lambda: nc.vector.tensor_copy(out=kvst[:bt, :], in_=pb[:bt, :]), r=[pbB], w=[kvstB])
                    if sample:
                        dma(SP, kws[:, 127, :], kvst[:bt, 0:256], reads=[kvstB])
                        dma(SP, vws[:, 127, :], kvst[:bt, 256:512], reads=[kvstB])
                    else:
                        dma(SP, kwp[seq], kvst[:, 0:256], reads=[kvstB])
                        dma(SP, vwp[seq], kvst[:, 256:512], reads=[kvstB])

            if sample:
                oall, oallB = pf[4], pfB[4]
                oall4 = oall[:, 0:256].rearrange("p (s a m) -> p s a m", s=16, a=2, m=8)
                for s in range(NSAMP):
                    i = s % 2
                    dma(SP, kcf[i][:, :], ck[s], writes=[kcfB[i]])
                    dma(POOL, vc[i][:, :], cv[s], writes=[vcB[i]])
                    pb, pbB = gpf()
                    for a in range(2):
                        P(lambda a=a: nc.tensor.transpose(out=pb[:, a * 128:(a + 1) * 128],
                                                          in_=kcf[i][:, a * 128:(a + 1) * 128], identity=identf[:, :]),
                          r=[kcfB[i], constB], w=[pbB], sig=(a == 1))
                    V(lambda: nc.vector.tensor_copy(out=kc[i][:, :, :], in_=pb[:, 0:256].rearrange("p (a t) -> p a t", a=2)),
                      r=[pbB], w=[kcB[i]])
                    bank, bB = pf[2 + i], pfB[2 + i]
                    sv = bank[:, :].rearrange("p (a k) -> p a k", a=2)
                    for a in range(2):
                        P(lambda a=a: mm(sv[0:8, a, 0:144], lhsT=selb[:, s * 8:(s + 1) * 8], rhs=msamp[:, 0:144],
                                         start=(a == 0), stop=False), r=[constB], w=[bB], sig=False)
                        P(lambda a=a: mm(sv[0:8, a, 0:128], lhsT=qbd5[:, s, a, :, :], rhs=kc[i][:, a, :],
                                         start=False, stop=False), r=[qbdB, kcB[i]], w=[bB], sig=False)
                        P(lambda a=a: mm(sv[0:8, a, 128:144], lhsT=qbd5[:, s, a, :, :], rhs=kbuf[:, a, 128:144],
                                         start=False, stop=(a == 1)), r=[qbdB, kbufB], w=[bB], sig=(a == 1))
                    st, stB = gst()
                    V(lambda: nc.vector.tensor_reduce(out=st[0:8, 2:4], in_=sv[0:8, :, 0:144], axis=AX.X, op=ALU.max,
                                                      negate=True), r=[bB], w=[stB])
                    V(lambda: nc.vector.tensor_tensor(out=st[0:8, 6:8], in0=st[0:8, 2:4], in1=sink8[0:8, 0:2], op=ALU.add),
                      r=[stB, constB], w=[stB])
                    Et = TF.get(); E, EB = Et
                    Ev = E[:, :].rearrange("p (a k) -> p a k", a=2)
                    for a in range(2):
                        A(lambda a=a: act(out=Ev[0:8, a, 0:144], in_=sv[0:8, a, 0:144], func=AF.Exp,
                                          bias=st[0:8, 2 + a:3 + a], scale=1.0, accum_out=st[0:8, 4 + a:5 + a]),
                          r=[bB, stB], w=[EB, stB])
                    A(lambda: act(out=st[0:8, 6:8], in_=st[0:8, 6:8], func=AF.Exp), r=[stB], w=[stB])
                    V(lambda: nc.vector.tensor_tensor(out=st[0:8, 6:8], in0=st[0:8, 6:8], in1=st[0:8, 4:6], op=ALU.add),
                      r=[stB], w=[stB])
                    V(lambda: nc.vector.reciprocal(out=st[0:8, 6:8], in_=st[0:8, 6:8]), r=[stB], w=[stB])
                    Pt = TB.get(); Pn, PnB = Pt
                    Pv = Pn[:, :].rearrange("p (a k) -> p a k", a=2)
                    for a in range(2):
                        V(lambda a=a: nc.vector.tensor_scalar(out=Pv[0:8, a, 0:144], in0=Ev[0:8, a, 0:144],
                                                              scalar1=st[0:8, 6 + a:7 + a], scalar2=None, op0=ALU.mult),
                          r=[EB, stB], w=[PnB])
                    TF.put(Et)
                    pt, ptb = gpt()
                    ptv = pt[:, 0:32].rearrange("p (i t) -> p i t", t=8)
                    for a in range(2):
                        P(lambda a=a: nc.tensor.transpose(out=ptv[:, 2 * a, :], in_=Pv[0:8, a, 0:128],
                                                          identity=identb[0:8, 0:8]),
                          r=[PnB, constB], w=[ptb], sig=False)
                        P(lambda a=a: nc.tensor.transpose(out=ptv[0:16, 2 * a + 1, :], in_=Pv[0:8, a, 128:144],
                                                          identity=identb[0:8, 0:8]),
                          r=[PnB, constB], w=[ptb], sig=(a == 1))
                    PTt = TB.get(); PT, PTB = PTt
                    PTv = PT[:, 0:32].rearrange("p (i t) -> p i t", t=8)
                    A(lambda: act(out=PTv[:, 0::2, :], in_=ptv[:, 0::2, :], func=AF.Copy), r=[ptb], w=[PTB])
                    A(lambda: act(out=PTv[0:16, 1::2, :], in_=ptv[0:16, 1::2, :], func=AF.Copy), r=[ptb], w=[PTB])
                    TB.put(Pt)
                    for a in range(2):
                        P(lambda a=a: mm(oall4[:, s, a, :], lhsT=vc[i][:, a * 128:(a + 1) * 128], rhs=PTv[:, 2 * a, :],
                                         start=True, stop=False), r=[PTB, vcB[i]], w=[oallB], sig=False)
                        P(lambda a=a: mm(oall4[:, s, a, :], lhsT=vbuf[0:16, 1, a * 128:(a + 1) * 128],
                                         rhs=PTv[0:16, 2 * a + 1, :], start=False, stop=True),
                          r=[PTB, vbufB], w=[oallB], sig=(a == 1))
                    TB.put(PTt)
                oall5 = oall[:, 0:256].rearrange("p (s a h g) -> p s a h g", s=16, a=2, h=2, g=4)
                for hb in range(2):
                    for a in range(2):
                        V(lambda hb=hb, a=a: nc.vector.tensor_copy(
                            out=qT[hb * 64:(hb + 1) * 64, a * 4:(a + 1) * 4, 0:16],
                            in_=oall5[hb * 64:(hb + 1) * 64, :, a, hb, :].rearrange("p s g -> p g s")),
                          r=[oallB], w=[qTB(a * 4 + g_) for g_ in range(4)])
                TB.put(qbdt)

            units = []
            if sample:
                pass
            else:
                for blk in range(nbk):
                    for j in range(8):
                        units.append(("p", blk, j))
            upos = [0]
            pipe = AttnPipe()
            sstate = {}

            def sample_prep(s):
                i = s % 2
                dma(SP, kcf[i][:, :], ck[s], writes=[kcfB[i]])
                dma(POOL, vc[i][:, :], cv[s], writes=[vcB[i]])
                pb, pbB = gpf()
                for a in range(2):
                    P(lambda a=a: nc.tensor.transpose(out=pb[:, a * 128:(a + 1) * 128],
                                                      in_=kcf[i][:, a * 128:(a + 1) * 128], identity=identf[:, :]),
                      r=[kcfB[i], constB], w=[pbB], sig=(a == 1))
                V(lambda: nc.vector.tensor_copy(out=kc[i][:, :, :], in_=pb[:, 0:256].rearrange("p (a t) -> p a t", a=2)),
                  r=[pbB], w=[kcB[i]])

            def push_units(n):
                for _ in range(n):
                    if upos[0] >= len(units):
                        return
                    kind_, x_, j = units[upos[0]]
                    upos[0] += 1
                    a = j // 4
                    if kind_ == "s":
                        s = x_
                        i = s % 2
                        if j == 0:
                            sample_prep(s)
                        u = dict(nq=1, j=j, qcols=(s, s + 1), nk=144, ocols=(s, s + 1),
                                 kparts=[(lambda h, a=a, i=i: kc[i][h * 64:(h + 1) * 64, a, :], 128, [kcB[i]]),
                                         (lambda h, a=a: kbuf[h * 64:(h + 1) * 64, a, 128:144], 16, [kbufB])],
                                 vparts=[(vc[i][:, a * 128:(a + 1) * 128], 128, (0, 128), [vcB[i]]),
                                         (vbuf[:16, 1, a * 128:(a + 1) * 128], 16, (128, 144), [vbufB])],
                                 mask=(lambda h, out_ap, s=s: (lambda: mm(out_ap, lhsT=identb[:, s:s + 1],
                                                                          rhs=msamp[:, 0:144], start=(h == 0),
                                                                          stop=False))))
                    else:
                        blk = x_
                        mv = 1 if (first and blk == 0) else 0
                        u = dict(nq=128, j=j, qcols=(blk * 128, blk * 128 + 128), nk=256,
                                 ocols=(blk * 128, blk * 128 + 128),
                                 kparts=[(lambda h, a=a, blk=blk: kbuf[h * 64:(h + 1) * 64, a, blk * 128:blk * 128 + 256],
                                          256, [kbufB])],
                                 vparts=[(vbuf[:, blk, a * 128:(a + 1) * 128], 128, (0, 128), [vbufB]),
                                         (vbuf[:, blk + 1, a * 128:(a + 1) * 128], 128, (128, 256), [vbufB])],
                                 mask=(lambda h, out_ap, mv=mv: (lambda: mm(out_ap, lhsT=identb[:, :],
                                                                            rhs=maskb[:, mv, 0:256], start=(h == 0),
                                                                            stop=False))))
                    pipe.push(u)

            upp = (len(units) + 31) // 32

            hS, hSB = hS_t, hS_B

            def R1(c):
                sl = c % 4
                uct = TF.get(); uc, ucB = uct
                if sample:
                    V(lambda: nc.vector.tensor_scalar(out=uc[:, :NT], in0=scT[:, c, 0::3], scalar1=cw[:, 0, c:c + 1],
                                                      scalar2=pv[:, 0, c:c + 1], op0=ALU.mult, op1=ALU.add),
                      r=[sampB, constB], w=[ucB])
                    for i_ in (1, 2):
                        V(lambda i_=i_: nc.vector.scalar_tensor_tensor(out=uc[:, :NT], in0=scT[:, c, i_::3],
                                                                       scalar=cw[:, i_, c:c + 1], in1=uc[:, :NT],
                                                                       op0=ALU.mult, op1=ALU.add),
                          r=[sampB, constB], w=[ucB])
                    V(lambda: nc.vector.scalar_tensor_tensor(out=uc[:, :NT], in0=ubuf[:, sl, 3:3 + NT],
                                                             scalar=cw[:, 3, c:c + 1], in1=uc[:, :NT],
                                                             op0=ALU.mult, op1=ALU.add),
                      r=[ubB(c), constB], w=[ucB])
                    V(lambda: nc.vector.tensor_copy(out=uS_t[:, c * 16:(c + 1) * 16], in_=ubuf[:, sl, 3:3 + NT]),
                      r=ubB(c), w=[sampB])
                else:
                    V(lambda: nc.vector.tensor_scalar(out=uc[:, :NT], in0=ubuf[:, sl, 0:NT], scalar1=cw[:, 0, c:c + 1],
                                                      scalar2=pv[:, 0, c:c + 1], op0=ALU.mult, op1=ALU.add),
                      r=[ubB(c), constB], w=[ucB])
                    for i_ in (1, 2, 3):
                        V(lambda i_=i_: nc.vector.scalar_tensor_tensor(out=uc[:, :NT], in0=ubuf[:, sl, i_:i_ + NT],
                                                                       scalar=cw[:, i_, c:c + 1], in1=uc[:, :NT],
                                                                       op0=ALU.mult, op1=ALU.add),
                          r=[ubB(c), constB], w=[ucB])
                    V(lambda: nc.vector.tensor_copy(out=uh[:, c, :], in_=ubuf[:, sl, NT:NT + 3]),
                      r=ubB(c), w=[uhB])
                ucbt = TB.get(); ucb, ucbB = ucbt
                A(lambda: act(out=ucb[:, :NT], in_=uc[:, :NT], func=AF.Copy), r=[ucB], w=[ucbB])
                return dict(c=c, uct=uct, ucbt=ucbt)

            def R2(stt_):
                c = stt_["c"]
                ucb, ucbB = stt_["ucbt"]
                trt = TF.get(); tr, trB = trt
                tit = TF.get(); ti_, tiB = tit
                P(lambda: mm(pf[5][:, :NT], lhsT=wbd[:, 0, c, :], rhs=ucb[:, :NT], start=True, stop=True),
                  r=[ucbB, constB], w=[pfB[5]])
                A(lambda: act(out=tr[:, :NT], in_=pf[5][:, :NT], func=AF.Tanh, bias=pv[:, 1, c:c + 1], scale=0.5),
                  r=[pfB[5], constB], w=[trB])
                P(lambda: mm(pf[5][:, :NT], lhsT=wbd[:, 1, c, :], rhs=ucb[:, :NT], start=True, stop=True),
                  r=[ucbB, constB], w=[pfB[5]])
                A(lambda: act(out=ti_[:, :NT], in_=pf[5][:, :NT], func=AF.Tanh, bias=pv[:, 2, c:c + 1], scale=0.5),
                  r=[pfB[5], constB], w=[tiB])
                TB.put(stt_["ucbt"])
                stt_["trt"] = trt
                stt_["tit"] = tit

            def R3a(stt_):
                c = stt_["c"]
                tr, trB = stt_["trt"]
                aat = TF.get(); aa, aaB = aat
                A(lambda: act(out=aa[:, :NT], in_=tr[:, :NT], func=AF.Exp, bias=pv[:, 4, c:c + 1],
                              scale=pv[:, 4, c:c + 1]), r=[trB, constB], w=[aaB])
                A(lambda: act(out=tr[:, :NT], in_=tr[:, :NT], func=AF.Exp, bias=pv[:, 5, c:c + 1],
                              scale=pv[:, 5, c:c + 1]), r=[constB], w=[trB])
                stt_["aat"] = aat

            def R3b(stt_):
                tr, trB = stt_["trt"]
                A(lambda: act(out=tr[:, :NT], in_=tr[:, :NT], func=AF.Sqrt, bias=epsT[:, 1:2], scale=-1.0),
                  r=[constB], w=[trB])
                if first:
                    V(lambda: nc.vector.memset(tr[:, 0:1], 1.0), w=[trB])

            def R3c(stt_):
                c = stt_["c"]
                sl = c % 4
                tr, trB = stt_["trt"]
                ti_, tiB = stt_["tit"]
                aa, aaB = stt_["aat"]
                uc, ucB = stt_["uct"]
                V(lambda: nc.vector.scalar_tensor_tensor(out=ti_[:, :NT], in0=ti_[:, :NT], scalar=1.0, in1=tr[:, :NT],
                                                         op0=ALU.add, op1=ALU.mult), r=[trB], w=[tiB])
                V(lambda: nc.vector.scalar_tensor_tensor(out=ti_[:, :NT], in0=ti_[:, :NT], scalar=0.5, in1=uc[:, :NT],
                                                         op0=ALU.mult, op1=ALU.mult), r=[ucB], w=[tiB])
                if sample:
                    hv = hS[:, c * 16:(c + 1) * 16]
                    V(lambda: nc.vector.tensor_tensor(out=aa[:, :NT], in0=aa[:, :NT], in1=h0T[:, c, :], op=ALU.mult),
                      r=[sampB], w=[aaB])
                    V(lambda: nc.vector.tensor_tensor(out=hv, in0=aa[:, :NT], in1=ti_[:, :NT], op=ALU.add),
                      r=[aaB, tiB], w=[hSB])
                    hB_ = hSB
                else:
                    hv = tr[:, :NT]
                    hB_ = trB
                    V(lambda: nc.vector.tensor_tensor_scan(out=hv, data0=aa[:, :NT], data1=ti_[:, :NT],
                                                           initial=hst[:, c:c + 1], op0=ALU.mult, op1=ALU.add),
                      r=[aaB, tiB, hstB], w=[trB])
                    V(lambda: nc.vector.tensor_copy(out=hst[:, c:c + 1], in_=tr[:, NT - 1:NT]), r=[trB], w=[hstB])
                V(lambda: nc.vector.scalar_tensor_tensor(out=merged[:, c, :NT], in0=hv, scalar=0.5, in1=gs[:, sl, :NT],
                                                         op0=ALU.mult, op1=ALU.mult),
                  r=[hB_, gsB(c)], w=mgB(c))
                TF.put(stt_["uct"]); TF.put(stt_["trt"]); TF.put(stt_["tit"]); TF.put(stt_["aat"])

            marks.append(("rnnpieces", len(Ctx.ops)))
            prev = []
            for cp in range(4):
                c0, c1 = 2 * cp, 2 * cp + 1
                rg, rgB = piece(ti, 3 + 2 * cp)
                gts = []
                for ci in range(2):
                    pb, pbB = proj(rg, rgB, ci, NT, nbk)
                    gtt = TF.get()
                    g32, g32B = gtt
                    A(lambda: act(out=g32[:, :NT], in_=pb[:, :NT], func=AF.Gelu_apprx_tanh), r=[pbB], w=[g32B])
                    gts.append(gtt)
                    push_units(upp)
                    if prev:
                        R2(prev[ci])
                for ci in range(2):
                    c = c0 + ci
                    pb, pbB = proj(rg, rgB, 2 + ci, NT, nbk)
                    trt = TF.get()
                    A(lambda trt=trt: act(out=trt[0][:, :NT], in_=pb[:, :NT], func=AF.Tanh, scale=0.5),
                      r=[pbB], w=[trt[1]])
                    V(lambda trt=trt, ci=ci, c=c: nc.vector.scalar_tensor_tensor(
                        out=gs[:, c % 4, :NT], in0=trt[0][:, :NT], scalar=1.0, in1=gts[ci][0][:, :NT],
                        op0=ALU.add, op1=ALU.mult), r=[trt[1], gts[ci][1]], w=gsB(c))
                    TF.put(trt)
                    TF.put(gts[ci])
                    push_units(upp)
                    if prev:
                        if ci == 0:
                            R3a(prev[0]); R3a(prev[1])
                        else:
                            R3b(prev[0]); R3b(prev[1])
                            R3c(prev[0]); R3c(prev[1])
                prev = []
                rg, rgB = piece(ti, 4 + 2 * cp)
                for ci in range(2):
                    c = c0 + ci
                    sl = c % 4
                    if not sample:
                        V(lambda c=c, sl=sl: nc.vector.tensor_copy(out=ubuf[:, sl, 0:3], in_=uh[:, c, :]),
                          r=[uhB], w=ubB(c))
                    pb, pbB = proj(rg, rgB, ci, NT, nbk)
                    V(lambda sl=sl, c=c: nc.vector.tensor_copy(out=ubuf[:, sl, 3:3 + NT], in_=pb[:, :NT]),
                      r=[pbB], w=ubB(c))
                    push_units(upp)
                    prev.append(R1(c))
                for ci in range(2):
                    c = c0 + ci
                    pb, pbB = proj(rg, rgB, 2 + ci, NT, nbk)
                    A(lambda c=c: act(out=sa[:, c, :NT], in_=pb[:, :NT], func=AF.Tanh, scale=0.5), r=[pbB], w=saB(c))
                    push_units(upp)
            push_units(len(units))
            R2(prev[0]); R2(prev[1])
            R3a(prev[0]); R3a(prev[1])
            R3b(prev[0]); R3b(prev[1])
            pipe.drain()
            R3c(prev[0]); R3c(prev[1])

            marks.append(("merge", len(Ctx.ops)))
            for c in range(8):
                tmt = TF.get(); tm, tmB = tmt
                V(lambda c=c: nc.vector.scalar_tensor_tensor(out=tm[:, :NT], in0=sa[:, c, :NT], scalar=1.0,
                                                             in1=qT[:, c, :NT], op0=ALU.add, op1=ALU.mult),
                  r=[saB(c), qTB(c)], w=[tmB])
                V(lambda c=c: nc.vector.scalar_tensor_tensor(out=merged[:, c, :NT], in0=tm[:, :NT], scalar=0.5,
                                                             in1=merged[:, c, :NT], op0=ALU.mult, op1=ALU.add),
                  r=[tmB], w=mgB(c))
                TF.put(tmt)
            if debug and not sample:
                dma(SP, dbg_m[:, :, :], merged[:, :, :], reads=[mgB(c) for c in range(8)])

            if sample:
                out_rows_from_chunks([hS[:, c * 16:(c + 1) * 16] for c in range(8)], 16, [hSB], lrs[:, :])
                out_rows_from_chunks([uS_t[:, c * 16:(c + 1) * 16] for c in range(8)], 16, [sampB], cvs[:, 2, :])
            elif lastt:
                pb, pbB = gpf()
                P(lambda: nc.tensor.transpose(out=pb[:8, 0:128], in_=hst[:, 0:8], identity=identf[:, :]),
                  r=[hstB, constB], w=[pbB])
                V(lambda: nc.vector.tensor_copy(out=stg[:8, 0:128], in_=pb[:8, 0:128]), r=[pbB], w=[stgB])
                lv_ = lrp[seq].rearrange("(a hb g d) -> a hb g d", a=2, hb=2, g=4, d=64)
                for a in range(2):
                    for hb in range(2):
                        dma(SP, lv_[a, hb], stg[a * 4:(a + 1) * 4, hb * 64:(hb + 1) * 64], reads=[stgB])
                out_rows_from_chunks([uh[:, c, :] for c in range(8)], 3, [uhB], cvp[seq])
            if not sample and not lastt:
                V(lambda: nc.vector.tensor_copy(out=kbuf[:, :, 0:128], in_=kbuf[:, :, 512:640]), r=[kbufB], w=[kbufB])
                V(lambda: nc.vector.tensor_copy(out=vbuf[:, 0, :], in_=vbuf[:, 4, :]), r=[vbufB], w=[vbufB])

            marks.append(("wout", len(Ctx.ops)))
            ro0, ro0B = piece(ti, 11)
            ro1, ro1B = piece(ti, 12, held=1)

            def wout_mm(blk):
                base = 2 * (blk % 2)
                for hf, (ro, roB) in enumerate(((ro0, ro0B), (ro1, ro1B))):
                    for k in range(8):
                        P(lambda k=k, hf=hf, ro=ro: mm(pf[base + hf][:bt, :], lhsT=merged[:, k, blk * 128:blk * 128 + bt],
                                                       rhs=ro[:, k, :], start=(k == 0), stop=(k == 7)),
                          r=[roB, mgB(k)], w=[pfB[base + hf]], sig=(k == 7))

            def wout_post(blk):
                base = 2 * (blk % 2)
                st, stB = gst()
                jkt = TB.get(); jk, jkB = jkt
                for hf in range(2):
                    A(lambda hf=hf: act(out=jk[:bt, :], in_=pf[base + hf][:bt, :], func=AF.Square,
                                        accum_out=st[:bt, 2 + hf:3 + hf]), r=[pfB[base + hf]], w=[jkB, stB])
                TB.put(jkt)
                V(lambda: nc.vector.tensor_tensor(out=st[:bt, 4:5], in0=st[:bt, 2:3], in1=st[:bt, 3:4], op=ALU.add),
                  r=[stB], w=[stB])
                rs, rsB = rstd_from(st[:bt, 4:5], stB, bt)
                for hf in range(2):
                    tmt = TF.get(); tm, tmB = tmt
                    V(lambda hf=hf: nc.vector.tensor_tensor(out=tm[:bt, :], in0=pf[base + hf][:bt, :],
                                                            in1=gbc[:bt, 1, hf * 512:(hf + 1) * 512], op=ALU.mult),
                      r=[pfB[base + hf], constB], w=[tmB])
                    V(lambda hf=hf: nc.vector.scalar_tensor_tensor(
                        out=xt[:bt, blk, hf * 512:(hf + 1) * 512], in0=tm[:bt, :], scalar=rs,
                        in1=xt[:bt, blk, hf * 512:(hf + 1) * 512], op0=ALU.mult, op1=ALU.add),
                      r=[tmB, rsB], w=[xtB[blk]])
                    TF.put(tmt)
                return norm_a(xt[:bt, blk, :], xtB[blk], 2, bt)

            wout_mm(0)
            pend = None
            for blk in range(nbk):
                if blk + 1 < nbk:
                    wout_mm(blk + 1)
                i_ = wout_post(blk)
                if pend is not None:
                    norm_b(pend[0], pend[1], bt)
                pend = (i_, blk)
            norm_b(pend[0], pend[1], bt)
            if debug and not sample:
                dma(SP, dbg_x1[:, :, :], xt[:, :, :], reads=xtB)

            marks.append(("ffnup", len(Ctx.ops)))
            for q in range(8):
                rg, rgB = piece(ti, 13 + q)
                for cc in range(4):
                    fc = q * 4 + cc
                    pb, pbB = proj(rg, rgB, cc, NT, nbk)
                    rlt = TB.get(); rl, rlB = rlt
                    A(lambda: act(out=rl[:, :NT], in_=pb[:, :NT], func=AF.Relu), r=[pbB], w=[rlB])
                    V(lambda: nc.vector.tensor_tensor(out=hT[:, fc, :NT], in0=rl[:, :NT], in1=rl[:, :NT], op=ALU.mult),
                      r=[rlB], w=hTB(fc))
                    TB.put(rlt)
            marks.append(("ffndown", len(Ctx.ops)))
            nxt = ti + 1 if ti + 1 < len(tiles) else None
            nsteps = []
            if nxt is not None:
                nb2 = tinfo(nxt)["nbk"]
                for b2 in range(nb2):
                    nsteps.append(("l", b2))
                    nsteps.append(("t", b2))
            step = [0]

            def next_stage_step(n=1):
                for _ in range(n):
                    if step[0] < len(nsteps):
                        k_, b2 = nsteps[step[0]]
                        step[0] += 1
                        if k_ == "l":
                            stage_a_load(nxt, b2)
                        else:
                            stage_a_tr(nxt, b2)

            ssb, ssbB = ssb_t[:, :], ssb_B
            for hf in range(2):
                for q in range(4):
                    rg, rgB = piece(ti, 21 + hf * 4 + q)
                    for kk in range(8):
                        fc = q * 8 + kk
                        for blk in range(nbk):
                            P(lambda kk=kk, fc=fc, blk=blk: mm(pf[2 + blk][:bt, :], lhsT=hT[:, fc, blk * 128:blk * 128 + bt],
                                                               rhs=rg[:, kk, :], start=(fc == 0), stop=(fc == 31)),
                              r=[rgB, hTB(fc)], w=[pfB[2 + blk]], sig=(kk == 7 and blk == nbk - 1))
                    next_stage_step(1)
                for blk in range(nbk):
                    jkt = TB.get(); jk, jkB = jkt
                    A(lambda blk=blk, hf=hf: act(out=jk[:bt, :], in_=pf[2 + blk][:bt, :], func=AF.Square,
                                                 accum_out=ssb[:bt, 2 * blk + hf:2 * blk + hf + 1]),
                      r=[pfB[2 + blk]], w=[jkB, ssbB])
                    TB.put(jkt)
                    if hf == 0:
                        V(lambda blk=blk: nc.vector.tensor_tensor(out=fbuf[:bt, blk, :], in0=pf[2 + blk][:bt, :],
                                                                  in1=gbc[:bt, 3, 0:512], op=ALU.mult),
                          r=[pfB[2 + blk], constB], w=fbB(blk))
            for blk in range(nbk):
                st, stB = gst()
                V(lambda blk=blk: nc.vector.tensor_tensor(out=st[:bt, 4:5], in0=ssb[:bt, 2 * blk:2 * blk + 1],
                                                          in1=ssb[:bt, 2 * blk + 1:2 * blk + 2], op=ALU.add),
                  r=[ssbB], w=[stB])
                rs, rsB = rstd_from(st[:bt, 4:5], stB, bt)
                V(lambda blk=blk: nc.vector.scalar_tensor_tensor(out=xt[:bt, blk, 0:512], in0=fbuf[:bt, blk, :], scalar=rs,
                                                                 in1=xt[:bt, blk, 0:512], op0=ALU.mult, op1=ALU.add),
                  r=[fbB(blk), rsB], w=[xtB[blk]])
                tmt = TF.get(); tm, tmB = tmt
                V(lambda blk=blk: nc.vector.tensor_tensor(out=tm[:bt, :], in0=pf[2 + blk][:bt, :],
                                                          in1=gbc[:bt, 3, 512:1024], op=ALU.mult),
                  r=[pfB[2 + blk], constB], w=[tmB])
                V(lambda blk=blk: nc.vector.scalar_tensor_tensor(out=xt[:bt, blk, 512:1024], in0=tm[:bt, :], scalar=rs,
                                                                 in1=xt[:bt, blk, 512:1024], op0=ALU.mult, op1=ALU.add),
                  r=[tmB, rsB], w=[xtB[blk]])
                TF.put(tmt)
            if sample:
                dma(SP, y_s[:, :], xt[:bt, 0, :], reads=[xtB[0]])
            else:
                dma(SP, y_p[seq, t0:t0 + TT, :].rearrange("(b p) d -> p b d", p=128), xt[:, :, :], reads=xtB)
            next_stage_step(len(nsteps))
            return nxt is not None

        outB = Buf("outcopy")
        dma(SP, kws[:, 0:127, :], ck[:, 1:128, :], writes=[outB])
        dma(SP, vws[:, 0:127, :], cv[:, 1:128, :], writes=[outB])
        dma(SP, cvs[:, 0:2, :], s_conv[:, 1:3, :], writes=[outB])

        vec_setup()
        pre = False
        for ti in range(len(tiles)):
            pre = run_tile(ti, pre)

        assert Ctx.pend is None
        order, est = schedule(Ctx.ops)
        build_program.est_us = est
        for i in order:
            o = Ctx.ops[i]
            if o.kind == "dma":
                dma_emit(o)
            else:
                o.eng.emit(o)
        nc = nc_real
        for sc in all_dma_sems:
            nc.gpsimd.wait_ge(sc.s, sc.cnt)
        for e in (PE, ACT, DVE, SP):
            nc.gpsimd.wait_ge(e.sc.s, e.sc.cnt)
    return nc


_CACHE = {}


def _consts():
    ident = np.eye(128, dtype=np.float32)
    qi = np.arange(128)[:, None]
    kj = np.arange(256)[None, :]
    diff = qi + 128 - kj
    band = (diff >= 0) & (diff <= 128)
    m0 = np.where(band, 0.0, NEG).astype(np.float32)
    m1 = m0.copy()
    m1[:, :128] = NEG
    mask = np.stack([np.concatenate([m0, m0], axis=1), np.concatenate([m1, m1], axis=1)])
    ms = np.full((16, 256), NEG, dtype=np.float32)
    ms[:, :128] = 0.0
    for s in range(16):
        ms[s, 128 + s] = 0.0
    msamp = np.concatenate([ms, ms], axis=1)
    sel = np.zeros((128, 128), dtype=np.float32)
    for s_ in range(16):
        sel[s_, s_ * 8:(s_ + 1) * 8] = 1.0
    return ident, np.ascontiguousarray(mask), np.ascontiguousarray(msamp), sel


def kernel(x_prompt, x_sample, cache_k_win, cache_v_win, state_conv, state_lru,
           w_in, w_out, sinks, conv_w, conv_b, lru_w_a, lru_b_a, lru_w_x, lru_b_x, lru_lambda,
           w_up, w_down, g_pre_mix, g_post_mix, g_pre_ffn, g_post_ffn):
    f = lambda a: np.ascontiguousarray(np.asarray(a, dtype=np.float32))
    if "nc" not in _CACHE:
        _CACHE["nc"] = build_program()
    nc = _CACHE["nc"]
    ident, mask, msamp, sel = _consts()
    sk = f(sinks)[0].reshape(2, 2, 4).transpose(0, 2, 1).reshape(16)
    shared = {
        "w_in": f(w_in)[0], "w_out": f(w_out)[0], "sinks": np.ascontiguousarray(sk),
        "conv_w": f(conv_w)[0], "conv_b": f(conv_b)[0],
        "lru_w_a": f(lru_w_a)[0], "lru_b_a": f(lru_b_a)[0], "lru_w_x": f(lru_w_x)[0], "lru_b_x": f(lru_b_x)[0],
        "lru_lambda": f(lru_lambda)[0], "w_up": f(w_up)[0], "w_down": f(w_down)[0],
        "g_pre_mix": f(g_pre_mix)[0], "g_post_mix": f(g_post_mix)[0],
        "g_pre_ffn": f(g_pre_ffn)[0], "g_post_ffn": f(g_post_ffn)[0],
        "c_ident": ident, "c_mask": mask, "c_msamp": msamp, "c_sel": sel,
    }
    xp = f(x_prompt)
    xs = f(x_sample)[:, 0, :]
    ckk = f(cache_k_win)[0].reshape(128, 128, 256)
    cvv = f(cache_v_win)[0].reshape(128, 128, 256)
    sc = f(state_conv)[0]
    sl = f(state_lru)[0]
    in_maps = []
    for i in range(NCORES):
        m = dict(shared)
        m["x_prompt"] = np.ascontiguousarray(xp[2 * i:2 * i + 2])
        m["x_sample"] = np.ascontiguousarray(xs[16 * i:16 * i + 16])
        m["cache_k"] = np.ascontiguousarray(ckk[16 * i:16 * i + 16])
        m["cache_v"] = np.ascontiguousarray(cvv[16 * i:16 * i + 16])
        m["state_conv"] = np.ascontiguousarray(sc[16 * i:16 * i + 16])
        m["state_lru"] = np.ascontiguousarray(sl[16 * i:16 * i + 16])
        in_maps.append(m)
    res = run_bass_kernel_spmd(nc, in_maps, core_ids=list(range(NCORES)))
    R = res.results
    cat = lambda k: np.concatenate([np.asarray(r[k], dtype=np.float32) for r in R], axis=0)
    y_prompt = cat("y_prompt")
    y_sample = cat("y_sample").reshape(128, 1, D)
    kwp = cat("k_win_prompt").reshape(1, 16, 128, 4, 64)
    vwp = cat("v_win_prompt").reshape(1, 16, 128, 4, 64)
    cvp = cat("conv_prompt").reshape(1, 16, 3, D)
    lrp = cat("lru_prompt").reshape(1, 16, D)
    kws = cat("k_win_sample").reshape(1, 128, 128, 4, 64)
    vws = cat("v_win_sample").reshape(1, 128, 128, 4, 64)
    cvs = cat("conv_sample").reshape(1, 128, 3, D)
    lrs = cat("lru_sample").reshape(1, 128, D)
    return (y_prompt, y_sample, kwp, vwp, cvp, lrp, kws, vws, cvs, lrs)
```

```python
from contextlib import ExitStack
import numpy as np
import concourse.bass as bass
import concourse.mybir as mybir
from concourse.bass_utils import run_bass_kernel_spmd

F32 = mybir.dt.float32
BF16 = mybir.dt.bfloat16
AF = mybir.ActivationFunctionType
ALU = mybir.AluOpType
AX = mybir.AxisListType

NCORES = 8
D = 1024
SEQ = 2048
NSEQ = 2
NSAMP = 16
TT = 512
NPIECE = 29
NR = 5
EPS = 1e-6
NEG = -30000.0
GC1 = 0.7978845608028654
GC2 = 0.044715
STOP = 99


class Buf:
    __slots__ = ("name", "w", "r", "dsem", "dcnt", "excl")

    def __init__(self, name, excl=False):
        self.name = name
        self.excl = excl
        self.w = None
        self.r = []
        self.dsem = None
        self.dcnt = 0


class SemC:
    __slots__ = ("s", "cnt", "is_dma")

    def __init__(self, s, is_dma):
        self.s = s
        self.cnt = 0
        self.is_dma = is_dma


class Ctx:
    recording = True
    ops = []
    cap = []
    pend = None


class Op:
    __slots__ = ("eng", "calls", "reads", "writes", "dur", "ts", "kind", "dma", "idx", "lat")


def _free_size(ap):
    try:
        return int(ap.free_size())
    except Exception:
        return 512


def _estimate(engname, calls):
    d = 0.0
    ts = None
    for (m, a, k) in calls:
        name = getattr(m, "__name__", "")
        if engname == "PE":
            ap = k.get("rhs", k.get("identity"))
            n = _free_size(ap) if ap is not None else 128
            d += (max(n, 48) + 16) / 1900.0
        elif engname == "ACT":
            ap = k.get("in_")
            n = _free_size(ap) if ap is not None else 512
            d += 0.2 + n / 1100.0
            f = k.get("func")
            if f == AF.Exp:
                ts = "E"
            elif f == AF.Tanh:
                ts = "T"
            elif f == AF.Sqrt:
                ts = "S"
            elif f == AF.Ln:
                ts = "L"
            elif f == AF.Gelu_apprx_tanh:
                ts = "G"
        elif engname == "DVE":
            ap = k.get("in0", k.get("in_", k.get("data0", k.get("out", k.get("ap")))))
            if ap is None and a:
                ap = a[0]
            n = _free_size(ap) if ap is not None else 512
            mult = 2.0 if "scan" in name else 1.0
            d += 0.1 + mult * n / 760.0
        else:
            ap = k.get("in0", k.get("in_", k.get("out", k.get("ap"))))
            if ap is None and a:
                ap = a[0]
            n = _free_size(ap) if ap is not None else 512
            d += 0.25 + n / 450.0
    return d, ts


class _Dummy:
    def then_inc(self, *a, **k):
        return self


class Rec:
    def __init__(self, real):
        object.__setattr__(self, "_real", real)

    def __getattr__(self, name):
        real_m = getattr(object.__getattribute__(self, "_real"), name)

        def f(*a, **k):
            Ctx.cap.append((real_m, a, k))
            return _Dummy()
        f.__name__ = name
        return f


class NcProxy:
    def __init__(self, real):
        self._real = real
        self.tensor = Rec(real.tensor)
        self.vector = Rec(real.vector)
        self.scalar = Rec(real.scalar)
        self.gpsimd = Rec(real.gpsimd)

    def __getattr__(self, name):
        return getattr(self._real, name)


class Eng:
    def __init__(self, h, semc, name):
        self.h = h
        self.sc = semc
        self.name = name
        self.waited = {}

    def wait(self, tok):
        sc, val = tok
        if sc.is_dma:
            val = sc.cnt
        if self.waited.get(id(sc), 0) >= val:
            return
        self.h.wait_ge(sc.s, val)
        self.waited[id(sc)] = val

    def _deps(self, reads, writes):
        for b in reads:
            if b.w is not None:
                self.wait(b.w)
        for b in writes:
            if b.w is not None:
                self.wait(b.w)
            for t in b.r:
                self.wait(t)

    def op(self, fn, reads=(), writes=(), sig=True):
        del Ctx.cap[:]
        fn()
        calls = list(Ctx.cap)
        reads = list(reads)
        writes = list(writes)
        if Ctx.pend is not None:
            pc, pr, pw = Ctx.pend
            calls = pc + calls
            reads = pr + [b for b in reads if b not in pr]
            writes = pw + [b for b in writes if b not in pw]
            Ctx.pend = None
        if not sig:
            Ctx.pend = (calls, reads, writes)
            return
        ex = [b for b in reads if b.excl]
        if ex:
            writes = writes + [b for b in ex if b not in writes]
            reads = [b for b in reads if not b.excl]
        o = Op()
        o.eng, o.calls, o.reads, o.writes, o.kind, o.dma = self, calls, reads, writes, "c", None
        o.dur, o.ts = _estimate(self.name, calls)
        o.lat = 0.0
        o.idx = len(Ctx.ops)
        Ctx.ops.append(o)

    def emit(self, o):
        self._deps(o.reads, o.writes)
        inst = None
        for (m, a, k) in o.calls:
            inst = m(*a, **k)
        inst.then_inc(self.sc.s, 1)
        self.sc.cnt += 1
        tok = (self.sc, self.sc.cnt)
        for b in o.reads:
            b.r.append(tok)
        for b in o.writes:
            b.w = tok
            b.r = []


def schedule(ops):
    n = len(ops)
    if not hasattr(schedule, 'n_setup'):
        schedule.n_setup = 0
    lastw, readers = {}, {}
    preds = [set() for _ in range(n)]
    for i, o in enumerate(ops):
        for b in o.reads:
            w = lastw.get(id(b))
            if w is not None:
                preds[i].add(w)
        for b in o.writes:
            w = lastw.get(id(b))
            if w is not None:
                preds[i].add(w)
            for r in readers.get(id(b), ()):
                preds[i].add(r)
        for b in o.reads:
            readers.setdefault(id(b), []).append(i)
        for b in o.writes:
            lastw[id(b)] = i
            readers[id(b)] = []
        preds[i].discard(i)
    succs = [[] for _ in range(n)]
    npred = [0] * n
    for i in range(n):
        npred[i] = len(preds[i])
        for p in preds[i]:
            succs[p].append(i)
    ready_time = [0.0] * n
    finish = [0.0] * n
    start = [0.0] * n
    engs = {}
    for o in ops:
        engs.setdefault(id(o.eng), o.eng)
    eng_free = {k: 0.0 for k in engs}
    ready = {k: [] for k in engs}
    cur_ts = [None]
    for i in range(n):
        if npred[i] == 0:
            ready[id(ops[i].eng)].append(i)
    done = 0
    order = []
    while done < n:
        best = None
        for k, lst in ready.items():
            if not lst:
                continue
            ef = eng_free[k]
            isact = engs[k].name == "ACT"
            for i in lst[:32]:
                st = max(ready_time[i], ef)
                pen = 0.0
                if isact and ops[i].ts is not None:
                    t_ = ops[i].ts
                    if t_ == "T":
                        if cur_ts[0] not in ("E", "G"):
                            pen = 1.3
                    elif t_ != cur_ts[0]:
                        pen = 1.3
                if i < schedule.n_setup:
                    st, pen = 0.0, 0.0
                key = (st + pen, i)
                if best is None or key < best[0]:
                    best = (key, i, k, st + pen)
        _, i, k, st = best
        o = ops[i]
        ready[k].remove(i)
        start[i] = st
        if o.kind == "dma":
            eng_free[k] = st + o.dur
            finish[i] = st + o.dur + o.lat
        else:
            finish[i] = st + o.dur
            eng_free[k] = finish[i]
            if o.eng.name == "ACT" and o.ts is not None:
                if o.ts == "T":
                    if cur_ts[0] not in ("E", "G"):
                        cur_ts[0] = "E"
                else:
                    cur_ts[0] = o.ts
        order.append(i)
        done += 1
        for sidx in succs[i]:
            hop = 0.0 if ops[sidx].eng is o.eng and o.kind != "dma" else 0.25
            rt = finish[i] + hop
            if rt > ready_time[sidx]:
                ready_time[sidx] = rt
            npred[sidx] -= 1
            if npred[sidx] == 0:
                lst = ready[id(ops[sidx].eng)]
                lo, hi = 0, len(lst)
                while lo < hi:
                    mid = (lo + hi) // 2
                    if lst[mid] < sidx:
                        lo = mid + 1
                    else:
                        hi = mid
                lst.insert(lo, sidx)
    schedule.start = start
    schedule.finish = finish
    return order, max(finish) if n else 0.0


def _flat(lst):
    out = []
    for x in lst:
        if isinstance(x, (list, tuple)):
            out.extend(_flat(x))
        elif x is not None:
            out.append(x)
    return out


def build_program(debug=False):
    nc = bass.Bass("TRN2", target_bir_lowering=False)

    def din(name, shape, dt=F32):
        return nc.dram_tensor(name, list(shape), dt, kind="ExternalInput").ap()

    def dout(name, shape, dt=F32):
        return nc.dram_tensor(name, list(shape), dt, kind="ExternalOutput").ap()

    x_p = din("x_prompt", [NSEQ, SEQ, D])
    x_s = din("x_sample", [NSAMP, D])
    ck = din("cache_k", [NSAMP, 128, 256])
    cv = din("cache_v", [NSAMP, 128, 256])
    s_conv = din("state_conv", [NSAMP, 3, D])
    s_lru = din("state_lru", [NSAMP, D])
    w_in = din("w_in", [D, 5632])
    w_out = din("w_out", [D, D])
    sinks = din("sinks", [16])
    conv_w = din("conv_w", [4, D])
    conv_b = din("conv_b", [D])
    lw_a = din("lru_w_a", [16, 64, 64])
    lb_a = din("lru_b_a", [D])
    lw_x = din("lru_w_x", [16, 64, 64])
    lb_x = din("lru_b_x", [D])
    lam = din("lru_lambda", [D])
    w_up = din("w_up", [D, 4096])
    w_down = din("w_down", [4096, D])
    g1 = din("g_pre_mix", [D])
    g2 = din("g_post_mix", [D])
    g3 = din("g_pre_ffn", [D])
    g4 = din("g_post_ffn", [D])
    c_ident = din("c_ident", [128, 128])
    c_mask = din("c_mask", [2, 128, 512])
    c_msamp = din("c_msamp", [16, 512])
    c_sel = din("c_sel", [128, 128])

    y_p = dout("y_prompt", [NSEQ, SEQ, D])
    y_s = dout("y_sample", [NSAMP, D])
    kwp = dout("k_win_prompt", [NSEQ, 128, 256])
    vwp = dout("v_win_prompt", [NSEQ, 128, 256])
    cvp = dout("conv_prompt", [NSEQ, 3, D])
    lrp = dout("lru_prompt", [NSEQ, D])
    kws = dout("k_win_sample", [NSAMP, 128, 256])
    vws = dout("v_win_sample", [NSAMP, 128, 256])
    cvs = dout("conv_sample", [NSAMP, 3, D])
    lrs = dout("lru_sample", [NSAMP, D])

    scr = nc.dram_tensor("wscr", [NPIECE, 128, 8, 512], BF16, kind="Internal").ap()
    if debug:
        dbg_m = dout("dbg_merged", [128, 8, 512], BF16)
        dbg_a = dout("dbg_attn", [128, 8, 512], BF16)
        dbg_x1 = dout("dbg_x1", [128, 4, D])

    with ExitStack() as es:
        def sb(name, shape, dt=F32):
            return es.enter_context(nc.sbuf_tensor(name, list(shape), dt))

        def ps(name, shape, dt=F32):
            return es.enter_context(nc.psum_tensor(name, list(shape), dt))

        nsem = [0]

        def newsem(is_dma):
            nsem[0] += 1
            return SemC(es.enter_context(nc.semaphore("s%d" % nsem[0])), is_dma)

        ring = [sb("ring%d" % i, [128, 8, 512], BF16) for i in range(NR)]
        ringB = [Buf("ring%d" % i) for i in range(NR)]
        xa = [sb("xa%d" % i, [128, D]) for i in range(2)]
        xaB = [Buf("xa") for _ in range(2)]
        xnb = [sb("xnb%d" % i, [128, D], BF16) for i in range(2)]
        xnbB = [Buf("xnb") for _ in range(2)]
        xnT = sb("xnT", [128, 8, TT], BF16)
        xnTB = [Buf("xnT%d" % b) for b in range(4)]
        xt = sb("xt", [128, 4, D])
        xtB = [Buf("xt%d" % b) for b in range(4)]
        kbuf = sb("kbuf", [128, 2, 640], BF16)
        kbufB = Buf("kbuf")
        vbuf = sb("vbuf", [128, 5, 256], BF16)
        vbufB = Buf("vbuf")
        USL = 40
        U = sb("U", [128, USL * 256])
        UB = [Buf("U%d" % i) for i in range(USL)]

        def uview_bf(slot0, nslots):
            return U[:, slot0 * 256:(slot0 + nslots) * 256].bitcast(BF16).rearrange(
                "p (c t) -> p c t", t=512)

        qT = uview_bf(0, 8)
        sa = uview_bf(8, 8)
        merged = uview_bf(16, 8)
        gs = uview_bf(24, 4)
        ubuf = U[:, 28 * 256:28 * 256 + 4 * 516].rearrange("p (c t) -> p c t", t=516)
        hT = uview_bf(0, 32)
        fbuf = U[:, 32 * 256:40 * 256].rearrange("p (b t) -> p b t", t=512)

        def qTB(j): return [UB[j]]
        def saB(c): return [UB[8 + c]]
        def mgB(c): return [UB[16 + c]]
        def gsB(c): return [UB[24 + c % 4]]
        def ubB(c):
            lo = 28 * 1024 + (c % 4) * 2064
            return [UB[i] for i in range(lo // 1024, (lo + 2063) // 1024 + 1)]
        def hTB(fc): return [UB[fc]]
        def fbB(b): return [UB[32 + 2 * b], UB[33 + 2 * b]]

        class Pool_:
            def __init__(self, name, n, dt):
                self.free = [(sb("%s%d" % (name, i), [128, 512], dt), Buf(name)) for i in range(n)]

            def get(self):
                assert self.free, "temp pool exhausted"
                return self.free.pop(0)

            def put(self, t):
                self.free.append(t)

        TF = Pool_("tf", 14, F32)
        TB = Pool_("tb", 12, BF16)
        cnt = {"st": 0, "xa": 0, "pf": 0, "pt": 0, "au": 0}

        NST = 12
        stt = sb("stt", [128, NST, 8])
        sttB = [Buf("st") for _ in range(NST)]

        def gst():
            i = cnt["st"] % NST
            cnt["st"] += 1
            return stt[:, i, :], sttB[i]

        stg = sb("stg", [128, D])
        stgB = Buf("stg")
        kvst = sb("kvst", [128, 512])
        kvstB = Buf("kvst")
        hst = sb("hst", [128, 8])
        hstB = Buf("hst")
        uh = sb("uh", [128, 8, 3])
        uhB = Buf("uh")
        kcf = [sb("kcf%d" % i, [128, 256]) for i in range(2)]
        kcfB = [Buf("kcf") for _ in range(2)]
        kc = [sb("kc%d" % i, [128, 2, 128], BF16) for i in range(2)]
        kcB = [Buf("kc") for _ in range(2)]
        vc = [sb("vc%d" % i, [128, 256], BF16) for i in range(2)]
        vcB = [Buf("vc") for _ in range(2)]
        scT = sb("scT", [128, 8, 48])
        h0T = sb("h0T", [128, 8, 16])
        sampB = Buf("samp")
        hS_t = sb("hS_t", [128, 128])
        hS_B = Buf("hS")
        uS_t = sb("uS_t", [128, 128])
        ssb_t = sb("ssb_t", [128, 8])
        ssb_B = Buf("ssb")
        gbc = sb("gbc", [128, 4, D])
        identf = sb("identf", [128, 128])
        identb = sb("identb", [128, 128], BF16)
        maskb = sb("maskb", [128, 2, 512], BF16)
        msamp = sb("msamp", [128, 256], BF16)
        selb = sb("selb", [128, 128], BF16)
        sink8 = sb("sink8", [8, 2])
        wbd = sb("wbd", [128, 2, 8, 128], BF16)
        cw = sb("cw", [128, 4, 8])
        pv = sb("pv", [128, 8, 8])
        sinkp = sb("sinkp", [128, 16])
        epsT = sb("epsT", [128, 2])
        cI, cM, cG, cS, cV, cW, cE = (Buf("cI"), Buf("cM"), Buf("cG"), Buf("cS"), Buf("cV"), Buf("cW"), Buf("cE"))
        constB = [cI, cM, cG, cS, cV, cW, cE]

        NPF = 6
        pf = [ps("pf%d" % i, [128, 512]) for i in range(NPF)]
        pfB = [Buf("pf%d" % i, excl=True) for i in range(NPF)]
        ptt = [ps("pt%d" % i, [128, 1024], BF16) for i in range(2)]
        ptB = [Buf("pt%d" % i, excl=True) for i in range(2)]

        es.enter_context(nc.Block())

        PE = Eng(nc.tensor, newsem(False), "PE")
        ACT = Eng(nc.scalar, newsem(False), "ACT")
        DVE = Eng(nc.vector, newsem(False), "DVE")
        POOL = Eng(nc.gpsimd, newsem(False), "POOL")
        SP = Eng(nc.sync, newsem(False), "SP")
        nc_real = nc
        nc = NcProxy(nc_real)
        Ctx.recording = True
        Ctx.ops = []
        Ctx.cap = []
        Ctx.pend = None

        all_dma_sems = []

        def dma(q, out, in_, reads=(), writes=(), nonc=False):
            reads = _flat(reads)
            writes = _flat(writes)
            o = Op()
            o.eng, o.calls, o.reads, o.writes, o.kind = q, None, reads, writes, "dma"
            o.dma = (out, in_, nonc)
            o.ts = None
            try:
                nbytes = int(out.nbytes())
            except Exception:
                nbytes = 4096
            cast = (out.dtype != in_.dtype)
            o.dur = (1.0 if q is POOL else 0.15) + (nbytes / 80e3 if cast else 0.0)
            o.lat = 2.0 + nbytes / 150e3
            o.idx = len(Ctx.ops)
            Ctx.ops.append(o)

        def dma_emit(o):
            q = o.eng
            out, in_, nonc = o.dma
            reads, writes = o.reads, o.writes
            q._deps(reads, writes)
            tgt = (writes + reads)[0]
            if tgt.dsem is None:
                tgt.dsem = {}
            if id(q) not in tgt.dsem:
                tgt.dsem[id(q)] = newsem(True)
                all_dma_sems.append(tgt.dsem[id(q)])
            sc = tgt.dsem[id(q)]
            if nonc:
                with nc_real.allow_non_contiguous_dma(reason="small strided"):
                    q.h.dma_start(out=out, in_=in_).then_inc(sc.s, 16)
            else:
                q.h.dma_start(out=out, in_=in_).then_inc(sc.s, 16)
            sc.cnt += 16
            tok = (sc, sc.cnt)
            for b in reads:
                b.r.append(tok)
            for b in writes:
                b.w = tok
                b.r = []

        def A(fn, r=(), w=()):
            return ACT.op(fn, _flat(r), _flat(w))

        def V(fn, r=(), w=()):
            return DVE.op(fn, _flat(r), _flat(w))

        def G(fn, r=(), w=()):
            return POOL.op(fn, _flat(r), _flat(w))

        def P(fn, r=(), w=(), sig=True):
            return PE.op(fn, _flat(r), _flat(w), sig=sig)

        act = nc.scalar.activation
        mm = nc.tensor.matmul

        dma(SP, identf[:, :], c_ident[:, :], writes=[cI])
        dma(POOL, identb[:, :], c_ident[:, :], writes=[cI])
        dma(POOL, maskb[:, :, :], c_mask.rearrange("v p c -> p v c"), writes=[cM])
        G(lambda: nc.gpsimd.memset(msamp[:, :], 0.0), w=[cM])
        dma(POOL, msamp[0:16, :], c_msamp[:, 0:256], writes=[cM])
        dma(POOL, selb[:, :], c_sel[:, :], writes=[cM])
        sk8 = sinks.rearrange("(a g hb) -> hb g a", a=2, g=4, hb=2)
        for hb in range(2):
            dma(SP, sink8[hb * 4:(hb + 1) * 4, :], sk8[hb], writes=[cS], nonc=True)
        for i, g in enumerate((g1, g2, g3, g4)):
            dma(SP, gbc[:, i, :], g.partition_broadcast(128), writes=[cG])
        dma(SP, sinkp[:, :], sinks.partition_broadcast(128), writes=[cS])
        G(lambda: nc.gpsimd.memset(wbd[:, :, :, :], 0.0), w=[cW])
        G(lambda: nc.gpsimd.memset(epsT[:, 0:1], EPS), w=[cE])
        G(lambda: nc.gpsimd.memset(epsT[:, 1:2], 1.0), w=[cE])
        G(lambda: nc.gpsimd.memset(kbuf[:, :, :], 0.0), w=[kbufB])
        G(lambda: nc.gpsimd.memset(vbuf[:, :, :], 0.0), w=[vbufB])
        G(lambda: nc.gpsimd.memset(U[:, :], 0.0), w=UB)
        for gi, lw in enumerate((lw_a, lw_x)):
            lwv = lw.rearrange("(a hb g) ci d -> hb ci a g d", a=2, hb=2, g=4)
            for hb in range(2):
                for a in range(2):
                    dma(POOL, wbd[hb * 64:(hb + 1) * 64, gi, a * 4:(a + 1) * 4, hb * 64:(hb + 1) * 64],
                        lwv[hb][:, a], writes=[cW])
        schedule.n_setup = len(Ctx.ops)
        scrB = [Buf("scr%d" % i) for i in range(NPIECE)]
        win_v = w_in.rearrange("(k p) e -> p k e", p=128)

        def head_cols(base, h):
            return win_v[:, :, base + h * 64: base + (h + 1) * 64]

        specs = [[] for _ in range(NPIECE)]

        def spec(pi, c0, c1, src, p0=0, p1=128, k0=0, k1=8):
            specs[pi].append((p0, p1, k0, k1, c0, c1, src))

        for pi in range(2):
            for jj in range(4):
                j = pi * 4 + jj
                a, g = j // 4, j % 4
                spec(pi, jj * 128, jj * 128 + 64, head_cols(0, 8 * a + g))
                spec(pi, jj * 128 + 64, jj * 128 + 128, head_cols(0, 8 * a + 4 + g))
        spec(2, 0, 512, win_v[:, :, 1024:1536])
        for cp in range(4):
            for which, bases in ((0, (2560, 4608)), (1, (1536, 3584))):
                pi = 3 + 2 * cp + which
                for bi, base in enumerate(bases):
                    for ci in range(2):
                        c = 2 * cp + ci
                        a, g = c // 4, c % 4
                        col = (bi * 2 + ci) * 128
                        spec(pi, col, col + 64, head_cols(base, 8 * a + g))
                        spec(pi, col + 64, col + 128, head_cols(base, 8 * a + 4 + g))
        wo_v = w_out.rearrange("(a hb g p) e -> hb p a g e", a=2, hb=2, g=4)
        for hf in range(2):
            for hb in range(2):
                for a in range(2):
                    spec(11 + hf, 0, 512, wo_v[hb][:, a, :, hf * 512:(hf + 1) * 512],
                         p0=hb * 64, p1=(hb + 1) * 64, k0=a * 4, k1=(a + 1) * 4)
        wu_v = w_up.rearrange("(k p) e -> p k e", p=128)
        for q in range(8):
            spec(13 + q, 0, 512, wu_v[:, :, q * 512:(q + 1) * 512])
        wd_v = w_down.rearrange("(q k p) e -> q p k e", k=8, p=128)
        for hf in range(2):
            for q in range(4):
                spec(21 + hf * 4 + q, 0, 512, wd_v[q][:, :, hf * 512:(hf + 1) * 512])

        ntiles = 1 + NSEQ * (SEQ // TT)
        total_pieces = ntiles * NPIECE
        wstate = {"loaded": 0}

        def load_upto(n):
            while wstate["loaded"] < min(n, total_pieces):
                g = wstate["loaded"]
                pi = g % NPIECE
                s_ = g % NR
                if g < NPIECE:
                    for (p0, p1, k0, k1, c0, c1, src) in specs[pi]:
                        dma(POOL, ring[s_][p0:p1, k0:k1, c0:c1], src, writes=[ringB[s_]])
                    if ntiles > 1:
                        dma(SP, scr[pi], ring[s_][:, :, :], reads=[ringB[s_]], writes=[scrB[pi]])
                else:
                    dma(SP, ring[s_][:, :, :], scr[pi], reads=[scrB[pi]], writes=[ringB[s_]])
                wstate["loaded"] += 1

        def piece(tile_idx, pi, held=0):
            g = tile_idx * NPIECE + pi
            load_upto(g + NR - held)
            s_ = g % NR
            return ring[s_], ringB[s_]

        def gpf():
            i = cnt["pf"] % 2
            cnt["pf"] += 1
            return pf[i], pfB[i]

        def gpt():
            i = cnt["pt"] % 2
            cnt["pt"] += 1
            return ptt[i], ptB[i]

        def rstd_from(ss_ap, ssB, n):
            st, stB = gst()
            A(lambda: act(out=st[:n, 0:1], in_=ss_ap, func=AF.Sqrt, bias=epsT[:n, 0:1], scale=1.0 / D),
              r=[ssB, constB], w=[stB])
            V(lambda: nc.vector.reciprocal(out=st[:n, 1:2], in_=st[:n, 0:1]), r=[stB], w=[stB])
            return st[:n, 1:2], stB

        def norm_a(src_ap, srcB, gi, bt):
            i = cnt["xa"] % 2
            cnt["xa"] += 1
            xb, xbB = xnb[i], xnbB[i]
            st, stB = gst()
            A(lambda: act(out=xb[:bt, :], in_=src_ap, func=AF.Square, accum_out=st[:bt, 0:1]),
              r=[srcB], w=[xbB, stB])
            rs, rsB = rstd_from(st[:bt, 0:1], stB, bt)
            V(lambda: nc.vector.scalar_tensor_tensor(out=xb[:bt, :], in0=src_ap, scalar=rs, in1=gbc[:bt, gi, :],
                                                     op0=ALU.mult, op1=ALU.mult),
              r=[srcB, rsB, constB], w=[xbB])
            return i

        def norm_b(i, blk, bt):
            xb, xbB = xnb[i], xnbB[i]
            pt, ptb = gpt()
            ptv = pt[:, :].rearrange("p (k t) -> p k t", t=128)
            for k in range(8):
                P(lambda k=k: nc.tensor.transpose(out=ptv[:, k, :bt], in_=xb[:bt, k * 128:(k + 1) * 128],
                                                  identity=identb[:bt, :bt]),
                  r=[xbB, constB], w=[ptb], sig=(k == 7))
            A(lambda: act(out=xnT[:, :, blk * 128:blk * 128 + bt], in_=ptv[:, :, :bt], func=AF.Copy),
              r=[ptb], w=[xnTB[blk]])

        def proj(rg, rgB, cc, NT, nbk):
            pb, pbB = gpf()
            for k in range(8):
                P(lambda k=k: mm(pb[:, :NT], lhsT=rg[:, k, cc * 128:(cc + 1) * 128], rhs=xnT[:, k, :NT],
                                 start=(k == 0), stop=(k == 7)),
                  r=[rgB, xnTB[:nbk]], w=[pbB], sig=(k == 7))
            return pb, pbB

        def dma_perm_out(q, dst_dram, src_sb, n, reads):
            dv = dst_dram.rearrange("n (a hb g d) -> n a hb g d", a=2, hb=2, g=4, d=64)
            sv_ = src_sb.rearrange("n (a g hb d) -> n a hb g d", a=2, hb=2, g=4, d=64)
            for a in range(2):
                for hb in range(2):
                    dma(q, dv[:, a, hb], sv_[:, a, hb], reads=reads)

        def dma_perm_in(q, dst_sb, src_dram, n, writes):
            sv_ = src_dram.rearrange("n (a hb g d) -> n a hb g d", a=2, hb=2, g=4, d=64)
            dv = dst_sb.rearrange("n (a g hb d) -> n a hb g d", a=2, hb=2, g=4, d=64)
            for a in range(2):
                for hb in range(2):
                    dma(q, dv[:, a, hb], sv_[:, a, hb], writes=writes)

        def vec_setup():
            dma_perm_in(SP, stg[0:4, :], conv_w[:, :], 4, [stgB])
            for i, v in enumerate((conv_b, lb_a, lb_x, lam)):
                dma_perm_in(SP, stg[4 + i:5 + i, :], v.rearrange("(o d) -> o d", o=1), 1, [stgB])
            pb, pbB = pf[0], pfB[0]
            for c in range(8):
                P(lambda c=c: nc.tensor.transpose(out=pb[:, c * 8:(c + 1) * 8], in_=stg[0:8, c * 128:(c + 1) * 128],
                                                  identity=identf[0:8, 0:8]),
                  r=[stgB, cI], w=[pbB], sig=(c == 7))
            pb3 = pb[:, 0:64].rearrange("p (c v) -> p v c", v=8)
            V(lambda: nc.vector.tensor_copy(out=cw[:, :, :], in_=pb3[:, 0:4, :]), r=[pbB], w=[cV])
            V(lambda: nc.vector.tensor_copy(out=pv[:, 0:4, :], in_=pb3[:, 4:8, :]), r=[pbB], w=[cV])
            A(lambda: act(out=pv[:, 6, :], in_=pv[:, 3, :], func=AF.Exp, scale=-1.0), r=[cV, cE], w=[cV])
            A(lambda: act(out=pv[:, 6, :], in_=pv[:, 6, :], func=AF.Ln, bias=epsT[:, 1:2], scale=1.0),
              r=[cV, cE], w=[cV])
            V(lambda: nc.vector.tensor_scalar(out=pv[:, 4, :], in0=pv[:, 6, :], scalar1=-4.0, scalar2=None,
                                              op0=ALU.mult), r=[cV, cE], w=[cV])
            V(lambda: nc.vector.tensor_scalar(out=pv[:, 5, :], in0=pv[:, 6, :], scalar1=-8.0, scalar2=None,
                                              op0=ALU.mult), r=[cV, cE], w=[cV])
            for i_ in (1, 2):
                V(lambda i_=i_: nc.vector.tensor_scalar(out=pv[:, i_, :], in0=pv[:, i_, :], scalar1=0.5, scalar2=None,
                                                        op0=ALU.mult), r=[cV, cE], w=[cV])

        def out_rows_from_chunks(srcs, n, srcBs, dst_dram):
            for half in range(2):
                pb, pbB = gpf()
                for cc in range(4):
                    c = half * 4 + cc
                    P(lambda c=c, cc=cc: nc.tensor.transpose(out=pb[:n, cc * 128:(cc + 1) * 128], in_=srcs[c],
                                                             identity=identf[:, :]),
                      r=[srcBs, constB], w=[pbB], sig=(cc == 3))
                V(lambda: nc.vector.tensor_copy(out=stg[:n, half * 512:(half + 1) * 512], in_=pb[:n, :]),
                  r=[pbB], w=[stgB])
            dma_perm_out(SP, dst_dram, stg[:n, :], n, [stgB])

        class AttnPipe:
            def __init__(self):
                self.q2 = []
                self.q3 = []

            def push(self, u):
                st1 = self.phase1(u)
                self.q2.append(st1)
                if len(self.q2) > 3:
                    self.q3.append(self.phase2(self.q2.pop(0)))
                if len(self.q3) > 1:
                    self.phase3(self.q3.pop(0))

            def drain(self):
                while self.q2:
                    self.q3.append(self.phase2(self.q2.pop(0)))
                    if len(self.q3) > 1:
                        self.phase3(self.q3.pop(0))
                while self.q3:
                    self.phase3(self.q3.pop(0))

            def phase1(self, u):
                nq, j, qcols, kparts, maskmm, nk = u["nq"], u["j"], u["qcols"], u["kparts"], u["mask"], u["nk"]
                par = cnt["au"] % 3
                cnt["au"] += 1
                bi_ = (2, 3, 5)[par]
                bank, bB = pf[bi_], pfB[bi_]
                sv = bank[:, :].rearrange("p (h k) -> p h k", h=2)
                nparts = len(kparts)
                for h in range(2):
                    P(maskmm(h, sv[:nq, h, :nk]), r=[constB], w=[bB], sig=False)
                    off = 0
                    for pi_, (kfn, n, kb) in enumerate(kparts):
                        last = (h == 1 and pi_ == nparts - 1)
                        P(lambda h=h, off=off, n=n, kfn=kfn, last=last: mm(
                            sv[:nq, h, off:off + n], lhsT=qT[h * 64:(h + 1) * 64, j, qcols[0]:qcols[1]],
                            rhs=kfn(h), start=False, stop=last),
                          r=[qTB(j), kb], w=[bB], sig=last)
                        off += n
                st, stB = gst()
                V(lambda: nc.vector.tensor_reduce(out=st[:nq, 2:4], in_=sv[:nq, :, :nk], axis=AX.X, op=ALU.max,
                                                  negate=True), r=[bB], w=[stB])
                V(lambda: nc.vector.tensor_tensor(out=st[:nq, 6:8], in0=st[:nq, 2:4], in1=sinkp[:nq, 2 * j:2 * j + 2],
                                                  op=ALU.add), r=[stB, constB], w=[stB])
                Et = TB.get()
                E, EB = Et
                Ev = E[:, :].rearrange("p (h k) -> p h k", h=2)
                for h in range(2):
                    A(lambda h=h: act(out=Ev[:nq, h, :nk], in_=sv[:nq, h, :nk], func=AF.Exp,
                                      bias=st[:nq, 2 + h:3 + h], scale=1.0, accum_out=st[:nq, 4 + h:5 + h]),
                      r=[bB, stB], w=[EB, stB])
                A(lambda: act(out=st[:nq, 6:8], in_=st[:nq, 6:8], func=AF.Exp), r=[stB], w=[stB])
                V(lambda: nc.vector.tensor_tensor(out=st[:nq, 6:8], in0=st[:nq, 6:8], in1=st[:nq, 4:6], op=ALU.add),
                  r=[stB], w=[stB])
                V(lambda: nc.vector.reciprocal(out=st[:nq, 6:8], in_=st[:nq, 6:8]), r=[stB], w=[stB])
                Pt = TB.get()
                Pn, PnB = Pt
                Pv = Pn[:, :].rearrange("p (h k) -> p h k", h=2)
                for h in range(2):
                    V(lambda h=h: nc.vector.tensor_scalar(out=Pv[:nq, h, :nk], in0=Ev[:nq, h, :nk],
                                                          scalar1=st[:nq, 6 + h:7 + h], scalar2=None, op0=ALU.mult),
                      r=[EB, stB], w=[PnB])
                TB.put(Et)
                return (u, Pt)

            def phase2(self, state):
                u, Pt = state
                nq, vparts = u["nq"], u["vparts"]
                Pn, PnB = Pt
                Pv = Pn[:, :].rearrange("p (h k) -> p h k", h=2)
                pt, ptb = gpt()
                ptv = pt[:, 0:512].rearrange("p (i t) -> p i t", t=128)
                nvp = len(vparts)
                for h in range(2):
                    for vi, (lv, K, (c0, c1), vb) in enumerate(vparts):
                        last = (h == 1 and vi == nvp - 1)
                        P(lambda h=h, vi=vi, K=K, c0=c0, c1=c1: nc.tensor.transpose(
                            out=ptv[:K, h * 2 + vi, :nq], in_=Pv[:nq, h, c0:c1], identity=identb[:nq, :nq]),
                          r=[PnB, constB], w=[ptb], sig=last)
                PTt = TB.get()
                PT, PTB = PTt
                PTv = PT[:, :].rearrange("p (i t) -> p i t", t=128)
                if nq == 128:
                    A(lambda: act(out=PTv[:, :, :], in_=ptv[:, :, :], func=AF.Copy), r=[ptb], w=[PTB])
                else:
                    for vi, (lv, K, cc_, vb) in enumerate(vparts):
                        A(lambda vi=vi, K=K: act(out=PTv[:K, vi::2, :nq], in_=ptv[:K, vi::2, :nq], func=AF.Copy),
                          r=[ptb], w=[PTB])
                TB.put(Pt)
                return (u, PTt)

            def phase3(self, state):
                u, PTt = state
                nq, j, vparts, ocols = u["nq"], u["j"], u["vparts"], u["ocols"]
                PT, PTB = PTt
                PTv = PT[:, :].rearrange("p (i t) -> p i t", t=128)
                ob, obB = pf[4], pfB[4]
                nvp = len(vparts)
                for h in range(2):
                    for vi, (lv, K, cc_, vb) in enumerate(vparts):
                        last = (h == 1 and vi == nvp - 1)
                        P(lambda h=h, vi=vi, K=K, lv=lv: mm(ob[:, h * 128:h * 128 + nq], lhsT=lv,
                                                            rhs=PTv[:K, h * 2 + vi, :nq],
                                                            start=(vi == 0), stop=(vi == nvp - 1)),
                          r=[PTB, vb], w=[obB], sig=last)
                V(lambda: nc.vector.tensor_copy(out=qT[0:64, j, ocols[0]:ocols[1]], in_=ob[0:64, 0:nq]),
                  r=[obB], w=qTB(j))
                V(lambda: nc.vector.tensor_copy(out=qT[64:128, j, ocols[0]:ocols[1]], in_=ob[64:128, 128:128 + nq]),
                  r=[obB], w=qTB(j))
                TB.put(PTt)

        tiles = [("p", seq, t0) for seq in range(NSEQ) for t0 in range(0, SEQ, TT)] + [("s", 0, 0)]

        def tinfo(ti):
            kind, seq, t0 = tiles[ti]
            sample = (kind == "s")
            return dict(sample=sample, seq=seq, t0=t0, NT=(NSAMP if sample else TT), nbk=(1 if sample else 4),
                        bt=(NSAMP if sample else 128),
                        xsrc=(x_s if sample else x_p[seq, t0:t0 + TT, :]))

        stA = {}
        marks = []
        build_program.marks = marks

        def stage_a_load(ti, blk):
            t = tinfo(ti)
            bt = t["bt"]
            i = cnt["xa"] % 2
            dma(SP, xa[i][:bt, :], t["xsrc"][blk * 128:blk * 128 + bt, :], writes=[xaB[i]])
            stA[(ti, blk)] = norm_a(xa[i][:bt, :], xaB[i], 0, bt)

        def stage_a_tr(ti, blk):
            t = tinfo(ti)
            norm_b(stA.pop((ti, blk)), blk, t["bt"])

        def run_tile(ti, prestaged):
            t = tinfo(ti)
            marks.append(("tile%d" % ti, len(Ctx.ops)))
            sample, seq, t0, NT, nbk, bt, xsrc = (t["sample"], t["seq"], t["t0"], t["NT"], t["nbk"], t["bt"],
                                                  t["xsrc"])
            first = (not sample) and t0 == 0
            lastt = (not sample) and t0 + TT == SEQ

            if not prestaged:
                for blk in range(nbk):
                    stage_a_load(ti, blk)
                    stage_a_tr(ti, blk)
            if sample:
                dma(SP, xt[:bt, 0, :], xsrc, writes=[xtB[0]])
            else:
                dma(SP, xt[:, :, :], xsrc.rearrange("(b p) d -> p b d", p=128), writes=xtB)

            if sample:
                dma_perm_in(SP, stg[:48, :], s_conv.rearrange("s i d -> (s i) d"), 48, [stgB])
                for half in range(2):
                    pb, pbB = gpf()
                    pv3 = pb[:, 0:192].rearrange("p (c t) -> p c t", t=48)
                    for cc in range(4):
                        c = half * 4 + cc
                        P(lambda c=c, cc=cc: nc.tensor.transpose(out=pv3[:, cc, :], in_=stg[:48, c * 128:(c + 1) * 128],
                                                                 identity=identf[:48, :48]),
                          r=[stgB, constB], w=[pbB], sig=(cc == 3))
                    V(lambda: nc.vector.tensor_copy(out=scT[:, half * 4:half * 4 + 4, :], in_=pv3[:, :, :]),
                      r=[pbB], w=[sampB])
                dma_perm_in(SP, stg[:16, :], s_lru[:, :], 16, [stgB])
                for half in range(2):
                    pb, pbB = gpf()
                    pv3 = pb[:, 0:64].rearrange("p (c t) -> p c t", t=16)
                    for cc in range(4):
                        c = half * 4 + cc
                        P(lambda c=c, cc=cc: nc.tensor.transpose(out=pv3[:, cc, :], in_=stg[:16, c * 128:(c + 1) * 128],
                                                                 identity=identf[:16, :16]),
                          r=[stgB, constB], w=[pbB], sig=(cc == 3))
                    V(lambda: nc.vector.tensor_copy(out=h0T[:, half * 4:half * 4 + 4, :], in_=pv3[:, :, :]),
                      r=[pbB], w=[sampB])
            elif first:
                V(lambda: nc.vector.memset(hst[:, :], 0.0), w=[hstB])
                V(lambda: nc.vector.memset(uh[:, :, :], 0.0), w=[uhB])

            if sample:
                qbdt = TB.get()
                qbd, qbdB = qbdt
                qbd5 = qbd[:, 0:256].rearrange("p (s a h g) -> p s a h g", s=16, a=2, h=2, g=4)
                V(lambda: nc.vector.memset(qbd[:, 0:256], 0.0), w=[qbdB])
            for pi in range(2):
                rg, rgB = piece(ti, pi)
                for cc in range(4):
                    j = pi * 4 + cc
                    pb, pbB = proj(rg, rgB, cc, NT, nbk)
                    if sample:
                        a_, g_ = j // 4, j % 4
                        V(lambda: nc.vector.tensor_scalar(out=qbd5[0:64, :, a_, 0, g_], in0=pb[0:64, :NT], scalar1=0.125,
                                                          scalar2=None, op0=ALU.mult), r=[pbB], w=[qbdB])
                        V(lambda: nc.vector.tensor_scalar(out=qbd5[64:128, :, a_, 1, g_], in0=pb[64:128, :NT], scalar1=0.125,
                                                          scalar2=None, op0=ALU.mult), r=[pbB], w=[qbdB])
                        continue
                    if True:
                        A(lambda: act(out=qT[:, j, :NT], in_=pb[:, :NT], func=AF.Copy, scale=0.125), r=[pbB], w=qTB(j))
                    else:
                        V(lambda: nc.vector.tensor_scalar(out=qT[:, j, :NT], in0=pb[:, :NT], scalar1=0.125, scalar2=None,
                                                          op0=ALU.mult), r=[pbB], w=qTB(j))
            rg, rgB = piece(ti, 2)
            for a in range(2):
                pb, pbB = proj(rg, rgB, a, NT, nbk)
                A(lambda: act(out=kbuf[:, a, 128:128 + NT], in_=pb[:, :NT], func=AF.Copy), r=[pbB], w=[kbufB])
            for blk in range(nbk):
                pb, pbB = gpf()
                c0_ = 0 if (sample or (lastt and blk == nbk - 1)) else 256
                for k in range(8):
                    P(lambda k=k: mm(pb[:bt, c0_:512], lhsT=xnT[:, k, blk * 128:blk * 128 + bt], rhs=rg[:, k, c0_:512],
                                     start=(k == 0), stop=(k == 7)),
                      r=[rgB, xnTB[blk]], w=[pbB], sig=(k == 7))
                A(lambda: act(out=vbuf[:bt, 1 + blk, :], in_=pb[:bt, 256:512], func=AF.Copy), r=[pbB], w=[vbufB])
                if sample or (lastt and blk == nbk - 1):
                    V(lambda: nc.vector.tensor_copy(out=kvst[:bt, :], in_=pb[:bt, :]), r=[pbB], w=[kvstB])
                    if sample:
                        dma(SP, kws[:, 127, :], kvst[:bt, 0:256], reads=[kvstB])
                        dma(SP, vws[:, 127, :], kvst[:bt, 256:512], reads=[kvstB])
                    else:
                        dma(SP, kwp[seq], kvst[:, 0:256], reads=[kvstB])
                        dma(SP, vwp[seq], kvst[:, 256:512], reads=[kvstB])

            if sample:
                oall, oallB = pf[4], pfB[4]
                oall4 = oall[:, 0:256].rearrange("p (s a m) -> p s a m", s=16, a=2, m=8)
                for s in range(NSAMP):
                    i = s % 2
                    dma(SP, kcf[i][:, :], ck[s], writes=[kcfB[i]])
                    dma(POOL, vc[i][:, :], cv[s], writes=[vcB[i]])
                    pb, pbB = gpf()
                    for a in range(2):
                        P(lambda a=a: nc.tensor.transpose(out=pb[:, a * 128:(a + 1) * 128],
                                                          in_=kcf[i][:, a * 128:(a + 1) * 128], identity=identf[:, :]),
                          r=[kcfB[i], constB], w=[pbB], sig=(a == 1))
                    V(lambda: nc.vector.tensor_copy(out=kc[i][:, :, :], in_=pb[:, 0:256].rearrange("p (a t) -> p a t", a=2)),
                      r=[pbB], w=[kcB[i]])
                    bank, bB = pf[2 + i], pfB[2 + i]
                    sv = bank[:, :].rearrange("p (a k) -> p a k", a=2)
                    for a in range(2):
                        P(lambda a=a: mm(sv[0:8, a, 0:144], lhsT=selb[:, s * 8:(s + 1) * 8], rhs=msamp[:, 0:144],
                                         start=(a == 0), stop=False), r=[constB], w=[bB], sig=False)
                        P(lambda a=a: mm(sv[0:8, a, 0:128], lhsT=qbd5[:, s, a, :, :], rhs=kc[i][:, a, :],
                                         start=False, stop=False), r=[qbdB, kcB[i]], w=[bB], sig=False)
                        P(lambda a=a: mm(sv[0:8, a, 128:144], lhsT=qbd5[:, s, a, :, :], rhs=kbuf[:, a, 128:144],
                                         start=False, stop=(a == 1)), r=[qbdB, kbufB], w=[bB], sig=(a == 1))
                    st, stB = gst()
                    V(lambda: nc.vector.tensor_reduce(out=st[0:8, 2:4], in_=sv[0:8, :, 0:144], axis=AX.X, op=ALU.max,
                                                      negate=True), r=[bB], w=[stB])
                    V(lambda: nc.vector.tensor_tensor(out=st[0:8, 6:8], in0=st[0:8, 2:4], in1=sink8[0:8, 0:2], op=ALU.add),
                      r=[stB, constB], w=[stB])
                    Et = TF.get(); E, EB = Et
                    Ev = E[:, :].rearrange("p (a k) -> p a k", a=2)
                    for a in range(2):
                        A(lambda a=a: act(out=Ev[0:8, a, 0:144], in_=sv[0:8, a, 0:144], func=AF.Exp,
                                          bias=st[0:8, 2 + a:3 + a], scale=1.0, accum_out=st[0:8, 4 + a:5 + a]),
                          r=[bB, stB], w=[EB, stB])
                    A(lambda: act(out=st[0:8, 6:8], in_=st[0:8, 6:8], func=AF.Exp), r=[stB], w=[stB])
                    V(lambda: nc.vector.tensor_tensor(out=st[0:8, 6:8], in0=st[0:8, 6:8], in1=st[0:8, 4:6], op=ALU.add),
                      r=[stB], w=[stB])
                    V(lambda: nc.vector.reciprocal(out=st[0:8, 6:8], in_=st[0:8, 6:8]), r=[stB], w=[stB])
                    Pt = TB.get(); Pn, PnB = Pt
                    Pv = Pn[:, :].rearrange("p (a k) -> p a k", a=2)
                    for a in range(2):
                        V(lambda a=a: nc.vector.tensor_scalar(out=Pv[0:8, a, 0:144], in0=Ev[0:8, a, 0:144],
                                                              scalar1=st[0:8, 6 + a:7 + a], scalar2=None, op0=ALU.mult),
                          r=[EB, stB], w=[PnB])
                    TF.put(Et)
                    pt, ptb = gpt()
                    ptv = pt[:, 0:32].rearrange("p (i t) -> p i t", t=8)
                    for a in range(2):
                        P(lambda a=a: nc.tensor.transpose(out=ptv[:, 2 * a, :], in_=Pv[0:8, a, 0:128],
                                                          identity=identb[0:8, 0:8]),
                          r=[PnB, constB], w=[ptb], sig=False)
                        P(lambda a=a: nc.tensor.transpose(out=ptv[0:16, 2 * a + 1, :], in_=Pv[0:8, a, 128:144],
                                                          identity=identb[0:8, 0:8]),
                          r=[PnB, constB], w=[ptb], sig=(a == 1))
                    PTt = TB.get(); PT, PTB = PTt
                    PTv = PT[:, 0:32].rearrange("p (i t) -> p i t", t=8)
                    A(lambda: act(out=PTv[:, 0::2, :], in_=ptv[:, 0::2, :], func=AF.Copy), r=[ptb], w=[PTB])
                    A(lambda: act(out=PTv[0:16, 1::2, :], in_=ptv[0:16, 1::2, :], func=AF.Copy), r=[ptb], w=[PTB])
                    TB.put(Pt)
                    for a in range(2):
                        P(lambda a=a: mm(oall4[:, s, a, :], lhsT=vc[i][:, a * 128:(a + 1) * 128], rhs=PTv[:, 2 * a, :],
                                         start=True, stop=False), r=[PTB, vcB[i]], w=[oallB], sig=False)
                        P(lambda a=a: mm(oall4[:, s, a, :], lhsT=vbuf[0:16, 1, a * 128:(a + 1) * 128],
                                         rhs=PTv[0:16, 2 * a + 1, :], start=False, stop=True),
                          r=[PTB, vbufB], w=[oallB], sig=(a == 1))
                    TB.put(PTt)
                oall5 = oall[:, 0:256].rearrange("p (s a h g) -> p s a h g", s=16, a=2, h=2, g=4)
                for hb in range(2):
                    for a in range(2):
                        V(lambda hb=hb, a=a: nc.vector.tensor_copy(
                            out=qT[hb * 64:(hb + 1) * 64, a * 4:(a + 1) * 4, 0:16],
                            in_=oall5[hb * 64:(hb + 1) * 64, :, a, hb, :].rearrange("p s g -> p g s")),
                          r=[oallB], w=[qTB(a * 4 + g_) for g_ in range(4)])
                TB.put(qbdt)

            units = []
            if sample:
                pass
            else:
                for blk in range(nbk):
                    for j in range(8):
                        units.append(("p", blk, j))
            upos = [0]
            pipe = AttnPipe()
            sstate = {}

            def sample_prep(s):
                i = s % 2
                dma(SP, kcf[i][:, :], ck[s], writes=[kcfB[i]])
                dma(POOL, vc[i][:, :], cv[s], writes=[vcB[i]])
                pb, pbB = gpf()
                for a in range(2):
                    P(lambda a=a: nc.tensor.transpose(out=pb[:, a * 128:(a + 1) * 128],
                                                      in_=kcf[i][:, a * 128:(a + 1) * 128], identity=identf[:, :]),
                      r=[kcfB[i], constB], w=[pbB], sig=(a == 1))
                V(lambda: nc.vector.tensor_copy(out=kc[i][:, :, :], in_=pb[:, 0:256].rearrange("p (a t) -> p a t", a=2)),
                  r=[pbB], w=[kcB[i]])

            def push_units(n):
                for _ in range(n):
                    if upos[0] >= len(units):
                        return
                    kind_, x_, j = units[upos[0]]
                    upos[0] += 1
                    a = j // 4
                    if kind_ == "s":
                        s = x_
                        i = s % 2
                        if j == 0:
                            sample_prep(s)
                        u = dict(nq=1, j=j, qcols=(s, s + 1), nk=144, ocols=(s, s + 1),
                                 kparts=[(lambda h, a=a, i=i: kc[i][h * 64:(h + 1) * 64, a, :], 128, [kcB[i]]),
                                         (lambda h, a=a: kbuf[h * 64:(h + 1) * 64, a, 128:144], 16, [kbufB])],
                                 vparts=[(vc[i][:, a * 128:(a + 1) * 128], 128, (0, 128), [vcB[i]]),
                                         (vbuf[:16, 1, a * 128:(a + 1) * 128], 16, (128, 144), [vbufB])],
                                 mask=(lambda h, out_ap, s=s: (lambda: mm(out_ap, lhsT=identb[:, s:s + 1],
                                                                          rhs=msamp[:, 0:144], start=(h == 0),
                                                                          stop=False))))
                    else:
                        blk = x_
                        mv = 1 if (first and blk == 0) else 0
                        u = dict(nq=128, j=j, qcols=(blk * 128, blk * 128 + 128), nk=256,
                                 ocols=(blk * 128, blk * 128 + 128),
                                 kparts=[(lambda h, a=a, blk=blk: kbuf[h * 64:(h + 1) * 64, a, blk * 128:blk * 128 + 256],
                                          256, [kbufB])],
                                 vparts=[(vbuf[:, blk, a * 128:(a + 1) * 128], 128, (0, 128), [vbufB]),
                                         (vbuf[:, blk + 1, a * 128:(a + 1) * 128], 128, (128, 256), [vbufB])],
                                 mask=(lambda h, out_ap, mv=mv: (lambda: mm(out_ap, lhsT=identb[:, :],
                                                                            rhs=maskb[:, mv, 0:256], start=(h == 0),
                                                                            stop=False))))
                    pipe.push(u)

            upp = (len(units) + 31) // 32

            hS, hSB = hS_t, hS_B

            def R1(c):
                sl = c % 4
                uct = TF.get(); uc, ucB = uct
                if sample:
                    V(lambda: nc.vector.tensor_scalar(out=uc[:, :NT], in0=scT[:, c, 0::3], scalar1=cw[:, 0, c:c + 1],
                                                      scalar2=pv[:, 0, c:c + 1], op0=ALU.mult, op1=ALU.add),
                      r=[sampB, constB], w=[ucB])
                    for i_ in (1, 2):
                        V(lambda i_=i_: nc.vector.scalar_tensor_tensor(out=uc[:, :NT], in0=scT[:, c, i_::3],
                                                                       scalar=cw[:, i_, c:c + 1], in1=uc[:, :NT],
                                                                       op0=ALU.mult, op1=ALU.add),
                          r=[sampB, constB], w=[ucB])
                    V(lambda: nc.vector.scalar_tensor_tensor(out=uc[:, :NT], in0=ubuf[:, sl, 3:3 + NT],
                                                             scalar=cw[:, 3, c:c + 1], in1=uc[:, :NT],
                                                             op0=ALU.mult, op1=ALU.add),
                      r=[ubB(c), constB], w=[ucB])
                    V(lambda: nc.vector.tensor_copy(out=uS_t[:, c * 16:(c + 1) * 16], in_=ubuf[:, sl, 3:3 + NT]),
                      r=ubB(c), w=[sampB])
                else:
                    V(lambda: nc.vector.tensor_scalar(out=uc[:, :NT], in0=ubuf[:, sl, 0:NT], scalar1=cw[:, 0, c:c + 1],
                                                      scalar2=pv[:, 0, c:c + 1], op0=ALU.mult, op1=ALU.add),
                      r=[ubB(c), constB], w=[ucB])
                    for i_ in (1, 2, 3):
                        V(lambda i_=i_: nc.vector.scalar_tensor_tensor(out=uc[:, :NT], in0=ubuf[:, sl, i_:i_ + NT],
                                                                       scalar=cw[:, i_, c:c + 1], in1=uc[:, :NT],
                                                                       op0=ALU.mult, op1=ALU.add),
                          r=[ubB(c), constB], w=[ucB])
                    V(lambda: nc.vector.tensor_copy(out=uh[:, c, :], in_=ubuf[:, sl, NT:NT + 3]),
                      r=ubB(c), w=[uhB])
                ucbt = TB.get(); ucb, ucbB = ucbt
                A(lambda: act(out=ucb[:, :NT], in_=uc[:, :NT], func=AF.Copy), r=[ucB], w=[ucbB])
                return dict(c=c, uct=uct, ucbt=ucbt)

            def R2(stt_):
                c = stt_["c"]
                ucb, ucbB = stt_["ucbt"]
                trt = TF.get(); tr, trB = trt
                tit = TF.get(); ti_, tiB = tit
                P(lambda: mm(pf[5][:, :NT], lhsT=wbd[:, 0, c, :], rhs=ucb[:, :NT], start=True, stop=True),
                  r=[ucbB, constB], w=[pfB[5]])
                A(lambda: act(out=tr[:, :NT], in_=pf[5][:, :NT], func=AF.Tanh, bias=pv[:, 1, c:c + 1], scale=0.5),
                  r=[pfB[5], constB], w=[trB])
                P(lambda: mm(pf[5][:, :NT], lhsT=wbd[:, 1, c, :], rhs=ucb[:, :NT], start=True, stop=True),
                  r=[ucbB, constB], w=[pfB[5]])
                A(lambda: act(out=ti_[:, :NT], in_=pf[5][:, :NT], func=AF.Tanh, bias=pv[:, 2, c:c + 1], scale=0.5),
                  r=[pfB[5], constB], w=[tiB])
                TB.put(stt_["ucbt"])
                stt_["trt"] = trt
                stt_["tit"] = tit

            def R3a(stt_):
                c = stt_["c"]
                tr, trB = stt_["trt"]
                aat = TF.get(); aa, aaB = aat
                A(lambda: act(out=aa[:, :NT], in_=tr[:, :NT], func=AF.Exp, bias=pv[:, 4, c:c + 1],
                              scale=pv[:, 4, c:c + 1]), r=[trB, constB], w=[aaB])
                A(lambda: act(out=tr[:, :NT], in_=tr[:, :NT], func=AF.Exp, bias=pv[:, 5, c:c + 1],
                              scale=pv[:, 5, c:c + 1]), r=[constB], w=[trB])
                stt_["aat"] = aat

            def R3b(stt_):
                tr, trB = stt_["trt"]
                A(lambda: act(out=tr[:, :NT], in_=tr[:, :NT], func=AF.Sqrt, bias=epsT[:, 1:2], scale=-1.0),
                  r=[constB], w=[trB])
                if first:
                    V(lambda: nc.vector.memset(tr[:, 0:1], 1.0), w=[trB])

            def R3c(stt_):
                c = stt_["c"]
                sl = c % 4
                tr, trB = stt_["trt"]
                ti_, tiB = stt_["tit"]
                aa, aaB = stt_["aat"]
                uc, ucB = stt_["uct"]
                V(lambda: nc.vector.scalar_tensor_tensor(out=ti_[:, :NT], in0=ti_[:, :NT], scalar=1.0, in1=tr[:, :NT],
                                                         op0=ALU.add, op1=ALU.mult), r=[trB], w=[tiB])
                V(lambda: nc.vector.scalar_tensor_tensor(out=ti_[:, :NT], in0=ti_[:, :NT], scalar=0.5, in1=uc[:, :NT],
                                                         op0=ALU.mult, op1=ALU.mult), r=[ucB], w=[tiB])
                if sample:
                    hv = hS[:, c * 16:(c + 1) * 16]
                    V(lambda: nc.vector.tensor_tensor(out=aa[:, :NT], in0=aa[:, :NT], in1=h0T[:, c, :], op=ALU.mult),
                      r=[sampB], w=[aaB])
                    V(lambda: nc.vector.tensor_tensor(out=hv, in0=aa[:, :NT], in1=ti_[:, :NT], op=ALU.add),
                      r=[aaB, tiB], w=[hSB])
                    hB_ = hSB
                else:
                    hv = tr[:, :NT]
                    hB_ = trB
                    V(lambda: nc.vector.tensor_tensor_scan(out=hv, data0=aa[:, :NT], data1=ti_[:, :NT],
                                                           initial=hst[:, c:c + 1], op0=ALU.mult, op1=ALU.add),
                      r=[aaB, tiB, hstB], w=[trB])
                    V(lambda: nc.vector.tensor_copy(out=hst[:, c:c + 1], in_=tr[:, NT - 1:NT]), r=[trB], w=[hstB])
                V(lambda: nc.vector.scalar_tensor_tensor(out=merged[:, c, :NT], in0=hv, scalar=0.5, in1=gs[:, sl, :NT],
                                                         op0=ALU.mult, op1=ALU.mult),
                  r=[hB_, gsB(c)], w=mgB(c))
                TF.put(stt_["uct"]); TF.put(stt_["trt"]); TF.put(stt_["tit"]); TF.put(stt_["aat"])

            marks.append(("rnnpieces", len(Ctx.ops)))
            prev = []
            for cp in range(4):
                c0, c1 = 2 * cp, 2 * cp + 1
                rg, rgB = piece(ti, 3 + 2 * cp)
                gts = []
                for ci in range(2):
                    pb, pbB = proj(rg, rgB, ci, NT, nbk)
                    gtt = TF.get()
                    g32, g32B = gtt
                    A(lambda: act(out=g32[:, :NT], in_=pb[:, :NT], func=AF.Gelu_apprx_tanh), r=[pbB], w=[g32B])
                    gts.append(gtt)
                    push_units(upp)
                    if prev:
                        R2(prev[ci])
                for ci in range(2):
                    c = c0 + ci
                    pb, pbB = proj(rg, rgB, 2 + ci, NT, nbk)
                    trt = TF.get()
                    A(lambda trt=trt: act(out=trt[0][:, :NT], in_=pb[:, :NT], func=AF.Tanh, scale=0.5),
                      r=[pbB], w=[trt[1]])
                    V(lambda trt=trt, ci=ci, c=c: nc.vector.scalar_tensor_tensor(
                        out=gs[:, c % 4, :NT], in0=trt[0][:, :NT], scalar=1.0, in1=gts[ci][0][:, :NT],
                        op0=ALU.add, op1=ALU.mult), r=[trt[1], gts[ci][1]], w=gsB(c))
                    TF.put(trt)
                    TF.put(gts[ci])
                    push_units(upp)
                    if prev:
                        if ci == 0:
                            R3a(prev[0]); R3a(prev[1])
                        else:
                            R3b(prev[0]); R3b(prev[1])
                            R3c(prev[0]); R3c(prev[1])
                prev = []
                rg, rgB = piece(ti, 4 + 2 * cp)
                for ci in range(2):
                    c = c0 + ci
                    sl = c % 4
                    if not sample:
                        V(lambda c=c, sl=sl: nc.vector.tensor_copy(out=ubuf[:, sl, 0:3], in_=uh[:, c, :]),
                          r=[uhB], w=ubB(c))
                    pb, pbB = proj(rg, rgB, ci, NT, nbk)
                    V(lambda sl=sl, c=c: nc.vector.tensor_copy(out=ubuf[:, sl, 3:3 + NT], in_=pb[:, :NT]),
                      r=[pbB], w=ubB(c))
                    push_units(upp)
                    prev.append(R1(c))
                for ci in range(2):
                    c = c0 + ci
                    pb, pbB = proj(rg, rgB, 2 + ci, NT, nbk)
                    A(lambda c=c: act(out=sa[:, c, :NT], in_=pb[:, :NT], func=AF.Tanh, scale=0.5), r=[pbB], w=saB(c))
                    push_units(upp)
            push_units(len(units))
            R2(prev[0]); R2(prev[1])
            R3a(prev[0]); R3a(prev[1])
            R3b(prev[0]); R3b(prev[1])
            pipe.drain()
            R3c(prev[0]); R3c(prev[1])

            marks.append(("merge", len(Ctx.ops)))
            for c in range(8):
                tmt = TF.get(); tm, tmB = tmt
                V(lambda c=c: nc.vector.scalar_tensor_tensor(out=tm[:, :NT], in0=sa[:, c, :NT], scalar=1.0,
                                                             in1=qT[:, c, :NT], op0=ALU.add, op1=ALU.mult),
                  r=[saB(c), qTB(c)], w=[tmB])
                V(lambda c=c: nc.vector.scalar_tensor_tensor(out=merged[:, c, :NT], in0=tm[:, :NT], scalar=0.5,
                                                             in1=merged[:, c, :NT], op0=ALU.mult, op1=ALU.add),
                  r=[tmB], w=mgB(c))
                TF.put(tmt)
            if debug and not sample:
                dma(SP, dbg_m[:, :, :], merged[:, :, :], reads=[mgB(c) for c in range(8)])

            if sample:
                out_rows_from_chunks([hS[:, c * 16:(c + 1) * 16] for c in range(8)], 16, [hSB], lrs[:, :])
                out_rows_from_chunks([uS_t[:, c * 16:(c + 1) * 16] for c in range(8)], 16, [sampB], cvs[:, 2, :])
            elif lastt:
                pb, pbB = gpf()
                P(lambda: nc.tensor.transpose(out=pb[:8, 0:128], in_=hst[:, 0:8], identity=identf[:, :]),
                  r=[hstB, constB], w=[pbB])
                V(lambda: nc.vector.tensor_copy(out=stg[:8, 0:128], in_=pb[:8, 0:128]), r=[pbB], w=[stgB])
                lv_ = lrp[seq].rearrange("(a hb g d) -> a hb g d", a=2, hb=2, g=4, d=64)
                for a in range(2):
                    for hb in range(2):
                        dma(SP, lv_[a, hb], stg[a * 4:(a + 1) * 4, hb * 64:(hb + 1) * 64], reads=[stgB])
                out_rows_from_chunks([uh[:, c, :] for c in range(8)], 3, [uhB], cvp[seq])
            if not sample and not lastt:
                V(lambda: nc.vector.tensor_copy(out=kbuf[:, :, 0:128], in_=kbuf[:, :, 512:640]), r=[kbufB], w=[kbufB])
                V(lambda: nc.vector.tensor_copy(out=vbuf[:, 0, :], in_=vbuf[:, 4, :]), r=[vbufB], w=[vbufB])

            marks.append(("wout", len(Ctx.ops)))
            ro0, ro0B = piece(ti, 11)
            ro1, ro1B = piece(ti, 12, held=1)

            def wout_mm(blk):
                base = 2 * (blk % 2)
                for hf, (ro, roB) in enumerate(((ro0, ro0B), (ro1, ro1B))):
                    for k in range(8):
                        P(lambda k=k, hf=hf, ro=ro: mm(pf[base + hf][:bt, :], lhsT=merged[:, k, blk * 128:blk * 128 + bt],
                                                       rhs=ro[:, k, :], start=(k == 0), stop=(k == 7)),
                          r=[roB, mgB(k)], w=[pfB[base + hf]], sig=(k == 7))

            def wout_post(blk):
                base = 2 * (blk % 2)
                st, stB = gst()
                jkt = TB.get(); jk, jkB = jkt
                for hf in range(2):
                    A(lambda hf=hf: act(out=jk[:bt, :], in_=pf[base + hf][:bt, :], func=AF.Square,
                                        accum_out=st[:bt, 2 + hf:3 + hf]), r=[pfB[base + hf]], w=[jkB, stB])
                TB.put(jkt)
                V(lambda: nc.vector.tensor_tensor(out=st[:bt, 4:5], in0=st[:bt, 2:3], in1=st[:bt, 3:4], op=ALU.add),
                  r=[stB], w=[stB])
                rs, rsB = rstd_from(st[:bt, 4:5], stB, bt)
                for hf in range(2):
                    tmt = TF.get(); tm, tmB = tmt
                    V(lambda hf=hf: nc.vector.tensor_tensor(out=tm[:bt, :], in0=pf[base + hf][:bt, :],
                                                            in1=gbc[:bt, 1, hf * 512:(hf + 1) * 512], op=ALU.mult),
                      r=[pfB[base + hf], constB], w=[tmB])
                    V(lambda hf=hf: nc.vector.scalar_tensor_tensor(
                        out=xt[:bt, blk, hf * 512:(hf + 1) * 512], in0=tm[:bt, :], scalar=rs,
                        in1=xt[:bt, blk, hf * 512:(hf + 1) * 512], op0=ALU.mult, op1=ALU.add),
                      r=[tmB, rsB], w=[xtB[blk]])
                    TF.put(tmt)
                return norm_a(xt[:bt, blk, :], xtB[blk], 2, bt)

            wout_mm(0)
            pend = None
            for blk in range(nbk):
                if blk + 1 < nbk:
                    wout_mm(blk + 1)
                i_ = wout_post(blk)
                if pend is not None:
                    norm_b(pend[0], pend[1], bt)
                pend = (i_, blk)
            norm_b(pend[0], pend[1], bt)
            if debug and not sample:
                dma(SP, dbg_x1[:, :, :], xt[:, :, :], reads=xtB)

            marks.append(("ffnup", len(Ctx.ops)))
            for q in range(8):
                rg, rgB = piece(ti, 13 + q)
                for cc in range(4):
                    fc = q * 4 + cc
                    pb, pbB = proj(rg, rgB, cc, NT, nbk)
                    rlt = TB.get(); rl, rlB = rlt
                    A(lambda: act(out=rl[:, :NT], in_=pb[:, :NT], func=AF.Relu), r=[pbB], w=[rlB])
                    V(lambda: nc.vector.tensor_tensor(out=hT[:, fc, :NT], in0=rl[:, :NT], in1=rl[:, :NT], op=ALU.mult),
                      r=[rlB], w=hTB(fc))
                    TB.put(rlt)
            marks.append(("ffndown", len(Ctx.ops)))
            nxt = ti + 1 if ti + 1 < len(tiles) else None
            nsteps = []
            if nxt is not None:
                nb2 = tinfo(nxt)["nbk"]
                for b2 in range(nb2):
                    nsteps.append(("l", b2))
                    nsteps.append(("t", b2))
            step = [0]

            def next_stage_step(n=1):
                for _ in range(n):
                    if step[0] < len(nsteps):
                        k_, b2 = nsteps[step[0]]
                        step[0] += 1
                        if k_ == "l":
                            stage_a_load(nxt, b2)
                        else:
                            stage_a_tr(nxt, b2)

            ssb, ssbB = ssb_t[:, :], ssb_B
            for hf in range(2):
                for q in range(4):
                    rg, rgB = piece(ti, 21 + hf * 4 + q)
                    for kk in range(8):
                        fc = q * 8 + kk
                        for blk in range(nbk):
                            P(lambda kk=kk, fc=fc, blk=blk: mm(pf[2 + blk][:bt, :], lhsT=hT[:, fc, blk * 128:blk * 128 + bt],
                                                               rhs=rg[:, kk, :], start=(fc == 0), stop=(fc == 31)),
                              r=[rgB, hTB(fc)], w=[pfB[2 + blk]], sig=(kk == 7 and blk == nbk - 1))
                    next_stage_step(1)
                for blk in range(nbk):
                    jkt = TB.get(); jk, jkB = jkt
                    A(lambda blk=blk, hf=hf: act(out=jk[:bt, :], in_=pf[2 + blk][:bt, :], func=AF.Square,
                                                 accum_out=ssb[:bt, 2 * blk + hf:2 * blk + hf + 1]),
                      r=[pfB[2 + blk]], w=[jkB, ssbB])
                    TB.put(jkt)
                    if hf == 0:
                        V(lambda blk=blk: nc.vector.tensor_tensor(out=fbuf[:bt, blk, :], in0=pf[2 + blk][:bt, :],
                                                                  in1=gbc[:bt, 3, 0:512], op=ALU.mult),
                          r=[pfB[2 + blk], constB], w=fbB(blk))
            for blk in range(nbk):
                st, stB = gst()
                V(lambda blk=blk: nc.vector.tensor_tensor(out=st[:bt, 4:5], in0=ssb[:bt, 2 * blk:2 * blk + 1],
                                                          in1=ssb[:bt, 2 * blk + 1:2 * blk + 2], op=ALU.add),
                  r=[ssbB], w=[stB])
                rs, rsB = rstd_from(st[:bt, 4:5], stB, bt)
                V(lambda blk=blk: nc.vector.scalar_tensor_tensor(out=xt[:bt, blk, 0:512], in0=fbuf[:bt, blk, :], scalar=rs,
                                                                 in1=xt[:bt, blk, 0:512], op0=ALU.mult, op1=ALU.add),
                  r=[fbB(blk), rsB], w=[xtB[blk]])
                tmt = TF.get(); tm, tmB = tmt
                V(lambda blk=blk: nc.vector.tensor_tensor(out=tm[:bt, :], in0=pf[2 + blk][:bt, :],
                                                          in1=gbc[:bt, 3, 512:1024], op=ALU.mult),
                  r=[pfB[2 + blk], constB], w=[tmB])
                V(lambda blk=blk: nc.vector.scalar_tensor_tensor(out=xt[:bt, blk, 512:1024], in0=tm[:bt, :], scalar=rs,
                                                                 in1=xt[:bt, blk, 512:1024], op0=ALU.mult, op1=ALU.add),
                  r=[tmB, rsB], w=[xtB[blk]])
                TF.put(tmt)
            if sample:
                dma(SP, y_s[:, :], xt[:bt, 0, :], reads=[xtB[0]])
            else:
                dma(SP, y_p[seq, t0:t0 + TT, :].rearrange("(b p) d -> p b d", p=128), xt[:, :, :], reads=xtB)
            next_stage_step(len(nsteps))
            return nxt is not None

        outB = Buf("outcopy")
        dma(SP, kws[:, 0:127, :], ck[:, 1:128, :], writes=[outB])
        dma(SP, vws[:, 0:127, :], cv[:, 1:128, :], writes=[outB])
        dma(SP, cvs[:, 0:2, :], s_conv[:, 1:3, :], writes=[outB])

        vec_setup()
        pre = False
        for ti in range(len(tiles)):
            pre = run_tile(ti, pre)

        assert Ctx.pend is None
        order, est = schedule(Ctx.ops)
        build_program.est_us = est
        for i in order:
            o = Ctx.ops[i]
            if o.kind == "dma":
                dma_emit(o)
            else:
                o.eng.emit(o)
        nc = nc_real
        for sc in all_dma_sems:
            nc.gpsimd.wait_ge(sc.s, sc.cnt)
        for e in (PE, ACT, DVE, SP):
            nc.gpsimd.wait_ge(e.sc.s, e.sc.cnt)
    return nc


_CACHE = {}


def _consts():
    ident = np.eye(128, dtype=np.float32)
    qi = np.arange(128)[:, None]
    kj = np.arange(256)[None, :]
    diff = qi + 128 - kj
    band = (diff >= 0) & (diff <= 128)
    m0 = np.where(band, 0.0, NEG).astype(np.float32)
    m1 = m0.copy()
    m1[:, :128] = NEG
    mask = np.stack([np.concatenate([m0, m0], axis=1), np.concatenate([m1, m1], axis=1)])
    ms = np.full((16, 256), NEG, dtype=np.float32)
    ms[:, :128] = 0.0
    for s in range(16):
        ms[s, 128 + s] = 0.0
    msamp = np.concatenate([ms, ms], axis=1)
    sel = np.zeros((128, 128), dtype=np.float32)
    for s_ in range(16):
        sel[s_, s_ * 8:(s_ + 1) * 8] = 1.0
    return ident, np.ascontiguousarray(mask), np.ascontiguousarray(msamp), sel


def kernel(x_prompt, x_sample, cache_k_win, cache_v_win, state_conv, state_lru,
           w_in, w_out, sinks, conv_w, conv_b, lru_w_a, lru_b_a, lru_w_x, lru_b_x, lru_lambda,
           w_up, w_down, g_pre_mix, g_post_mix, g_pre_ffn, g_post_ffn):
    f = lambda a: np.ascontiguousarray(np.asarray(a, dtype=np.float32))
    if "nc" not in _CACHE:
        _CACHE["nc"] = build_program()
    nc = _CACHE["nc"]
    ident, mask, msamp, sel = _consts()
    sk = f(sinks)[0].reshape(2, 2, 4).transpose(0, 2, 1).reshape(16)
    shared = {
        "w_in": f(w_in)[0], "w_out": f(w_out)[0], "sinks": np.ascontiguousarray(sk),
        "conv_w": f(conv_w)[0], "conv_b": f(conv_b)[0],
        "lru_w_a": f(lru_w_a)[0], "lru_b_a": f(lru_b_a)[0], "lru_w_x": f(lru_w_x)[0], "lru_b_x": f(lru_b_x)[0],
        "lru_lambda": f(lru_lambda)[0], "w_up": f(w_up)[0], "w_down": f(w_down)[0],
        "g_pre_mix": f(g_pre_mix)[0], "g_post_mix": f(g_post_mix)[0],
        "g_pre_ffn": f(g_pre_ffn)[0], "g_post_ffn": f(g_post_ffn)[0],
        "c_ident": ident, "c_mask": mask, "c_msamp": msamp, "c_sel": sel,
    }
    xp = f(x_prompt)
    xs = f(x_sample)[:, 0, :]
    ckk = f(cache_k_win)[0].reshape(128, 128, 256)
    cvv = f(cache_v_win)[0].reshape(128, 128, 256)
    sc = f(state_conv)[0]
    sl = f(state_lru)[0]
    in_maps = []
    for i in range(NCORES):
        m = dict(shared)
        m["x_prompt"] = np.ascontiguousarray(xp[2 * i:2 * i + 2])
        m["x_sample"] = np.ascontiguousarray(xs[16 * i:16 * i + 16])
        m["cache_k"] = np.ascontiguousarray(ckk[16 * i:16 * i + 16])
        m["cache_v"] = np.ascontiguousarray(cvv[16 * i:16 * i + 16])
        m["state_conv"] = np.ascontiguousarray(sc[16 * i:16 * i + 16])
        m["state_lru"] = np.ascontiguousarray(sl[16 * i:16 * i + 16])
        in_maps.append(m)
    res = run_bass_kernel_spmd(nc, in_maps, core_ids=list(range(NCORES)))
    R = res.results
    cat = lambda k: np.concatenate([np.asarray(r[k], dtype=np.float32) for r in R], axis=0)
    y_prompt = cat("y_prompt")
    y_sample = cat("y_sample").reshape(128, 1, D)
    kwp = cat("k_win_prompt").reshape(1, 16, 128, 4, 64)
    vwp = cat("v_win_prompt").reshape(1, 16, 128, 4, 64)
    cvp = cat("conv_prompt").reshape(1, 16, 3, D)
    lrp = cat("lru_prompt").reshape(1, 16, D)
    kws = cat("k_win_sample").reshape(1, 128, 128, 4, 64)
    vws = cat("v_win_sample").reshape(1, 128, 128, 4, 64)
    cvs = cat("conv_sample").reshape(1, 128, 3, D)
    lrs = cat("lru_sample").reshape(1, 128, D)
    return (y_prompt, y_sample, kwp, vwp, cvp, lrp, kws, vws, cvs, lrs)
```

```python
from contextlib import ExitStack
import numpy as np
import concourse.bass as bass
import concourse.mybir as mybir
from concourse.bass_utils import run_bass_kernel_spmd

F32 = mybir.dt.float32
BF16 = mybir.dt.bfloat16
AF = mybir.ActivationFunctionType
ALU = mybir.AluOpType
AX = mybir.AxisListType

NCORES = 8
D = 1024
SEQ = 2048
NSEQ = 2
NSAMP = 16
TT = 512
NPIECE = 29
NR = 5
EPS = 1e-6
NEG = -30000.0
GC1 = 0.7978845608028654
GC2 = 0.044715
STOP = 99


class Buf:
    __slots__ = ("name", "w", "r", "dsem", "dcnt", "excl")

    def __init__(self, name, excl=False):
        self.name = name
        self.excl = excl
        self.w = None
        self.r = []
        self.dsem = None
        self.dcnt = 0


class SemC:
    __slots__ = ("s", "cnt", "is_dma")

    def __init__(self, s, is_dma):
        self.s = s
        self.cnt = 0
        self.is_dma = is_dma


class Ctx:
    recording = True
    ops = []
    cap = []
    pend = None


class Op:
    __slots__ = ("eng", "calls", "reads", "writes", "dur", "ts", "kind", "dma", "idx", "lat")


def _free_size(ap):
    try:
        return int(ap.free_size())
    except Exception:
        return 512


def _estimate(engname, calls):
    d = 0.0
    ts = None
    for (m, a, k) in calls:
        name = getattr(m, "__name__", "")
        if engname == "PE":
            ap = k.get("rhs", k.get("identity"))
            n = _free_size(ap) if ap is not None else 128
            d += (max(n, 48) + 16) / 1900.0
        elif engname == "ACT":
            ap = k.get("in_")
            n = _free_size(ap) if ap is not None else 512
            d += 0.2 + n / 1100.0
            f = k.get("func")
            if f == AF.Exp:
                ts = "E"
            elif f == AF.Tanh:
                ts = "T"
            elif f == AF.Sqrt:
                ts = "S"
            elif f == AF.Ln:
                ts = "L"
            elif f == AF.Gelu_apprx_tanh:
                ts = "G"
        elif engname == "DVE":
            ap = k.get("in0", k.get("in_", k.get("data0", k.get("out", k.get("ap")))))
            if ap is None and a:
                ap = a[0]
            n = _free_size(ap) if ap is not None else 512
            mult = 2.0 if "scan" in name else 1.0
            d += 0.1 + mult * n / 760.0
        else:
            ap = k.get("in0", k.get("in_", k.get("out", k.get("ap"))))
            if ap is None and a:
                ap = a[0]
            n = _free_size(ap) if ap is not None else 512
            d += 0.25 + n / 450.0
    return d, ts


class _Dummy:
    def then_inc(self, *a, **k):
        return self


class Rec:
    def __init__(self, real):
        object.__setattr__(self, "_real", real)

    def __getattr__(self, name):
        real_m = getattr(object.__getattribute__(self, "_real"), name)

        def f(*a, **k):
            Ctx.cap.append((real_m, a, k))
            return _Dummy()
        f.__name__ = name
        return f


class NcProxy:
    def __init__(self, real):
        self._real = real
        self.tensor = Rec(real.tensor)
        self.vector = Rec(real.vector)
        self.scalar = Rec(real.scalar)
        self.gpsimd = Rec(real.gpsimd)

    def __getattr__(self, name):
        return getattr(self._real, name)


class Eng:
    def __init__(self, h, semc, name):
        self.h = h
        self.sc = semc
        self.name = name
        self.waited = {}

    def wait(self, tok):
        sc, val = tok
        if sc.is_dma:
            val = sc.cnt
        if self.waited.get(id(sc), 0) >= val:
            return
        self.h.wait_ge(sc.s, val)
        self.waited[id(sc)] = val

    def _deps(self, reads, writes):
        for b in reads:
            if b.w is not None:
                self.wait(b.w)
        for b in writes:
            if b.w is not None:
                self.wait(b.w)
            for t in b.r:
                self.wait(t)

    def op(self, fn, reads=(), writes=(), sig=True):
        del Ctx.cap[:]
        fn()
        calls = list(Ctx.cap)
        reads = list(reads)
        writes = list(writes)
        if Ctx.pend is not None:
            pc, pr, pw = Ctx.pend
            calls = pc + calls
            reads = pr + [b for b in reads if b not in pr]
            writes = pw + [b for b in writes if b not in pw]
            Ctx.pend = None
        if not sig:
            Ctx.pend = (calls, reads, writes)
            return
        ex = [b for b in reads if b.excl]
        if ex:
            writes = writes + [b for b in ex if b not in writes]
            reads = [b for b in reads if not b.excl]
        o = Op()
        o.eng, o.calls, o.reads, o.writes, o.kind, o.dma = self, calls, reads, writes, "c", None
        o.dur, o.ts = _estimate(self.name, calls)
        o.lat = 0.0
        o.idx = len(Ctx.ops)
        Ctx.ops.append(o)

    def emit(self, o):
        self._deps(o.reads, o.writes)
        inst = None
        for (m, a, k) in o.calls:
            inst = m(*a, **k)
        inst.then_inc(self.sc.s, 1)
        self.sc.cnt += 1
        tok = (self.sc, self.sc.cnt)
        for b in o.reads:
            b.r.append(tok)
        for b in o.writes:
            b.w = tok
            b.r = []


def schedule(ops):
    n = len(ops)
    if not hasattr(schedule, 'n_setup'):
        schedule.n_setup = 0
    lastw, readers = {}, {}
    preds = [set() for _ in range(n)]
    for i, o in enumerate(ops):
        for b in o.reads:
            w = lastw.get(id(b))
            if w is not None:
                preds[i].add(w)
        for b in o.writes:
            w = lastw.get(id(b))
            if w is not None:
                preds[i].add(w)
            for r in readers.get(id(b), ()):
                preds[i].add(r)
        for b in o.reads:
            readers.setdefault(id(b), []).append(i)
        for b in o.writes:
            lastw[id(b)] = i
            readers[id(b)] = []
        preds[i].discard(i)
    succs = [[] for _ in range(n)]
    npred = [0] * n
    for i in range(n):
        npred[i] = len(preds[i])
        for p in preds[i]:
            succs[p].append(i)
    ready_time = [0.0] * n
    finish = [0.0] * n
    start = [0.0] * n
    engs = {}
    for o in ops:
        engs.setdefault(id(o.eng), o.eng)
    eng_free = {k: 0.0 for k in engs}
    ready = {k: [] for k in engs}
    cur_ts = [None]
    for i in range(n):
        if npred[i] == 0:
            ready[id(ops[i].eng)].append(i)
    done = 0
    order = []
    while done < n:
        best = None
        for k, lst in ready.items():
            if not lst:
                continue
            ef = eng_free[k]
            isact = engs[k].name == "ACT"
            for i in lst[:32]:
                st = max(ready_time[i], ef)
                pen = 0.0
                if isact and ops[i].ts is not None:
                    t_ = ops[i].ts
                    if t_ == "T":
                        if cur_ts[0] not in ("E", "G"):
                            pen = 1.3
                    elif t_ != cur_ts[0]:
                        pen = 1.3
                if i < schedule.n_setup:
                    st, pen = 0.0, 0.0
                key = (st + pen, i)
                if best is None or key < best[0]:
                    best = (key, i, k, st + pen)
        _, i, k, st = best
        o = ops[i]
        ready[k].remove(i)
        start[i] = st
        if o.kind == "dma":
            eng_free[k] = st + o.dur
            finish[i] = st + o.dur + o.lat
        else:
            finish[i] = st + o.dur
            eng_free[k] = finish[i]
            if o.eng.name == "ACT" and o.ts is not None:
                if o.ts == "T":
                    if cur_ts[0] not in ("E", "G"):
                        cur_ts[0] = "E"
                else:
                    cur_ts[0] = o.ts
        order.append(i)
        done += 1
        for sidx in succs[i]:
            hop = 0.0 if ops[sidx].eng is o.eng and o.kind != "dma" else 0.25
            rt = finish[i] + hop
            if rt > ready_time[sidx]:
                ready_time[sidx] = rt
            npred[sidx] -= 1
            if npred[sidx] == 0:
                lst = ready[id(ops[sidx].eng)]
                lo, hi = 0, len(lst)
                while lo < hi:
                    mid = (lo + hi) // 2
                    if lst[mid] < sidx:
                        lo = mid + 1
                    else:
                        hi = mid
                lst.insert(lo, sidx)
    schedule.start = start
    schedule.finish = finish
    return order, max(finish) if n else 0.0


def _flat(lst):
    out = []
    for x in lst:
        if isinstance(x, (list, tuple)):
            out.extend(_flat(x))
        elif x is not None:
            out.append(x)
    return out


def build_program(debug=False):
    nc = bass.Bass("TRN2", target_bir_lowering=False)

    def din(name, shape, dt=F32):
        return nc.dram_tensor(name, list(shape), dt, kind="ExternalInput").ap()

    def dout(name, shape, dt=F32):
        return nc.dram_tensor(name, list(shape), dt, kind="ExternalOutput").ap()

    x_p = din("x_prompt", [NSEQ, SEQ, D])
    x_s = din("x_sample", [NSAMP, D])
    ck = din("cache_k", [NSAMP, 128, 256])
    cv = din("cache_v", [NSAMP, 128, 256])
    s_conv = din("state_conv", [NSAMP, 3, D])
    s_lru = din("state_lru", [NSAMP, D])
    w_in = din("w_in", [D, 5632])
    w_out = din("w_out", [D, D])
    sinks = din("sinks", [16])
    conv_w = din("conv_w", [4, D])
    conv_b = din("conv_b", [D])
    lw_a = din("lru_w_a", [16, 64, 64])
    lb_a = din("lru_b_a", [D])
    lw_x = din("lru_w_x", [16, 64, 64])
    lb_x = din("lru_b_x", [D])
    lam = din("lru_lambda", [D])
    w_up = din("w_up", [D, 4096])
    w_down = din("w_down", [4096, D])
    g1 = din("g_pre_mix", [D])
    g2 = din("g_post_mix", [D])
    g3 = din("g_pre_ffn", [D])
    g4 = din("g_post_ffn", [D])
    c_ident = din("c_ident", [128, 128])
    c_mask = din("c_mask", [2, 128, 512])
    c_msamp = din("c_msamp", [16, 512])
    c_sel = din("c_sel", [128, 128])

    y_p = dout("y_prompt", [NSEQ, SEQ, D])
    y_s = dout("y_sample", [NSAMP, D])
    kwp = dout("k_win_prompt", [NSEQ, 128, 256])
    vwp = dout("v_win_prompt", [NSEQ, 128, 256])
    cvp = dout("conv_prompt", [NSEQ, 3, D])
    lrp = dout("lru_prompt", [NSEQ, D])
    kws = dout("k_win_sample", [NSAMP, 128, 256])
    vws = dout("v_win_sample", [NSAMP, 128, 256])
    cvs = dout("conv_sample", [NSAMP, 3, D])
    lrs = dout("lru_sample", [NSAMP, D])

    scr = nc.dram_tensor("wscr", [NPIECE, 128, 8, 512], BF16, kind="Internal").ap()
    if debug:
        dbg_m = dout("dbg_merged", [128, 8, 512], BF16)
        dbg_a = dout("dbg_attn", [128, 8, 512], BF16)
        dbg_x1 = dout("dbg_x1", [128, 4, D])

    with ExitStack() as es:
        def sb(name, shape, dt=F32):
            return es.enter_context(nc.sbuf_tensor(name, list(shape), dt))

        def ps(name, shape, dt=F32):
            return es.enter_context(nc.psum_tensor(name, list(shape), dt))

        nsem = [0]

        def newsem(is_dma):
            nsem[0] += 1
            return SemC(es.enter_context(nc.semaphore("s%d" % nsem[0])), is_dma)

        ring = [sb("ring%d" % i, [128, 8, 512], BF16) for i in range(NR)]
        ringB = [Buf("ring%d" % i) for i in range(NR)]
        xa = [sb("xa%d" % i, [128, D]) for i in range(2)]
        xaB = [Buf("xa") for _ in range(2)]
        xnb = [sb("xnb%d" % i, [128, D], BF16) for i in range(2)]
        xnbB = [Buf("xnb") for _ in range(2)]
        xnT = sb("xnT", [128, 8, TT], BF16)
        xnTB = [Buf("xnT%d" % b) for b in range(4)]
        xt = sb("xt", [128, 4, D])
        xtB = [Buf("xt%d" % b) for b in range(4)]
        kbuf = sb("kbuf", [128, 2, 640], BF16)
        kbufB = Buf("kbuf")
        vbuf = sb("vbuf", [128, 5, 256], BF16)
        vbufB = Buf("vbuf")
        USL = 40
        U = sb("U", [128, USL * 256])
        UB = [Buf("U%d" % i) for i in range(USL)]

        def uview_bf(slot0, nslots):
            return U[:, slot0 * 256:(slot0 + nslots) * 256].bitcast(BF16).rearrange(
                "p (c t) -> p c t", t=512)

        qT = uview_bf(0, 8)
        sa = uview_bf(8, 8)
        merged = uview_bf(16, 8)
        gs = uview_bf(24, 4)
        ubuf = U[:, 28 * 256:28 * 256 + 4 * 516].rearrange("p (c t) -> p c t", t=516)
        hT = uview_bf(0, 32)
        fbuf = U[:, 32 * 256:40 * 256].rearrange("p (b t) -> p b t", t=512)

        def qTB(j): return [UB[j]]
        def saB(c): return [UB[8 + c]]
        def mgB(c): return [UB[16 + c]]
        def gsB(c): return [UB[24 + c % 4]]
        def ubB(c):
            lo = 28 * 1024 + (c % 4) * 2064
            return [UB[i] for i in range(lo // 1024, (lo + 2063) // 1024 + 1)]
        def hTB(fc): return [UB[fc]]
        def fbB(b): return [UB[32 + 2 * b], UB[33 + 2 * b]]

        class Pool_:
            def __init__(self, name, n, dt):
                self.free = [(sb("%s%d" % (name, i), [128, 512], dt), Buf(name)) for i in range(n)]

            def get(self):
                assert self.free, "temp pool exhausted"
                return self.free.pop(0)

            def put(self, t):
                self.free.append(t)

        TF = Pool_("tf", 14, F32)
        TB = Pool_("tb", 12, BF16)
        cnt = {"st": 0, "xa": 0, "pf": 0, "pt": 0, "au": 0}

        NST = 12
        stt = sb("stt", [128, NST, 8])
        sttB = [Buf("st") for _ in range(NST)]

        def gst():
            i = cnt["st"] % NST
            cnt["st"] += 1
            return stt[:, i, :], sttB[i]

        stg = sb("stg", [128, D])
        stgB = Buf("stg")
        kvst = sb("kvst", [128, 512])
        kvstB = Buf("kvst")
        hst = sb("hst", [128, 8])
        hstB = Buf("hst")
        uh = sb("uh", [128, 8, 3])
        uhB = Buf("uh")
        kcf = [sb("kcf%d" % i, [128, 256]) for i in range(2)]
        kcfB = [Buf("kcf") for _ in range(2)]
        kc = [sb("kc%d" % i, [128, 2, 128], BF16) for i in range(2)]
        kcB = [Buf("kc") for _ in range(2)]
        vc = [sb("vc%d" % i, [128, 256], BF16) for i in range(2)]
        vcB = [Buf("vc") for _ in range(2)]
        scT = sb("scT", [128, 8, 48])
        h0T = sb("h0T", [128, 8, 16])
        sampB = Buf("samp")
        hS_t = sb("hS_t", [128, 128])
        hS_B = Buf("hS")
        uS_t = sb("uS_t", [128, 128])
        ssb_t = sb("ssb_t", [128, 8])
        ssb_B = Buf("ssb")
        gbc = sb("gbc", [128, 4, D])
        identf = sb("identf", [128, 128])
        identb = sb("identb", [128, 128], BF16)
        maskb = sb("maskb", [128, 2, 512], BF16)
        msamp = sb("msamp", [128, 256], BF16)
        selb = sb("selb", [128, 128], BF16)
        sink8 = sb("sink8", [8, 2])
        wbd = sb("wbd", [128, 2, 8, 128], BF16)
        cw = sb("cw", [128, 4, 8])
        pv = sb("pv", [128, 8, 8])
        sinkp = sb("sinkp", [128, 16])
        epsT = sb("epsT", [128, 2])
        cI, cM, cG, cS, cV, cW, cE = (Buf("cI"), Buf("cM"), Buf("cG"), Buf("cS"), Buf("cV"), Buf("cW"), Buf("cE"))
        constB = [cI, cM, cG, cS, cV, cW, cE]

        NPF = 6
        pf = [ps("pf%d" % i, [128, 512]) for i in range(NPF)]
        pfB = [Buf("pf%d" % i, excl=True) for i in range(NPF)]
        ptt = [ps("pt%d" % i, [128, 1024], BF16) for i in range(2)]
        ptB = [Buf("pt%d" % i, excl=True) for i in range(2)]

        es.enter_context(nc.Block())

        PE = Eng(nc.tensor, newsem(False), "PE")
        ACT = Eng(nc.scalar, newsem(False), "ACT")
        DVE = Eng(nc.vector, newsem(False), "DVE")
        POOL = Eng(nc.gpsimd, newsem(False), "POOL")
        SP = Eng(nc.sync, newsem(False), "SP")
        nc_real = nc
        nc = NcProxy(nc_real)
        Ctx.recording = True
        Ctx.ops = []
        Ctx.cap = []
        Ctx.pend = None

        all_dma_sems = []

        def dma(q, out, in_, reads=(), writes=(), nonc=False):
            reads = _flat(reads)
            writes = _flat(writes)
            o = Op()
            o.eng, o.calls, o.reads, o.writes, o.kind = q, None, reads, writes, "dma"
            o.dma = (out, in_, nonc)
            o.ts = None
            try:
                nbytes = int(out.nbytes())
            except Exception:
                nbytes = 4096
            cast = (out.dtype != in_.dtype)
            o.dur = (1.0 if q is POOL else 0.15) + (nbytes / 80e3 if cast else 0.0)
            o.lat = 2.0 + nbytes / 150e3
            o.idx = len(Ctx.ops)
            Ctx.ops.append(o)

        def dma_emit(o):
            q = o.eng
            out, in_, nonc = o.dma
            reads, writes = o.reads, o.writes
            q._deps(reads, writes)
            tgt = (writes + reads)[0]
            if tgt.dsem is None:
                tgt.dsem = {}
            if id(q) not in tgt.dsem:
                tgt.dsem[id(q)] = newsem(True)
                all_dma_sems.append(tgt.dsem[id(q)])
            sc = tgt.dsem[id(q)]
            if nonc:
                with nc_real.allow_non_contiguous_dma(reason="small strided"):
                    q.h.dma_start(out=out, in_=in_).then_inc(sc.s, 16)
            else:
                q.h.dma_start(out=out, in_=in_).then_inc(sc.s, 16)
            sc.cnt += 16
            tok = (sc, sc.cnt)
            for b in reads:
                b.r.append(tok)
            for b in writes:
                b.w = tok
                b.r = []

        def A(fn, r=(), w=()):
            return ACT.op(fn, _flat(r), _flat(w))

        def V(fn, r=(), w=()):
            return DVE.op(fn, _flat(r), _flat(w))

        def G(fn, r=(), w=()):
            return POOL.op(fn, _flat(r), _flat(w))

        def P(fn, r=(), w=(), sig=True):
            return PE.op(fn, _flat(r), _flat(w), sig=sig)

        act = nc.scalar.activation
        mm = nc.tensor.matmul

        dma(SP, identf[:, :], c_ident[:, :], writes=[cI])
        dma(POOL, identb[:, :], c_ident[:, :], writes=[cI])
        dma(POOL, maskb[:, :, :], c_mask.rearrange("v p c -> p v c"), writes=[cM])
        G(lambda: nc.gpsimd.memset(msamp[:, :], 0.0), w=[cM])
        dma(POOL, msamp[0:16, :], c_msamp[:, 0:256], writes=[cM])
        dma(POOL, selb[:, :], c_sel[:, :], writes=[cM])
        sk8 = sinks.rearrange("(a g hb) -> hb g a", a=2, g=4, hb=2)
        for hb in range(2):
            dma(SP, sink8[hb * 4:(hb + 1) * 4, :], sk8[hb], writes=[cS], nonc=True)
        for i, g in enumerate((g1, g2, g3, g4)):
            dma(SP, gbc[:, i, :], g.partition_broadcast(128), writes=[cG])
        dma(SP, sinkp[:, :], sinks.partition_broadcast(128), writes=[cS])
        G(lambda: nc.gpsimd.memset(wbd[:, :, :, :], 0.0), w=[cW])
        G(lambda: nc.gpsimd.memset(epsT[:, 0:1], EPS), w=[cE])
        G(lambda: nc.gpsimd.memset(epsT[:, 1:2], 1.0), w=[cE])
        G(lambda: nc.gpsimd.memset(kbuf[:, :, :], 0.0), w=[kbufB])
        G(lambda: nc.gpsimd.memset(vbuf[:, :, :], 0.0), w=[vbufB])
        G(lambda: nc.gpsimd.memset(U[:, :], 0.0), w=UB)
        for gi, lw in enumerate((lw_a, lw_x)):
            lwv = lw.rearrange("(a hb g) ci d -> hb ci a g d", a=2, hb=2, g=4)
            for hb in range(2):
                for a in range(2):
                    dma(POOL, wbd[hb * 64:(hb + 1) * 64, gi, a * 4:(a + 1) * 4, hb * 64:(hb + 1) * 64],
                        lwv[hb][:, a], writes=[cW])
        schedule.n_setup = len(Ctx.ops)
        scrB = [Buf("scr%d" % i) for i in range(NPIECE)]
        win_v = w_in.rearrange("(k p) e -> p k e", p=128)

        def head_cols(base, h):
            return win_v[:, :, base + h * 64: base + (h + 1) * 64]

        specs = [[] for _ in range(NPIECE)]

        def spec(pi, c0, c1, src, p0=0, p1=128, k0=0, k1=8):
            specs[pi].append((p0, p1, k0, k1, c0, c1, src))

        for pi in range(2):
            for jj in range(4):
                j = pi * 4 + jj
                a, g = j // 4, j % 4
                spec(pi, jj * 128, jj * 128 + 64, head_cols(0, 8 * a + g))
                spec(pi, jj * 128 + 64, jj * 128 + 128, head_cols(0, 8 * a + 4 + g))
        spec(2, 0, 512, win_v[:, :, 1024:1536])
        for cp in range(4):
            for which, bases in ((0, (2560, 4608)), (1, (1536, 3584))):
                pi = 3 + 2 * cp + which
                for bi, base in enumerate(bases):
                    for ci in range(2):
                        c = 2 * cp + ci
                        a, g = c // 4, c % 4
                        col = (bi * 2 + ci) * 128
                        spec(pi, col, col + 64, head_cols(base, 8 * a + g))
                        spec(pi, col + 64, col + 128, head_cols(base, 8 * a + 4 + g))
        wo_v = w_out.rearrange("(a hb g p) e -> hb p a g e", a=2, hb=2, g=4)
        for hf in range(2):
            for hb in range(2):
                for a in range(2):
                    spec(11 + hf, 0, 512, wo_v[hb][:, a, :, hf * 512:(hf + 1) * 512],
                         p0=hb * 64, p1=(hb + 1) * 64, k0=a * 4, k1=(a + 1) * 4)
        wu_v = w_up.rearrange("(k p) e -> p k e", p=128)
        for q in range(8):
            spec(13 + q, 0, 512, wu_v[:, :, q * 512:(q + 1) * 512])
        wd_v = w_down.rearrange("(q k p) e -> q p k e", k=8, p=128)
        for hf in range(2):
            for q in range(4):
                spec(21 + hf * 4 + q, 0, 512, wd_v[q][:, :, hf * 512:(hf + 1) * 512])

        ntiles = 1 + NSEQ * (SEQ // TT)
        total_pieces = ntiles * NPIECE
        wstate = {"loaded": 0}

        def load_upto(n):
            while wstate["loaded"] < min(n, total_pieces):
                g = wstate["loaded"]
                pi = g % NPIECE
                s_ = g % NR
                if g < NPIECE:
                    for (p0, p1, k0, k1, c0, c1, src) in specs[pi]:
                        dma(POOL, ring[s_][p0:p1, k0:k1, c0:c1], src, writes=[ringB[s_]])
                    if ntiles > 1:
                        dma(SP, scr[pi], ring[s_][:, :, :], reads=[ringB[s_]], writes=[scrB[pi]])
                else:
                    dma(SP, ring[s_][:, :, :], scr[pi], reads=[scrB[pi]], writes=[ringB[s_]])
                wstate["loaded"] += 1

        def piece(tile_idx, pi, held=0):
            g = tile_idx * NPIECE + pi
            load_upto(g + NR - held)
            s_ = g % NR
            return ring[s_], ringB[s_]

        def gpf():
            i = cnt["pf"] % 2
            cnt["pf"] += 1
            return pf[i], pfB[i]

        def gpt():
            i = cnt["pt"] % 2
            cnt["pt"] += 1
            return ptt[i], ptB[i]

        def rstd_from(ss_ap, ssB, n):
            st, stB = gst()
            A(lambda: act(out=st[:n, 0:1], in_=ss_ap, func=AF.Sqrt, bias=epsT[:n, 0:1], scale=1.0 / D),
              r=[ssB, constB], w=[stB])
            V(lambda: nc.vector.reciprocal(out=st[:n, 1:2], in_=st[:n, 0:1]), r=[stB], w=[stB])
            return st[:n, 1:2], stB

        def norm_a(src_ap, srcB, gi, bt):
            i = cnt["xa"] % 2
            cnt["xa"] += 1
            xb, xbB = xnb[i], xnbB[i]
            st, stB = gst()
            A(lambda: act(out=xb[:bt, :], in_=src_ap, func=AF.Square, accum_out=st[:bt, 0:1]),
              r=[srcB], w=[xbB, stB])
            rs, rsB = rstd_from(st[:bt, 0:1], stB, bt)
            V(lambda: nc.vector.scalar_tensor_tensor(out=xb[:bt, :], in0=src_ap, scalar=rs, in1=gbc[:bt, gi, :],
                                                     op0=ALU.mult, op1=ALU.mult),
              r=[srcB, rsB, constB], w=[xbB])
            return i

        def norm_b(i, blk, bt):
            xb, xbB = xnb[i], xnbB[i]
            pt, ptb = gpt()
            ptv = pt[:, :].rearrange("p (k t) -> p k t", t=128)
            for k in range(8):
                P(lambda k=k: nc.tensor.transpose(out=ptv[:, k, :bt], in_=xb[:bt, k * 128:(k + 1) * 128],
                                                  identity=identb[:bt, :bt]),
                  r=[xbB, constB], w=[ptb], sig=(k == 7))
            A(lambda: act(out=xnT[:, :, blk * 128:blk * 128 + bt], in_=ptv[:, :, :bt], func=AF.Copy),
              r=[ptb], w=[xnTB[blk]])

        def proj(rg, rgB, cc, NT, nbk):
            pb, pbB = gpf()
            for k in range(8):
                P(lambda k=k: mm(pb[:, :NT], lhsT=rg[:, k, cc * 128:(cc + 1) * 128], rhs=xnT[:, k, :NT],
                                 start=(k == 0), stop=(k == 7)),
                  r=[rgB, xnTB[:nbk]], w=[pbB], sig=(k == 7))
            return pb, pbB

        def dma_perm_out(q, dst_dram, src_sb, n, reads):
            dv = dst_dram.rearrange("n (a hb g d) -> n a hb g d", a=2, hb=2, g=4, d=64)
            sv_ = src_sb.rearrange("n (a g hb d) -> n a hb g d", a=2, hb=2, g=4, d=64)
            for a in range(2):
                for hb in range(2):
                    dma(q, dv[:, a, hb], sv_[:, a, hb], reads=reads)

        def dma_perm_in(q, dst_sb, src_dram, n, writes):
            sv_ = src_dram.rearrange("n (a hb g d) -> n a hb g d", a=2, hb=2, g=4, d=64)
            dv = dst_sb.rearrange("n (a g hb d) -> n a hb g d", a=2, hb=2, g=4, d=64)
            for a in range(2):
                for hb in range(2):
                    dma(q, dv[:, a, hb], sv_[:, a, hb], writes=writes)

        def vec_setup():
            dma_perm_in(SP, stg[0:4, :], conv_w[:, :], 4, [stgB])
            for i, v in enumerate((conv_b, lb_a, lb_x, lam)):
                dma_perm_in(SP, stg[4 + i:5 + i, :], v.rearrange("(o d) -> o d", o=1), 1, [stgB])
            pb, pbB = pf[0], pfB[0]
            for c in range(8):
                P(lambda c=c: nc.tensor.transpose(out=pb[:, c * 8:(c + 1) * 8], in_=stg[0:8, c * 128:(c + 1) * 128],
                                                  identity=identf[0:8, 0:8]),
                  r=[stgB, cI], w=[pbB], sig=(c == 7))
            pb3 = pb[:, 0:64].rearrange("p (c v) -> p v c", v=8)
            V(lambda: nc.vector.tensor_copy(out=cw[:, :, :], in_=pb3[:, 0:4, :]), r=[pbB], w=[cV])
            V(lambda: nc.vector.tensor_copy(out=pv[:, 0:4, :], in_=pb3[:, 4:8, :]), r=[pbB], w=[cV])
            A(lambda: act(out=pv[:, 6, :], in_=pv[:, 3, :], func=AF.Exp, scale=-1.0), r=[cV, cE], w=[cV])
            A(lambda: act(out=pv[:, 6, :], in_=pv[:, 6, :], func=AF.Ln, bias=epsT[:, 1:2], scale=1.0),
              r=[cV, cE], w=[cV])
            V(lambda: nc.vector.tensor_scalar(out=pv[:, 4, :], in0=pv[:, 6, :], scalar1=-4.0, scalar2=None,
                                              op0=ALU.mult), r=[cV, cE], w=[cV])
            V(lambda: nc.vector.tensor_scalar(out=pv[:, 5, :], in0=pv[:, 6, :], scalar1=-8.0, scalar2=None,
                                              op0=ALU.mult), r=[cV, cE], w=[cV])
            for i_ in (1, 2):
                V(lambda i_=i_: nc.vector.tensor_scalar(out=pv[:, i_, :], in0=pv[:, i_, :], scalar1=0.5, scalar2=None,
                                                        op0=ALU.mult), r=[cV, cE], w=[cV])

        def out_rows_from_chunks(srcs, n, srcBs, dst_dram):
            for half in range(2):
                pb, pbB = gpf()
                for cc in range(4):
                    c = half * 4 + cc
                    P(lambda c=c, cc=cc: nc.tensor.transpose(out=pb[:n, cc * 128:(cc + 1) * 128], in_=srcs[c],
                                                             identity=identf[:, :]),
                      r=[srcBs, constB], w=[pbB], sig=(cc == 3))
                V(lambda: nc.vector.tensor_copy(out=stg[:n, half * 512:(half + 1) * 512], in_=pb[:n, :]),
                  r=[pbB], w=[stgB])
            dma_perm_out(SP, dst_dram, stg[:n, :], n, [stgB])

        class AttnPipe:
            def __init__(self):
                self.q2 = []
                self.q3 = []

            def push(self, u):
                st1 = self.phase1(u)
                self.q2.append(st1)
                if len(self.q2) > 3:
                    self.q3.append(self.phase2(self.q2.pop(0)))
                if len(self.q3) > 1:
                    self.phase3(self.q3.pop(0))

            def drain(self):
                while self.q2:
                    self.q3.append(self.phase2(self.q2.pop(0)))
                    if len(self.q3) > 1:
                        self.phase3(self.q3.pop(0))
                while self.q3:
                    self.phase3(self.q3.pop(0))

            def phase1(self, u):
                nq, j, qcols, kparts, maskmm, nk = u["nq"], u["j"], u["qcols"], u["kparts"], u["mask"], u["nk"]
                par = cnt["au"] % 3
                cnt["au"] += 1
                bi_ = (2, 3, 5)[par]
                bank, bB = pf[bi_], pfB[bi_]
                sv = bank[:, :].rearrange("p (h k) -> p h k", h=2)
                nparts = len(kparts)
                for h in range(2):
                    P(maskmm(h, sv[:nq, h, :nk]), r=[constB], w=[bB], sig=False)
                    off = 0
                    for pi_, (kfn, n, kb) in enumerate(kparts):
                        last = (h == 1 and pi_ == nparts - 1)
                        P(lambda h=h, off=off, n=n, kfn=kfn, last=last: mm(
                            sv[:nq, h, off:off + n], lhsT=qT[h * 64:(h + 1) * 64, j, qcols[0]:qcols[1]],
                            rhs=kfn(h), start=False, stop=last),
                          r=[qTB(j), kb], w=[bB], sig=last)
                        off += n
                st, stB = gst()
                V(lambda: nc.vector.tensor_reduce(out=st[:nq, 2:4], in_=sv[:nq, :, :nk], axis=AX.X, op=ALU.max,
                                                  negate=True), r=[bB], w=[stB])
                V(lambda: nc.vector.tensor_tensor(out=st[:nq, 6:8], in0=st[:nq, 2:4], in1=sinkp[:nq, 2 * j:2 * j + 2],
                                                  op=ALU.add), r=[stB, constB], w=[stB])
                Et = TB.get()
                E, EB = Et
                Ev = E[:, :].rearrange("p (h k) -> p h k", h=2)
                for h in range(2):
                    A(lambda h=h: act(out=Ev[:nq, h, :nk], in_=sv[:nq, h, :nk], func=AF.Exp,
                                      bias=st[:nq, 2 + h:3 + h], scale=1.0, accum_out=st[:nq, 4 + h:5 + h]),
                      r=[bB, stB], w=[EB, stB])
                A(lambda: act(out=st[:nq, 6:8], in_=st[:nq, 6:8], func=AF.Exp), r=[stB], w=[stB])
                V(lambda: nc.vector.tensor_tensor(out=st[:nq, 6:8], in0=st[:nq, 6:8], in1=st[:nq, 4:6], op=ALU.add),
                  r=[stB], w=[stB])
                V(lambda: nc.vector.reciprocal(out=st[:nq, 6:8], in_=st[:nq, 6:8]), r=[stB], w=[stB])
                Pt = TB.get()
                Pn, PnB = Pt
                Pv = Pn[:, :].rearrange("p (h k) -> p h k", h=2)
                for h in range(2):
                    V(lambda h=h: nc.vector.tensor_scalar(out=Pv[:nq, h, :nk], in0=Ev[:nq, h, :nk],
                                                          scalar1=st[:nq, 6 + h:7 + h], scalar2=None, op0=ALU.mult),
                      r=[EB, stB], w=[PnB])
                TB.put(Et)
                return (u, Pt)

            def phase2(self, state):
                u, Pt = state
                nq, vparts = u["nq"], u["vparts"]
                Pn, PnB = Pt
                Pv = Pn[:, :].rearrange("p (h k) -> p h k", h=2)
                pt, ptb = gpt()
                ptv = pt[:, 0:512].rearrange("p (i t) -> p i t", t=128)
                nvp = len(vparts)
                for h in range(2):
                    for vi, (lv, K, (c0, c1), vb) in enumerate(vparts):
                        last = (h == 1 and vi == nvp - 1)
                        P(lambda h=h, vi=vi, K=K, c0=c0, c1=c1: nc.tensor.transpose(
                            out=ptv[:K, h * 2 + vi, :nq], in_=Pv[:nq, h, c0:c1], identity=identb[:nq, :nq]),
                          r=[PnB, constB], w=[ptb], sig=last)
                PTt = TB.get()
                PT, PTB = PTt
                PTv = PT[:, :].rearrange("p (i t) -> p i t", t=128)
                if nq == 128:
                    A(lambda: act(out=PTv[:, :, :], in_=ptv[:, :, :], func=AF.Copy), r=[ptb], w=[PTB])
                else:
                    for vi, (lv, K, cc_, vb) in enumerate(vparts):
                        A(lambda vi=vi, K=K: act(out=PTv[:K, vi::2, :nq], in_=ptv[:K, vi::2, :nq], func=AF.Copy),
                          r=[ptb], w=[PTB])
                TB.put(Pt)
                return (u, PTt)

            def phase3(self, state):
                u, PTt = state
                nq, j, vparts, ocols = u["nq"], u["j"], u["vparts"], u["ocols"]
                PT, PTB = PTt
                PTv = PT[:, :].rearrange("p (i t) -> p i t", t=128)
                ob, obB = pf[4], pfB[4]
                nvp = len(vparts)
                for h in range(2):
                    for vi, (lv, K, cc_, vb) in enumerate(vparts):
                        last = (h == 1 and vi == nvp - 1)
                        P(lambda h=h, vi=vi, K=K, lv=lv: mm(ob[:, h * 128:h * 128 + nq], lhsT=lv,
                                                            rhs=PTv[:K, h * 2 + vi, :nq],
                                                            start=(vi == 0), stop=(vi == nvp - 1)),
                          r=[PTB, vb], w=[obB], sig=last)
                V(lambda: nc.vector.tensor_copy(out=qT[0:64, j, ocols[0]:ocols[1]], in_=ob[0:64, 0:nq]),
                  r=[obB], w=qTB(j))
                V(lambda: nc.vector.tensor_copy(out=qT[64:128, j, ocols[0]:ocols[1]], in_=ob[64:128, 128:128 + nq]),
                  r=[obB], w=qTB(j))
                TB.put(PTt)

        tiles = [("p", seq, t0) for seq in range(NSEQ) for t0 in range(0, SEQ, TT)] + [("s", 0, 0)]

        def tinfo(ti):
            kind, seq, t0 = tiles[ti]
            sample = (kind == "s")
            return dict(sample=sample, seq=seq, t0=t0, NT=(NSAMP if sample else TT), nbk=(1 if sample else 4),
                        bt=(NSAMP if sample else 128),
                        xsrc=(x_s if sample else x_p[seq, t0:t0 + TT, :]))

        stA = {}
        marks = []
        build_program.marks = marks

        def stage_a_load(ti, blk):
            t = tinfo(ti)
            bt = t["bt"]
            i = cnt["xa"] % 2
            dma(SP, xa[i][:bt, :], t["xsrc"][blk * 128:blk * 128 + bt, :], writes=[xaB[i]])
            stA[(ti, blk)] = norm_a(xa[i][:bt, :], xaB[i], 0, bt)

        def stage_a_tr(ti, blk):
            t = tinfo(ti)
            norm_b(stA.pop((ti, blk)), blk, t["bt"])

        def run_tile(ti, prestaged):
            t = tinfo(ti)
            marks.append(("tile%d" % ti, len(Ctx.ops)))
            sample, seq, t0, NT, nbk, bt, xsrc = (t["sample"], t["seq"], t["t0"], t["NT"], t["nbk"], t["bt"],
                                                  t["xsrc"])
            first = (not sample) and t0 == 0
            lastt = (not sample) and t0 + TT == SEQ

            if not prestaged:
                for blk in range(nbk):
                    stage_a_load(ti, blk)
                    stage_a_tr(ti, blk)
            if sample:
                dma(SP, xt[:bt, 0, :], xsrc, writes=[xtB[0]])
            else:
                dma(SP, xt[:, :, :], xsrc.rearrange("(b p) d -> p b d", p=128), writes=xtB)

            if sample:
                dma_perm_in(SP, stg[:48, :], s_conv.rearrange("s i d -> (s i) d"), 48, [stgB])
                for half in range(2):
                    pb, pbB = gpf()
                    pv3 = pb[:, 0:192].rearrange("p (c t) -> p c t", t=48)
                    for cc in range(4):
                        c = half * 4 + cc
                        P(lambda c=c, cc=cc: nc.tensor.transpose(out=pv3[:, cc, :], in_=stg[:48, c * 128:(c + 1) * 128],
                                                                 identity=identf[:48, :48]),
                          r=[stgB, constB], w=[pbB], sig=(cc == 3))
                    V(lambda: nc.vector.tensor_copy(out=scT[:, half * 4:half * 4 + 4, :], in_=pv3[:, :, :]),
                      r=[pbB], w=[sampB])
                dma_perm_in(SP, stg[:16, :], s_lru[:, :], 16, [stgB])
                for half in range(2):
                    pb, pbB = gpf()
                    pv3 = pb[:, 0:64].rearrange("p (c t) -> p c t", t=16)
                    for cc in range(4):
                        c = half * 4 + cc
                        P(lambda c=c, cc=cc: nc.tensor.transpose(out=pv3[:, cc, :], in_=stg[:16, c * 128:(c + 1) * 128],
                                                                 identity=identf[:16, :16]),
                          r=[stgB, constB], w=[pbB], sig=(cc == 3))
                    V(lambda: nc.vector.tensor_copy(out=h0T[:, half * 4:half * 4 + 4, :], in_=pv3[:, :, :]),
                      r=[pbB], w=[sampB])
            elif first:
                V(lambda: nc.vector.memset(hst[:, :], 0.0), w=[hstB])
                V(lambda: nc.vector.memset(uh[:, :, :], 0.0), w=[uhB])

            if sample:
                qbdt = TB.get()
                qbd, qbdB = qbdt
                qbd5 = qbd[:, 0:256].rearrange("p (s a h g) -> p s a h g", s=16, a=2, h=2, g=4)
                V(lambda: nc.vector.memset(qbd[:, 0:256], 0.0), w=[qbdB])
            for pi in range(2):
                rg, rgB = piece(ti, pi)
                for cc in range(4):
                    j = pi * 4 + cc
                    pb, pbB = proj(rg, rgB, cc, NT, nbk)
                    if sample:
                        a_, g_ = j // 4, j % 4
                        V(lambda: nc.vector.tensor_scalar(out=qbd5[0:64, :, a_, 0, g_], in0=pb[0:64, :NT], scalar1=0.125,
                                                          scalar2=None, op0=ALU.mult), r=[pbB], w=[qbdB])
                        V(lambda: nc.vector.tensor_scalar(out=qbd5[64:128, :, a_, 1, g_], in0=pb[64:128, :NT], scalar1=0.125,
                                                          scalar2=None, op0=ALU.mult), r=[pbB], w=[qbdB])
                        continue
                    if True:
                        A(lambda: act(out=qT[:, j, :NT], in_=pb[:, :NT], func=AF.Copy, scale=0.125), r=[pbB], w=qTB(j))
                    else:
                        V(lambda: nc.vector.tensor_scalar(out=qT[:, j, :NT], in0=pb[:, :NT], scalar1=0.125, scalar2=None,
                                                          op0=ALU.mult), r=[pbB], w=qTB(j))
            rg, rgB = piece(ti, 2)
            for a in range(2):
                pb, pbB = proj(rg, rgB, a, NT, nbk)
                A(lambda: act(out=kbuf[:, a, 128:128 + NT], in_=pb[:, :NT], func=AF.Copy), r=[pbB], w=[kbufB])
            for blk in range(nbk):
                pb, pbB = gpf()
                c0_ = 0 if (sample or (lastt and blk == nbk - 1)) else 256
                for k in range(8):
                    P(lambda k=k: mm(pb[:bt, c0_:512], lhsT=xnT[:, k, blk * 128:blk * 128 + bt], rhs=rg[:, k, c0_:512],
                                     start=(k == 0), stop=(k == 7)),
                      r=[rgB, xnTB[blk]], w=[pbB], sig=(k == 7))
                A(lambda: act(out=vbuf[:bt, 1 + blk, :], in_=pb[:bt, 256:512], func=AF.Copy), r=[pbB], w=[vbufB])
                if sample or (lastt and blk == nbk - 1):
                    V(lambda: nc.vector.tensor_copy(out=kvst[:bt, :], in_=pb[:bt, :]), r=[pbB], w=[kvstB])
                    if sample:
                        dma(SP, kws[:, 127, :], kvst[:bt, 0:256], reads=[kvstB])
                        dma(SP, vws[:, 127, :], kvst[:bt, 256:512], reads=[kvstB])
                    else:
                        dma(SP, kwp[seq], kvst[:, 0:256], reads=[kvstB])
                        dma(SP, vwp[seq], kvst[:, 256:512], reads=[kvstB])

            if sample:
                oall, oallB = pf[4], pfB[4]
                oall4 = oall[:, 0:256].rearrange("p (s a m) -> p s a m", s=16, a=2, m=8)
                for s in range(NSAMP):
                    i = s % 2
                    dma(SP, kcf[i][:, :], ck[s], writes=[kcfB[i]])
                    dma(POOL, vc[i][:, :], cv[s], writes=[vcB[i]])
                    pb, pbB = gpf()
                    for a in range(2):
                        P(lambda a=a: nc.tensor.transpose(out=pb[:, a * 128:(a + 1) * 128],
                                                          in_=kcf[i][:, a * 128:(a + 1) * 128], identity=identf[:, :]),
                          r=[kcfB[i], constB], w=[pbB], sig=(a == 1))
                    V(lambda: nc.vector.tensor_copy(out=kc[i][:, :, :], in_=pb[:, 0:256].rearrange("p (a t) -> p a t", a=2)),
                      r=[pbB], w=[kcB[i]])
                    bank, bB = pf[2 + i], pfB[2 + i]
                    sv = bank[:, :].rearrange("p (a k) -> p a k", a=2)
                    for a in range(2):
                        P(lambda a=a: mm(sv[0:8, a, 0:144], lhsT=selb[:, s * 8:(s + 1) * 8], rhs=msamp[:, 0:144],
                                         start=(a == 0), stop=False), r=[constB], w=[bB], sig=False)
                        P(lambda a=a: mm(sv[0:8, a, 0:128], lhsT=qbd5[:, s, a, :, :], rhs=kc[i][:, a, :],
                                         start=False, stop=False), r=[qbdB, kcB[i]], w=[bB], sig=False)
                        P(lambda a=a: mm(sv[0:8, a, 128:144], lhsT=qbd5[:, s, a, :, :], rhs=kbuf[:, a, 128:144],
                                         start=False, stop=(a == 1)), r=[qbdB, kbufB], w=[bB], sig=(a == 1))
                    st, stB = gst()
                    V(lambda: nc.vector.tensor_reduce(out=st[0:8, 2:4], in_=sv[0:8, :, 0:144], axis=AX.X, op=ALU.max,
                                                      negate=True), r=[bB], w=[stB])
                    V(lambda: nc.vector.tensor_tensor(out=st[0:8, 6:8], in0=st[0:8, 2:4], in1=sink8[0:8, 0:2], op=ALU.add),
                      r=[stB, constB], w=[stB])
                    Et = TF.get(); E, EB = Et
                    Ev = E[:, :].rearrange("p (a k) -> p a k", a=2)
                    for a in range(2):
                        A(lambda a=a: act(out=Ev[0:8, a, 0:144], in_=sv[0:8, a, 0:144], func=AF.Exp,
                                          bias=st[0:8, 2 + a:3 + a], scale=1.0, accum_out=st[0:8, 4 + a:5 + a]),
                          r=[bB, stB], w=[EB, stB])
                    A(lambda: act(out=st[0:8, 6:8], in_=st[0:8, 6:8], func=AF.Exp), r=[stB], w=[stB])
                    V(lambda: nc.vector.tensor_tensor(out=st[0:8, 6:8], in0=st[0:8, 6:8], in1=st[0:8, 4:6], op=ALU.add),
                      r=[stB], w=[stB])
                    V(lambda: nc.vector.reciprocal(out=st[0:8, 6:8], in_=st[0:8, 6:8]), r=[stB], w=[stB])
                    Pt = TB.get(); Pn, PnB = Pt
                    Pv = Pn[:, :].rearrange("p (a k) -> p a k", a=2)
                    for a in range(2):
                        V(lambda a=a: nc.vector.tensor_scalar(out=Pv[0:8, a, 0:144], in0=Ev[0:8, a, 0:144],
                                                              scalar1=st[0:8, 6 + a:7 + a], scalar2=None, op0=ALU.mult),
                          r=[EB, stB], w=[PnB])
                    TF.put(Et)
                    pt, ptb = gpt()
                    ptv = pt[:, 0:32].rearrange("p (i t) -> p i t", t=8)
                    for a in range(2):
                        P(lambda a=a: nc.tensor.transpose(out=ptv[:, 2 * a, :], in_=Pv[0:8, a, 0:128],
                                                          identity=identb[0:8, 0:8]),
                          r=[PnB, constB], w=[ptb], sig=False)
                        P(lambda a=a: nc.tensor.transpose(out=ptv[0:16, 2 * a + 1, :], in_=Pv[0:8, a, 128:144],
                                                          identity=identb[0:8, 0:8]),
                          r=[PnB, constB], w=[ptb], sig=(a == 1))
                    PTt = TB.get(); PT, PTB = PTt
                    PTv = PT[:, 0:32].rearrange("p (i t) -> p i t", t=8)
                    A(lambda: act(out=PTv[:, 0::2, :], in_=ptv[:, 0::2, :], func=AF.Copy), r=[ptb], w=[PTB])
                    A(lambda: act(out=PTv[0:16, 1::2, :], in_=ptv[0:16, 1::2, :], func=AF.Copy), r=[ptb], w=[PTB])
                    TB.put(Pt)
                    for a in range(2):
                        P(lambda a=a: mm(oall4[:, s, a, :], lhsT=vc[i][:, a * 128:(a + 1) * 128], rhs=PTv[:, 2 * a, :],
                                         start=True, stop=False), r=[PTB, vcB[i]], w=[oallB], sig=False)
                        P(lambda a=a: mm(oall4[:, s, a, :], lhsT=vbuf[0:16, 1, a * 128:(a + 1) * 128],
                                         rhs=PTv[0:16, 2 * a + 1, :], start=False, stop=True),
                          r=[PTB, vbufB], w=[oallB], sig=(a == 1))
                    TB.put(PTt)
                oall5 = oall[:, 0:256].rearrange("p (s a h g) -> p s a h g", s=16, a=2, h=2, g=4)
                for hb in range(2):
                    for a in range(2):
                        V(lambda hb=hb, a=a: nc.vector.tensor_copy(
                            out=qT[hb * 64:(hb + 1) * 64, a * 4:(a + 1) * 4, 0:16],
                            in_=oall5[hb * 64:(hb + 1) * 64, :, a, hb, :].rearrange("p s g -> p g s")),
                          r=[oallB], w=[qTB(a * 4 + g_) for g_ in range(4)])
                TB.put(qbdt)

            units = []
            if sample:
                pass
            else:
                for blk in range(nbk):
                    for j in range(8):
                        units.append(("p", blk, j))
            upos = [0]
            pipe = AttnPipe()
            sstate = {}

            def sample_prep(s):
                i = s % 2
                dma(SP, kcf[i][:, :], ck[s], writes=[kcfB[i]])
                dma(POOL, vc[i][:, :], cv[s], writes=[vcB[i]])
                pb, pbB = gpf()
                for a in range(2):
                    P(lambda a=a: nc.tensor.transpose(out=pb[:, a * 128:(a + 1) * 128],
                                                      in_=kcf[i][:, a * 128:(a + 1) * 128], identity=identf[:, :]),
                      r=[kcfB[i], constB], w=[pbB], sig=(a == 1))
                V(lambda: nc.vector.tensor_copy(out=kc[i][:, :, :], in_=pb[:, 0:256].rearrange("p (a t) -> p a t", a=2)),
                  r=[pbB], w=[kcB[i]])

            def push_units(n):
                for _ in range(n):
                    if upos[0] >= len(units):
                        return
                    kind_, x_, j = units[upos[0]]
                    upos[0] += 1
                    a = j // 4
                    if kind_ == "s":
                        s = x_
                        i = s % 2
                        if j == 0:
                            sample_prep(s)
                        u = dict(nq=1, j=j, qcols=(s, s + 1), nk=144, ocols=(s, s + 1),
                                 kparts=[(lambda h, a=a, i=i: kc[i][h * 64:(h + 1) * 64, a, :], 128, [kcB[i]]),
                                         (lambda h, a=a: kbuf[h * 64:(h + 1) * 64, a, 128:144], 16, [kbufB])],
                                 vparts=[(vc[i][:, a * 128:(a + 1) * 128], 128, (0, 128), [vcB[i]]),
                                         (vbuf[:16, 1, a * 128:(a + 1) * 128], 16, (128, 144), [vbufB])],
                                 mask=(lambda h, out_ap, s=s: (lambda: mm(out_ap, lhsT=identb[:, s:s + 1],
                                                                          rhs=msamp[:, 0:144], start=(h == 0),
                                                                          stop=False))))
                    else:
                        blk = x_
                        mv = 1 if (first and blk == 0) else 0
                        u = dict(nq=128, j=j, qcols=(blk * 128, blk * 128 + 128), nk=256,
                                 ocols=(blk * 128, blk * 128 + 128),
                                 kparts=[(lambda h, a=a, blk=blk: kbuf[h * 64:(h + 1) * 64, a, blk * 128:blk * 128 + 256],
                                          256, [kbufB])],
                                 vparts=[(vbuf[:, blk, a * 128:(a + 1) * 128], 128, (0, 128), [vbufB]),
                                         (vbuf[:, blk + 1, a * 128:(a + 1) * 128], 128, (128, 256), [vbufB])],
                                 mask=(lambda h, out_ap, mv=mv: (lambda: mm(out_ap, lhsT=identb[:, :],
                                                                            rhs=maskb[:, mv, 0:256], start=(h == 0),
                                                                            stop=False))))
                    pipe.push(u)

            upp = (len(units) + 31) // 32

            hS, hSB = hS_t, hS_B

            def R1(c):
                sl = c % 4
                uct = TF.get(); uc, ucB = uct
                if sample:
                    V(lambda: nc.vector.tensor_scalar(out=uc[:, :NT], in0=scT[:, c, 0::3], scalar1=cw[:, 0, c:c + 1],
                                                      scalar2=pv[:, 0, c:c + 1], op0=ALU.mult, op1=ALU.add),
                      r=[sampB, constB], w=[ucB])
                    for i_ in (1, 2):
                        V(lambda i_=i_: nc.vector.scalar_tensor_tensor(out=uc[:, :NT], in0=scT[:, c, i_::3],
                                                                       scalar=cw[:, i_, c:c + 1], in1=uc[:, :NT],
                                                                       op0=ALU.mult, op1=ALU.add),
                          r=[sampB, constB], w=[ucB])
                    V(lambda: nc.vector.scalar_tensor_tensor(out=uc[:, :NT], in0=ubuf[:, sl, 3:3 + NT],
                                                             scalar=cw[:, 3, c:c + 1], in1=uc[:, :NT],
                                                             op0=ALU.mult, op1=ALU.add),
                      r=[ubB(c), constB], w=[ucB])
                    V(lambda: nc.vector.tensor_copy(out=uS_t[:, c * 16:(c + 1) * 16], in_=ubuf[:, sl, 3:3 + NT]),
                      r=ubB(c), w=[sampB])
                else:
                    V(lambda: nc.vector.tensor_scalar(out=uc[:, :NT], in0=ubuf[:, sl, 0:NT], scalar1=cw[:, 0, c:c + 1],
                                                      scalar2=pv[:, 0, c:c + 1], op0=ALU.mult, op1=ALU.add),
                      r=[ubB(c), constB], w=[ucB])
                    for i_ in (1, 2, 3):
                        V(lambda i_=i_: nc.vector.scalar_tensor_tensor(out=uc[:, :NT], in0=ubuf[:, sl, i_:i_ + NT],
                                                                       scalar=cw[:, i_, c:c + 1], in1=uc[:, :NT],
                                                                       op0=ALU.mult, op1=ALU.add),
                          r=[ubB(c), constB], w=[ucB])
                    V(lambda: nc.vector.tensor_copy(out=uh[:, c, :], in_=ubuf[:, sl, NT:NT + 3]),
                      r=ubB(c), w=[uhB])
                ucbt = TB.get(); ucb, ucbB = ucbt
                A(lambda: act(out=ucb[:, :NT], in_=uc[:, :NT], func=AF.Copy), r=[ucB], w=[ucbB])
                return dict(c=c, uct=uct, ucbt=ucbt)

            def R2(stt_):
                c = stt_["c"]
                ucb, ucbB = stt_["ucbt"]
                trt = TF.get(); tr, trB = trt
                tit = TF.get(); ti_, tiB = tit
                P(lambda: mm(pf[5][:, :NT], lhsT=wbd[:, 0, c, :], rhs=ucb[:, :NT], start=True, stop=True),
                  r=[ucbB, constB], w=[pfB[5]])
                A(lambda: act(out=tr[:, :NT], in_=pf[5][:, :NT], func=AF.Tanh, bias=pv[:, 1, c:c + 1], scale=0.5),
                  r=[pfB[5], constB], w=[trB])
                P(lambda: mm(pf[5][:, :NT], lhsT=wbd[:, 1, c, :], rhs=ucb[:, :NT], start=True, stop=True),
                  r=[ucbB, constB], w=[pfB[5]])
                A(lambda: act(out=ti_[:, :NT], in_=pf[5][:, :NT], func=AF.Tanh, bias=pv[:, 2, c:c + 1], scale=0.5),
                  r=[pfB[5], constB], w=[tiB])
                TB.put(stt_["ucbt"])
                stt_["trt"] = trt
                stt_["tit"] = tit

            def R3a(stt_):
                c = stt_["c"]
                tr, trB = stt_["trt"]
                aat = TF.get(); aa, aaB = aat
                A(lambda: act(out=aa[:, :NT], in_=tr[:, :NT], func=AF.Exp, bias=pv[:, 4, c:c + 1],
                              scale=pv[:, 4, c:c + 1]), r=[trB, constB], w=[aaB])
                A(lambda: act(out=tr[:, :NT], in_=tr[:, :NT], func=AF.Exp, bias=pv[:, 5, c:c + 1],
                              scale=pv[:, 5, c:c + 1]), r=[constB], w=[trB])
                stt_["aat"] = aat

            def R3b(stt_):
                tr, trB = stt_["trt"]
                A(lambda: act(out=tr[:, :NT], in_=tr[:, :NT], func=AF.Sqrt, bias=epsT[:, 1:2], scale=-1.0),
                  r=[constB], w=[trB])
                if first:
                    V(lambda: nc.vector.memset(tr[:, 0:1], 1.0), w=[trB])

            def R3c(stt_):
                c = stt_["c"]
                sl = c % 4
                tr, trB = stt_["trt"]
                ti_, tiB = stt_["tit"]
                aa, aaB = stt_["aat"]
                uc, ucB = stt_["uct"]
                V(lambda: nc.vector.scalar_tensor_tensor(out=ti_[:, :NT], in0=ti_[:, :NT], scalar=1.0, in1=tr[:, :NT],
                                                         op0=ALU.add, op1=ALU.mult), r=[trB], w=[tiB])
                V(lambda: nc.vector.scalar_tensor_tensor(out=ti_[:, :NT], in0=ti_[:, :NT], scalar=0.5, in1=uc[:, :NT],
                                                         op0=ALU.mult, op1=ALU.mult), r=[ucB], w=[tiB])
                if sample:
                    hv = hS[:, c * 16:(c + 1) * 16]
                    V(lambda: nc.vector.tensor_tensor(out=aa[:, :NT], in0=aa[:, :NT], in1=h0T[:, c, :], op=ALU.mult),
                      r=[sampB], w=[aaB])
                    V(lambda: nc.vector.tensor_tensor(out=hv, in0=aa[:, :NT], in1=ti_[:, :NT], op=ALU.add),
                      r=[aaB, tiB], w=[hSB])
                    hB_ = hSB
                else:
                    hv = tr[:, :NT]
                    hB_ = trB
                    V(lambda: nc.vector.tensor_tensor_scan(out=hv, data0=aa[:, :NT], data1=ti_[:, :NT],
                                                           initial=hst[:, c:c + 1], op0=ALU.mult, op1=ALU.add),
                      r=[aaB, tiB, hstB], w=[trB])
                    V(lambda: nc.vector.tensor_copy(out=hst[:, c:c + 1], in_=tr[:, NT - 1:NT]), r=[trB], w=[hstB])
                V(lambda: nc.vector.scalar_tensor_tensor(out=merged[:, c, :NT], in0=hv, scalar=0.5, in1=gs[:, sl, :NT],
                                                         op0=ALU.mult, op1=ALU.mult),
                  r=[hB_, gsB(c)], w=mgB(c))
                TF.put(stt_["uct"]); TF.put(stt_["trt"]); TF.put(stt_["tit"]); TF.put(stt_["aat"])

            marks.append(("rnnpieces", len(Ctx.ops)))
            prev = []
            for cp in range(4):
                c0, c1 = 2 * cp, 2 * cp + 1
                rg, rgB = piece(ti, 3 + 2 * cp)
                gts = []
                for ci in range(2):
                    pb, pbB = proj(rg, rgB, ci, NT, nbk)
                    gtt = TF.get()
                    g32, g32B = gtt
                    A(lambda: act(out=g32[:, :NT], in_=pb[:, :NT], func=AF.Gelu_apprx_tanh), r=[pbB], w=[g32B])
                    gts.append(gtt)
                    push_units(upp)
                    if prev:
                        R2(prev[ci])
                for ci in range(2):
                    c = c0 + ci
                    pb, pbB = proj(rg, rgB, 2 + ci, NT, nbk)
                    trt = TF.get()
                    A(lambda trt=trt: act(out=trt[0][:, :NT], in_=pb[:, :NT], func=AF.Tanh, scale=0.5),
                      r=[pbB], w=[trt[1]])
                    V(lambda trt=trt, ci=ci, c=c: nc.vector.scalar_tensor_tensor(
                        out=gs[:, c % 4, :NT], in0=trt[0][:, :NT], scalar=1.0, in1=gts[ci][0][:, :NT],
                        op0=ALU.add, op1=ALU.mult), r=[trt[1], gts[ci][1]], w=gsB(c))
                    TF.put(trt)
                    TF.put(gts[ci])
                    push_units(upp)
                    if prev:
                        if ci == 0:
                            R3a(prev[0]); R3a(prev[1])
                        else:
                            R3b(prev[0]); R3b(prev[1])
                            R3c(prev[0]); R3c(prev[1])
                prev = []
                rg, rgB = piece(ti, 4 + 2 * cp)
                for ci in range(2):
                    c = c0 + ci
                    sl = c % 4
                    if not sample:
                        V(lambda c=c, sl=sl: nc.vector.tensor_copy(out=ubuf[:, sl, 0:3], in_=uh[:, c, :]),
                          r=[uhB], w=ubB(c))
                    pb, pbB = proj(rg, rgB, ci, NT, nbk)
                    A(lambda sl=sl, c=c: act(out=ubuf[:, sl, 3:3 + NT], in_=pb[:, :NT], func=AF.Copy),
                      r=[pbB], w=ubB(c))
                    push_units(upp)
                    prev.append(R1(c))
                for ci in range(2):
                    c = c0 + ci
                    pb, pbB = proj(rg, rgB, 2 + ci, NT, nbk)
                    A(lambda c=c: act(out=sa[:, c, :NT], in_=pb[:, :NT], func=AF.Tanh, scale=0.5), r=[pbB], w=saB(c))
                    push_units(upp)
            push_units(len(units))
            R2(prev[0]); R2(prev[1])
            R3a(prev[0]); R3a(prev[1])
            R3b(prev[0]); R3b(prev[1])
            pipe.drain()
            R3c(prev[0]); R3c(prev[1])

            marks.append(("merge", len(Ctx.ops)))
            for c in range(8):
                tmt = TF.get(); tm, tmB = tmt
                V(lambda c=c: nc.vector.scalar_tensor_tensor(out=tm[:, :NT], in0=sa[:, c, :NT], scalar=1.0,
                                                             in1=qT[:, c, :NT], op0=ALU.add, op1=ALU.mult),
                  r=[saB(c), qTB(c)], w=[tmB])
                V(lambda c=c: nc.vector.scalar_tensor_tensor(out=merged[:, c, :NT], in0=tm[:, :NT], scalar=0.5,
                                                             in1=merged[:, c, :NT], op0=ALU.mult, op1=ALU.add),
                  r=[tmB], w=mgB(c))
                TF.put(tmt)
            if debug and not sample:
                dma(SP, dbg_m[:, :, :], merged[:, :, :], reads=[mgB(c) for c in range(8)])

            if sample:
                out_rows_from_chunks([hS[:, c * 16:(c + 1) * 16] for c in range(8)], 16, [hSB], lrs[:, :])
                out_rows_from_chunks([uS_t[:, c * 16:(c + 1) * 16] for c in range(8)], 16, [sampB], cvs[:, 2, :])
            elif lastt:
                pb, pbB = gpf()
                P(lambda: nc.tensor.transpose(out=pb[:8, 0:128], in_=hst[:, 0:8], identity=identf[:, :]),
                  r=[hstB, constB], w=[pbB])
                V(lambda: nc.vector.tensor_copy(out=stg[:8, 0:128], in_=pb[:8, 0:128]), r=[pbB], w=[stgB])
                lv_ = lrp[seq].rearrange("(a hb g d) -> a hb g d", a=2, hb=2, g=4, d=64)
                for a in range(2):
                    for hb in range(2):
                        dma(SP, lv_[a, hb], stg[a * 4:(a + 1) * 4, hb * 64:(hb + 1) * 64], reads=[stgB])
                out_rows_from_chunks([uh[:, c, :] for c in range(8)], 3, [uhB], cvp[seq])
            if not sample and not lastt:
                V(lambda: nc.vector.tensor_copy(out=kbuf[:, :, 0:128], in_=kbuf[:, :, 512:640]), r=[kbufB], w=[kbufB])
                V(lambda: nc.vector.tensor_copy(out=vbuf[:, 0, :], in_=vbuf[:, 4, :]), r=[vbufB], w=[vbufB])

            marks.append(("wout", len(Ctx.ops)))
            ro0, ro0B = piece(ti, 11)
            ro1, ro1B = piece(ti, 12, held=1)

            def wout_mm(blk):
                base = 2 * (blk % 2)
                for hf, (ro, roB) in enumerate(((ro0, ro0B), (ro1, ro1B))):
                    for k in range(8):
                        P(lambda k=k, hf=hf, ro=ro: mm(pf[base + hf][:bt, :], lhsT=merged[:, k, blk * 128:blk * 128 + bt],
                                                       rhs=ro[:, k, :], start=(k == 0), stop=(k == 7)),
                          r=[roB, mgB(k)], w=[pfB[base + hf]], sig=(k == 7))

            def wout_post(blk):
                base = 2 * (blk % 2)
                st, stB = gst()
                jkt = TB.get(); jk, jkB = jkt
                for hf in range(2):
                    A(lambda hf=hf: act(out=jk[:bt, :], in_=pf[base + hf][:bt, :], func=AF.Square,
                                        accum_out=st[:bt, 2 + hf:3 + hf]), r=[pfB[base + hf]], w=[jkB, stB])
                TB.put(jkt)
                V(lambda: nc.vector.tensor_tensor(out=st[:bt, 4:5], in0=st[:bt, 2:3], in1=st[:bt, 3:4], op=ALU.add),
                  r=[stB], w=[stB])
                rs, rsB = rstd_from(st[:bt, 4:5], stB, bt)
                for hf in range(2):
                    tmt = TF.get(); tm, tmB = tmt
                    V(lambda hf=hf: nc.vector.tensor_tensor(out=tm[:bt, :], in0=pf[base + hf][:bt, :],
                                                            in1=gbc[:bt, 1, hf * 512:(hf + 1) * 512], op=ALU.mult),
                      r=[pfB[base + hf], constB], w=[tmB])
                    V(lambda hf=hf: nc.vector.scalar_tensor_tensor(
                        out=xt[:bt, blk, hf * 512:(hf + 1) * 512], in0=tm[:bt, :], scalar=rs,
                        in1=xt[:bt, blk, hf * 512:(hf + 1) * 512], op0=ALU.mult, op1=ALU.add),
                      r=[tmB, rsB], w=[xtB[blk]])
                    TF.put(tmt)
                return norm_a(xt[:bt, blk, :], xtB[blk], 2, bt)

            wout_mm(0)
            pend = None
            for blk in range(nbk):
                if blk + 1 < nbk:
                    wout_mm(blk + 1)
                i_ = wout_post(blk)
                if pend is not None:
                    norm_b(pend[0], pend[1], bt)
                pend = (i_, blk)
            norm_b(pend[0], pend[1], bt)
            if debug and not sample:
                dma(SP, dbg_x1[:, :, :], xt[:, :, :], reads=xtB)

            marks.append(("ffnup", len(Ctx.ops)))
            for q in range(8):
                rg, rgB = piece(ti, 13 + q)
                for cc in range(4):
                    fc = q * 4 + cc
                    pb, pbB = proj(rg, rgB, cc, NT, nbk)
                    rlt = TB.get(); rl, rlB = rlt
                    A(lambda: act(out=rl[:, :NT], in_=pb[:, :NT], func=AF.Relu), r=[pbB], w=[rlB])
                    V(lambda: nc.vector.tensor_tensor(out=hT[:, fc, :NT], in0=rl[:, :NT], in1=rl[:, :NT], op=ALU.mult),
                      r=[rlB], w=hTB(fc))
                    TB.put(rlt)
            marks.append(("ffndown", len(Ctx.ops)))
            nxt = ti + 1 if ti + 1 < len(tiles) else None
            nsteps = []
            if nxt is not None:
                nb2 = tinfo(nxt)["nbk"]
                for b2 in range(nb2):
                    nsteps.append(("l", b2))
                    nsteps.append(("t", b2))
            step = [0]

            def next_stage_step(n=1):
                for _ in range(n):
                    if step[0] < len(nsteps):
                        k_, b2 = nsteps[step[0]]
                        step[0] += 1
                        if k_ == "l":
                            stage_a_load(nxt, b2)
                        else:
                            stage_a_tr(nxt, b2)

            ssb, ssbB = ssb_t[:, :], ssb_B
            for hf in range(2):
                for q in range(4):
                    rg, rgB = piece(ti, 21 + hf * 4 + q)
                    for kk in range(8):
                        fc = q * 8 + kk
                        for blk in range(nbk):
                            P(lambda kk=kk, fc=fc, blk=blk: mm(pf[2 + blk][:bt, :], lhsT=hT[:, fc, blk * 128:blk * 128 + bt],
                                                               rhs=rg[:, kk, :], start=(fc == 0), stop=(fc == 31)),
                              r=[rgB, hTB(fc)], w=[pfB[2 + blk]], sig=(kk == 7 and blk == nbk - 1))
                    next_stage_step(1)
                for blk in range(nbk):
                    jkt = TB.get(); jk, jkB = jkt
                    A(lambda blk=blk, hf=hf: act(out=jk[:bt, :], in_=pf[2 + blk][:bt, :], func=AF.Square,
                                                 accum_out=ssb[:bt, 2 * blk + hf:2 * blk + hf + 1]),
                      r=[pfB[2 + blk]], w=[jkB, ssbB])
                    TB.put(jkt)
                    if hf == 0:
                        V(lambda blk=blk: nc.vector.tensor_tensor(out=fbuf[:bt, blk, :], in0=pf[2 + blk][:bt, :],
                                                                  in1=gbc[:bt, 3, 0:512], op=ALU.mult),
                          r=[pfB[2 + blk], constB], w=fbB(blk))
            for blk in range(nbk):
                st, stB = gst()
                V(lambda blk=blk: nc.vector.tensor_tensor(out=st[:bt, 4:5], in0=ssb[:bt, 2 * blk:2 * blk + 1],
                                                          in1=ssb[:bt, 2 * blk + 1:2 * blk + 2], op=ALU.add),
                  r=[ssbB], w=[stB])
                rs, rsB = rstd_from(st[:bt, 4:5], stB, bt)
                V(lambda blk=blk: nc.vector.scalar_tensor_tensor(out=xt[:bt, blk, 0:512], in0=fbuf[:bt, blk, :], scalar=rs,
                                                                 in1=xt[:bt, blk, 0:512], op0=ALU.mult, op1=ALU.add),
                  r=[fbB(blk), rsB], w=[xtB[blk]])
                tmt = TF.get(); tm, tmB = tmt
                V(lambda blk=blk: nc.vector.tensor_tensor(out=tm[:bt, :], in0=pf[2 + blk][:bt, :],
                                                          in1=gbc[:bt, 3, 512:1024], op=ALU.mult),
                  r=[pfB[2 + blk], constB], w=[tmB])
                V(lambda blk=blk: nc.vector.scalar_tensor_tensor(out=xt[:bt, blk, 512:1024], in0=tm[:bt, :], scalar=rs,
                                                                 in1=xt[:bt, blk, 512:1024], op0=ALU.mult, op1=ALU.add),
                  r=[tmB, rsB], w=[xtB[blk]])
                TF.put(tmt)
            if sample:
                dma(SP, y_s[:, :], xt[:bt, 0, :], reads=[xtB[0]])
            else:
                dma(SP, y_p[seq, t0:t0 + TT, :].rearrange("(b p) d -> p b d", p=128), xt[:, :, :], reads=xtB)
            next_stage_step(len(nsteps))
            return nxt is not None

        outB = Buf("outcopy")
        dma(SP, kws[:, 0:127, :], ck[:, 1:128, :], writes=[outB])
        dma(SP, vws[:, 0:127, :], cv[:, 1:128, :], writes=[outB])
        dma(SP, cvs[:, 0:2, :], s_conv[:, 1:3, :], writes=[outB])

        vec_setup()
        pre = False
        for ti in range(len(tiles)):
            pre = run_tile(ti, pre)

        assert Ctx.pend is None
        order, est = schedule(Ctx.ops)
        build_program.est_us = est
        for i in order:
            o = Ctx.ops[i]
            if o.kind == "dma":
                dma_emit(o)
            else:
                o.eng.emit(o)
        nc = nc_real
        for sc in all_dma_sems:
            nc.gpsimd.wait_ge(sc.s, sc.cnt)
        for e in (PE, ACT, DVE, SP):
            nc.gpsimd.wait_ge(e.sc.s, e.sc.cnt)
    return nc


_CACHE = {}


def _consts():
    ident = np.eye(128, dtype=np.float32)
    qi = np.arange(128)[:, None]
    kj = np.arange(256)[None, :]
    diff = qi + 128 - kj
    band = (diff >= 0) & (diff <= 128)
    m0 = np.where(band, 0.0, NEG).astype(np.float32)
    m1 = m0.copy()
    m1[:, :128] = NEG
    mask = np.stack([np.concatenate([m0, m0], axis=1), np.concatenate([m1, m1], axis=1)])
    ms = np.full((16, 256), NEG, dtype=np.float32)
    ms[:, :128] = 0.0
    for s in range(16):
        ms[s, 128 + s] = 0.0
    msamp = np.concatenate([ms, ms], axis=1)
    sel = np.zeros((128, 128), dtype=np.float32)
    for s_ in range(16):
        sel[s_, s_ * 8:(s_ + 1) * 8] = 1.0
    return ident, np.ascontiguousarray(mask), np.ascontiguousarray(msamp), sel


def kernel(x_prompt, x_sample, cache_k_win, cache_v_win, state_conv, state_lru,
           w_in, w_out, sinks, conv_w, conv_b, lru_w_a, lru_b_a, lru_w_x, lru_b_x, lru_lambda,
           w_up, w_down, g_pre_mix, g_post_mix, g_pre_ffn, g_post_ffn):
    f = lambda a: np.ascontiguousarray(np.asarray(a, dtype=np.float32))
    if "nc" not in _CACHE:
        _CACHE["nc"] = build_program()
    nc = _CACHE["nc"]
    ident, mask, msamp, sel = _consts()
    sk = f(sinks)[0].reshape(2, 2, 4).transpose(0, 2, 1).reshape(16)
    shared = {
        "w_in": f(w_in)[0], "w_out": f(w_out)[0], "sinks": np.ascontiguousarray(sk),
        "conv_w": f(conv_w)[0], "conv_b": f(conv_b)[0],
        "lru_w_a": f(lru_w_a)[0], "lru_b_a": f(lru_b_a)[0], "lru_w_x": f(lru_w_x)[0], "lru_b_x": f(lru_b_x)[0],
        "lru_lambda": f(lru_lambda)[0], "w_up": f(w_up)[0], "w_down": f(w_down)[0],
        "g_pre_mix": f(g_pre_mix)[0], "g_post_mix": f(g_post_mix)[0],
        "g_pre_ffn": f(g_pre_ffn)[0], "g_post_ffn": f(g_post_ffn)[0],
        "c_ident": ident, "c_mask": mask, "c_msamp": msamp, "c_sel": sel,
    }
    xp = f(x_prompt)
    xs = f(x_sample)[:, 0, :]
    ckk = f(cache_k_win)[0].reshape(128, 128, 256)
    cvv = f(cache_v_win)[0].reshape(128, 128, 256)
    sc = f(state_conv)[0]
    sl = f(state_lru)[0]
    in_maps = []
    for i in range(NCORES):
        m = dict(shared)
        m["x_prompt"] = np.ascontiguousarray(xp[2 * i:2 * i + 2])
        m["x_sample"] = np.ascontiguousarray(xs[16 * i:16 * i + 16])
        m["cache_k"] = np.ascontiguousarray(ckk[16 * i:16 * i + 16])
        m["cache_v"] = np.ascontiguousarray(cvv[16 * i:16 * i + 16])
        m["state_conv"] = np.ascontiguousarray(sc[16 * i:16 * i + 16])
        m["state_lru"] = np.ascontiguousarray(sl[16 * i:16 * i + 16])
        in_maps.append(m)
    res = run_bass_kernel_spmd(nc, in_maps, core_ids=list(range(NCORES)))
    R = res.results
    cat = lambda k: np.concatenate([np.asarray(r[k], dtype=np.float32) for r in R], axis=0)
    y_prompt = cat("y_prompt")
    y_sample = cat("y_sample").reshape(128, 1, D)
    kwp = cat("k_win_prompt").reshape(1, 16, 128, 4, 64)
    vwp = cat("v_win_prompt").reshape(1, 16, 128, 4, 64)
    cvp = cat("conv_prompt").reshape(1, 16, 3, D)
    lrp = cat("lru_prompt").reshape(1, 16, D)
    kws = cat("k_win_sample").reshape(1, 128, 128, 4, 64)
    vws = cat("v_win_sample").reshape(1, 128, 128, 4, 64)
    cvs = cat("conv_sample").reshape(1, 128, 3, D)
    lrs = cat("lru_sample").reshape(1, 128, D)
    return (y_prompt, y_sample, kwp, vwp, cvp, lrp, kws, vws, cvs, lrs)
```
